# Optimizing a Trainium2 kernel written in Bass

```python
import math
import jax, jax.numpy as jnp
from jax import lax
import numpy as np

D_MODEL = 1024
BATCH = 8
SEQ = 4096
DEPTH = 2

GRID_W = 64
CTX_LEN = 256
N_HEADS = 8
Q_LORA = 256
KV_LORA = 128
QK_NOPE = 64
QK_ROPE = 32
V_HEAD = 64
ATT_WIDTH = N_HEADS * V_HEAD
ATT_SCALE = (QK_NOPE + QK_ROPE) ** -0.5
ROPE_BASE = 10000.0
Q_BLOCK = 128
FOURIER_WIDTH = 256
POOL_WINDOWS = (2, 4, 8, 16)
POOL_GROUP = 64
POOL_WIDTH = POOL_GROUP * len(POOL_WINDOWS)
N_BRANCH = 3
IN_WIDTH = Q_LORA + KV_LORA + QK_ROPE + FOURIER_WIDTH + POOL_WIDTH + N_BRANCH * D_MODEL
SPLITS = (Q_LORA, Q_LORA + KV_LORA, Q_LORA + KV_LORA + QK_ROPE,
          Q_LORA + KV_LORA + QK_ROPE + FOURIER_WIDTH,
          Q_LORA + KV_LORA + QK_ROPE + FOURIER_WIDTH + POOL_WIDTH)
PEER_HEADS = 8
N_KEYS = 128
N_EXPERTS = N_KEYS * N_KEYS
PEER_QDIM = 256
PEER_TOPK = 16
TOKEN_BLOCK = 128
EPS = 1e-6

kernel_name = "hybrid_mla_fnet_pool_peer_dit"


def _rmsnorm(x, g):
    xf = x.astype(jnp.float32)
    y = xf * lax.rsqrt(jnp.mean(xf * xf, axis=-1, keepdims=True) + EPS)
    return (y * g.astype(jnp.float32)).astype(x.dtype)


def _modulation(cvec, w_mod, b_mod):
    m = jax.nn.silu(cvec) @ w_mod + b_mod
    return jnp.split(m[:, None, :], 6, axis=-1)


def _modulate(x, g, shift, scale):
    return _rmsnorm(x, g) * (1 + scale) + shift


def _axial_angles(L):
    rows = L // GRID_W
    r, cl = jnp.meshgrid(jnp.arange(rows, dtype=jnp.float32),
                         jnp.arange(GRID_W, dtype=jnp.float32), indexing="ij")
    half = QK_ROPE // 2
    inv_freq = ROPE_BASE ** (-jnp.arange(0, half, 2, dtype=jnp.float32) / half)
    ang_r = r.reshape(L, 1, 1) * inv_freq
    ang_c = cl.reshape(L, 1, 1) * inv_freq
    return ang_r, ang_c


def _rotate_half(x, ang):
    x1, x2 = jnp.split(x.astype(jnp.float32), 2, axis=-1)
    cos, sin = jnp.cos(ang), jnp.sin(ang)
    return jnp.concatenate([x1 * cos - x2 * sin, x1 * sin + x2 * cos], axis=-1)


def _rope_2d(x, ang):
    ang_r, ang_c = ang
    xr, xc = jnp.split(x, 2, axis=-1)
    return jnp.concatenate([_rotate_half(xr, ang_r), _rotate_half(xc, ang_c)], axis=-1).astype(x.dtype)


def _mla_q(cq, lp, ang):
    B, L, _ = cq.shape
    q = (_rmsnorm(cq, lp["q_norm_g"]) @ lp["w_uq"]).reshape(B, L, N_HEADS, QK_NOPE + QK_ROPE)
    q_nope, q_rope = q[..., :QK_NOPE], q[..., QK_NOPE:]
    if ang is not None:
        q_rope = _rope_2d(q_rope, ang)
    return jnp.concatenate([q_nope, q_rope], axis=-1)


def _mla_kv(ckv, kr, lp, ang):
    B, L, _ = ckv.shape
    kv = (_rmsnorm(ckv, lp["kv_norm_g"]) @ lp["w_ukv"]).reshape(B, L, N_HEADS, QK_NOPE + V_HEAD)
    k_nope, v = kv[..., :QK_NOPE], kv[..., QK_NOPE:]
    k_rope = kr[:, :, None, :]
    if ang is not None:
        k_rope = _rope_2d(k_rope, ang)
    k = jnp.concatenate([k_nope, jnp.broadcast_to(k_rope, (B, L, N_HEADS, QK_ROPE))], axis=-1)
    return k, v


def _attend(q, k, v):
    s = jnp.einsum("bqhd,bkhd->bhqk", q, k, preferred_element_type=jnp.float32) * ATT_SCALE
    p = jax.nn.softmax(s, axis=-1).astype(v.dtype)
    return jnp.einsum("bhqk,bkhd->bqhd", p, v)


def _latent_attention(q, k, v, kv_ctx):
    k_ctx, v_ctx = kv_ctx
    k_all = jnp.concatenate([k_ctx, k], axis=1)
    v_all = jnp.concatenate([v_ctx, v], axis=1)
    B, L, H, Dq = q.shape
    nb = L // Q_BLOCK
    qb = jnp.moveaxis(q.reshape(B, nb, Q_BLOCK, H, Dq), 1, 0)
    o = lax.map(lambda qq: _attend(qq, k_all, v_all), qb)
    return jnp.moveaxis(o, 0, 1).reshape(B, L, ATT_WIDTH)


def _fourier_mix(z):
    zf = z.astype(jnp.float32)
    return jnp.real(jnp.fft.fft2(zf, axes=(1, 2), norm="ortho")).astype(z.dtype)


def _pool_mix(z, w_grp, pool_scale):
    B, L, _ = z.shape
    zf = z.astype(jnp.float32)
    cs = jnp.concatenate([jnp.zeros((B, 1, POOL_WIDTH), jnp.float32), jnp.cumsum(zf, axis=1)], axis=1)
    t = jnp.arange(L)
    outs = []
    for gi, w in enumerate(POOL_WINDOWS):
        lo = jnp.clip(t - w // 2, 0, L)
        hi = jnp.clip(t - w // 2 + w, 0, L)
        seg = cs[:, :, gi * POOL_GROUP:(gi + 1) * POOL_GROUP]
        cnt = (hi - lo).astype(jnp.float32)[None, :, None]
        outs.append((seg[:, hi] - seg[:, lo]) / cnt - zf[:, :, gi * POOL_GROUP:(gi + 1) * POOL_GROUP])
    pooled = jnp.concatenate(outs, axis=-1).astype(z.dtype).reshape(B, L, len(POOL_WINDOWS), POOL_GROUP)
    y = jnp.einsum("blgc,gcd->blgd", pooled, w_grp).reshape(B, L, POOL_WIDTH)
    return y * pool_scale


def _token_mix(h, lp, ang, kv_ctx):
    B, L, _ = h.shape
    z = h @ lp["w_in"]
    cq, ckv, kr, zf, zp, zg = jnp.split(z, SPLITS, axis=-1)
    q = _mla_q(cq, lp, ang)
    k, v = _mla_kv(ckv, kr, lp, ang)
    if kv_ctx is None:
        att = _attend(q, k, v).reshape(B, L, ATT_WIDTH)
    else:
        att = _latent_attention(q, k, v, kv_ctx)
    y_a = att @ lp["w_oa"]
    y_b = _fourier_mix(zf) @ lp["w_ob"]
    y_c = _pool_mix(zp, lp["w_grp"], lp["pool_scale"]) @ lp["w_oc"]
    g = jax.nn.sigmoid((zg + lp["b_gate"]).astype(jnp.float32)).astype(h.dtype)
    g_a, g_b, g_c = jnp.split(g, N_BRANCH, axis=-1)
    return (g_a * y_a + g_b * y_b + g_c * y_c) @ lp["w_out"], (k, v)


def _ctx_keys_values(hc, lp):
    z = hc @ lp["w_in"][:, Q_LORA:Q_LORA + KV_LORA + QK_ROPE]
    ckv, kr = z[..., :KV_LORA], z[..., KV_LORA:]
    return _mla_kv(ckv, kr, lp, None)


def _peer(h, w_pq, peer_keys, peer_down, peer_up):
    B, L, D = h.shape
    q = (h @ w_pq).reshape(B, L, PEER_HEADS, 2, PEER_QDIM // 2)
    s = jnp.einsum("blhpd,hpkd->blhpk", q, peer_keys, preferred_element_type=jnp.float32)
    sv, si = lax.top_k(s, PEER_TOPK)
    cand = (sv[..., 0, :, None] + sv[..., 1, None, :]).reshape(B, L, PEER_HEADS, PEER_TOPK * PEER_TOPK)
    cidx = (si[..., 0, :, None] * N_KEYS + si[..., 1, None, :]).reshape(B, L, PEER_HEADS, PEER_TOPK * PEER_TOPK)
    top_s, pos = lax.top_k(cand, PEER_TOPK)
    eidx = jnp.take_along_axis(cidx, pos, axis=-1)
    gate = jax.nn.softmax(top_s, axis=-1)
    T = B * L
    nb = T // TOKEN_BLOCK
    hb = h.reshape(nb, TOKEN_BLOCK, D)
    eb = eidx.reshape(nb, TOKEN_BLOCK, PEER_HEADS * PEER_TOPK)
    gb = gate.reshape(nb, TOKEN_BLOCK, PEER_HEADS * PEER_TOPK)

    def one(args):
        hh, ee, gg = args
        a = jnp.einsum("tkd,td->tk", peer_down[ee], hh, preferred_element_type=jnp.float32)
        act = (gg * jax.nn.gelu(a)).astype(hh.dtype)
        return jnp.einsum("tk,tkd->td", act, peer_up[ee])

    return lax.map(one, (hb, eb, gb)).reshape(B, L, D)


def setup_inputs(seed: int = 0) -> dict:
    key = jax.random.key(seed)
    ks = jax.random.split(key, 25)
    D = D_MODEL

    def n(k, shape, s):
        return jax.random.normal(k, shape, jnp.float32) * s

    return {
        "x": n(ks[0], (BATCH, SEQ, D), 1.0),
        "c": n(ks[1], (BATCH, D), 1.0),
        "ctx": n(ks[2], (BATCH, CTX_LEN, D), 1.0),
        "c_ctx": n(ks[3], (D,), 1.0),
        "w_mod": n(ks[4], (DEPTH, D, 6 * D), 0.5 * D ** -0.5),
        "b_mod": n(ks[5], (DEPTH, 6 * D), 0.01),
        "norm1_g": 1.0 + n(ks[6], (DEPTH, D), 0.02),
        "norm2_g": 1.0 + n(ks[7], (DEPTH, D), 0.02),
        "w_in": n(ks[8], (DEPTH, D, IN_WIDTH), D ** -0.5),
        "b_gate": n(ks[9], (DEPTH, N_BRANCH * D), 0.01),
        "q_norm_g": 1.0 + n(ks[10], (DEPTH, Q_LORA), 0.02),
        "w_uq": n(ks[11], (DEPTH, Q_LORA, N_HEADS * (QK_NOPE + QK_ROPE)), Q_LORA ** -0.5),
        "kv_norm_g": 1.0 + n(ks[12], (DEPTH, KV_LORA), 0.02),
        "w_ukv": n(ks[13], (DEPTH, KV_LORA, N_HEADS * (QK_NOPE + V_HEAD)), KV_LORA ** -0.5),
        "w_oa": n(ks[14], (DEPTH, ATT_WIDTH, D), ATT_WIDTH ** -0.5),
        "w_ob": n(ks[15], (DEPTH, FOURIER_WIDTH, D), FOURIER_WIDTH ** -0.5),
        "w_grp": n(ks[16], (DEPTH, len(POOL_WINDOWS), POOL_GROUP, POOL_GROUP), POOL_GROUP ** -0.5),
        "pool_scale": 1.0 + n(ks[17], (DEPTH, POOL_WIDTH), 0.02),
        "w_oc": n(ks[18], (DEPTH, POOL_WIDTH, D), POOL_WIDTH ** -0.5),
        "w_out": n(ks[19], (DEPTH, D, D), D ** -0.5),
        "w_pq": n(ks[20], (DEPTH, D, PEER_HEADS * PEER_QDIM), D ** -0.5),
        "peer_keys": n(ks[21], (DEPTH, PEER_HEADS, 2, N_KEYS, PEER_QDIM // 2), (PEER_QDIM // 2) ** -0.5),
        "peer_down": n(ks[22], (DEPTH, N_EXPERTS, D), D ** -0.5),
        "peer_up": n(ks[23], (DEPTH, N_EXPERTS, D), PEER_HEADS ** -0.5),
        "final_g": 1.0 + n(ks[24], (D,), 0.02),
    }


def reference(x, c, ctx, c_ctx, w_mod, b_mod, norm1_g, norm2_g, w_in, b_gate, q_norm_g, w_uq,
              kv_norm_g, w_ukv, w_oa, w_ob, w_grp, pool_scale, w_oc, w_out, w_pq, peer_keys,
              peer_down, peer_up, final_g):
    L = x.shape[1]
    ang = _axial_angles(L)
    for l in range(DEPTH):
        last = l == DEPTH - 1
        lp = {"w_in": w_in[l], "b_gate": b_gate[l], "q_norm_g": q_norm_g[l], "w_uq": w_uq[l],
              "kv_norm_g": kv_norm_g[l], "w_ukv": w_ukv[l], "w_oa": w_oa[l], "w_ob": w_ob[l],
              "w_grp": w_grp[l], "pool_scale": pool_scale[l], "w_oc": w_oc[l], "w_out": w_out[l]}
        m_x = _modulation(c, w_mod[l], b_mod[l])
        m_c = _modulation(c_ctx[None, :], w_mod[l], b_mod[l])
        h = _modulate(x, norm1_g[l], m_x[0], m_x[1])
        hc = _modulate(ctx, norm1_g[l], m_c[0], m_c[1])
        if last:
            kv_c = _ctx_keys_values(hc, lp)
        else:
            mix_c, kv_c = _token_mix(hc, lp, None, None)
        mix_x, _ = _token_mix(h, lp, ang, kv_c)
        x = x + m_x[2] * mix_x
        h2 = _modulate(x, norm2_g[l], m_x[3], m_x[4])
        x = x + m_x[5] * _peer(h2, w_pq[l], peer_keys[l], peer_down[l], peer_up[l])
        if not last:
            ctx = ctx + m_c[2] * mix_c
            hc2 = _modulate(ctx, norm2_g[l], m_c[3], m_c[4])
            ctx = ctx + m_c[5] * _peer(hc2, w_pq[l], peer_keys[l], peer_down[l], peer_up[l])
    return _rmsnorm(x, final_g)
```

```python
import contextlib
import math
import numpy as np
import ml_dtypes
import concourse.bass as bass
import concourse.mybir as mybir
from concourse.bass_utils import run_bass_kernel_spmd

F32 = mybir.dt.float32
BF16 = mybir.dt.bfloat16
U32 = mybir.dt.uint32
I32 = mybir.dt.int32
AF = mybir.ActivationFunctionType
ALU = mybir.AluOpType
AX = mybir.AxisListType

ENGS = ("sync", "scalar", "vector", "gpsimd", "tensor")
N_DMA_SEMS = 90

D = 1024
SEQ = 4096
CTX = 256
T = SEQ + CTX
DEPTH = 2
NH = 8
EPS = 1e-6
ATT_SCALE = 96 ** -0.5
IN_W = 4000
NEXP = 16384


class Sched:
    def __init__(self, nc):
        self.nc = nc
        self.ops = {e: [] for e in ENGS}
        self.count = {e: 0 for e in ENGS}
        self.known = {e: {} for e in ENGS}
        self.last_w = {}
        self.readers = {}
        self.dma_next = 0
        self.dma_target = [0] * N_DMA_SEMS
        self.pending = {e: [] for e in ENGS}
        self.n_instr = 0

    def _need(self, eng, tok, waits):
        if tok is None:
            return
        kind, src, val = tok
        if kind == "e" and src == eng and eng == "tensor":
            return
        k = (kind, src)
        if self.known[eng].get(k, 0) >= val:
            return
        self.known[eng][k] = val
        waits.append(tok)

    def _deps(self, eng, r, w):
        waits = []
        for tok in self.pending[eng]:
            self._need(eng, tok, waits)
        self.pending[eng] = []
        for key in r:
            self._need(eng, self.last_w.get(key), waits)
        for key in w:
            self._need(eng, self.last_w.get(key), waits)
            for tok in self.readers.get(key, ()):
                self._need(eng, tok, waits)
        best = {}
        for kind, src, val in waits:
            k = (kind, src)
            if best.get(k, 0) < val:
                best[k] = val
        return [(k[0], k[1], v) for k, v in best.items()]

    def _commit(self, tok, r, w):
        for key in r:
            self.readers.setdefault(key, []).append(tok)
        for key in w:
            self.last_w[key] = tok
            self.readers[key] = []

    def op(self, eng, fn, r=(), w=()):
        waits = self._deps(eng, r, w)
        self.count[eng] += 1
        tok = ("e", eng, self.count[eng])
        self._commit(tok, r, w)
        self.ops[eng].append(("op", fn, waits, None))
        self.n_instr += 1 + len(waits)
        return tok

    def dma(self, eng, fn, r=(), w=()):
        s = self.dma_next % N_DMA_SEMS
        self.dma_next += 1
        waits = self._deps(eng, r, w)
        prev = self.dma_target[s]
        if prev > 0:
            k = ("d", s)
            if self.known[eng].get(k, 0) < prev:
                self.known[eng][k] = prev
                waits.append(("d", s, prev))
        self.dma_target[s] = prev + 16
        tok = ("d", s, prev + 16)
        self._commit(tok, r, w)
        self.ops[eng].append(("dma", fn, waits, s))
        self.n_instr += 1 + len(waits)
        return tok

    def barrier(self):
        waits = []
        for e in ENGS:
            if e != "sync" and self.count[e] > 0:
                self._need("sync", ("e", e, self.count[e]), waits)
        for s in range(N_DMA_SEMS):
            if self.dma_target[s] > 0:
                self._need("sync", ("d", s, self.dma_target[s]), waits)
        self.count["sync"] += 1
        tok = ("e", "sync", self.count["sync"])
        self.ops["sync"].append(("op", lambda e: e.nop(), waits, None))
        self.n_instr += 1 + len(waits)
        for e in ENGS:
            if e != "sync":
                self.pending[e].append(tok)
                for e2 in ENGS:
                    self.known[e][("e", e2)] = max(self.known[e].get(("e", e2), 0),
                                                   self.count[e2] if e2 != "sync" else 0)
                for s in range(N_DMA_SEMS):
                    self.known[e][("d", s)] = self.dma_target[s]
                self.known[e].pop(("e", "sync"), None)
        self.last_w = {}
        self.readers = {}

    def emit(self):
        nc = self.nc
        with contextlib.ExitStack() as st:
            esem = {e: st.enter_context(nc.semaphore("c_" + e)) for e in ENGS}
            dsem = [st.enter_context(nc.semaphore("d%d" % i)) for i in range(N_DMA_SEMS)]
            block = st.enter_context(nc.Block())

            def mk(ename):
                def body(eng):
                    for kind, fn, waits, s in self.ops[ename]:
                        for wk, src, val in waits:
                            eng.wait_ge(esem[src] if wk == "e" else dsem[src], val)
                        ins = fn(eng)
                        if kind == "op":
                            ins.then_inc(esem[ename], 1)
                        else:
                            ins.then_inc(dsem[s], 16)
                    if ename == "sync":
                        for i in range(N_DMA_SEMS):
                            if self.dma_target[i] > 0:
                                eng.wait_ge(dsem[i], self.dma_target[i])
                        for e2 in ENGS:
                            if e2 != "sync" and self.count[e2] > 0:
                                eng.wait_ge(esem[e2], self.count[e2])
                return body

            block.sync(mk("sync"))
            block.scalar(mk("scalar"))
            block.vector(mk("vector"))
            block.gpsimd(mk("gpsimd"))
            block.tensor(mk("tensor"))


def _const_tables():
    c = {}
    half = 16
    inv_freq = (10000.0 ** (-np.arange(0, half, 2, dtype=np.float32) / half)).astype(np.float32)
    t = np.arange(SEQ)
    r = (t // 64).astype(np.float32)
    cl = (t % 64).astype(np.float32)
    ang_r = (r[None, :] * inv_freq[:, None]).astype(np.float32)
    ang_c = (cl[None, :] * inv_freq[:, None]).astype(np.float32)
    ang = np.concatenate([ang_r, ang_r, ang_c, ang_c], axis=0)
    c["ropecos"] = np.cos(ang).astype(np.float32)
    c["ropesin"] = np.sin(ang).astype(np.float32)
    def dft(n):
        k = np.arange(n, dtype=np.int64)
        kk = (k[:, None] * k[None, :]) % n
        a = 2.0 * np.pi * kk.astype(np.float64) / n
        return np.cos(a), np.sin(a)
    cc, sc = dft(256)
    c["ccs"] = np.concatenate([cc, -sc], axis=1).astype(ml_dtypes.bfloat16)
    c["cl256"] = cc.astype(ml_dtypes.bfloat16)
    c["sl256"] = sc.astype(ml_dtypes.bfloat16)
    cL, sL = dft(SEQ)
    c["cl4096"] = cL.astype(ml_dtypes.bfloat16)
    c["sl4096"] = sL.astype(ml_dtypes.bfloat16)
    for L, name in ((SEQ, "icnt4096"), (CTX, "icnt256")):
        tt = np.arange(L)
        tab = np.zeros((2, 128, L), np.float32)
        for gi, w in enumerate((2, 4, 8, 16)):
            lo = np.clip(tt - w // 2, 0, L)
            hi = np.clip(tt - w // 2 + w, 0, L)
            tab[gi // 2, (gi % 2) * 64:(gi % 2) * 64 + 64, :] = (1.0 / (hi - lo).astype(np.float32))[None, :]
        c[name] = tab
    return c


CONST_SHAPES = {
    "ropecos": ([32, SEQ], F32), "ropesin": ([32, SEQ], F32),
    "ccs": ([256, 512], BF16), "cl256": ([256, 256], BF16), "sl256": ([256, 256], BF16),
    "cl4096": ([SEQ, SEQ], BF16), "sl4096": ([SEQ, SEQ], BF16),
    "icnt4096": ([2, 128, SEQ], F32), "icnt256": ([2, 128, CTX], F32),
}

PARAM_SHAPES = {
    "w_mod": [DEPTH, D, 6 * D], "b_mod": [DEPTH, 6 * D], "norm1_g": [DEPTH, D], "norm2_g": [DEPTH, D],
    "w_in": [DEPTH, D, IN_W], "b_gate": [DEPTH, 3 * D], "q_norm_g": [DEPTH, 256], "w_uq": [DEPTH, 256, 768],
    "kv_norm_g": [DEPTH, 128], "w_ukv": [DEPTH, 128, 1024], "w_oa": [DEPTH, 512, D], "w_ob": [DEPTH, 256, D],
    "w_grp": [DEPTH, 4, 64, 64], "pool_scale": [DEPTH, 256], "w_oc": [DEPTH, 256, D], "w_out": [DEPTH, D, D],
    "w_pq": [DEPTH, D, 2048], "peer_keys": [DEPTH, 8, 2, 128, 128], "peer_down0": [NEXP, D], "peer_down1": [NEXP, D],
    "peer_up0": [NEXP, D], "peer_up1": [NEXP, D], "final_g": [D],
}

GROUPS = [(0, CTX)] + [(CTX + 512 * g, 512) for g in range(SEQ // 512)]


def build(stop_after=None, dbg=(), peer_tiles=None):
    nc = bass.Bass("TRN2", target_bir_lowering=False)
    S = Sched(nc)

    def dram_in(name, shape, dt=F32):
        return nc.dram_tensor(name, shape, dt, kind="ExternalInput").ap()

    def dram_scratch(name, shape, dt):
        kind = "ExternalOutput" if name in dbg else "Internal"
        return nc.dram_tensor(name, shape, dt, kind=kind).ap()

    xin = dram_in("x", [SEQ, D])
    ctxin = dram_in("ctx", [CTX, D])
    cvec = dram_in("cvec", [2, D])
    P = {k: dram_in(k, v) for k, v in PARAM_SHAPES.items()}
    C = {k: dram_in(k, v[0], v[1]) for k, v in CONST_SHAPES.items()}
    out = nc.dram_tensor("out", [SEQ, D], F32, kind="ExternalOutput").ap()

    XT = dram_scratch("XT", [8, 128, T], F32)
    GT = dram_scratch("GT", [24, 128, T], BF16)
    QT = dram_scratch("QT", [NH, 96, T], BF16)
    KT = dram_scratch("KT", [NH, 96, T], BF16)
    VV = dram_scratch("VV", [T, 512], BF16)
    ZFT = dram_scratch("ZFT", [2, 128, T], BF16)
    ZPT = dram_scratch("ZPT", [2, 128, T], F32)
    ATT = dram_scratch("ATT", [4, 128, T], BF16)
    YFT = dram_scratch("YFT", [2, 128, T], BF16)
    YCT = dram_scratch("YCT", [2, 128, T], BF16)
    H2T = dram_scratch("H2T", [8, 128, T], BF16)
    PQT = dram_scratch("PQT", [16, 128, T], BF16)

    V = lambda fn, r=(), w=(): S.op("vector", fn, r, w)
    A = lambda fn, r=(), w=(): S.op("scalar", fn, r, w)
    G = lambda fn, r=(), w=(): S.op("gpsimd", fn, r, w)
    TE = lambda fn, r=(), w=(): S.op("tensor", fn, r, w)
    DM = lambda fn, r=(), w=(), q="sync": S.dma(q, fn, r, w)

    rr = [0]

    def evac(out_ap, in_ap, r, w, scale=None):
        rr[0] += 1
        if rr[0] % 2 == 0:
            if scale is None:
                V(lambda e: e.tensor_copy(out_ap, in_ap), r, w)
            else:
                V(lambda e: e.tensor_single_scalar(out_ap, in_ap, float(scale), op=ALU.mult), r, w)
        else:
            A(lambda e: e.activation(out=out_ap, in_=in_ap, func=AF.Copy,
                                     scale=1.0 if scale is None else float(scale)), r, w)

    with contextlib.ExitStack() as top:
        uniq = [0]

        def sb(st, name, shape, dt):
            uniq[0] += 1
            return st.enter_context(nc.sbuf_tensor("%s_%d" % (name, uniq[0]), shape, dt))

        def pst(st, name, shape, dt):
            uniq[0] += 1
            return st.enter_context(nc.psum_tensor("%s_%d" % (name, uniq[0]), shape, dt))

        ident_f = sb(top, "ident_f", [128, 128], F32)
        ident_b = sb(top, "ident_b", [128, 128], BF16)
        ones_f = sb(top, "ones_f", [128, 128], F32)
        ones_b = sb(top, "ones_b", [128, 128], BF16)
        iot = sb(top, "iot", [128, 128], F32)
        gm1 = sb(top, "gm1", [128, 8, 2], F32)
        sh1 = sb(top, "sh1", [128, 8, 2], F32)
        gt1 = sb(top, "gt1", [128, 8, 2], F32)
        gm2 = sb(top, "gm2", [128, 8, 2], F32)
        sh2 = sb(top, "sh2", [128, 8, 2], F32)
        gt2 = sb(top, "gt2", [128, 8, 2], F32)
        epsb = sb(top, "epsb", [128, 1], F32)

        G(lambda e: e.iota(iot[:], [[1, 128]], base=0, channel_multiplier=-1,
                           allow_small_or_imprecise_dtypes=True), w=["iot"])
        V(lambda e: e.tensor_single_scalar(ident_f[:], iot[:], 0.0, op=ALU.is_equal), r=["iot"], w=["ident_f"])
        V(lambda e: e.tensor_single_scalar(ident_b[:], iot[:], 0.0, op=ALU.is_equal), r=["iot"], w=["ident_b"])
        V(lambda e: e.memset(ones_f[:], 1.0), w=["ones_f"])
        V(lambda e: e.memset(ones_b[:], 1.0), w=["ones_b"])
        V(lambda e: e.memset(epsb[:], EPS), w=["epsb"])

        def phase_load_x():
            with contextlib.ExitStack() as st:
                xt = [sb(st, "ld_x%d" % i, [128, D], F32) for i in range(2)]
                xo = [sb(st, "ld_o%d" % i, [128, 8, 128], F32) for i in range(2)]
                ps = [pst(st, "ld_ps%d" % i, [128, 4, 128], F32) for i in range(4)]
                ntile = T // 128
                for ti in range(ntile):
                    b = ti % 2
                    src = ctxin[ti * 128:(ti + 1) * 128, :] if ti < 2 else xin[(ti - 2) * 128:(ti - 1) * 128, :]
                    DM(lambda e, b=b, src=src: e.dma_start(out=xt[b][:], in_=src), w=["ld_x%d" % b])
                    for half in range(2):
                        pi = (ti * 2 + half) % 4
                        for j in range(4):
                            k = half * 4 + j
                            TE(lambda e, b=b, pi=pi, j=j, k=k: e.transpose(ps[pi][:, j, :], xt[b][:, k * 128:(k + 1) * 128], ident_f[:]),
                               r=["ld_x%d" % b, "ident_f"], w=["ld_ps%d" % pi])
                        evac(xo[b][:, half * 4:(half + 1) * 4, :], ps[pi][:], r=["ld_ps%d" % pi], w=[("ld_o", b, half)])
                    DM(lambda e, b=b, ti=ti: e.dma_start(out=XT[:, :, ti * 128:(ti + 1) * 128].rearrange("k p t -> p k t"), in_=xo[b][:]),
                       r=[("ld_o", b, 0), ("ld_o", b, 1)], w=[("XT", ti)])
            S.barrier()

        def phase_mod(l):
            with contextlib.ExitStack() as st:
                cT = sb(st, "md_c", [128, 2, 8], F32)
                scT = sb(st, "md_sc", [128, 8, 2], F32)
                bm = sb(st, "md_b", [128, 48], F32)
                g1 = sb(st, "md_g1", [128, 8], F32)
                g2 = sb(st, "md_g2", [128, 8], F32)
                modT = sb(st, "md_mod", [128, 48, 2], F32)
                wm = [sb(st, "md_w%d" % i, [128, 8, 512], F32) for i in range(2)]
                mps = pst(st, "md_ps", [128, 48, 2], F32)
                for r_ in range(2):
                    DM(lambda e, r_=r_: e.dma_start(out=cT[:, r_, :], in_=cvec[r_].rearrange("(k p) -> p k", p=128), allow_slow_non_contiguous=True), w=["md_c"])
                DM(lambda e: e.dma_start(out=bm[:], in_=P["b_mod"][l].rearrange("(n p) -> p n", p=128), allow_slow_non_contiguous=True), w=["md_b"])
                DM(lambda e: e.dma_start(out=g1[:], in_=P["norm1_g"][l].rearrange("(n p) -> p n", p=128), allow_slow_non_contiguous=True), w=["md_g1"])
                DM(lambda e: e.dma_start(out=g2[:], in_=P["norm2_g"][l].rearrange("(n p) -> p n", p=128), allow_slow_non_contiguous=True), w=["md_g2"])
                for r_ in range(2):
                    A(lambda e, r_=r_: e.activation(out=scT[:, :, r_], in_=cT[:, r_, :], func=AF.Silu), r=["md_c"], w=["md_sc"])
                for j in range(12):
                    b = j % 2
                    DM(lambda e, b=b, j=j: e.dma_start(out=wm[b][:], in_=P["w_mod"][l][:, j * 512:(j + 1) * 512].rearrange("(k p) n -> p k n", p=128)),
                       w=["md_w%d" % b])
                    for i in range(4):
                        n = j * 4 + i
                        for k in range(8):
                            TE(lambda e, b=b, i=i, k=k, n=n: e.matmul(mps[:, n, :], lhsT=wm[b][:, k, i * 128:(i + 1) * 128], rhs=scT[:, k, :],
                                                                      start=(k == 0), stop=(k == 7)),
                               r=["md_w%d" % b, "md_sc"], w=["md_ps"])
                V(lambda e: e.tensor_tensor(out=modT[:], in0=mps[:], in1=bm[:].unsqueeze(2).to_broadcast([128, 48, 2]), op=ALU.add),
                  r=["md_ps", "md_b"], w=["md_mod"])
                V(lambda e: e.tensor_copy(sh1[:], modT[:, 0:8, :]), r=["md_mod"], w=["sh1"])
                V(lambda e: e.scalar_tensor_tensor(out=gm1[:], in0=modT[:, 8:16, :], scalar=1.0, in1=g1[:].unsqueeze(2).to_broadcast([128, 8, 2]),
                                                   op0=ALU.add, op1=ALU.mult), r=["md_mod", "md_g1"], w=["gm1"])
                V(lambda e: e.tensor_copy(gt1[:], modT[:, 16:24, :]), r=["md_mod"], w=["gt1"])
                V(lambda e: e.tensor_copy(sh2[:], modT[:, 24:32, :]), r=["md_mod"], w=["sh2"])
                V(lambda e: e.scalar_tensor_tensor(out=gm2[:], in0=modT[:, 32:40, :], scalar=1.0, in1=g2[:].unsqueeze(2).to_broadcast([128, 8, 2]),
                                                   op0=ALU.add, op1=ALU.mult), r=["md_mod", "md_g2"], w=["gm2"])
                V(lambda e: e.tensor_copy(gt2[:], modT[:, 40:48, :]), r=["md_mod"], w=["gt2"])
            S.barrier()

        def rms_stats(src, nk, n, nfeat, sq, ps_stat, rstd, keys_r, ksq, kps, krstd):
            for k in range(nk):
                A(lambda e, k=k: e.activation(out=sq[:, k, :n], in_=src[:, k, :n], func=AF.Square), r=keys_r, w=[ksq])
            for k in range(nk):
                TE(lambda e, k=k: e.matmul(ps_stat[:, :n], lhsT=ones_f[:], rhs=sq[:, k, :n], start=(k == 0), stop=(k == nk - 1)),
                   r=[ksq, "ones_f"], w=[kps])
            A(lambda e: e.activation(out=rstd[:, :n], in_=ps_stat[:, :n], func=AF.Sqrt, scale=1.0 / nfeat, bias=epsb[:]),
              r=[kps, "epsb"], w=[krstd])
            V(lambda e: e.reciprocal(rstd[:, :n], rstd[:, :n]), r=[krstd], w=[krstd])

        def phase_proj(l, last):
            with contextlib.ExitStack() as st:
                w_in = sb(st, "pj_win", [128, 8, IN_W], BF16)
                w_krot = sb(st, "pj_wkrot", [128, 8, 32], BF16)
                wq_raw = sb(st, "pj_wqraw", [128, 2, 768], BF16)
                wq_rot = sb(st, "pj_wqrot", [128, 2, NH, 32], BF16)
                wkv = sb(st, "pj_wkv", [128, 1024], BF16)
                wv = sb(st, "pj_wv", [128, NH, 64], BF16)
                qg = sb(st, "pj_qg", [128, 2], F32)
                kvg = sb(st, "pj_kvg", [128, 1], F32)
                bg = sb(st, "pj_bg", [128, 24], F32)
                rcos = sb(st, "pj_cos", [32, SEQ], F32)
                rsin = sb(st, "pj_sin", [32, SEQ], F32)
                xg = sb(st, "pj_xg", [128, 8, 512], F32)
                sq = sb(st, "pj_sq", [128, 8, 512], F32)
                rstd = sb(st, "pj_rstd", [128, 512], F32)
                hT = sb(st, "pj_hT", [128, 8, 512], BF16)
                cq = sb(st, "pj_cq", [128, 2, 512], F32)
                cqn = sb(st, "pj_cqn", [128, 2, 512], BF16)
                ckv = sb(st, "pj_ckv", [128, 1, 512], F32)
                cn = sb(st, "pj_cn", [128, 512], BF16)
                rstd2 = sb(st, "pj_rstd2", [128, 512], F32)
                ob = [sb(st, "pj_ob%d" % i, [128, 512], BF16) for i in range(4)]
                of = [sb(st, "pj_of%d" % i, [128, 512], F32) for i in range(2)]
                t1 = sb(st, "pj_t1", [32, 512], F32)
                t2 = sb(st, "pj_t2", [32, 512], F32)
                ps = [pst(st, "pj_ps%d" % i, [128, 512], F32) for i in range(6)]
                ps_st = pst(st, "pj_pst", [128, 512], F32)
                ps_rot = pst(st, "pj_prot", [128, 512], F32)

                for k in range(8):
                    for hh in range(2):
                        DM(lambda e, k=k, hh=hh: e.dma_start(out=w_in[:, k, hh * 2000:(hh + 1) * 2000],
                                                             in_=P["w_in"][l][k * 128:(k + 1) * 128, hh * 2000:(hh + 1) * 2000]),
                           w=["pj_win"], q="gpsimd")
                DM(lambda e: e.dma_start(out=wq_raw[:], in_=P["w_uq"][l].rearrange("(k p) n -> p k n", p=128)), w=["pj_wqraw"], q="gpsimd")
                DM(lambda e: e.dma_start(out=wkv[:], in_=P["w_ukv"][l]), w=["pj_wkv"], q="gpsimd")
                DM(lambda e: e.dma_start(out=qg[:], in_=P["q_norm_g"][l].rearrange("(n p) -> p n", p=128), allow_slow_non_contiguous=True), w=["pj_qg"])
                DM(lambda e: e.dma_start(out=kvg[:], in_=P["kv_norm_g"][l].rearrange("(n p) -> p n", p=128), allow_slow_non_contiguous=True), w=["pj_kvg"])
                DM(lambda e: e.dma_start(out=bg[:], in_=P["b_gate"][l].rearrange("(n p) -> p n", p=128), allow_slow_non_contiguous=True), w=["pj_bg"])
                DM(lambda e: e.dma_start(out=rcos[:], in_=C["ropecos"]), w=["pj_cos"])
                DM(lambda e: e.dma_start(out=rsin[:], in_=C["ropesin"]), w=["pj_sin"])
                kr_cols = w_in[:, :, 384:416].rearrange("p k (rc x f) -> p k rc x f", rc=2, x=2)
                kro = w_krot[:].rearrange("p k (rc x f) -> p k rc x f", rc=2, x=2)
                V(lambda e: e.tensor_single_scalar(kro[:, :, :, 0, :], kr_cols[:, :, :, 1, :], -1.0, op=ALU.mult), r=["pj_win"], w=["pj_wkrot"])
                V(lambda e: e.tensor_copy(kro[:, :, :, 1, :], kr_cols[:, :, :, 0, :]), r=["pj_win"], w=["pj_wkrot"])
                for k in range(2):
                    qr = wq_raw[:, k, :].rearrange("p (h c) -> p h c", c=96)[:, :, 64:96].rearrange("p h (rc x f) -> p h rc x f", rc=2, x=2)
                    qo = wq_rot[:, k, :, :].rearrange("p h (rc x f) -> p h rc x f", rc=2, x=2)
                    for rc in range(2):
                        V(lambda e, qo=qo, qr=qr, rc=rc: e.tensor_single_scalar(qo[:, :, rc, 0, :], qr[:, :, rc, 1, :], -1.0, op=ALU.mult),
                          r=["pj_wqraw"], w=["pj_wqrot"])
                        V(lambda e, qo=qo, qr=qr, rc=rc: e.tensor_copy(qo[:, :, rc, 1, :], qr[:, :, rc, 0, :]), r=["pj_wqraw"], w=["pj_wqrot"])
                V(lambda e: e.tensor_copy(wv[:], wkv[:].rearrange("p (h c) -> p h c", c=128)[:, :, 64:128]), r=["pj_wkv"], w=["pj_wv"])

                obi = [0]

                def nxt_ob():
                    obi[0] += 1
                    return obi[0] % 4

                psi = [0]

                def nxt_ps():
                    psi[0] += 1
                    return psi[0] % 6

                def do_group(gi, t0, n):
                    is_ctx = gi == 0
                    r_ = 1 if is_ctx else 0
                    s0 = t0 - CTX
                    kv_only = last and is_ctx
                    DM(lambda e, t0=t0, n=n: e.dma_start(out=xg[:, :, :n], in_=XT[:, :, t0:t0 + n].rearrange("k p t -> p k t")), r=["XT"], w=["pj_xg"])
                    rms_stats(xg, 8, n, D, sq, ps_st, rstd, ["pj_xg"], "pj_sq", "pj_pst", "pj_rstd")
                    for k in range(8):
                        V(lambda e, k=k, n=n: e.tensor_tensor(out=sq[:, k, :n], in0=xg[:, k, :n], in1=rstd[:, :n], op=ALU.mult),
                          r=["pj_xg", "pj_rstd"], w=["pj_sq"])
                        V(lambda e, k=k, n=n, r_=r_: e.tensor_scalar(hT[:, k, :n], sq[:, k, :n], gm1[:, k, r_:r_ + 1], sh1[:, k, r_:r_ + 1],
                                                                     op0=ALU.mult, op1=ALU.add),
                          r=["pj_sq", "gm1", "sh1"], w=["pj_hT"])

                    p = nxt_ps()
                    for k in range(8):
                        TE(lambda e, k=k, p=p: e.matmul(ps[p][:, :n], lhsT=w_in[:, k, 256:384], rhs=hT[:, k, :n], start=(k == 0), stop=(k == 7)),
                           r=["pj_hT", "pj_win"], w=["pj_ps%d" % p])
                    V(lambda e, p=p: e.tensor_copy(ckv[:, 0, :n], ps[p][:, :n]), r=["pj_ps%d" % p], w=["pj_ckv"])
                    rms_stats(ckv, 1, n, 128, sq, ps_st, rstd2, ["pj_ckv"], "pj_sq", "pj_pst", "pj_rstd2")
                    V(lambda e: e.tensor_tensor(out=sq[:, 0, :n], in0=ckv[:, 0, :n], in1=rstd2[:, :n], op=ALU.mult),
                      r=["pj_ckv", "pj_rstd2"], w=["pj_sq"])
                    V(lambda e: e.tensor_scalar(cn[:, :n], sq[:, 0, :n], kvg[:, 0:1], None, op0=ALU.mult), r=["pj_sq", "pj_kvg"], w=["pj_cn"])
                    for h in range(NH):
                        p = nxt_ps()
                        TE(lambda e, p=p, h=h: e.matmul(ps[p][:64, :n], lhsT=wkv[:, h * 128:h * 128 + 64], rhs=cn[:, :n], start=True, stop=True),
                           r=["pj_cn", "pj_wkv"], w=["pj_ps%d" % p])
                        o = nxt_ob()
                        evac(ob[o][:64, :n], ps[p][:64, :n], r=["pj_ps%d" % p], w=["pj_ob%d" % o])
                        DM(lambda e, o=o, h=h: e.dma_start(out=KT[h, 32:96, t0:t0 + n], in_=ob[o][:64, :n]), r=["pj_ob%d" % o], w=[("KT", S.dma_next)])
                    for j in range(n // 128):
                        p = nxt_ps()
                        TE(lambda e, p=p, j=j: e.matmul(ps[p][:, :], lhsT=cn[:, j * 128:(j + 1) * 128], rhs=wv[:].rearrange("p h c -> p (h c)"), start=True, stop=True),
                           r=["pj_cn", "pj_wv"], w=["pj_ps%d" % p])
                        o = nxt_ob()
                        evac(ob[o][:, :], ps[p][:, :], r=["pj_ps%d" % p], w=["pj_ob%d" % o])
                        DM(lambda e, o=o, j=j: e.dma_start(out=VV[t0 + j * 128:t0 + (j + 1) * 128, :], in_=ob[o][:, :]), r=["pj_ob%d" % o], w=[("VV", S.dma_next)])
                    p = nxt_ps()
                    for k in range(8):
                        TE(lambda e, k=k, p=p: e.matmul(ps[p][:32, :n], lhsT=w_in[:, k, 384:416], rhs=hT[:, k, :n], start=(k == 0), stop=(k == 7)),
                           r=["pj_hT", "pj_win"], w=["pj_ps%d" % p])
                    o = nxt_ob()
                    if is_ctx:
                        evac(ob[o][:32, :n], ps[p][:32, :n], r=["pj_ps%d" % p], w=["pj_ob%d" % o])
                    else:
                        for k in range(8):
                            TE(lambda e, k=k: e.matmul(ps_rot[:32, :n], lhsT=w_krot[:, k, :], rhs=hT[:, k, :n], start=(k == 0), stop=(k == 7)),
                               r=["pj_hT", "pj_wkrot"], w=["pj_prot"])
                        V(lambda e, p=p: e.tensor_tensor(out=t1[:, :n], in0=ps[p][:32, :n], in1=rcos[:, s0:s0 + n], op=ALU.mult),
                          r=["pj_ps%d" % p, "pj_cos"], w=["pj_t1"])
                        V(lambda e: e.tensor_tensor(out=t2[:, :n], in0=ps_rot[:32, :n], in1=rsin[:, s0:s0 + n], op=ALU.mult),
                          r=["pj_prot", "pj_sin"], w=["pj_t2"])
                        V(lambda e, o=o: e.tensor_tensor(out=ob[o][:32, :n], in0=t1[:, :n], in1=t2[:, :n], op=ALU.add),
                          r=["pj_t1", "pj_t2"], w=["pj_ob%d" % o])
                    for h in range(NH):
                        DM(lambda e, o=o, h=h: e.dma_start(out=KT[h, 0:32, t0:t0 + n], in_=ob[o][:32, :n]), r=["pj_ob%d" % o], w=[("KT", S.dma_next)])
                    if kv_only:
                        return
                    for c in range(2):
                        p = nxt_ps()
                        for k in range(8):
                            TE(lambda e, k=k, p=p, c=c: e.matmul(ps[p][:, :n], lhsT=w_in[:, k, c * 128:(c + 1) * 128], rhs=hT[:, k, :n], start=(k == 0), stop=(k == 7)),
                               r=["pj_hT", "pj_win"], w=["pj_ps%d" % p])
                        V(lambda e, p=p, c=c: e.tensor_copy(cq[:, c, :n], ps[p][:, :n]), r=["pj_ps%d" % p], w=["pj_cq"])
                    rms_stats(cq, 2, n, 256, sq, ps_st, rstd2, ["pj_cq"], "pj_sq", "pj_pst", "pj_rstd2")
                    for c in range(2):
                        V(lambda e, c=c: e.tensor_tensor(out=sq[:, c, :n], in0=cq[:, c, :n], in1=rstd2[:, :n], op=ALU.mult),
                          r=["pj_cq", "pj_rstd2"], w=["pj_sq"])
                        V(lambda e, c=c: e.tensor_scalar(cqn[:, c, :n], sq[:, c, :n], qg[:, c:c + 1], None, op0=ALU.mult), r=["pj_sq", "pj_qg"], w=["pj_cqn"])
                    for h in range(NH):
                        p = nxt_ps()
                        for c in range(2):
                            TE(lambda e, p=p, c=c, h=h: e.matmul(ps[p][:64, :n], lhsT=wq_raw[:, c, h * 96:h * 96 + 64], rhs=cqn[:, c, :n], start=(c == 0), stop=(c == 1)),
                               r=["pj_cqn", "pj_wqraw"], w=["pj_ps%d" % p])
                        o = nxt_ob()
                        evac(ob[o][:64, :n], ps[p][:64, :n], r=["pj_ps%d" % p], w=["pj_ob%d" % o])
                        DM(lambda e, o=o, h=h: e.dma_start(out=QT[h, 32:96, t0:t0 + n], in_=ob[o][:64, :n]), r=["pj_ob%d" % o], w=[("QT", S.dma_next)])
                        p = nxt_ps()
                        for c in range(2):
                            TE(lambda e, p=p, c=c, h=h: e.matmul(ps[p][:32, :n], lhsT=wq_raw[:, c, h * 96 + 64:h * 96 + 96], rhs=cqn[:, c, :n], start=(c == 0), stop=(c == 1)),
                               r=["pj_cqn", "pj_wqraw"], w=["pj_ps%d" % p])
                        o = nxt_ob()
                        if is_ctx:
                            evac(ob[o][:32, :n], ps[p][:32, :n], r=["pj_ps%d" % p], w=["pj_ob%d" % o])
                        else:
                            for c in range(2):
                                TE(lambda e, c=c, h=h: e.matmul(ps_rot[:32, :n], lhsT=wq_rot[:, c, h, :], rhs=cqn[:, c, :n], start=(c == 0), stop=(c == 1)),
                                   r=["pj_cqn", "pj_wqrot"], w=["pj_prot"])
                            V(lambda e, p=p: e.tensor_tensor(out=t1[:, :n], in0=ps[p][:32, :n], in1=rcos[:, s0:s0 + n], op=ALU.mult),
                              r=["pj_ps%d" % p, "pj_cos"], w=["pj_t1"])
                            V(lambda e: e.tensor_tensor(out=t2[:, :n], in0=ps_rot[:32, :n], in1=rsin[:, s0:s0 + n], op=ALU.mult),
                              r=["pj_prot", "pj_sin"], w=["pj_t2"])
                            V(lambda e, o=o: e.tensor_tensor(out=ob[o][:32, :n], in0=t1[:, :n], in1=t2[:, :n], op=ALU.add),
                              r=["pj_t1", "pj_t2"], w=["pj_ob%d" % o])
                        DM(lambda e, o=o, h=h: e.dma_start(out=QT[h, 0:32, t0:t0 + n], in_=ob[o][:32, :n]), r=["pj_ob%d" % o], w=[("QT", S.dma_next)])
                    for c in range(2):
                        p = nxt_ps()
                        for k in range(8):
                            TE(lambda e, k=k, p=p, c=c: e.matmul(ps[p][:, :n], lhsT=w_in[:, k, 416 + c * 128:416 + (c + 1) * 128], rhs=hT[:, k, :n], start=(k == 0), stop=(k == 7)),
                               r=["pj_hT", "pj_win"], w=["pj_ps%d" % p])
                        o = nxt_ob()
                        evac(ob[o][:, :n], ps[p][:, :n], r=["pj_ps%d" % p], w=["pj_ob%d" % o])
                        DM(lambda e, o=o, c=c: e.dma_start(out=ZFT[c, :, t0:t0 + n], in_=ob[o][:, :n]), r=["pj_ob%d" % o], w=[("ZFT", S.dma_next)])
                    for c in range(2):
                        p = nxt_ps()
                        for k in range(8):
                            TE(lambda e, k=k, p=p, c=c: e.matmul(ps[p][:, :n], lhsT=w_in[:, k, 672 + c * 128:672 + (c + 1) * 128], rhs=hT[:, k, :n], start=(k == 0), stop=(k == 7)),
                               r=["pj_hT", "pj_win"], w=["pj_ps%d" % p])
                        evac(of[c][:, :n], ps[p][:, :n], r=["pj_ps%d" % p], w=["pj_of%d" % c])
                        DM(lambda e, c=c: e.dma_start(out=ZPT[c, :, t0:t0 + n], in_=of[c][:, :n]), r=["pj_of%d" % c], w=[("ZPT", S.dma_next)])
                    for c in range(24):
                        p = nxt_ps()
                        for k in range(8):
                            TE(lambda e, k=k, p=p, c=c: e.matmul(ps[p][:, :n], lhsT=w_in[:, k, 928 + c * 128:928 + (c + 1) * 128], rhs=hT[:, k, :n], start=(k == 0), stop=(k == 7)),
                               r=["pj_hT", "pj_win"], w=["pj_ps%d" % p])
                        o = nxt_ob()
                        A(lambda e, o=o, p=p, c=c: e.activation(out=ob[o][:, :n], in_=ps[p][:, :n], func=AF.Sigmoid, bias=bg[:, c:c + 1], scale=1.0),
                          r=["pj_ps%d" % p, "pj_bg"], w=["pj_ob%d" % o])
                        DM(lambda e, o=o, c=c: e.dma_start(out=GT[c, :, t0:t0 + n], in_=ob[o][:, :n]), r=["pj_ob%d" % o], w=[("GT", S.dma_next)])

                for gi, (t0, n) in enumerate(GROUPS):
                    do_group(gi, t0, n)
            S.barrier()

        def phase_attn(l, last):
            with contextlib.ExitStack() as st:
                v_sb = sb(st, "at_v", [128, T // 128, 512], BF16)
                kth = [sb(st, "at_k%d" % i, [96, T], BF16) for i in range(2)]
                qg = [sb(st, "at_q%d" % i, [96, 512], BF16) for i in range(2)]
                pt = [sb(st, "at_p%d" % i, [128, 512], BF16) for i in range(3)]
                rs = sb(st, "at_rs", [64, 512], F32)
                osb = [sb(st, "at_o%d" % i, [64, 512], BF16) for i in range(2)]
                ps_s = [pst(st, "at_pss%d" % i, [128, 512], F32) for i in range(3)]
                ps_o = [pst(st, "at_pso%d" % i, [128, 512], F32) for i in range(2)]
                ps_m = [pst(st, "at_psm%d" % i, [128, 512], F32) for i in range(2)]
                DM(lambda e: e.dma_start(out=v_sb[:], in_=VV.rearrange("(kt p) c -> p kt c", p=128)), w=["at_v"])
                cnt = [0, 0]
                segs = ([] if last else [(0, CTX, 2)]) + [(t0, n, T // 128) for (t0, n) in GROUPS[1:]]

                def do_head(h):
                    kb = h % 2
                    DM(lambda e: e.dma_start(out=kth[kb][:], in_=KT[h]), w=["at_k%d" % kb])

                    def do_seg(t0, n, nkt):
                        cnt[0] += 1
                        qb = cnt[0] % 2
                        DM(lambda e: e.dma_start(out=qg[qb][:, :n], in_=QT[h, :, t0:t0 + n]), w=["at_q%d" % qb])

                        def do_kt(kt):
                            cnt[1] += 1
                            si = cnt[1] % 3
                            TE(lambda e: e.matmul(ps_s[si][:, :n], lhsT=kth[kb][:, kt * 128:(kt + 1) * 128], rhs=qg[qb][:, :n], start=True, stop=True),
                               r=["at_k%d" % kb, "at_q%d" % qb], w=["at_pss%d" % si])
                            A(lambda e: e.activation(out=pt[si][:, :n], in_=ps_s[si][:, :n], func=AF.Exp, scale=ATT_SCALE),
                              r=["at_pss%d" % si], w=["at_p%d" % si])
                            TE(lambda e: e.matmul(ps_o[qb][:64, :n], lhsT=v_sb[:, kt, h * 64:(h + 1) * 64], rhs=pt[si][:, :n], start=(kt == 0), stop=(kt == nkt - 1)),
                               r=["at_v", "at_p%d" % si], w=["at_pso%d" % qb])
                            TE(lambda e: e.matmul(ps_m[qb][:64, :n], lhsT=ones_b[:, :64], rhs=pt[si][:, :n], start=(kt == 0), stop=(kt == nkt - 1)),
                               r=["ones_b", "at_p%d" % si], w=["at_psm%d" % qb])
                        for kt in range(nkt):
                            do_kt(kt)
                        V(lambda e: e.reciprocal(rs[:, :n], ps_m[qb][:64, :n]), r=["at_psm%d" % qb], w=["at_rs"])
                        V(lambda e: e.tensor_tensor(out=osb[qb][:, :n], in0=ps_o[qb][:64, :n], in1=rs[:, :n], op=ALU.mult),
                          r=["at_pso%d" % qb, "at_rs"], w=["at_o%d" % qb])
                        DM(lambda e: e.dma_start(out=ATT[h // 2, (h % 2) * 64:(h % 2) * 64 + 64, t0:t0 + n], in_=osb[qb][:, :n]),
                           r=["at_o%d" % qb], w=[("ATT", S.dma_next)])
                    for (t0, n, nkt) in segs:
                        do_seg(t0, n, nkt)
                for h in range(NH):
                    do_head(h)
            S.barrier()

        def phase_fourier(l, last):
            segs = ([] if last else [(0, CTX, "cl256", "sl256")]) + [(CTX, SEQ, "cl4096", "sl4096")]

            def do_seg(t0, L, cln, sln):
                ntt = L // 128
                KW = 256
                with contextlib.ExitStack() as st:
                    ccs = sb(st, "fo_ccs", [128, 2, 512], BF16)
                    zft = sb(st, "fo_z", [128, 2, L], BF16)
                    uv = sb(st, "fo_uv", [128, ntt, 512], BF16)
                    clb = [sb(st, "fo_cl%d" % i, [128, ntt, KW], BF16) for i in range(2)]
                    slb = [sb(st, "fo_sl%d" % i, [128, ntt, KW], BF16) for i in range(2)]
                    yo = [sb(st, "fo_y%d" % i, [128, KW], BF16) for i in range(2)]
                    ps = [pst(st, "fo_ps%d" % i, [128, 512], F32) for i in range(4)]
                    DM(lambda e: e.dma_start(out=ccs[:], in_=C["ccs"].rearrange("(c p) n -> p c n", p=128)), w=["fo_ccs"])
                    DM(lambda e: e.dma_start(out=zft[:], in_=ZFT[:, :, t0:t0 + L].rearrange("c p t -> p c t")), w=["fo_z"])

                    def do_tt(tt):
                        pi = tt % 4
                        for c in range(2):
                            TE(lambda e, c=c: e.matmul(ps[pi][:, :], lhsT=zft[:, c, tt * 128:(tt + 1) * 128], rhs=ccs[:, c, :], start=(c == 0), stop=(c == 1)),
                               r=["fo_z", "fo_ccs"], w=["fo_ps%d" % pi])
                        evac(uv[:, tt, :], ps[pi][:, :], r=["fo_ps%d" % pi], w=[("fo_uv", tt)])
                    for tt in range(ntt):
                        do_tt(tt)
                    uvkeys = [("fo_uv", tt) for tt in range(ntt)]
                    scale = 1.0 / math.sqrt(L * 256.0)

                    def do_kc(kc):
                        b = kc % 2
                        DM(lambda e: e.dma_start(out=clb[b][:], in_=C[cln][:, kc * KW:(kc + 1) * KW].rearrange("(tt p) k -> p tt k", p=128)), w=["fo_cl%d" % b])
                        DM(lambda e: e.dma_start(out=slb[b][:], in_=C[sln][:, kc * KW:(kc + 1) * KW].rearrange("(tt p) k -> p tt k", p=128)), w=["fo_sl%d" % b])

                        def do_m(m):
                            pi = (kc * 2 + m) % 4
                            for tt in range(ntt):
                                TE(lambda e, tt=tt: e.matmul(ps[pi][:, :KW], lhsT=uv[:, tt, m * 128:(m + 1) * 128], rhs=clb[b][:, tt, :], start=(tt == 0), stop=False),
                                   r=uvkeys + ["fo_cl%d" % b], w=["fo_ps%d" % pi])
                                TE(lambda e, tt=tt: e.matmul(ps[pi][:, :KW], lhsT=uv[:, tt, 256 + m * 128:256 + (m + 1) * 128], rhs=slb[b][:, tt, :], start=False, stop=(tt == ntt - 1)),
                                   r=uvkeys + ["fo_sl%d" % b], w=["fo_ps%d" % pi])
                            evac(yo[m][:, :], ps[pi][:, :KW], r=["fo_ps%d" % pi], w=["fo_y%d" % m], scale=scale)
                            DM(lambda e: e.dma_start(out=YFT[m, :, t0 + kc * KW:t0 + (kc + 1) * KW], in_=yo[m][:, :]), r=["fo_y%d" % m], w=[("YFT", S.dma_next)])
                        for m in range(2):
                            do_m(m)
                    for kc in range(L // KW):
                        do_kc(kc)
                S.barrier()
            for sg in segs:
                do_seg(*sg)

        def phase_pool(l, last):
            segs = ([] if last else [(0, CTX, "icnt256")]) + [(CTX, SEQ, "icnt4096")]

            def do_seg(t0, L, icn):
                with contextlib.ExitStack() as st:
                    zpp = sb(st, "po_z", [128, 2, L + 16], F32)
                    sA = sb(st, "po_sA", [128, L + 16], F32)
                    sB = sb(st, "po_sB", [128, L + 16], F32)
                    icnt = sb(st, "po_ic", [128, 2, L], F32)
                    tmp = sb(st, "po_tmp", [128, L], F32)
                    poolT = sb(st, "po_pT", [128, 2, L], BF16)
                    wbd = [sb(st, "po_w%d" % i, [128, 128], BF16) for i in range(2)]
                    psc = sb(st, "po_sc", [128, 2], F32)
                    yo = [sb(st, "po_y%d" % i, [128, 512], BF16) for i in range(2)]
                    ps = [pst(st, "po_ps%d" % i, [128, 512], F32) for i in range(2)]
                    V(lambda e: e.memset(zpp[:], 0.0), w=["po_z"])
                    for ch in range(2):
                        V(lambda e, ch=ch: e.memset(wbd[ch][:], 0.0), w=["po_w%d" % ch])
                    DM(lambda e: e.dma_start(out=zpp[:, :, 8:8 + L], in_=ZPT[:, :, t0:t0 + L].rearrange("c p t -> p c t")), w=["po_z"])
                    DM(lambda e: e.dma_start(out=icnt[:], in_=C[icn].rearrange("c p t -> p c t")), w=["po_ic"])
                    DM(lambda e: e.dma_start(out=psc[:], in_=P["pool_scale"][l].rearrange("(c p) -> p c", p=128), allow_slow_non_contiguous=True), w=["po_sc"])
                    for g in range(4):
                        o = (g % 2) * 64
                        DM(lambda e, g=g, o=o: e.dma_start(out=wbd[g // 2][o:o + 64, o:o + 64], in_=P["w_grp"][l, g]), w=["po_w%d" % (g // 2)], q="gpsimd")

                    def do_ch(ch):
                        V(lambda e: e.tensor_tensor(out=sA[:, 1:L + 16], in0=zpp[:, ch, 0:L + 15], in1=zpp[:, ch, 1:L + 16], op=ALU.add), r=["po_z"], w=["po_sA"])
                        V(lambda e: e.tensor_tensor(out=sB[:, 2:L + 15], in0=sA[:, 1:L + 14], in1=sA[:, 3:L + 16], op=ALU.add), r=["po_sA"], w=["po_sB"])
                        if ch == 1:
                            V(lambda e: e.tensor_tensor(out=sA[:, 4:L + 13], in0=sB[:, 2:L + 11], in1=sB[:, 6:L + 15], op=ALU.add), r=["po_sB"], w=["po_sA"])
                            V(lambda e: e.tensor_tensor(out=sB[:, 8:L + 9], in0=sA[:, 4:L + 5], in1=sA[:, 12:L + 13], op=ALU.add), r=["po_sA"], w=["po_sB"])
                        V(lambda e: e.tensor_tensor(out=tmp[0:64, :], in0=sA[0:64, 8:8 + L], in1=icnt[0:64, ch, :], op=ALU.mult), r=["po_sA", "po_ic"], w=["po_tmp"])
                        V(lambda e: e.tensor_tensor(out=tmp[64:128, :], in0=sB[64:128, 8:8 + L], in1=icnt[64:128, ch, :], op=ALU.mult), r=["po_sB", "po_ic"], w=["po_tmp"])
                        V(lambda e: e.tensor_tensor(out=poolT[:, ch, :], in0=tmp[:, :], in1=zpp[:, ch, 8:8 + L], op=ALU.subtract), r=["po_tmp", "po_z"], w=[("po_pT", ch)])

                        def do_tc(tc_):
                            n = min(512, L - tc_ * 512)
                            pi = tc_ % 2
                            TE(lambda e: e.matmul(ps[pi][:, :n], lhsT=wbd[ch][:], rhs=poolT[:, ch, tc_ * 512:tc_ * 512 + n], start=True, stop=True),
                               r=[("po_pT", ch), "po_w%d" % ch], w=["po_ps%d" % pi])
                            V(lambda e: e.tensor_scalar(yo[pi][:, :n], ps[pi][:, :n], psc[:, ch:ch + 1], None, op0=ALU.mult), r=["po_ps%d" % pi, "po_sc"], w=["po_y%d" % pi])
                            DM(lambda e: e.dma_start(out=YCT[ch, :, t0 + tc_ * 512:t0 + tc_ * 512 + n], in_=yo[pi][:, :n]), r=["po_y%d" % pi], w=[("YCT", S.dma_next)])
                        for tc_ in range((L + 511) // 512):
                            do_tc(tc_)
                    for ch in range(2):
                        do_ch(ch)
                S.barrier()
            for sg in segs:
                do_seg(*sg)

        def phase_combine(l, last):
            with contextlib.ExitStack() as st:
                w_oa = sb(st, "cb_woa", [128, 4, D], BF16)
                w_ob = sb(st, "cb_wob", [128, 2, D], BF16)
                w_oc = sb(st, "cb_woc", [128, 2, D], BF16)
                w_out = sb(st, "cb_wout", [128, 8, D], BF16)
                w_pq = sb(st, "cb_wpq", [128, 8, 2048], BF16)
                att = sb(st, "cb_att", [128, 4, 512], BF16)
                yf = sb(st, "cb_yf", [128, 2, 512], BF16)
                yc = sb(st, "cb_yc", [128, 2, 512], BF16)
                gT = sb(st, "cb_g", [128, 24, 512], BF16)
                xg = sb(st, "cb_xg", [128, 8, 512], F32)
                sq = sb(st, "cb_sq", [128, 8, 512], F32)
                rstd = sb(st, "cb_rstd", [128, 512], F32)
                u = sb(st, "cb_u", [128, 8, 512], BF16)
                h2 = sb(st, "cb_h2", [128, 8, 512], BF16)
                t1 = sb(st, "cb_t1", [128, 512], F32)
                t2 = sb(st, "cb_t2", [128, 512], F32)
                t3 = sb(st, "cb_t3", [128, 512], F32)
                ob = [sb(st, "cb_ob%d" % i, [128, 512], BF16) for i in range(3)]
                ps = [pst(st, "cb_ps%d" % i, [128, 512], F32) for i in range(6)]
                ps_st = pst(st, "cb_pst", [128, 512], F32)
                for k in range(4):
                    DM(lambda e, k=k: e.dma_start(out=w_oa[:, k, :], in_=P["w_oa"][l][k * 128:(k + 1) * 128, :]), w=["cb_woa"], q="gpsimd")
                for k in range(2):
                    DM(lambda e, k=k: e.dma_start(out=w_ob[:, k, :], in_=P["w_ob"][l][k * 128:(k + 1) * 128, :]), w=["cb_wob"], q="gpsimd")
                    DM(lambda e, k=k: e.dma_start(out=w_oc[:, k, :], in_=P["w_oc"][l][k * 128:(k + 1) * 128, :]), w=["cb_woc"], q="gpsimd")
                for k in range(8):
                    DM(lambda e, k=k: e.dma_start(out=w_out[:, k, :], in_=P["w_out"][l][k * 128:(k + 1) * 128, :]), w=["cb_wout"], q="gpsimd")
                    DM(lambda e, k=k: e.dma_start(out=w_pq[:, k, :], in_=P["w_pq"][l][k * 128:(k + 1) * 128, :]), w=["cb_wpq"], q="gpsimd")
                psi = [0]

                def do_group(gi, t0, n):
                    r_ = 1 if gi == 0 else 0
                    DM(lambda e: e.dma_start(out=att[:, :, :n], in_=ATT[:, :, t0:t0 + n].rearrange("k p t -> p k t")), w=["cb_att"])
                    DM(lambda e: e.dma_start(out=yf[:, :, :n], in_=YFT[:, :, t0:t0 + n].rearrange("k p t -> p k t")), w=["cb_yf"])
                    DM(lambda e: e.dma_start(out=yc[:, :, :n], in_=YCT[:, :, t0:t0 + n].rearrange("k p t -> p k t")), w=["cb_yc"])
                    DM(lambda e: e.dma_start(out=gT[:, :, :n], in_=GT[:, :, t0:t0 + n].rearrange("k p t -> p k t")), w=["cb_g"])
                    DM(lambda e: e.dma_start(out=xg[:, :, :n], in_=XT[:, :, t0:t0 + n].rearrange("k p t -> p k t")), w=["cb_xg"])

                    def do_dch(dc):
                        pa, pb, pc = psi[0] % 6, (psi[0] + 1) % 6, (psi[0] + 2) % 6
                        psi[0] += 3
                        for k in range(4):
                            TE(lambda e, k=k: e.matmul(ps[pa][:, :n], lhsT=w_oa[:, k, dc * 128:(dc + 1) * 128], rhs=att[:, k, :n], start=(k == 0), stop=(k == 3)),
                               r=["cb_woa", "cb_att"], w=["cb_ps%d" % pa])
                        for k in range(2):
                            TE(lambda e, k=k: e.matmul(ps[pb][:, :n], lhsT=w_ob[:, k, dc * 128:(dc + 1) * 128], rhs=yf[:, k, :n], start=(k == 0), stop=(k == 1)),
                               r=["cb_wob", "cb_yf"], w=["cb_ps%d" % pb])
                        for k in range(2):
                            TE(lambda e, k=k: e.matmul(ps[pc][:, :n], lhsT=w_oc[:, k, dc * 128:(dc + 1) * 128], rhs=yc[:, k, :n], start=(k == 0), stop=(k == 1)),
                               r=["cb_woc", "cb_yc"], w=["cb_ps%d" % pc])
                        V(lambda e: e.tensor_tensor(out=t1[:, :n], in0=ps[pa][:, :n], in1=gT[:, dc, :n], op=ALU.mult), r=["cb_ps%d" % pa, "cb_g"], w=["cb_t1"])
                        V(lambda e: e.tensor_tensor(out=t2[:, :n], in0=ps[pb][:, :n], in1=gT[:, 8 + dc, :n], op=ALU.mult), r=["cb_ps%d" % pb, "cb_g"], w=["cb_t2"])
                        V(lambda e: e.tensor_tensor(out=t3[:, :n], in0=ps[pc][:, :n], in1=gT[:, 16 + dc, :n], op=ALU.mult), r=["cb_ps%d" % pc, "cb_g"], w=["cb_t3"])
                        G(lambda e: e.tensor_tensor(out=t1[:, :n], in0=t1[:, :n], in1=t2[:, :n], op=ALU.add), r=["cb_t1", "cb_t2"], w=["cb_t1"])
                        G(lambda e: e.tensor_tensor(out=u[:, dc, :n], in0=t1[:, :n], in1=t3[:, :n], op=ALU.add), r=["cb_t1", "cb_t3"], w=[("cb_u", dc)])
                    for dc in range(8):
                        do_dch(dc)
                    ukeys = [("cb_u", dc) for dc in range(8)]

                    def do_out(dc):
                        pm = psi[0] % 6
                        psi[0] += 1
                        for k in range(8):
                            TE(lambda e, k=k: e.matmul(ps[pm][:, :n], lhsT=w_out[:, k, dc * 128:(dc + 1) * 128], rhs=u[:, k, :n], start=(k == 0), stop=(k == 7)),
                               r=ukeys + ["cb_wout"], w=["cb_ps%d" % pm])
                        V(lambda e: e.scalar_tensor_tensor(out=xg[:, dc, :n], in0=ps[pm][:, :n], scalar=gt1[:, dc, r_:r_ + 1], in1=xg[:, dc, :n], op0=ALU.mult, op1=ALU.add),
                          r=["cb_ps%d" % pm, "gt1", "cb_xg"], w=["cb_xg"])
                    for dc in range(8):
                        do_out(dc)
                    DM(lambda e: e.dma_start(out=XT[:, :, t0:t0 + n].rearrange("k p t -> p k t"), in_=xg[:, :, :n]), r=["cb_xg"], w=[("XTw", gi)])
                    rms_stats(xg, 8, n, D, sq, ps_st, rstd, ["cb_xg"], "cb_sq", "cb_pst", "cb_rstd")

                    def do_h2(k):
                        V(lambda e: e.tensor_tensor(out=sq[:, k, :n], in0=xg[:, k, :n], in1=rstd[:, :n], op=ALU.mult), r=["cb_xg", "cb_rstd"], w=["cb_sq"])
                        V(lambda e: e.tensor_scalar(h2[:, k, :n], sq[:, k, :n], gm2[:, k, r_:r_ + 1], sh2[:, k, r_:r_ + 1], op0=ALU.mult, op1=ALU.add),
                          r=["cb_sq", "gm2", "sh2"], w=["cb_h2"])
                    for k in range(8):
                        do_h2(k)
                    DM(lambda e: e.dma_start(out=H2T[:, :, t0:t0 + n].rearrange("k p t -> p k t"), in_=h2[:, :, :n]), r=["cb_h2"], w=[("H2T", gi)])

                    def do_pq(nc_):
                        pm = psi[0] % 6
                        psi[0] += 1
                        o = nc_ % 3
                        for k in range(8):
                            TE(lambda e, k=k: e.matmul(ps[pm][:, :n], lhsT=w_pq[:, k, nc_ * 128:(nc_ + 1) * 128], rhs=h2[:, k, :n], start=(k == 0), stop=(k == 7)),
                               r=["cb_h2", "cb_wpq"], w=["cb_ps%d" % pm])
                        evac(ob[o][:, :n], ps[pm][:, :n], r=["cb_ps%d" % pm], w=["cb_ob%d" % o])
                        DM(lambda e: e.dma_start(out=PQT[nc_, :, t0:t0 + n], in_=ob[o][:, :n]), r=["cb_ob%d" % o], w=[("PQT", S.dma_next)])
                    for nc_ in range(16):
                        do_pq(nc_)
                for gi, (t0, n) in enumerate(GROUPS):
                    if last and gi == 0:
                        continue
                    do_group(gi, t0, n)
            S.barrier()

        def phase_peer(l, last):
            with contextlib.ExitStack() as st:
                kraw = sb(st, "pe_kraw", [128, 16, 128], F32)
                keysT = sb(st, "pe_keysT", [128, 16, 128], BF16)
                iotf = sb(st, "pe_iotf", [128, 16], F32)
                pq = sb(st, "pe_pq", [128, 16, 128], BF16)
                s_sb = sb(st, "pe_s", [128, 16, 128], F32)
                s2 = sb(st, "pe_s2", [128, 16, 128], F32)
                sv = sb(st, "pe_sv", [128, 16, 16], F32)
                si = sb(st, "pe_si", [128, 16, 16], U32)
                sif = sb(st, "pe_sif", [128, 16, 16], F32)
                cand = sb(st, "pe_cand", [128, 8, 256], F32)
                cand2 = sb(st, "pe_cand2", [128, 8, 256], F32)
                ts = sb(st, "pe_ts", [128, 8, 16], F32)
                pos = sb(st, "pe_pos", [128, 8, 16], U32)
                pa_i = sb(st, "pe_pai", [128, 8, 16], U32)
                pb_i = sb(st, "pe_pbi", [128, 8, 16], U32)
                pa_f = sb(st, "pe_paf", [128, 8, 16], F32)
                pb_f = sb(st, "pe_pbf", [128, 8, 16], F32)
                eq = sb(st, "pe_eq", [128, 8, 16, 16], F32)
                If = sb(st, "pe_If", [128, 8, 16], F32)
                Jf = sb(st, "pe_Jf", [128, 8, 16], F32)
                ef = sb(st, "pe_ef", [128, 8, 16], F32)
                eidx = sb(st, "pe_eidx", [128, 128], U32)
                gate = sb(st, "pe_gate", [128, 8, 16], F32)
                gsum = sb(st, "pe_gsum", [128, 8], F32)
                h2T = sb(st, "pe_h2T", [128, 8, 128], BF16)
                h2tok = sb(st, "pe_h2tok", [128, D], F32)
                rows = [sb(st, "pe_rows%d" % i, [128, D], F32) for i in range(6)]
                junk = sb(st, "pe_junk", [128, D], F32)
                a_sb = sb(st, "pe_a", [128, 128], F32)
                g1 = sb(st, "pe_g1", [128, 128], F32)
                g2 = sb(st, "pe_g2", [128, 128], F32)
                act = sb(st, "pe_act", [128, 128], F32)
                acc = sb(st, "pe_acc", [128, D], F32)
                xt = sb(st, "pe_xt", [128, 8, 128], F32)
                ps_sc = [pst(st, "pe_psc%d" % i, [128, 4, 128], F32) for i in range(4)]
                ps_h2 = pst(st, "pe_ph2", [128, 8, 128], BF16)
                ps_tr = [pst(st, "pe_ptr%d" % i, [128, 4, 128], F32) for i in range(2)]

                G(lambda e: e.iota(iotf[:], [[1, 16]], base=0, channel_multiplier=0, allow_small_or_imprecise_dtypes=True), w=["pe_iotf"])
                DM(lambda e: e.dma_start(out=kraw[:], in_=P["peer_keys"][l].rearrange("h p k d -> k (h p) d")), w=["pe_kraw"])
                for hp in range(16):
                    TE(lambda e, hp=hp: e.transpose(ps_sc[hp // 4][:, hp % 4, :], kraw[:, hp, :], ident_f[:]), r=["pe_kraw", "ident_f"], w=["pe_psc%d" % (hp // 4)])
                for b4 in range(4):
                    evac(keysT[:, b4 * 4:(b4 + 1) * 4, :], ps_sc[b4][:], r=["pe_psc%d" % b4], w=["pe_keysT"])
                rcnt = [0]

                def do_tile(ti):
                    c0 = ti * 128
                    r_ = 1 if ti < 2 else 0
                    DM(lambda e: e.dma_start(out=pq[:], in_=PQT[:, :, c0:c0 + 128].rearrange("n p t -> p n t")), w=["pe_pq"])
                    DM(lambda e: e.dma_start(out=h2T[:], in_=H2T[:, :, c0:c0 + 128].rearrange("k p t -> p k t")), w=["pe_h2T"])
                    DM(lambda e: e.dma_start(out=xt[:], in_=XT[:, :, c0:c0 + 128].rearrange("k p t -> p k t")), w=["pe_xt"])
                    for hp in range(16):
                        TE(lambda e, hp=hp: e.matmul(ps_sc[hp // 4][:, hp % 4, :], lhsT=pq[:, hp, :], rhs=keysT[:, hp, :], start=True, stop=True),
                           r=["pe_pq", "pe_keysT"], w=["pe_psc%d" % (hp // 4)])
                    for b4 in range(4):
                        evac(s_sb[:, b4 * 4:(b4 + 1) * 4, :], ps_sc[b4][:], r=["pe_psc%d" % b4], w=["pe_s"])
                    for k in range(8):
                        TE(lambda e, k=k: e.transpose(ps_h2[:, k, :], h2T[:, k, :], ident_b[:]), r=["pe_h2T", "ident_b"], w=["pe_ph2"])
                    A(lambda e: e.activation(out=h2tok[:], in_=ps_h2[:].rearrange("p k d -> p (k d)"), func=AF.Copy), r=["pe_ph2"], w=["pe_h2tok"])
                    for hp in range(16):
                        V(lambda e, hp=hp: e.max(out=sv[:, hp, 0:8], in_=s_sb[:, hp, :]), r=["pe_s"], w=["pe_sv"])
                        V(lambda e, hp=hp: e.max_index(out=si[:, hp, 0:8], in_max=sv[:, hp, 0:8], in_values=s_sb[:, hp, :]), r=["pe_s", "pe_sv"], w=["pe_si"])
                        V(lambda e, hp=hp: e.match_replace(out=s2[:, hp, :], in_to_replace=sv[:, hp, 0:8], in_values=s_sb[:, hp, :], imm_value=-1e30),
                          r=["pe_s", "pe_sv"], w=["pe_s2"])
                        V(lambda e, hp=hp: e.max(out=sv[:, hp, 8:16], in_=s2[:, hp, :]), r=["pe_s2"], w=["pe_sv"])
                        V(lambda e, hp=hp: e.max_index(out=si[:, hp, 8:16], in_max=sv[:, hp, 8:16], in_values=s2[:, hp, :]), r=["pe_s2", "pe_sv"], w=["pe_si"])
                    V(lambda e: e.tensor_copy(sif[:], si[:]), r=["pe_si"], w=["pe_sif"])
                    svv = sv[:].rearrange("p (h q) a -> p h q a", q=2)
                    sfv = sif[:].rearrange("p (h q) a -> p h q a", q=2)
                    candv = cand[:].rearrange("p h (a b) -> p h a b", b=16)
                    V(lambda e: e.tensor_tensor(out=candv, in0=svv[:, :, 0, :].unsqueeze(3).to_broadcast([128, 8, 16, 16]),
                                                in1=svv[:, :, 1, :].unsqueeze(2).to_broadcast([128, 8, 16, 16]), op=ALU.add), r=["pe_sv"], w=["pe_cand"])
                    for h in range(8):
                        V(lambda e, h=h: e.max(out=ts[:, h, 0:8], in_=cand[:, h, :]), r=["pe_cand"], w=["pe_ts"])
                        V(lambda e, h=h: e.max_index(out=pos[:, h, 0:8], in_max=ts[:, h, 0:8], in_values=cand[:, h, :]), r=["pe_cand", "pe_ts"], w=["pe_pos"])
                        V(lambda e, h=h: e.match_replace(out=cand2[:, h, :], in_to_replace=ts[:, h, 0:8], in_values=cand[:, h, :], imm_value=-1e30),
                          r=["pe_cand", "pe_ts"], w=["pe_cand2"])
                        V(lambda e, h=h: e.max(out=ts[:, h, 8:16], in_=cand2[:, h, :]), r=["pe_cand2"], w=["pe_ts"])
                        V(lambda e, h=h: e.max_index(out=pos[:, h, 8:16], in_max=ts[:, h, 8:16], in_values=cand2[:, h, :]), r=["pe_cand2", "pe_ts"], w=["pe_pos"])
                    V(lambda e: e.tensor_single_scalar(pa_i[:], pos[:], 4, op=ALU.logical_shift_right), r=["pe_pos"], w=["pe_pai"])
                    V(lambda e: e.tensor_single_scalar(pb_i[:], pos[:], 15, op=ALU.bitwise_and), r=["pe_pos"], w=["pe_pbi"])
                    V(lambda e: e.tensor_copy(pa_f[:], pa_i[:]), r=["pe_pai"], w=["pe_paf"])
                    V(lambda e: e.tensor_copy(pb_f[:], pb_i[:]), r=["pe_pbi"], w=["pe_pbf"])
                    iob = iotf[:].unsqueeze(1).unsqueeze(1).to_broadcast([128, 8, 16, 16])
                    for (pf, half, dst, kd) in ((pa_f, 0, If, "pe_If"), (pb_f, 1, Jf, "pe_Jf")):
                        V(lambda e, pf=pf: e.tensor_tensor(out=eq[:], in0=iob, in1=pf[:].unsqueeze(3).to_broadcast([128, 8, 16, 16]), op=ALU.is_equal),
                          r=["pe_iotf", "pe_paf", "pe_pbf"], w=["pe_eq"])
                        V(lambda e, half=half: e.tensor_tensor(out=eq[:], in0=eq[:], in1=sfv[:, :, half, :].unsqueeze(2).to_broadcast([128, 8, 16, 16]), op=ALU.mult),
                          r=["pe_eq", "pe_sif"], w=["pe_eq"])
                        V(lambda e, dst=dst: e.tensor_reduce(out=dst[:], in_=eq[:], axis=AX.X, op=ALU.add), r=["pe_eq"], w=[kd])
                    V(lambda e: e.scalar_tensor_tensor(out=ef[:], in0=If[:], scalar=128.0, in1=Jf[:], op0=ALU.mult, op1=ALU.add), r=["pe_If", "pe_Jf"], w=["pe_ef"])
                    V(lambda e: e.tensor_copy(eidx[:], ef[:].rearrange("p h k -> p (h k)")), r=["pe_ef"], w=["pe_eidx"])
                    V(lambda e: e.tensor_tensor(out=gate[:], in0=ts[:], in1=ts[:, :, 0:1].to_broadcast([128, 8, 16]), op=ALU.subtract), r=["pe_ts"], w=["pe_gate"])
                    A(lambda e: e.activation(out=gate[:], in_=gate[:], func=AF.Exp), r=["pe_gate"], w=["pe_gate"])
                    V(lambda e: e.tensor_reduce(out=gsum[:], in_=gate[:], axis=AX.X, op=ALU.add), r=["pe_gate"], w=["pe_gsum"])
                    V(lambda e: e.reciprocal(gsum[:], gsum[:]), r=["pe_gsum"], w=["pe_gsum"])
                    V(lambda e: e.tensor_tensor(out=gate[:], in0=gate[:], in1=gsum[:].unsqueeze(2).to_broadcast([128, 8, 16]), op=ALU.mult), r=["pe_gate", "pe_gsum"], w=["pe_gate"])

                    def do_down(sl):
                        rcnt[0] += 1
                        b = rcnt[0] % 6
                        DM(lambda e: e.indirect_dma_start(out=rows[b][:, :], out_offset=None, in_=P["peer_down%d" % l],
                                                          in_offset=bass.IndirectOffsetOnAxis(ap=eidx[:, sl:sl + 1], axis=0)),
                           r=["pe_eidx"], w=["pe_rows%d" % b], q="gpsimd")
                        V(lambda e: e.scalar_tensor_tensor(out=junk[:], in0=rows[b][:], scalar=1.0, in1=h2tok[:], op0=ALU.mult, op1=ALU.mult,
                                                           accum_out=a_sb[:, sl:sl + 1]),
                          r=["pe_rows%d" % b, "pe_h2tok"], w=["pe_a", "pe_junk"])
                    for sl in range(128):
                        do_down(sl)
                    V(lambda e: e.tensor_tensor(out=g1[:], in0=a_sb[:], in1=a_sb[:], op=ALU.mult), r=["pe_a"], w=["pe_g1"])
                    V(lambda e: e.tensor_scalar(g1[:], g1[:], 0.044715, 1.0, op0=ALU.mult, op1=ALU.add), r=["pe_g1"], w=["pe_g1"])
                    V(lambda e: e.tensor_tensor(out=g1[:], in0=g1[:], in1=a_sb[:], op=ALU.mult), r=["pe_g1", "pe_a"], w=["pe_g1"])
                    A(lambda e: e.activation(out=g2[:], in_=g1[:], func=AF.Sigmoid, scale=1.5957691216057308), r=["pe_g1"], w=["pe_g2"])
                    V(lambda e: e.tensor_tensor(out=g2[:], in0=g2[:], in1=a_sb[:], op=ALU.mult), r=["pe_g2", "pe_a"], w=["pe_g2"])
                    V(lambda e: e.tensor_tensor(out=act[:], in0=g2[:], in1=gate[:].rearrange("p h k -> p (h k)"), op=ALU.mult), r=["pe_g2", "pe_gate"], w=["pe_act"])
                    V(lambda e: e.memset(acc[:], 0.0), w=["pe_acc"])

                    def do_up(sl):
                        rcnt[0] += 1
                        b = rcnt[0] % 6
                        DM(lambda e: e.indirect_dma_start(out=rows[b][:, :], out_offset=None, in_=P["peer_up%d" % l],
                                                          in_offset=bass.IndirectOffsetOnAxis(ap=eidx[:, sl:sl + 1], axis=0)),
                           r=["pe_eidx"], w=["pe_rows%d" % b], q="gpsimd")
                        V(lambda e: e.scalar_tensor_tensor(out=acc[:], in0=rows[b][:], scalar=act[:, sl:sl + 1], in1=acc[:], op0=ALU.mult, op1=ALU.add),
                          r=["pe_rows%d" % b, "pe_act", "pe_acc"], w=["pe_acc"])
                    for sl in range(128):
                        do_up(sl)
                    for k in range(8):
                        TE(lambda e, k=k: e.transpose(ps_tr[k // 4][:, k % 4, :], acc[:, k * 128:(k + 1) * 128], ident_f[:]), r=["pe_acc", "ident_f"], w=["pe_ptr%d" % (k // 4)])
                    for k in range(8):
                        V(lambda e, k=k: e.scalar_tensor_tensor(out=xt[:, k, :], in0=ps_tr[k // 4][:, k % 4, :], scalar=gt2[:, k, r_:r_ + 1], in1=xt[:, k, :],
                                                               op0=ALU.mult, op1=ALU.add),
                          r=["pe_ptr%d" % (k // 4), "gt2", "pe_xt"], w=["pe_xt"])
                    DM(lambda e: e.dma_start(out=XT[:, :, c0:c0 + 128].rearrange("k p t -> p k t"), in_=xt[:]), r=["pe_xt"], w=[("XTw", ti)])
                for ti in range(2 if last else 0, T // 128):
                    if peer_tiles is not None and ti not in peer_tiles:
                        continue
                    do_tile(ti)
            S.barrier()

        def phase_final():
            with contextlib.ExitStack() as st:
                fg = sb(st, "fn_g", [128, 8], F32)
                xg = sb(st, "fn_xg", [128, 8, 512], F32)
                sq = sb(st, "fn_sq", [128, 8, 512], F32)
                rstd = sb(st, "fn_rstd", [128, 512], F32)
                ot = [sb(st, "fn_o%d" % i, [128, D], F32) for i in range(2)]
                ps_st = pst(st, "fn_pst", [128, 512], F32)
                ps = [pst(st, "fn_ps%d" % i, [128, 4, 128], F32) for i in range(4)]
                DM(lambda e: e.dma_start(out=fg[:], in_=P["final_g"].rearrange("(n p) -> p n", p=128), allow_slow_non_contiguous=True), w=["fn_g"])
                cnt = [0]

                def do_group(gi, t0, n):
                    DM(lambda e: e.dma_start(out=xg[:, :, :n], in_=XT[:, :, t0:t0 + n].rearrange("k p t -> p k t")), w=["fn_xg"])
                    rms_stats(xg, 8, n, D, sq, ps_st, rstd, ["fn_xg"], "fn_sq", "fn_pst", "fn_rstd")

                    def do_k(k):
                        V(lambda e: e.tensor_tensor(out=sq[:, k, :n], in0=xg[:, k, :n], in1=rstd[:, :n], op=ALU.mult), r=["fn_xg", "fn_rstd"], w=["fn_sq"])
                        V(lambda e: e.tensor_scalar(xg[:, k, :n], sq[:, k, :n], fg[:, k:k + 1], None, op0=ALU.mult), r=["fn_sq", "fn_g"], w=["fn_xg"])
                    for k in range(8):
                        do_k(k)

                    def do_j(j):
                        cnt[0] += 1
                        ob_ = cnt[0] % 2
                        for half in range(2):
                            pi = (cnt[0] * 2 + half) % 4
                            for kk in range(4):
                                k = half * 4 + kk
                                TE(lambda e, k=k, kk=kk, pi=pi: e.transpose(ps[pi][:, kk, :], xg[:, k, j * 128:(j + 1) * 128], ident_f[:]),
                                   r=["fn_xg", "ident_f"], w=["fn_ps%d" % pi])
                            evac(ot[ob_][:, half * 512:(half + 1) * 512], ps[pi][:].rearrange("p a b -> p (a b)"), r=["fn_ps%d" % pi], w=[("fn_o", ob_, half)])
                        row0 = t0 - CTX + j * 128
                        DM(lambda e: e.dma_start(out=out[row0:row0 + 128, :], in_=ot[ob_][:]), r=[("fn_o", ob_, 0), ("fn_o", ob_, 1)], w=[("out", row0)])
                    for j in range(n // 128):
                        do_j(j)
                for gi, (t0, n) in enumerate(GROUPS):
                    if gi == 0:
                        continue
                    do_group(gi, t0, n)
            S.barrier()

        phase_load_x()
        for l in range(DEPTH):
            last = l == DEPTH - 1
            phase_mod(l)
            phase_proj(l, last)
            if stop_after == ("proj", l):
                break
            phase_attn(l, last)
            phase_fourier(l, last)
            phase_pool(l, last)
            if stop_after == ("mix", l):
                break
            phase_combine(l, last)
            if stop_after == ("combine", l):
                break
            phase_peer(l, last)
            if stop_after == ("peer", l):
                break
        if stop_after is None:
            phase_final()
        S.emit()
    return nc, S


def make_in_maps(inputs, cores):
    consts = _const_tables()
    maps = []
    for b in cores:
        m = {"x": np.ascontiguousarray(inputs["x"][b]), "ctx": np.ascontiguousarray(inputs["ctx"][b]),
             "cvec": np.ascontiguousarray(np.stack([inputs["c"][b], inputs["c_ctx"]], axis=0))}
        for k in PARAM_SHAPES:
            if k.startswith("peer_down") or k.startswith("peer_up"):
                m[k] = np.ascontiguousarray(inputs[k[:-1]][int(k[-1])])
            else:
                m[k] = np.ascontiguousarray(inputs[k])
        m.update(consts)
        maps.append(m)
    return maps


def kernel(**inputs):
    inputs = {k: np.asarray(v) for k, v in inputs.items()}
    nc, _ = build()
    cores = list(range(8))
    res = run_bass_kernel_spmd(nc, make_in_maps(inputs, cores), core_ids=cores)
    return np.stack([res.results[i]["out"] for i in cores], axis=0).astype(np.float32)
```

```python
import contextlib
import math
import numpy as np
import ml_dtypes
import concourse.bass as bass
import concourse.mybir as mybir
from concourse.bass_utils import run_bass_kernel_spmd

F32 = mybir.dt.float32
BF16 = mybir.dt.bfloat16
U32 = mybir.dt.uint32
I32 = mybir.dt.int32
AF = mybir.ActivationFunctionType
ALU = mybir.AluOpType
AX = mybir.AxisListType

ENGS = ("sync", "scalar", "vector", "gpsimd", "tensor")
N_DMA_SEMS = 90

D = 1024
SEQ = 4096
CTX = 256
T = SEQ + CTX
DEPTH = 2
NH = 8
EPS = 1e-6
ATT_SCALE = 96 ** -0.5
IN_W = 4000
NEXP = 16384


class Sched:
    def __init__(self, nc):
        self.nc = nc
        self.ops = {e: [] for e in ENGS}
        self.count = {e: 0 for e in ENGS}
        self.known = {e: {} for e in ENGS}
        self.last_w = {}
        self.readers = {}
        self.dma_next = 0
        self.dma_target = [0] * N_DMA_SEMS
        self.pending = {e: [] for e in ENGS}
        self.n_instr = 0

    def _need(self, eng, tok, waits):
        if tok is None:
            return
        kind, src, val = tok
        if kind == "e" and src == eng and eng == "tensor":
            return
        k = (kind, src)
        if self.known[eng].get(k, 0) >= val:
            return
        self.known[eng][k] = val
        waits.append(tok)

    def _deps(self, eng, r, w):
        waits = []
        for tok in self.pending[eng]:
            self._need(eng, tok, waits)
        self.pending[eng] = []
        for key in r:
            self._need(eng, self.last_w.get(key), waits)
        for key in w:
            self._need(eng, self.last_w.get(key), waits)
            for tok in self.readers.get(key, ()):
                self._need(eng, tok, waits)
        best = {}
        for kind, src, val in waits:
            k = (kind, src)
            if best.get(k, 0) < val:
                best[k] = val
        return [(k[0], k[1], v) for k, v in best.items()]

    def _commit(self, tok, r, w):
        for key in r:
            self.readers.setdefault(key, []).append(tok)
        for key in w:
            self.last_w[key] = tok
            self.readers[key] = []

    def op(self, eng, fn, r=(), w=()):
        waits = self._deps(eng, r, w)
        self.count[eng] += 1
        tok = ("e", eng, self.count[eng])
        self._commit(tok, r, w)
        self.ops[eng].append(("op", fn, waits, None))
        self.n_instr += 1 + len(waits)
        return tok

    def dma(self, eng, fn, r=(), w=()):
        s = self.dma_next % N_DMA_SEMS
        self.dma_next += 1
        waits = self._deps(eng, r, w)
        prev = self.dma_target[s]
        if prev > 0:
            k = ("d", s)
            if self.known[eng].get(k, 0) < prev:
                self.known[eng][k] = prev
                waits.append(("d", s, prev))
        self.dma_target[s] = prev + 16
        tok = ("d", s, prev + 16)
        self._commit(tok, r, w)
        self.ops[eng].append(("dma", fn, waits, s))
        self.n_instr += 1 + len(waits)
        return tok

    def barrier(self):
        waits = []
        for e in ENGS:
            if e != "sync" and self.count[e] > 0:
                self._need("sync", ("e", e, self.count[e]), waits)
        for s in range(N_DMA_SEMS):
            if self.dma_target[s] > 0:
                self._need("sync", ("d", s, self.dma_target[s]), waits)
        self.count["sync"] += 1
        tok = ("e", "sync", self.count["sync"])
        self.ops["sync"].append(("op", lambda e: e.nop(), waits, None))
        self.n_instr += 1 + len(waits)
        for e in ENGS:
            if e != "sync":
                self.pending[e].append(tok)
                for e2 in ENGS:
                    self.known[e][("e", e2)] = max(self.known[e].get(("e", e2), 0),
                                                   self.count[e2] if e2 != "sync" else 0)
                for s in range(N_DMA_SEMS):
                    self.known[e][("d", s)] = self.dma_target[s]
                self.known[e].pop(("e", "sync"), None)
        self.last_w = {}
        self.readers = {}

    def emit(self):
        nc = self.nc
        with contextlib.ExitStack() as st:
            esem = {e: st.enter_context(nc.semaphore("c_" + e)) for e in ENGS}
            dsem = [st.enter_context(nc.semaphore("d%d" % i)) for i in range(N_DMA_SEMS)]
            block = st.enter_context(nc.Block())

            def mk(ename):
                def body(eng):
                    for kind, fn, waits, s in self.ops[ename]:
                        for wk, src, val in waits:
                            eng.wait_ge(esem[src] if wk == "e" else dsem[src], val)
                        ins = fn(eng)
                        if kind == "op":
                            ins.then_inc(esem[ename], 1)
                        else:
                            ins.then_inc(dsem[s], 16)
                    if ename == "sync":
                        for i in range(N_DMA_SEMS):
                            if self.dma_target[i] > 0:
                                eng.wait_ge(dsem[i], self.dma_target[i])
                        for e2 in ENGS:
                            if e2 != "sync" and self.count[e2] > 0:
                                eng.wait_ge(esem[e2], self.count[e2])
                return body

            block.sync(mk("sync"))
            block.scalar(mk("scalar"))
            block.vector(mk("vector"))
            block.gpsimd(mk("gpsimd"))
            block.tensor(mk("tensor"))


def _const_tables():
    c = {}
    half = 16
    inv_freq = (10000.0 ** (-np.arange(0, half, 2, dtype=np.float32) / half)).astype(np.float32)
    t = np.arange(SEQ)
    r = (t // 64).astype(np.float32)
    cl = (t % 64).astype(np.float32)
    ang_r = (r[None, :] * inv_freq[:, None]).astype(np.float32)
    ang_c = (cl[None, :] * inv_freq[:, None]).astype(np.float32)
    ang = np.concatenate([ang_r, ang_r, ang_c, ang_c], axis=0)
    c["ropecos"] = np.cos(ang).astype(np.float32)
    c["ropesin"] = np.sin(ang).astype(np.float32)
    def dft(n):
        k = np.arange(n, dtype=np.int64)
        kk = (k[:, None] * k[None, :]) % n
        a = 2.0 * np.pi * kk.astype(np.float64) / n
        return np.cos(a), np.sin(a)
    cc, sc = dft(256)
    c["ccs"] = np.concatenate([cc, -sc], axis=1).astype(ml_dtypes.bfloat16)
    c["cl256"] = cc.astype(ml_dtypes.bfloat16)
    c["sl256"] = sc.astype(ml_dtypes.bfloat16)
    cL, sL = dft(SEQ)
    c["cl4096"] = cL.astype(ml_dtypes.bfloat16)
    c["sl4096"] = sL.astype(ml_dtypes.bfloat16)
    for L, name in ((SEQ, "icnt4096"), (CTX, "icnt256")):
        tt = np.arange(L)
        tab = np.zeros((2, 128, L), np.float32)
        for gi, w in enumerate((2, 4, 8, 16)):
            lo = np.clip(tt - w // 2, 0, L)
            hi = np.clip(tt - w // 2 + w, 0, L)
            tab[gi // 2, (gi % 2) * 64:(gi % 2) * 64 + 64, :] = (1.0 / (hi - lo).astype(np.float32))[None, :]
        c[name] = tab
    return c


CONST_SHAPES = {
    "ropecos": ([32, SEQ], F32), "ropesin": ([32, SEQ], F32),
    "ccs": ([256, 512], BF16), "cl256": ([256, 256], BF16), "sl256": ([256, 256], BF16),
    "cl4096": ([SEQ, SEQ], BF16), "sl4096": ([SEQ, SEQ], BF16),
    "icnt4096": ([2, 128, SEQ], F32), "icnt256": ([2, 128, CTX], F32),
}

PARAM_SHAPES = {
    "w_mod": [DEPTH, D, 6 * D], "b_mod": [DEPTH, 6 * D], "norm1_g": [DEPTH, D], "norm2_g": [DEPTH, D],
    "w_in": [DEPTH, D, IN_W], "b_gate": [DEPTH, 3 * D], "q_norm_g": [DEPTH, 256], "w_uq": [DEPTH, 256, 768],
    "kv_norm_g": [DEPTH, 128], "w_ukv": [DEPTH, 128, 1024], "w_oa": [DEPTH, 512, D], "w_ob": [DEPTH, 256, D],
    "w_grp": [DEPTH, 4, 64, 64], "pool_scale": [DEPTH, 256], "w_oc": [DEPTH, 256, D], "w_out": [DEPTH, D, D],
    "w_pq": [DEPTH, D, 2048], "peer_keys": [DEPTH, 8, 2, 128, 128], "peer_down0": [NEXP, D], "peer_down1": [NEXP, D],
    "peer_up0": [NEXP, D], "peer_up1": [NEXP, D], "final_g": [D],
}

GROUPS = [(0, CTX)] + [(CTX + 512 * g, 512) for g in range(SEQ // 512)]


def build(stop_after=None, dbg=(), peer_tiles=None):
    nc = bass.Bass("TRN2", target_bir_lowering=False)
    S = Sched(nc)

    def dram_in(name, shape, dt=F32):
        return nc.dram_tensor(name, shape, dt, kind="ExternalInput").ap()

    def dram_scratch(name, shape, dt):
        kind = "ExternalOutput" if name in dbg else "Internal"
        return nc.dram_tensor(name, shape, dt, kind=kind).ap()

    xin = dram_in("x", [SEQ, D])
    ctxin = dram_in("ctx", [CTX, D])
    cvec = dram_in("cvec", [2, D])
    P = {k: dram_in(k, v) for k, v in PARAM_SHAPES.items()}
    C = {k: dram_in(k, v[0], v[1]) for k, v in CONST_SHAPES.items()}
    out = nc.dram_tensor("out", [SEQ, D], F32, kind="ExternalOutput").ap()

    XT = dram_scratch("XT", [8, 128, T], F32)
    GT = dram_scratch("GT", [24, 128, T], BF16)
    QT = dram_scratch("QT", [NH, 96, T], BF16)
    KT = dram_scratch("KT", [NH, 96, T], BF16)
    VV = dram_scratch("VV", [T, 512], BF16)
    ZFT = dram_scratch("ZFT", [2, 128, T], BF16)
    ZPT = dram_scratch("ZPT", [2, 128, T], F32)
    ATT = dram_scratch("ATT", [4, 128, T], BF16)
    YFT = dram_scratch("YFT", [2, 128, T], BF16)
    YCT = dram_scratch("YCT", [2, 128, T], BF16)
    H2T = dram_scratch("H2T", [8, 128, T], BF16)
    PQT = dram_scratch("PQT", [16, 128, T], BF16)
    DUB = [dram_scratch("DUB%d" % i, [NEXP, 2 * D], BF16) for i in range(DEPTH)]

    V = lambda fn, r=(), w=(): S.op("vector", fn, r, w)
    A = lambda fn, r=(), w=(): S.op("scalar", fn, r, w)
    G = lambda fn, r=(), w=(): S.op("gpsimd", fn, r, w)
    TE = lambda fn, r=(), w=(): S.op("tensor", fn, r, w)
    DM = lambda fn, r=(), w=(), q="sync": S.dma(q, fn, r, w)

    rr = [0]

    def evac(out_ap, in_ap, r, w, scale=None):
        rr[0] += 1
        if rr[0] % 2 == 0:
            if scale is None:
                V(lambda e: e.tensor_copy(out_ap, in_ap), r, w)
            else:
                V(lambda e: e.tensor_single_scalar(out_ap, in_ap, float(scale), op=ALU.mult), r, w)
        else:
            A(lambda e: e.activation(out=out_ap, in_=in_ap, func=AF.Copy,
                                     scale=1.0 if scale is None else float(scale)), r, w)

    with contextlib.ExitStack() as top:
        uniq = [0]

        def sb(st, name, shape, dt):
            uniq[0] += 1
            return st.enter_context(nc.sbuf_tensor("%s_%d" % (name, uniq[0]), shape, dt))

        def pst(st, name, shape, dt):
            uniq[0] += 1
            return st.enter_context(nc.psum_tensor("%s_%d" % (name, uniq[0]), shape, dt))

        ident_f = sb(top, "ident_f", [128, 128], F32)
        ident_b = sb(top, "ident_b", [128, 128], BF16)
        ones_f = sb(top, "ones_f", [128, 128], F32)
        ones_b = sb(top, "ones_b", [128, 128], BF16)
        iot = sb(top, "iot", [128, 128], F32)
        gm1 = sb(top, "gm1", [128, 8, 2], F32)
        sh1 = sb(top, "sh1", [128, 8, 2], F32)
        gt1 = sb(top, "gt1", [128, 8, 2], F32)
        gm2 = sb(top, "gm2", [128, 8, 2], F32)
        sh2 = sb(top, "sh2", [128, 8, 2], F32)
        gt2 = sb(top, "gt2", [128, 8, 2], F32)
        epsb = sb(top, "epsb", [128, 1], F32)

        G(lambda e: e.iota(iot[:], [[1, 128]], base=0, channel_multiplier=-1,
                           allow_small_or_imprecise_dtypes=True), w=["iot"])
        V(lambda e: e.tensor_single_scalar(ident_f[:], iot[:], 0.0, op=ALU.is_equal), r=["iot"], w=["ident_f"])
        V(lambda e: e.tensor_single_scalar(ident_b[:], iot[:], 0.0, op=ALU.is_equal), r=["iot"], w=["ident_b"])
        V(lambda e: e.memset(ones_f[:], 1.0), w=["ones_f"])
        V(lambda e: e.memset(ones_b[:], 1.0), w=["ones_b"])
        V(lambda e: e.memset(epsb[:], EPS), w=["epsb"])

        def phase_load_x():
            with contextlib.ExitStack() as st:
                xt = [sb(st, "ld_x%d" % i, [128, D], F32) for i in range(2)]
                xo = [sb(st, "ld_o%d" % i, [128, 8, 128], F32) for i in range(2)]
                ps = [pst(st, "ld_ps%d" % i, [128, 4, 128], F32) for i in range(4)]
                ntile = T // 128
                for ti in range(ntile):
                    b = ti % 2
                    src = ctxin[ti * 128:(ti + 1) * 128, :] if ti < 2 else xin[(ti - 2) * 128:(ti - 1) * 128, :]
                    DM(lambda e, b=b, src=src: e.dma_start(out=xt[b][:], in_=src), w=["ld_x%d" % b])
                    for half in range(2):
                        pi = (ti * 2 + half) % 4
                        for j in range(4):
                            k = half * 4 + j
                            TE(lambda e, b=b, pi=pi, j=j, k=k: e.transpose(ps[pi][:, j, :], xt[b][:, k * 128:(k + 1) * 128], ident_f[:]),
                               r=["ld_x%d" % b, "ident_f"], w=["ld_ps%d" % pi])
                        evac(xo[b][:, half * 4:(half + 1) * 4, :], ps[pi][:], r=["ld_ps%d" % pi], w=[("ld_o", b, half)])
                    DM(lambda e, b=b, ti=ti: e.dma_start(out=XT[:, :, ti * 128:(ti + 1) * 128].rearrange("k p t -> p k t"), in_=xo[b][:]),
                       r=[("ld_o", b, 0), ("ld_o", b, 1)], w=[("XT", ti)])
            S.barrier()

        def phase_mod(l):
            with contextlib.ExitStack() as st:
                cT = sb(st, "md_c", [128, 2, 8], F32)
                scT = sb(st, "md_sc", [128, 8, 2], F32)
                bm = sb(st, "md_b", [128, 48], F32)
                g1 = sb(st, "md_g1", [128, 8], F32)
                g2 = sb(st, "md_g2", [128, 8], F32)
                modT = sb(st, "md_mod", [128, 48, 2], F32)
                wm = [sb(st, "md_w%d" % i, [128, 8, 512], F32) for i in range(2)]
                mps = pst(st, "md_ps", [128, 48, 2], F32)
                for r_ in range(2):
                    DM(lambda e, r_=r_: e.dma_start(out=cT[:, r_, :], in_=cvec[r_].rearrange("(k p) -> p k", p=128), allow_slow_non_contiguous=True), w=["md_c"])
                DM(lambda e: e.dma_start(out=bm[:], in_=P["b_mod"][l].rearrange("(n p) -> p n", p=128), allow_slow_non_contiguous=True), w=["md_b"])
                DM(lambda e: e.dma_start(out=g1[:], in_=P["norm1_g"][l].rearrange("(n p) -> p n", p=128), allow_slow_non_contiguous=True), w=["md_g1"])
                DM(lambda e: e.dma_start(out=g2[:], in_=P["norm2_g"][l].rearrange("(n p) -> p n", p=128), allow_slow_non_contiguous=True), w=["md_g2"])
                for r_ in range(2):
                    A(lambda e, r_=r_: e.activation(out=scT[:, :, r_], in_=cT[:, r_, :], func=AF.Silu), r=["md_c"], w=["md_sc"])
                for j in range(12):
                    b = j % 2
                    DM(lambda e, b=b, j=j: e.dma_start(out=wm[b][:], in_=P["w_mod"][l][:, j * 512:(j + 1) * 512].rearrange("(k p) n -> p k n", p=128)),
                       w=["md_w%d" % b])
                    for i in range(4):
                        n = j * 4 + i
                        for k in range(8):
                            TE(lambda e, b=b, i=i, k=k, n=n: e.matmul(mps[:, n, :], lhsT=wm[b][:, k, i * 128:(i + 1) * 128], rhs=scT[:, k, :],
                                                                      start=(k == 0), stop=(k == 7)),
                               r=["md_w%d" % b, "md_sc"], w=["md_ps"])
                V(lambda e: e.tensor_tensor(out=modT[:], in0=mps[:], in1=bm[:].unsqueeze(2).to_broadcast([128, 48, 2]), op=ALU.add),
                  r=["md_ps", "md_b"], w=["md_mod"])
                V(lambda e: e.tensor_copy(sh1[:], modT[:, 0:8, :]), r=["md_mod"], w=["sh1"])
                V(lambda e: e.scalar_tensor_tensor(out=gm1[:], in0=modT[:, 8:16, :], scalar=1.0, in1=g1[:].unsqueeze(2).to_broadcast([128, 8, 2]),
                                                   op0=ALU.add, op1=ALU.mult), r=["md_mod", "md_g1"], w=["gm1"])
                V(lambda e: e.tensor_copy(gt1[:], modT[:, 16:24, :]), r=["md_mod"], w=["gt1"])
                V(lambda e: e.tensor_copy(sh2[:], modT[:, 24:32, :]), r=["md_mod"], w=["sh2"])
                V(lambda e: e.scalar_tensor_tensor(out=gm2[:], in0=modT[:, 32:40, :], scalar=1.0, in1=g2[:].unsqueeze(2).to_broadcast([128, 8, 2]),
                                                   op0=ALU.add, op1=ALU.mult), r=["md_mod", "md_g2"], w=["gm2"])
                V(lambda e: e.tensor_copy(gt2[:], modT[:, 40:48, :]), r=["md_mod"], w=["gt2"])
            S.barrier()

        def rms_stats(src, nk, n, nfeat, sq, ps_stat, rstd, keys_r, ksq, kps, krstd):
            for k in range(nk):
                A(lambda e, k=k: e.activation(out=sq[:, k, :n], in_=src[:, k, :n], func=AF.Square), r=keys_r, w=[ksq])
            for k in range(nk):
                TE(lambda e, k=k: e.matmul(ps_stat[:, :n], lhsT=ones_f[:], rhs=sq[:, k, :n], start=(k == 0), stop=(k == nk - 1)),
                   r=[ksq, "ones_f"], w=[kps])
            A(lambda e: e.activation(out=rstd[:, :n], in_=ps_stat[:, :n], func=AF.Sqrt, scale=1.0 / nfeat, bias=epsb[:]),
              r=[kps, "epsb"], w=[krstd])
            V(lambda e: e.reciprocal(rstd[:, :n], rstd[:, :n]), r=[krstd], w=[krstd])

        def phase_proj(l, last):
            with contextlib.ExitStack() as st:
                w_in = sb(st, "pj_win", [128, 8, IN_W], BF16)
                w_krot = sb(st, "pj_wkrot", [128, 8, 32], BF16)
                wq_raw = sb(st, "pj_wqraw", [128, 2, 768], BF16)
                wq_rot = sb(st, "pj_wqrot", [128, 2, NH, 32], BF16)
                wkv = sb(st, "pj_wkv", [128, 1024], BF16)
                wv = sb(st, "pj_wv", [128, NH, 64], BF16)
                qg = sb(st, "pj_qg", [128, 2], F32)
                kvg = sb(st, "pj_kvg", [128, 1], F32)
                bg = sb(st, "pj_bg", [128, 24], F32)
                rcos = sb(st, "pj_cos", [32, SEQ], F32)
                rsin = sb(st, "pj_sin", [32, SEQ], F32)
                xg = sb(st, "pj_xg", [128, 8, 512], F32)
                sq = sb(st, "pj_sq", [128, 8, 512], F32)
                rstd = sb(st, "pj_rstd", [128, 512], F32)
                hT = sb(st, "pj_hT", [128, 8, 512], BF16)
                cq = sb(st, "pj_cq", [128, 2, 512], F32)
                cqn = sb(st, "pj_cqn", [128, 2, 512], BF16)
                ckv = sb(st, "pj_ckv", [128, 1, 512], F32)
                cn = sb(st, "pj_cn", [128, 512], BF16)
                rstd2 = sb(st, "pj_rstd2", [128, 512], F32)
                ob = [sb(st, "pj_ob%d" % i, [128, 512], BF16) for i in range(4)]
                of = [sb(st, "pj_of%d" % i, [128, 512], F32) for i in range(2)]
                t1 = sb(st, "pj_t1", [32, 512], F32)
                t2 = sb(st, "pj_t2", [32, 512], F32)
                ps = [pst(st, "pj_ps%d" % i, [128, 512], F32) for i in range(6)]
                ps_st = pst(st, "pj_pst", [128, 512], F32)
                ps_rot = pst(st, "pj_prot", [128, 512], F32)

                for k in range(8):
                    for hh in range(2):
                        DM(lambda e, k=k, hh=hh: e.dma_start(out=w_in[:, k, hh * 2000:(hh + 1) * 2000],
                                                             in_=P["w_in"][l][k * 128:(k + 1) * 128, hh * 2000:(hh + 1) * 2000]),
                           w=["pj_win"], q="gpsimd")
                DM(lambda e: e.dma_start(out=wq_raw[:], in_=P["w_uq"][l].rearrange("(k p) n -> p k n", p=128)), w=["pj_wqraw"], q="gpsimd")
                DM(lambda e: e.dma_start(out=wkv[:], in_=P["w_ukv"][l]), w=["pj_wkv"], q="gpsimd")
                DM(lambda e: e.dma_start(out=qg[:], in_=P["q_norm_g"][l].rearrange("(n p) -> p n", p=128), allow_slow_non_contiguous=True), w=["pj_qg"])
                DM(lambda e: e.dma_start(out=kvg[:], in_=P["kv_norm_g"][l].rearrange("(n p) -> p n", p=128), allow_slow_non_contiguous=True), w=["pj_kvg"])
                DM(lambda e: e.dma_start(out=bg[:], in_=P["b_gate"][l].rearrange("(n p) -> p n", p=128), allow_slow_non_contiguous=True), w=["pj_bg"])
                DM(lambda e: e.dma_start(out=rcos[:], in_=C["ropecos"]), w=["pj_cos"])
                DM(lambda e: e.dma_start(out=rsin[:], in_=C["ropesin"]), w=["pj_sin"])
                kr_cols = w_in[:, :, 384:416].rearrange("p k (rc x f) -> p k rc x f", rc=2, x=2)
                kro = w_krot[:].rearrange("p k (rc x f) -> p k rc x f", rc=2, x=2)
                V(lambda e: e.tensor_single_scalar(kro[:, :, :, 0, :], kr_cols[:, :, :, 1, :], -1.0, op=ALU.mult), r=["pj_win"], w=["pj_wkrot"])
                V(lambda e: e.tensor_copy(kro[:, :, :, 1, :], kr_cols[:, :, :, 0, :]), r=["pj_win"], w=["pj_wkrot"])
                for k in range(2):
                    qr = wq_raw[:, k, :].rearrange("p (h c) -> p h c", c=96)[:, :, 64:96].rearrange("p h (rc x f) -> p h rc x f", rc=2, x=2)
                    qo = wq_rot[:, k, :, :].rearrange("p h (rc x f) -> p h rc x f", rc=2, x=2)
                    for rc in range(2):
                        V(lambda e, qo=qo, qr=qr, rc=rc: e.tensor_single_scalar(qo[:, :, rc, 0, :], qr[:, :, rc, 1, :], -1.0, op=ALU.mult),
                          r=["pj_wqraw"], w=["pj_wqrot"])
                        V(lambda e, qo=qo, qr=qr, rc=rc: e.tensor_copy(qo[:, :, rc, 1, :], qr[:, :, rc, 0, :]), r=["pj_wqraw"], w=["pj_wqrot"])
                V(lambda e: e.tensor_copy(wv[:], wkv[:].rearrange("p (h c) -> p h c", c=128)[:, :, 64:128]), r=["pj_wkv"], w=["pj_wv"])

                obi = [0]

                def nxt_ob():
                    obi[0] += 1
                    return obi[0] % 4

                psi = [0]

                def nxt_ps():
                    psi[0] += 1
                    return psi[0] % 6

                def do_group(gi, t0, n):
                    is_ctx = gi == 0
                    r_ = 1 if is_ctx else 0
                    s0 = t0 - CTX
                    kv_only = last and is_ctx
                    DM(lambda e, t0=t0, n=n: e.dma_start(out=xg[:, :, :n], in_=XT[:, :, t0:t0 + n].rearrange("k p t -> p k t")), r=["XT"], w=["pj_xg"])
                    rms_stats(xg, 8, n, D, sq, ps_st, rstd, ["pj_xg"], "pj_sq", "pj_pst", "pj_rstd")
                    for k in range(8):
                        V(lambda e, k=k, n=n: e.tensor_tensor(out=sq[:, k, :n], in0=xg[:, k, :n], in1=rstd[:, :n], op=ALU.mult),
                          r=["pj_xg", "pj_rstd"], w=["pj_sq"])
                        V(lambda e, k=k, n=n, r_=r_: e.tensor_scalar(hT[:, k, :n], sq[:, k, :n], gm1[:, k, r_:r_ + 1], sh1[:, k, r_:r_ + 1],
                                                                     op0=ALU.mult, op1=ALU.add),
                          r=["pj_sq", "gm1", "sh1"], w=["pj_hT"])

                    p = nxt_ps()
                    for k in range(8):
                        TE(lambda e, k=k, p=p: e.matmul(ps[p][:, :n], lhsT=w_in[:, k, 256:384], rhs=hT[:, k, :n], start=(k == 0), stop=(k == 7)),
                           r=["pj_hT", "pj_win"], w=["pj_ps%d" % p])
                    V(lambda e, p=p: e.tensor_copy(ckv[:, 0, :n], ps[p][:, :n]), r=["pj_ps%d" % p], w=["pj_ckv"])
                    rms_stats(ckv, 1, n, 128, sq, ps_st, rstd2, ["pj_ckv"], "pj_sq", "pj_pst", "pj_rstd2")
                    V(lambda e: e.tensor_tensor(out=sq[:, 0, :n], in0=ckv[:, 0, :n], in1=rstd2[:, :n], op=ALU.mult),
                      r=["pj_ckv", "pj_rstd2"], w=["pj_sq"])
                    V(lambda e: e.tensor_scalar(cn[:, :n], sq[:, 0, :n], kvg[:, 0:1], None, op0=ALU.mult), r=["pj_sq", "pj_kvg"], w=["pj_cn"])
                    for h in range(NH):
                        p = nxt_ps()
                        TE(lambda e, p=p, h=h: e.matmul(ps[p][:64, :n], lhsT=wkv[:, h * 128:h * 128 + 64], rhs=cn[:, :n], start=True, stop=True),
                           r=["pj_cn", "pj_wkv"], w=["pj_ps%d" % p])
                        o = nxt_ob()
                        evac(ob[o][:64, :n], ps[p][:64, :n], r=["pj_ps%d" % p], w=["pj_ob%d" % o])
                        DM(lambda e, o=o, h=h: e.dma_start(out=KT[h, 32:96, t0:t0 + n], in_=ob[o][:64, :n]), r=["pj_ob%d" % o], w=[("KT", S.dma_next)])
                    for j in range(n // 128):
                        p = nxt_ps()
                        TE(lambda e, p=p, j=j: e.matmul(ps[p][:, :], lhsT=cn[:, j * 128:(j + 1) * 128], rhs=wv[:].rearrange("p h c -> p (h c)"), start=True, stop=True),
                           r=["pj_cn", "pj_wv"], w=["pj_ps%d" % p])
                        o = nxt_ob()
                        evac(ob[o][:, :], ps[p][:, :], r=["pj_ps%d" % p], w=["pj_ob%d" % o])
                        DM(lambda e, o=o, j=j: e.dma_start(out=VV[t0 + j * 128:t0 + (j + 1) * 128, :], in_=ob[o][:, :]), r=["pj_ob%d" % o], w=[("VV", S.dma_next)])
                    p = nxt_ps()
                    for k in range(8):
                        TE(lambda e, k=k, p=p: e.matmul(ps[p][:32, :n], lhsT=w_in[:, k, 384:416], rhs=hT[:, k, :n], start=(k == 0), stop=(k == 7)),
                           r=["pj_hT", "pj_win"], w=["pj_ps%d" % p])
                    o = nxt_ob()
                    if is_ctx:
                        evac(ob[o][:32, :n], ps[p][:32, :n], r=["pj_ps%d" % p], w=["pj_ob%d" % o])
                    else:
                        for k in range(8):
                            TE(lambda e, k=k: e.matmul(ps_rot[:32, :n], lhsT=w_krot[:, k, :], rhs=hT[:, k, :n], start=(k == 0), stop=(k == 7)),
                               r=["pj_hT", "pj_wkrot"], w=["pj_prot"])
                        V(lambda e, p=p: e.tensor_tensor(out=t1[:, :n], in0=ps[p][:32, :n], in1=rcos[:, s0:s0 + n], op=ALU.mult),
                          r=["pj_ps%d" % p, "pj_cos"], w=["pj_t1"])
                        V(lambda e: e.tensor_tensor(out=t2[:, :n], in0=ps_rot[:32, :n], in1=rsin[:, s0:s0 + n], op=ALU.mult),
                          r=["pj_prot", "pj_sin"], w=["pj_t2"])
                        V(lambda e, o=o: e.tensor_tensor(out=ob[o][:32, :n], in0=t1[:, :n], in1=t2[:, :n], op=ALU.add),
                          r=["pj_t1", "pj_t2"], w=["pj_ob%d" % o])
                    for h in range(NH):
                        DM(lambda e, o=o, h=h: e.dma_start(out=KT[h, 0:32, t0:t0 + n], in_=ob[o][:32, :n]), r=["pj_ob%d" % o], w=[("KT", S.dma_next)])
                    if kv_only:
                        return
                    for c in range(2):
                        p = nxt_ps()
                        for k in range(8):
                            TE(lambda e, k=k, p=p, c=c: e.matmul(ps[p][:, :n], lhsT=w_in[:, k, c * 128:(c + 1) * 128], rhs=hT[:, k, :n], start=(k == 0), stop=(k == 7)),
                               r=["pj_hT", "pj_win"], w=["pj_ps%d" % p])
                        V(lambda e, p=p, c=c: e.tensor_copy(cq[:, c, :n], ps[p][:, :n]), r=["pj_ps%d" % p], w=["pj_cq"])
                    rms_stats(cq, 2, n, 256, sq, ps_st, rstd2, ["pj_cq"], "pj_sq", "pj_pst", "pj_rstd2")
                    for c in range(2):
                        V(lambda e, c=c: e.tensor_tensor(out=sq[:, c, :n], in0=cq[:, c, :n], in1=rstd2[:, :n], op=ALU.mult),
                          r=["pj_cq", "pj_rstd2"], w=["pj_sq"])
                        V(lambda e, c=c: e.tensor_scalar(cqn[:, c, :n], sq[:, c, :n], qg[:, c:c + 1], None, op0=ALU.mult), r=["pj_sq", "pj_qg"], w=["pj_cqn"])
                    for h in range(NH):
                        p = nxt_ps()
                        for c in range(2):
                            TE(lambda e, p=p, c=c, h=h: e.matmul(ps[p][:64, :n], lhsT=wq_raw[:, c, h * 96:h * 96 + 64], rhs=cqn[:, c, :n], start=(c == 0), stop=(c == 1)),
                               r=["pj_cqn", "pj_wqraw"], w=["pj_ps%d" % p])
                        o = nxt_ob()
                        evac(ob[o][:64, :n], ps[p][:64, :n], r=["pj_ps%d" % p], w=["pj_ob%d" % o])
                        DM(lambda e, o=o, h=h: e.dma_start(out=QT[h, 32:96, t0:t0 + n], in_=ob[o][:64, :n]), r=["pj_ob%d" % o], w=[("QT", S.dma_next)])
                        p = nxt_ps()
                        for c in range(2):
                            TE(lambda e, p=p, c=c, h=h: e.matmul(ps[p][:32, :n], lhsT=wq_raw[:, c, h * 96 + 64:h * 96 + 96], rhs=cqn[:, c, :n], start=(c == 0), stop=(c == 1)),
                               r=["pj_cqn", "pj_wqraw"], w=["pj_ps%d" % p])
                        o = nxt_ob()
                        if is_ctx:
                            evac(ob[o][:32, :n], ps[p][:32, :n], r=["pj_ps%d" % p], w=["pj_ob%d" % o])
                        else:
                            for c in range(2):
                                TE(lambda e, c=c, h=h: e.matmul(ps_rot[:32, :n], lhsT=wq_rot[:, c, h, :], rhs=cqn[:, c, :n], start=(c == 0), stop=(c == 1)),
                                   r=["pj_cqn", "pj_wqrot"], w=["pj_prot"])
                            V(lambda e, p=p: e.tensor_tensor(out=t1[:, :n], in0=ps[p][:32, :n], in1=rcos[:, s0:s0 + n], op=ALU.mult),
                              r=["pj_ps%d" % p, "pj_cos"], w=["pj_t1"])
                            V(lambda e: e.tensor_tensor(out=t2[:, :n], in0=ps_rot[:32, :n], in1=rsin[:, s0:s0 + n], op=ALU.mult),
                              r=["pj_prot", "pj_sin"], w=["pj_t2"])
                            V(lambda e, o=o: e.tensor_tensor(out=ob[o][:32, :n], in0=t1[:, :n], in1=t2[:, :n], op=ALU.add),
                              r=["pj_t1", "pj_t2"], w=["pj_ob%d" % o])
                        DM(lambda e, o=o, h=h: e.dma_start(out=QT[h, 0:32, t0:t0 + n], in_=ob[o][:32, :n]), r=["pj_ob%d" % o], w=[("QT", S.dma_next)])
                    for c in range(2):
                        p = nxt_ps()
                        for k in range(8):
                            TE(lambda e, k=k, p=p, c=c: e.matmul(ps[p][:, :n], lhsT=w_in[:, k, 416 + c * 128:416 + (c + 1) * 128], rhs=hT[:, k, :n], start=(k == 0), stop=(k == 7)),
                               r=["pj_hT", "pj_win"], w=["pj_ps%d" % p])
                        o = nxt_ob()
                        evac(ob[o][:, :n], ps[p][:, :n], r=["pj_ps%d" % p], w=["pj_ob%d" % o])
                        DM(lambda e, o=o, c=c: e.dma_start(out=ZFT[c, :, t0:t0 + n], in_=ob[o][:, :n]), r=["pj_ob%d" % o], w=[("ZFT", S.dma_next)])
                    for c in range(2):
                        p = nxt_ps()
                        for k in range(8):
                            TE(lambda e, k=k, p=p, c=c: e.matmul(ps[p][:, :n], lhsT=w_in[:, k, 672 + c * 128:672 + (c + 1) * 128], rhs=hT[:, k, :n], start=(k == 0), stop=(k == 7)),
                               r=["pj_hT", "pj_win"], w=["pj_ps%d" % p])
                        evac(of[c][:, :n], ps[p][:, :n], r=["pj_ps%d" % p], w=["pj_of%d" % c])
                        DM(lambda e, c=c: e.dma_start(out=ZPT[c, :, t0:t0 + n], in_=of[c][:, :n]), r=["pj_of%d" % c], w=[("ZPT", S.dma_next)])
                    for c in range(24):
                        p = nxt_ps()
                        for k in range(8):
                            TE(lambda e, k=k, p=p, c=c: e.matmul(ps[p][:, :n], lhsT=w_in[:, k, 928 + c * 128:928 + (c + 1) * 128], rhs=hT[:, k, :n], start=(k == 0), stop=(k == 7)),
                               r=["pj_hT", "pj_win"], w=["pj_ps%d" % p])
                        o = nxt_ob()
                        A(lambda e, o=o, p=p, c=c: e.activation(out=ob[o][:, :n], in_=ps[p][:, :n], func=AF.Sigmoid, bias=bg[:, c:c + 1], scale=1.0),
                          r=["pj_ps%d" % p, "pj_bg"], w=["pj_ob%d" % o])
                        DM(lambda e, o=o, c=c: e.dma_start(out=GT[c, :, t0:t0 + n], in_=ob[o][:, :n]), r=["pj_ob%d" % o], w=[("GT", S.dma_next)])

                for gi, (t0, n) in enumerate(GROUPS):
                    do_group(gi, t0, n)
            S.barrier()

        def phase_attn(l, last):
            with contextlib.ExitStack() as st:
                v_sb = sb(st, "at_v", [128, T // 128, 512], BF16)
                kth = [sb(st, "at_k%d" % i, [96, T], BF16) for i in range(2)]
                qg = [sb(st, "at_q%d" % i, [96, 512], BF16) for i in range(2)]
                pt = [sb(st, "at_p%d" % i, [128, 512], BF16) for i in range(3)]
                rs = sb(st, "at_rs", [64, 512], F32)
                osb = [sb(st, "at_o%d" % i, [64, 512], BF16) for i in range(2)]
                ps_s = [pst(st, "at_pss%d" % i, [128, 512], F32) for i in range(3)]
                ps_o = [pst(st, "at_pso%d" % i, [128, 512], F32) for i in range(2)]
                ps_m = [pst(st, "at_psm%d" % i, [128, 512], F32) for i in range(2)]
                DM(lambda e: e.dma_start(out=v_sb[:], in_=VV.rearrange("(kt p) c -> p kt c", p=128)), w=["at_v"])
                cnt = [0, 0]
                segs = ([] if last else [(0, CTX, 2)]) + [(t0, n, T // 128) for (t0, n) in GROUPS[1:]]

                def do_head(h):
                    kb = h % 2
                    DM(lambda e: e.dma_start(out=kth[kb][:], in_=KT[h]), w=["at_k%d" % kb])

                    def do_seg(t0, n, nkt):
                        cnt[0] += 1
                        qb = cnt[0] % 2
                        DM(lambda e: e.dma_start(out=qg[qb][:, :n], in_=QT[h, :, t0:t0 + n]), w=["at_q%d" % qb])

                        def do_kt(kt):
                            cnt[1] += 1
                            si = cnt[1] % 3
                            TE(lambda e: e.matmul(ps_s[si][:, :n], lhsT=kth[kb][:, kt * 128:(kt + 1) * 128], rhs=qg[qb][:, :n], start=True, stop=True),
                               r=["at_k%d" % kb, "at_q%d" % qb], w=["at_pss%d" % si])
                            A(lambda e: e.activation(out=pt[si][:, :n], in_=ps_s[si][:, :n], func=AF.Exp, scale=ATT_SCALE),
                              r=["at_pss%d" % si], w=["at_p%d" % si])
                            TE(lambda e: e.matmul(ps_o[qb][:64, :n], lhsT=v_sb[:, kt, h * 64:(h + 1) * 64], rhs=pt[si][:, :n], start=(kt == 0), stop=(kt == nkt - 1)),
                               r=["at_v", "at_p%d" % si], w=["at_pso%d" % qb])
                            TE(lambda e: e.matmul(ps_m[qb][:64, :n], lhsT=ones_b[:, :64], rhs=pt[si][:, :n], start=(kt == 0), stop=(kt == nkt - 1)),
                               r=["ones_b", "at_p%d" % si], w=["at_psm%d" % qb])
                        for kt in range(nkt):
                            do_kt(kt)
                        V(lambda e: e.reciprocal(rs[:, :n], ps_m[qb][:64, :n]), r=["at_psm%d" % qb], w=["at_rs"])
                        V(lambda e: e.tensor_tensor(out=osb[qb][:, :n], in0=ps_o[qb][:64, :n], in1=rs[:, :n], op=ALU.mult),
                          r=["at_pso%d" % qb, "at_rs"], w=["at_o%d" % qb])
                        DM(lambda e: e.dma_start(out=ATT[h // 2, (h % 2) * 64:(h % 2) * 64 + 64, t0:t0 + n], in_=osb[qb][:, :n]),
                           r=["at_o%d" % qb], w=[("ATT", S.dma_next)])
                    for (t0, n, nkt) in segs:
                        do_seg(t0, n, nkt)
                for h in range(NH):
                    do_head(h)
            S.barrier()

        def phase_fourier(l, last):
            segs = ([] if last else [(0, CTX, "cl256", "sl256")]) + [(CTX, SEQ, "cl4096", "sl4096")]

            def do_seg(t0, L, cln, sln):
                ntt = L // 128
                KW = 256
                with contextlib.ExitStack() as st:
                    ccs = sb(st, "fo_ccs", [128, 2, 512], BF16)
                    zft = sb(st, "fo_z", [128, 2, L], BF16)
                    uv = sb(st, "fo_uv", [128, ntt, 512], BF16)
                    clb = [sb(st, "fo_cl%d" % i, [128, ntt, KW], BF16) for i in range(2)]
                    slb = [sb(st, "fo_sl%d" % i, [128, ntt, KW], BF16) for i in range(2)]
                    yo = [sb(st, "fo_y%d" % i, [128, KW], BF16) for i in range(2)]
                    ps = [pst(st, "fo_ps%d" % i, [128, 512], F32) for i in range(4)]
                    DM(lambda e: e.dma_start(out=ccs[:], in_=C["ccs"].rearrange("(c p) n -> p c n", p=128)), w=["fo_ccs"])
                    DM(lambda e: e.dma_start(out=zft[:], in_=ZFT[:, :, t0:t0 + L].rearrange("c p t -> p c t")), w=["fo_z"])

                    def do_tt(tt):
                        pi = tt % 4
                        for c in range(2):
                            TE(lambda e, c=c: e.matmul(ps[pi][:, :], lhsT=zft[:, c, tt * 128:(tt + 1) * 128], rhs=ccs[:, c, :], start=(c == 0), stop=(c == 1)),
                               r=["fo_z", "fo_ccs"], w=["fo_ps%d" % pi])
                        evac(uv[:, tt, :], ps[pi][:, :], r=["fo_ps%d" % pi], w=[("fo_uv", tt)])
                    for tt in range(ntt):
                        do_tt(tt)
                    uvkeys = [("fo_uv", tt) for tt in range(ntt)]
                    scale = 1.0 / math.sqrt(L * 256.0)

                    def do_kc(kc):
                        b = kc % 2
                        DM(lambda e: e.dma_start(out=clb[b][:], in_=C[cln][:, kc * KW:(kc + 1) * KW].rearrange("(tt p) k -> p tt k", p=128)), w=["fo_cl%d" % b])
                        DM(lambda e: e.dma_start(out=slb[b][:], in_=C[sln][:, kc * KW:(kc + 1) * KW].rearrange("(tt p) k -> p tt k", p=128)), w=["fo_sl%d" % b])

                        def do_m(m):
                            pi = (kc * 2 + m) % 4
                            for tt in range(ntt):
                                TE(lambda e, tt=tt: e.matmul(ps[pi][:, :KW], lhsT=uv[:, tt, m * 128:(m + 1) * 128], rhs=clb[b][:, tt, :], start=(tt == 0), stop=False),
                                   r=uvkeys + ["fo_cl%d" % b], w=["fo_ps%d" % pi])
                                TE(lambda e, tt=tt: e.matmul(ps[pi][:, :KW], lhsT=uv[:, tt, 256 + m * 128:256 + (m + 1) * 128], rhs=slb[b][:, tt, :], start=False, stop=(tt == ntt - 1)),
                                   r=uvkeys + ["fo_sl%d" % b], w=["fo_ps%d" % pi])
                            evac(yo[m][:, :], ps[pi][:, :KW], r=["fo_ps%d" % pi], w=["fo_y%d" % m], scale=scale)
                            DM(lambda e: e.dma_start(out=YFT[m, :, t0 + kc * KW:t0 + (kc + 1) * KW], in_=yo[m][:, :]), r=["fo_y%d" % m], w=[("YFT", S.dma_next)])
                        for m in range(2):
                            do_m(m)
                    for kc in range(L // KW):
                        do_kc(kc)
                S.barrier()
            for sg in segs:
                do_seg(*sg)

        def phase_pool(l, last):
            segs = ([] if last else [(0, CTX, "icnt256")]) + [(CTX, SEQ, "icnt4096")]

            def do_seg(t0, L, icn):
                with contextlib.ExitStack() as st:
                    zpp = sb(st, "po_z", [128, 2, L + 16], F32)
                    sA = sb(st, "po_sA", [128, L + 16], F32)
                    sB = sb(st, "po_sB", [128, L + 16], F32)
                    icnt = sb(st, "po_ic", [128, 2, L], F32)
                    tmp = sb(st, "po_tmp", [128, L], F32)
                    poolT = sb(st, "po_pT", [128, 2, L], BF16)
                    wbd = [sb(st, "po_w%d" % i, [128, 128], BF16) for i in range(2)]
                    psc = sb(st, "po_sc", [128, 2], F32)
                    yo = [sb(st, "po_y%d" % i, [128, 512], BF16) for i in range(2)]
                    ps = [pst(st, "po_ps%d" % i, [128, 512], F32) for i in range(2)]
                    V(lambda e: e.memset(zpp[:], 0.0), w=["po_z"])
                    for ch in range(2):
                        V(lambda e, ch=ch: e.memset(wbd[ch][:], 0.0), w=["po_w%d" % ch])
                    DM(lambda e: e.dma_start(out=zpp[:, :, 8:8 + L], in_=ZPT[:, :, t0:t0 + L].rearrange("c p t -> p c t")), w=["po_z"])
                    DM(lambda e: e.dma_start(out=icnt[:], in_=C[icn].rearrange("c p t -> p c t")), w=["po_ic"])
                    DM(lambda e: e.dma_start(out=psc[:], in_=P["pool_scale"][l].rearrange("(c p) -> p c", p=128), allow_slow_non_contiguous=True), w=["po_sc"])
                    for g in range(4):
                        o = (g % 2) * 64
                        DM(lambda e, g=g, o=o: e.dma_start(out=wbd[g // 2][o:o + 64, o:o + 64], in_=P["w_grp"][l, g]), w=["po_w%d" % (g // 2)], q="gpsimd")

                    def do_ch(ch):
                        V(lambda e: e.tensor_tensor(out=sA[:, 1:L + 16], in0=zpp[:, ch, 0:L + 15], in1=zpp[:, ch, 1:L + 16], op=ALU.add), r=["po_z"], w=["po_sA"])
                        V(lambda e: e.tensor_tensor(out=sB[:, 2:L + 15], in0=sA[:, 1:L + 14], in1=sA[:, 3:L + 16], op=ALU.add), r=["po_sA"], w=["po_sB"])
                        if ch == 1:
                            V(lambda e: e.tensor_tensor(out=sA[:, 4:L + 13], in0=sB[:, 2:L + 11], in1=sB[:, 6:L + 15], op=ALU.add), r=["po_sB"], w=["po_sA"])
                            V(lambda e: e.tensor_tensor(out=sB[:, 8:L + 9], in0=sA[:, 4:L + 5], in1=sA[:, 12:L + 13], op=ALU.add), r=["po_sA"], w=["po_sB"])
                        V(lambda e: e.tensor_tensor(out=tmp[0:64, :], in0=sA[0:64, 8:8 + L], in1=icnt[0:64, ch, :], op=ALU.mult), r=["po_sA", "po_ic"], w=["po_tmp"])
                        V(lambda e: e.tensor_tensor(out=tmp[64:128, :], in0=sB[64:128, 8:8 + L], in1=icnt[64:128, ch, :], op=ALU.mult), r=["po_sB", "po_ic"], w=["po_tmp"])
                        V(lambda e: e.tensor_tensor(out=poolT[:, ch, :], in0=tmp[:, :], in1=zpp[:, ch, 8:8 + L], op=ALU.subtract), r=["po_tmp", "po_z"], w=[("po_pT", ch)])

                        def do_tc(tc_):
                            n = min(512, L - tc_ * 512)
                            pi = tc_ % 2
                            TE(lambda e: e.matmul(ps[pi][:, :n], lhsT=wbd[ch][:], rhs=poolT[:, ch, tc_ * 512:tc_ * 512 + n], start=True, stop=True),
                               r=[("po_pT", ch), "po_w%d" % ch], w=["po_ps%d" % pi])
                            V(lambda e: e.tensor_scalar(yo[pi][:, :n], ps[pi][:, :n], psc[:, ch:ch + 1], None, op0=ALU.mult), r=["po_ps%d" % pi, "po_sc"], w=["po_y%d" % pi])
                            DM(lambda e: e.dma_start(out=YCT[ch, :, t0 + tc_ * 512:t0 + tc_ * 512 + n], in_=yo[pi][:, :n]), r=["po_y%d" % pi], w=[("YCT", S.dma_next)])
                        for tc_ in range((L + 511) // 512):
                            do_tc(tc_)
                    for ch in range(2):
                        do_ch(ch)
                S.barrier()
            for sg in segs:
                do_seg(*sg)

        def phase_combine(l, last):
            with contextlib.ExitStack() as st:
                w_oa = sb(st, "cb_woa", [128, 4, D], BF16)
                w_ob = sb(st, "cb_wob", [128, 2, D], BF16)
                w_oc = sb(st, "cb_woc", [128, 2, D], BF16)
                w_out = sb(st, "cb_wout", [128, 8, D], BF16)
                w_pq = sb(st, "cb_wpq", [128, 8, 2048], BF16)
                att = sb(st, "cb_att", [128, 4, 512], BF16)
                yf = sb(st, "cb_yf", [128, 2, 512], BF16)
                yc = sb(st, "cb_yc", [128, 2, 512], BF16)
                gT = sb(st, "cb_g", [128, 24, 512], BF16)
                xg = sb(st, "cb_xg", [128, 8, 512], F32)
                sq = sb(st, "cb_sq", [128, 8, 512], F32)
                rstd = sb(st, "cb_rstd", [128, 512], F32)
                u = sb(st, "cb_u", [128, 8, 512], BF16)
                h2 = sb(st, "cb_h2", [128, 8, 512], BF16)
                t1 = sb(st, "cb_t1", [128, 512], F32)
                t2 = sb(st, "cb_t2", [128, 512], F32)
                t3 = sb(st, "cb_t3", [128, 512], F32)
                ob = [sb(st, "cb_ob%d" % i, [128, 512], BF16) for i in range(3)]
                ps = [pst(st, "cb_ps%d" % i, [128, 512], F32) for i in range(6)]
                ps_st = pst(st, "cb_pst", [128, 512], F32)
                for k in range(4):
                    DM(lambda e, k=k: e.dma_start(out=w_oa[:, k, :], in_=P["w_oa"][l][k * 128:(k + 1) * 128, :]), w=["cb_woa"], q="gpsimd")
                for k in range(2):
                    DM(lambda e, k=k: e.dma_start(out=w_ob[:, k, :], in_=P["w_ob"][l][k * 128:(k + 1) * 128, :]), w=["cb_wob"], q="gpsimd")
                    DM(lambda e, k=k: e.dma_start(out=w_oc[:, k, :], in_=P["w_oc"][l][k * 128:(k + 1) * 128, :]), w=["cb_woc"], q="gpsimd")
                for k in range(8):
                    DM(lambda e, k=k: e.dma_start(out=w_out[:, k, :], in_=P["w_out"][l][k * 128:(k + 1) * 128, :]), w=["cb_wout"], q="gpsimd")
                    DM(lambda e, k=k: e.dma_start(out=w_pq[:, k, :], in_=P["w_pq"][l][k * 128:(k + 1) * 128, :]), w=["cb_wpq"], q="gpsimd")
                psi = [0]

                def do_group(gi, t0, n):
                    r_ = 1 if gi == 0 else 0
                    DM(lambda e: e.dma_start(out=att[:, :, :n], in_=ATT[:, :, t0:t0 + n].rearrange("k p t -> p k t")), w=["cb_att"])
                    DM(lambda e: e.dma_start(out=yf[:, :, :n], in_=YFT[:, :, t0:t0 + n].rearrange("k p t -> p k t")), w=["cb_yf"])
                    DM(lambda e: e.dma_start(out=yc[:, :, :n], in_=YCT[:, :, t0:t0 + n].rearrange("k p t -> p k t")), w=["cb_yc"])
                    DM(lambda e: e.dma_start(out=gT[:, :, :n], in_=GT[:, :, t0:t0 + n].rearrange("k p t -> p k t")), w=["cb_g"])
                    DM(lambda e: e.dma_start(out=xg[:, :, :n], in_=XT[:, :, t0:t0 + n].rearrange("k p t -> p k t")), w=["cb_xg"])

                    def do_dch(dc):
                        pa, pb, pc = psi[0] % 6, (psi[0] + 1) % 6, (psi[0] + 2) % 6
                        psi[0] += 3
                        for k in range(4):
                            TE(lambda e, k=k: e.matmul(ps[pa][:, :n], lhsT=w_oa[:, k, dc * 128:(dc + 1) * 128], rhs=att[:, k, :n], start=(k == 0), stop=(k == 3)),
                               r=["cb_woa", "cb_att"], w=["cb_ps%d" % pa])
                        for k in range(2):
                            TE(lambda e, k=k: e.matmul(ps[pb][:, :n], lhsT=w_ob[:, k, dc * 128:(dc + 1) * 128], rhs=yf[:, k, :n], start=(k == 0), stop=(k == 1)),
                               r=["cb_wob", "cb_yf"], w=["cb_ps%d" % pb])
                        for k in range(2):
                            TE(lambda e, k=k: e.matmul(ps[pc][:, :n], lhsT=w_oc[:, k, dc * 128:(dc + 1) * 128], rhs=yc[:, k, :n], start=(k == 0), stop=(k == 1)),
                               r=["cb_woc", "cb_yc"], w=["cb_ps%d" % pc])
                        V(lambda e: e.tensor_tensor(out=t1[:, :n], in0=ps[pa][:, :n], in1=gT[:, dc, :n], op=ALU.mult), r=["cb_ps%d" % pa, "cb_g"], w=["cb_t1"])
                        V(lambda e: e.tensor_tensor(out=t2[:, :n], in0=ps[pb][:, :n], in1=gT[:, 8 + dc, :n], op=ALU.mult), r=["cb_ps%d" % pb, "cb_g"], w=["cb_t2"])
                        V(lambda e: e.tensor_tensor(out=t3[:, :n], in0=ps[pc][:, :n], in1=gT[:, 16 + dc, :n], op=ALU.mult), r=["cb_ps%d" % pc, "cb_g"], w=["cb_t3"])
                        G(lambda e: e.tensor_tensor(out=t1[:, :n], in0=t1[:, :n], in1=t2[:, :n], op=ALU.add), r=["cb_t1", "cb_t2"], w=["cb_t1"])
                        G(lambda e: e.tensor_tensor(out=u[:, dc, :n], in0=t1[:, :n], in1=t3[:, :n], op=ALU.add), r=["cb_t1", "cb_t3"], w=[("cb_u", dc)])
                    for dc in range(8):
                        do_dch(dc)
                    ukeys = [("cb_u", dc) for dc in range(8)]

                    def do_out(dc):
                        pm = psi[0] % 6
                        psi[0] += 1
                        for k in range(8):
                            TE(lambda e, k=k: e.matmul(ps[pm][:, :n], lhsT=w_out[:, k, dc * 128:(dc + 1) * 128], rhs=u[:, k, :n], start=(k == 0), stop=(k == 7)),
                               r=ukeys + ["cb_wout"], w=["cb_ps%d" % pm])
                        V(lambda e: e.scalar_tensor_tensor(out=xg[:, dc, :n], in0=ps[pm][:, :n], scalar=gt1[:, dc, r_:r_ + 1], in1=xg[:, dc, :n], op0=ALU.mult, op1=ALU.add),
                          r=["cb_ps%d" % pm, "gt1", "cb_xg"], w=["cb_xg"])
                    for dc in range(8):
                        do_out(dc)
                    DM(lambda e: e.dma_start(out=XT[:, :, t0:t0 + n].rearrange("k p t -> p k t"), in_=xg[:, :, :n]), r=["cb_xg"], w=[("XTw", gi)])
                    rms_stats(xg, 8, n, D, sq, ps_st, rstd, ["cb_xg"], "cb_sq", "cb_pst", "cb_rstd")

                    def do_h2(k):
                        V(lambda e: e.tensor_tensor(out=sq[:, k, :n], in0=xg[:, k, :n], in1=rstd[:, :n], op=ALU.mult), r=["cb_xg", "cb_rstd"], w=["cb_sq"])
                        V(lambda e: e.tensor_scalar(h2[:, k, :n], sq[:, k, :n], gm2[:, k, r_:r_ + 1], sh2[:, k, r_:r_ + 1], op0=ALU.mult, op1=ALU.add),
                          r=["cb_sq", "gm2", "sh2"], w=["cb_h2"])
                    for k in range(8):
                        do_h2(k)
                    DM(lambda e: e.dma_start(out=H2T[:, :, t0:t0 + n].rearrange("k p t -> p k t"), in_=h2[:, :, :n]), r=["cb_h2"], w=[("H2T", gi)])

                    def do_pq(nc_):
                        pm = psi[0] % 6
                        psi[0] += 1
                        o = nc_ % 3
                        for k in range(8):
                            TE(lambda e, k=k: e.matmul(ps[pm][:, :n], lhsT=w_pq[:, k, nc_ * 128:(nc_ + 1) * 128], rhs=h2[:, k, :n], start=(k == 0), stop=(k == 7)),
                               r=["cb_h2", "cb_wpq"], w=["cb_ps%d" % pm])
                        evac(ob[o][:, :n], ps[pm][:, :n], r=["cb_ps%d" % pm], w=["cb_ob%d" % o])
                        DM(lambda e: e.dma_start(out=PQT[nc_, :, t0:t0 + n], in_=ob[o][:, :n]), r=["cb_ob%d" % o], w=[("PQT", S.dma_next)])
                    for nc_ in range(16):
                        do_pq(nc_)
                for gi, (t0, n) in enumerate(GROUPS):
                    if last and gi == 0:
                        continue
                    do_group(gi, t0, n)
            S.barrier()

        def phase_cast_tables():
            with contextlib.ExitStack() as st:
                tb = [sb(st, "ct_t%d" % i, [128, 8, D], BF16) for i in range(4)]
                cnt = [0]

                def do_chunk(l, src, o, c):
                    cnt[0] += 1
                    b = cnt[0] % 4
                    DM(lambda e: e.dma_start(out=tb[b][:], in_=src[c * 1024:(c + 1) * 1024, :].rearrange("(p r) d -> p r d", r=8)),
                       w=["ct_t%d" % b], q="gpsimd")
                    DM(lambda e: e.dma_start(out=DUB[l][c * 1024:(c + 1) * 1024, o:o + D].rearrange("(p r) d -> p r d", r=8), in_=tb[b][:]),
                       r=["ct_t%d" % b], w=[("cast", S.dma_next)])
                for l in range(DEPTH):
                    for (src, o) in ((P["peer_down%d" % l], 0), (P["peer_up%d" % l], D)):
                        for c in range(16):
                            do_chunk(l, src, o, c)
            S.barrier()

        def phase_peer(l, last):
            with contextlib.ExitStack() as st:
                kraw = sb(st, "pe_kraw", [128, 16, 128], F32)
                keysT = sb(st, "pe_keysT", [128, 16, 128], BF16)
                iotf = sb(st, "pe_iotf", [128, 16], F32)
                pq = sb(st, "pe_pq", [128, 16, 128], BF16)
                s_sb = sb(st, "pe_s", [128, 16, 128], F32)
                s2 = sb(st, "pe_s2", [128, 16, 128], F32)
                sv = sb(st, "pe_sv", [128, 16, 16], F32)
                si = sb(st, "pe_si", [128, 16, 16], U32)
                sif = sb(st, "pe_sif", [128, 16, 16], F32)
                cand = sb(st, "pe_cand", [128, 8, 256], F32)
                cand2 = sb(st, "pe_cand2", [128, 8, 256], F32)
                ts = sb(st, "pe_ts", [128, 8, 16], F32)
                pos = sb(st, "pe_pos", [128, 8, 16], U32)
                pa_i = sb(st, "pe_pai", [128, 8, 16], U32)
                pb_i = sb(st, "pe_pbi", [128, 8, 16], U32)
                pa_f = sb(st, "pe_paf", [128, 8, 16], F32)
                pb_f = sb(st, "pe_pbf", [128, 8, 16], F32)
                eq = sb(st, "pe_eq", [128, 8, 16, 16], F32)
                If = sb(st, "pe_If", [128, 8, 16], F32)
                Jf = sb(st, "pe_Jf", [128, 8, 16], F32)
                ef = sb(st, "pe_ef", [128, 8, 16], F32)
                eidx = sb(st, "pe_eidx", [128, 128], U32)
                gate = sb(st, "pe_gate", [128, 8, 16], F32)
                gsum = sb(st, "pe_gsum", [128, 8], F32)
                h2T = sb(st, "pe_h2T", [128, 8, 128], BF16)
                h2tok = sb(st, "pe_h2tok", [128, D], BF16)
                NR = 16
                SG = 8
                rows = [sb(st, "pe_rows%d" % i, [128, 2 * D], BF16) for i in range(NR)]
                junk = [sb(st, "pe_junk%d" % i, [128, D], BF16) for i in range(3)]
                junk2 = sb(st, "pe_junk2", [128, D], BF16)
                diag = [sb(st, "pe_diag%d" % i, [128, 128], BF16) for i in range(3)]
                a_sb = sb(st, "pe_a", [128, 128], F32)
                g1 = sb(st, "pe_g1", [128, 128], F32)
                g2 = sb(st, "pe_g2", [128, 128], F32)
                act = sb(st, "pe_act", [128, 128], F32)
                acc = sb(st, "pe_acc", [128, D], F32)
                xt = sb(st, "pe_xt", [128, 8, 128], F32)
                ps_sc = [pst(st, "pe_psc%d" % i, [128, 4, 128], F32) for i in range(4)]
                ps_h2 = pst(st, "pe_ph2", [128, 8, 128], BF16)
                ps_acc = [pst(st, "pe_pacc%d" % i, [128, 512], F32) for i in range(2)]

                G(lambda e: e.iota(iotf[:], [[1, 16]], base=0, channel_multiplier=0, allow_small_or_imprecise_dtypes=True), w=["pe_iotf"])
                DM(lambda e: e.dma_start(out=kraw[:], in_=P["peer_keys"][l].rearrange("h p k d -> k (h p) d")), w=["pe_kraw"])
                for hp in range(16):
                    TE(lambda e, hp=hp: e.transpose(ps_sc[hp // 4][:, hp % 4, :], kraw[:, hp, :], ident_f[:]), r=["pe_kraw", "ident_f"], w=["pe_psc%d" % (hp // 4)])
                for b4 in range(4):
                    evac(keysT[:, b4 * 4:(b4 + 1) * 4, :], ps_sc[b4][:], r=["pe_psc%d" % b4], w=["pe_keysT"])
                rcnt = [0]

                def do_tile(ti):
                    c0 = ti * 128
                    r_ = 1 if ti < 2 else 0
                    DM(lambda e: e.dma_start(out=pq[:], in_=PQT[:, :, c0:c0 + 128].rearrange("n p t -> p n t")), w=["pe_pq"])
                    DM(lambda e: e.dma_start(out=h2T[:], in_=H2T[:, :, c0:c0 + 128].rearrange("k p t -> p k t")), w=["pe_h2T"])
                    DM(lambda e: e.dma_start(out=xt[:], in_=XT[:, :, c0:c0 + 128].rearrange("k p t -> p k t")), w=["pe_xt"])
                    for hp in range(16):
                        TE(lambda e, hp=hp: e.matmul(ps_sc[hp // 4][:, hp % 4, :], lhsT=pq[:, hp, :], rhs=keysT[:, hp, :], start=True, stop=True),
                           r=["pe_pq", "pe_keysT"], w=["pe_psc%d" % (hp // 4)])
                    for b4 in range(4):
                        evac(s_sb[:, b4 * 4:(b4 + 1) * 4, :], ps_sc[b4][:], r=["pe_psc%d" % b4], w=["pe_s"])
                    for k in range(8):
                        TE(lambda e, k=k: e.transpose(ps_h2[:, k, :], h2T[:, k, :], ident_b[:]), r=["pe_h2T", "ident_b"], w=["pe_ph2"])
                    A(lambda e: e.activation(out=h2tok[:], in_=ps_h2[:].rearrange("p k d -> p (k d)"), func=AF.Copy), r=["pe_ph2"], w=["pe_h2tok"])
                    for hp in range(16):
                        V(lambda e, hp=hp: e.max(out=sv[:, hp, 0:8], in_=s_sb[:, hp, :]), r=["pe_s"], w=["pe_sv"])
                        V(lambda e, hp=hp: e.max_index(out=si[:, hp, 0:8], in_max=sv[:, hp, 0:8], in_values=s_sb[:, hp, :]), r=["pe_s", "pe_sv"], w=["pe_si"])
                        V(lambda e, hp=hp: e.match_replace(out=s2[:, hp, :], in_to_replace=sv[:, hp, 0:8], in_values=s_sb[:, hp, :], imm_value=-1e30),
                          r=["pe_s", "pe_sv"], w=["pe_s2"])
                        V(lambda e, hp=hp: e.max(out=sv[:, hp, 8:16], in_=s2[:, hp, :]), r=["pe_s2"], w=["pe_sv"])
                        V(lambda e, hp=hp: e.max_index(out=si[:, hp, 8:16], in_max=sv[:, hp, 8:16], in_values=s2[:, hp, :]), r=["pe_s2", "pe_sv"], w=["pe_si"])
                    V(lambda e: e.tensor_copy(sif[:], si[:]), r=["pe_si"], w=["pe_sif"])
                    svv = sv[:].rearrange("p (h q) a -> p h q a", q=2)
                    sfv = sif[:].rearrange("p (h q) a -> p h q a", q=2)
                    candv = cand[:].rearrange("p h (a b) -> p h a b", b=16)
                    V(lambda e: e.tensor_tensor(out=candv, in0=svv[:, :, 0, :].unsqueeze(3).to_broadcast([128, 8, 16, 16]),
                                                in1=svv[:, :, 1, :].unsqueeze(2).to_broadcast([128, 8, 16, 16]), op=ALU.add), r=["pe_sv"], w=["pe_cand"])
                    for h in range(8):
                        V(lambda e, h=h: e.max(out=ts[:, h, 0:8], in_=cand[:, h, :]), r=["pe_cand"], w=["pe_ts"])
                        V(lambda e, h=h: e.max_index(out=pos[:, h, 0:8], in_max=ts[:, h, 0:8], in_values=cand[:, h, :]), r=["pe_cand", "pe_ts"], w=["pe_pos"])
                        V(lambda e, h=h: e.match_replace(out=cand2[:, h, :], in_to_replace=ts[:, h, 0:8], in_values=cand[:, h, :], imm_value=-1e30),
                          r=["pe_cand", "pe_ts"], w=["pe_cand2"])
                        V(lambda e, h=h: e.max(out=ts[:, h, 8:16], in_=cand2[:, h, :]), r=["pe_cand2"], w=["pe_ts"])
                        V(lambda e, h=h: e.max_index(out=pos[:, h, 8:16], in_max=ts[:, h, 8:16], in_values=cand2[:, h, :]), r=["pe_cand2", "pe_ts"], w=["pe_pos"])
                    V(lambda e: e.tensor_single_scalar(pa_i[:], pos[:], 4, op=ALU.logical_shift_right), r=["pe_pos"], w=["pe_pai"])
                    V(lambda e: e.tensor_single_scalar(pb_i[:], pos[:], 15, op=ALU.bitwise_and), r=["pe_pos"], w=["pe_pbi"])
                    V(lambda e: e.tensor_copy(pa_f[:], pa_i[:]), r=["pe_pai"], w=["pe_paf"])
                    V(lambda e: e.tensor_copy(pb_f[:], pb_i[:]), r=["pe_pbi"], w=["pe_pbf"])
                    iob = iotf[:].unsqueeze(1).unsqueeze(1).to_broadcast([128, 8, 16, 16])
                    for (pf, half, dst, kd) in ((pa_f, 0, If, "pe_If"), (pb_f, 1, Jf, "pe_Jf")):
                        V(lambda e, pf=pf: e.tensor_tensor(out=eq[:], in0=iob, in1=pf[:].unsqueeze(3).to_broadcast([128, 8, 16, 16]), op=ALU.is_equal),
                          r=["pe_iotf", "pe_paf", "pe_pbf"], w=["pe_eq"])
                        V(lambda e, half=half: e.tensor_tensor(out=eq[:], in0=eq[:], in1=sfv[:, :, half, :].unsqueeze(2).to_broadcast([128, 8, 16, 16]), op=ALU.mult),
                          r=["pe_eq", "pe_sif"], w=["pe_eq"])
                        V(lambda e, dst=dst: e.tensor_reduce(out=dst[:], in_=eq[:], axis=AX.X, op=ALU.add), r=["pe_eq"], w=[kd])
                    V(lambda e: e.scalar_tensor_tensor(out=ef[:], in0=If[:], scalar=128.0, in1=Jf[:], op0=ALU.mult, op1=ALU.add), r=["pe_If", "pe_Jf"], w=["pe_ef"])
                    V(lambda e: e.tensor_copy(eidx[:], ef[:].rearrange("p h k -> p (h k)")), r=["pe_ef"], w=["pe_eidx"])
                    V(lambda e: e.tensor_tensor(out=gate[:], in0=ts[:], in1=ts[:, :, 0:1].to_broadcast([128, 8, 16]), op=ALU.subtract), r=["pe_ts"], w=["pe_gate"])
                    A(lambda e: e.activation(out=gate[:], in_=gate[:], func=AF.Exp), r=["pe_gate"], w=["pe_gate"])
                    V(lambda e: e.tensor_reduce(out=gsum[:], in_=gate[:], axis=AX.X, op=ALU.add), r=["pe_gate"], w=["pe_gsum"])
                    V(lambda e: e.reciprocal(gsum[:], gsum[:]), r=["pe_gsum"], w=["pe_gsum"])
                    V(lambda e: e.tensor_tensor(out=gate[:], in0=gate[:], in1=gsum[:].unsqueeze(2).to_broadcast([128, 8, 16]), op=ALU.mult), r=["pe_gate", "pe_gsum"], w=["pe_gate"])

                    gatev = gate[:].rearrange("p h k -> p (h k)")

                    def do_slot_group(g0):
                        bufs = []
                        for sl in range(g0, g0 + SG):
                            rcnt[0] += 1
                            b = rcnt[0] % NR
                            jb = rcnt[0] % 3
                            bufs.append(b)
                            DM(lambda e, b=b, sl=sl: e.indirect_dma_start(out=rows[b][:, :], out_offset=None, in_=DUB[l],
                                                                          in_offset=bass.IndirectOffsetOnAxis(ap=eidx[:, sl:sl + 1], axis=0)),
                               r=["pe_eidx"], w=["pe_rows%d" % b], q="gpsimd")
                            V(lambda e, b=b, jb=jb: e.tensor_tensor(out=junk[jb][:], in0=rows[b][:, 0:D], in1=h2tok[:], op=ALU.mult),
                              r=["pe_rows%d" % b, "pe_h2tok"], w=["pe_junk%d" % jb])
                            A(lambda e, jb=jb, sl=sl: e.activation(out=junk2[:], in_=junk[jb][:], func=AF.Copy, accum_out=a_sb[:, sl:sl + 1]),
                              r=["pe_junk%d" % jb], w=[("pe_a", g0), "pe_junk2"])
                        gs = slice(g0, g0 + SG)
                        V(lambda e: e.tensor_tensor(out=g1[:, gs], in0=a_sb[:, gs], in1=a_sb[:, gs], op=ALU.mult), r=[("pe_a", g0)], w=[("pe_g1", g0)])
                        V(lambda e: e.tensor_scalar(g1[:, gs], g1[:, gs], 0.044715, 1.0, op0=ALU.mult, op1=ALU.add), r=[("pe_g1", g0)], w=[("pe_g1", g0)])
                        V(lambda e: e.tensor_tensor(out=g1[:, gs], in0=g1[:, gs], in1=a_sb[:, gs], op=ALU.mult), r=[("pe_g1", g0), ("pe_a", g0)], w=[("pe_g1", g0)])
                        A(lambda e: e.activation(out=g2[:, gs], in_=g1[:, gs], func=AF.Sigmoid, scale=1.5957691216057308), r=[("pe_g1", g0)], w=[("pe_g2", g0)])
                        V(lambda e: e.tensor_tensor(out=g2[:, gs], in0=g2[:, gs], in1=a_sb[:, gs], op=ALU.mult), r=[("pe_g2", g0), ("pe_a", g0)], w=[("pe_g2", g0)])
                        V(lambda e: e.tensor_tensor(out=act[:, gs], in0=g2[:, gs], in1=gatev[:, gs], op=ALU.mult), r=[("pe_g2", g0), "pe_gate"], w=[("pe_act", g0)])
                        for i_, sl in enumerate(range(g0, g0 + SG)):
                            b = bufs[i_]
                            db = sl % 3
                            A(lambda e, db=db, sl=sl: e.activation(out=diag[db][:], in_=ident_b[:], func=AF.Copy, scale=act[:, sl:sl + 1]),
                              r=["ident_b", ("pe_act", g0)], w=["pe_diag%d" % db])
                            for hf in range(2):
                                TE(lambda e, hf=hf, db=db, b=b, sl=sl: e.matmul(ps_acc[hf][:, :], lhsT=diag[db][:], rhs=rows[b][:, D + hf * 512:D + (hf + 1) * 512],
                                                                                start=(sl == 0), stop=(sl == 127)),
                                   r=["pe_diag%d" % db, "pe_rows%d" % b], w=["pe_pacc%d" % hf])
                    for g0 in range(0, 128, SG):
                        do_slot_group(g0)
                    for hf in range(2):
                        evac(acc[:, hf * 512:(hf + 1) * 512], ps_acc[hf][:, :], r=["pe_pacc%d" % hf], w=[("pe_acc", hf)])
                    for k in range(8):
                        TE(lambda e, k=k: e.transpose(ps_sc[k // 4][:, k % 4, :], acc[:, k * 128:(k + 1) * 128], ident_f[:]), r=[("pe_acc", k // 4), "ident_f"], w=["pe_psc%d" % (k // 4)])
                    for k in range(8):
                        V(lambda e, k=k: e.scalar_tensor_tensor(out=xt[:, k, :], in0=ps_sc[k // 4][:, k % 4, :], scalar=gt2[:, k, r_:r_ + 1], in1=xt[:, k, :],
                                                               op0=ALU.mult, op1=ALU.add),
                          r=["pe_psc%d" % (k // 4), "gt2", "pe_xt"], w=["pe_xt"])
                    DM(lambda e: e.dma_start(out=XT[:, :, c0:c0 + 128].rearrange("k p t -> p k t"), in_=xt[:]), r=["pe_xt"], w=[("XTw", ti)])
                for ti in range(2 if last else 0, T // 128):
                    if peer_tiles is not None and ti not in peer_tiles:
                        continue
                    do_tile(ti)
            S.barrier()

        def phase_final():
            with contextlib.ExitStack() as st:
                fg = sb(st, "fn_g", [128, 8], F32)
                xg = sb(st, "fn_xg", [128, 8, 512], F32)
                sq = sb(st, "fn_sq", [128, 8, 512], F32)
                rstd = sb(st, "fn_rstd", [128, 512], F32)
                ot = [sb(st, "fn_o%d" % i, [128, D], F32) for i in range(2)]
                ps_st = pst(st, "fn_pst", [128, 512], F32)
                ps = [pst(st, "fn_ps%d" % i, [128, 4, 128], F32) for i in range(4)]
                DM(lambda e: e.dma_start(out=fg[:], in_=P["final_g"].rearrange("(n p) -> p n", p=128), allow_slow_non_contiguous=True), w=["fn_g"])
                cnt = [0]

                def do_group(gi, t0, n):
                    DM(lambda e: e.dma_start(out=xg[:, :, :n], in_=XT[:, :, t0:t0 + n].rearrange("k p t -> p k t")), w=["fn_xg"])
                    rms_stats(xg, 8, n, D, sq, ps_st, rstd, ["fn_xg"], "fn_sq", "fn_pst", "fn_rstd")

                    def do_k(k):
                        V(lambda e: e.tensor_tensor(out=sq[:, k, :n], in0=xg[:, k, :n], in1=rstd[:, :n], op=ALU.mult), r=["fn_xg", "fn_rstd"], w=["fn_sq"])
                        V(lambda e: e.tensor_scalar(xg[:, k, :n], sq[:, k, :n], fg[:, k:k + 1], None, op0=ALU.mult), r=["fn_sq", "fn_g"], w=["fn_xg"])
                    for k in range(8):
                        do_k(k)

                    def do_j(j):
                        cnt[0] += 1
                        ob_ = cnt[0] % 2
                        for half in range(2):
                            pi = (cnt[0] * 2 + half) % 4
                            for kk in range(4):
                                k = half * 4 + kk
                                TE(lambda e, k=k, kk=kk, pi=pi: e.transpose(ps[pi][:, kk, :], xg[:, k, j * 128:(j + 1) * 128], ident_f[:]),
                                   r=["fn_xg", "ident_f"], w=["fn_ps%d" % pi])
                            evac(ot[ob_][:, half * 512:(half + 1) * 512], ps[pi][:].rearrange("p a b -> p (a b)"), r=["fn_ps%d" % pi], w=[("fn_o", ob_, half)])
                        row0 = t0 - CTX + j * 128
                        DM(lambda e: e.dma_start(out=out[row0:row0 + 128, :], in_=ot[ob_][:]), r=[("fn_o", ob_, 0), ("fn_o", ob_, 1)], w=[("out", row0)])
                    for j in range(n // 128):
                        do_j(j)
                for gi, (t0, n) in enumerate(GROUPS):
                    if gi == 0:
                        continue
                    do_group(gi, t0, n)
            S.barrier()

        phase_load_x()
        if stop_after is None or stop_after[0] in ("peer", "castonly"):
            phase_cast_tables()
        for l in range(DEPTH):
            if stop_after is not None and stop_after[0] == "castonly":
                break
            last = l == DEPTH - 1
            phase_mod(l)
            phase_proj(l, last)
            if stop_after == ("proj", l):
                break
            phase_attn(l, last)
            phase_fourier(l, last)
            phase_pool(l, last)
            if stop_after == ("mix", l):
                break
            phase_combine(l, last)
            if stop_after == ("combine", l):
                break
            phase_peer(l, last)
            if stop_after == ("peer", l):
                break
        if stop_after is None:
            phase_final()
        S.emit()
    return nc, S


def make_in_maps(inputs, cores):
    consts = _const_tables()
    maps = []
    for b in cores:
        m = {"x": np.ascontiguousarray(inputs["x"][b]), "ctx": np.ascontiguousarray(inputs["ctx"][b]),
             "cvec": np.ascontiguousarray(np.stack([inputs["c"][b], inputs["c_ctx"]], axis=0))}
        for k in PARAM_SHAPES:
            if k.startswith("peer_down") or k.startswith("peer_up"):
                m[k] = np.ascontiguousarray(inputs[k[:-1]][int(k[-1])])
            else:
                m[k] = np.ascontiguousarray(inputs[k])
        m.update(consts)
        maps.append(m)
    return maps


def kernel(**inputs):
    inputs = {k: np.asarray(v) for k, v in inputs.items()}
    nc, _ = build()
    cores = list(range(8))
    res = run_bass_kernel_spmd(nc, make_in_maps(inputs, cores), core_ids=cores)
    return np.stack([res.results[i]["out"] for i in cores], axis=0).astype(np.float32)
```

```python
import contextlib
import math
import numpy as np
import ml_dtypes
import concourse.bass as bass
import concourse.mybir as mybir
from concourse.bass_utils import run_bass_kernel_spmd

F32 = mybir.dt.float32
BF16 = mybir.dt.bfloat16
U32 = mybir.dt.uint32
I32 = mybir.dt.int32
AF = mybir.ActivationFunctionType
ALU = mybir.AluOpType
AX = mybir.AxisListType

ENGS = ("sync", "scalar", "vector", "gpsimd", "tensor")
N_DMA_SEMS = 90

D = 1024
SEQ = 4096
CTX = 256
T = SEQ + CTX
DEPTH = 2
NH = 8
EPS = 1e-6
ATT_SCALE = 96 ** -0.5
IN_W = 4000
NEXP = 16384


class Sched:
    def __init__(self, nc):
        self.nc = nc
        self.ops = {e: [] for e in ENGS}
        self.count = {e: 0 for e in ENGS}
        self.known = {e: {} for e in ENGS}
        self.last_w = {}
        self.readers = {}
        self.dma_next = 0
        self.dma_target = [0] * N_DMA_SEMS
        self.pending = {e: [] for e in ENGS}
        self.n_instr = 0

    def _need(self, eng, tok, waits):
        if tok is None:
            return
        kind, src, val = tok
        if kind == "e" and src == eng and eng == "tensor":
            return
        k = (kind, src)
        if self.known[eng].get(k, 0) >= val:
            return
        self.known[eng][k] = val
        waits.append(tok)

    def _deps(self, eng, r, w):
        waits = []
        for tok in self.pending[eng]:
            self._need(eng, tok, waits)
        self.pending[eng] = []
        for key in r:
            self._need(eng, self.last_w.get(key), waits)
        for key in w:
            self._need(eng, self.last_w.get(key), waits)
            for tok in self.readers.get(key, ()):
                self._need(eng, tok, waits)
        best = {}
        for kind, src, val in waits:
            k = (kind, src)
            if best.get(k, 0) < val:
                best[k] = val
        return [(k[0], k[1], v) for k, v in best.items()]

    def _commit(self, tok, r, w):
        for key in r:
            self.readers.setdefault(key, []).append(tok)
        for key in w:
            self.last_w[key] = tok
            self.readers[key] = []

    def op(self, eng, fn, r=(), w=()):
        waits = self._deps(eng, r, w)
        self.count[eng] += 1
        tok = ("e", eng, self.count[eng])
        self._commit(tok, r, w)
        self.ops[eng].append(("op", fn, waits, None))
        self.n_instr += 1 + len(waits)
        return tok

    def dma(self, eng, fn, r=(), w=()):
        s = self.dma_next % N_DMA_SEMS
        self.dma_next += 1
        waits = self._deps(eng, r, w)
        prev = self.dma_target[s]
        if prev > 0:
            k = ("d", s)
            if self.known[eng].get(k, 0) < prev:
                self.known[eng][k] = prev
                waits.append(("d", s, prev))
        self.dma_target[s] = prev + 16
        tok = ("d", s, prev + 16)
        self._commit(tok, r, w)
        self.ops[eng].append(("dma", fn, waits, s))
        self.n_instr += 1 + len(waits)
        return tok

    def barrier(self):
        waits = []
        for e in ENGS:
            if e != "sync" and self.count[e] > 0:
                self._need("sync", ("e", e, self.count[e]), waits)
        for s in range(N_DMA_SEMS):
            if self.dma_target[s] > 0:
                self._need("sync", ("d", s, self.dma_target[s]), waits)
        self.count["sync"] += 1
        tok = ("e", "sync", self.count["sync"])
        self.ops["sync"].append(("op", lambda e: e.nop(), waits, None))
        self.n_instr += 1 + len(waits)
        for e in ENGS:
            if e != "sync":
                self.pending[e].append(tok)
                for e2 in ENGS:
                    self.known[e][("e", e2)] = max(self.known[e].get(("e", e2), 0),
                                                   self.count[e2] if e2 != "sync" else 0)
                for s in range(N_DMA_SEMS):
                    self.known[e][("d", s)] = self.dma_target[s]
                self.known[e].pop(("e", "sync"), None)
        self.last_w = {}
        self.readers = {}

    def emit(self):
        nc = self.nc
        with contextlib.ExitStack() as st:
            esem = {e: st.enter_context(nc.semaphore("c_" + e)) for e in ENGS}
            dsem = [st.enter_context(nc.semaphore("d%d" % i)) for i in range(N_DMA_SEMS)]
            block = st.enter_context(nc.Block())

            def mk(ename):
                def body(eng):
                    for kind, fn, waits, s in self.ops[ename]:
                        for wk, src, val in waits:
                            eng.wait_ge(esem[src] if wk == "e" else dsem[src], val)
                        ins = fn(eng)
                        if kind == "op":
                            ins.then_inc(esem[ename], 1)
                        else:
                            ins.then_inc(dsem[s], 16)
                    if ename == "sync":
                        for i in range(N_DMA_SEMS):
                            if self.dma_target[i] > 0:
                                eng.wait_ge(dsem[i], self.dma_target[i])
                        for e2 in ENGS:
                            if e2 != "sync" and self.count[e2] > 0:
                                eng.wait_ge(esem[e2], self.count[e2])
                return body

            block.sync(mk("sync"))
            block.scalar(mk("scalar"))
            block.vector(mk("vector"))
            block.gpsimd(mk("gpsimd"))
            block.tensor(mk("tensor"))


def _const_tables():
    c = {}
    half = 16
    inv_freq = (10000.0 ** (-np.arange(0, half, 2, dtype=np.float32) / half)).astype(np.float32)
    t = np.arange(SEQ)
    r = (t // 64).astype(np.float32)
    cl = (t % 64).astype(np.float32)
    ang_r = (r[None, :] * inv_freq[:, None]).astype(np.float32)
    ang_c = (cl[None, :] * inv_freq[:, None]).astype(np.float32)
    ang = np.concatenate([ang_r, ang_r, ang_c, ang_c], axis=0)
    c["ropecos"] = np.cos(ang).astype(np.float32)
    c["ropesin"] = np.sin(ang).astype(np.float32)
    def dft(n):
        k = np.arange(n, dtype=np.int64)
        kk = (k[:, None] * k[None, :]) % n
        a = 2.0 * np.pi * kk.astype(np.float64) / n
        return np.cos(a), np.sin(a)
    cc, sc = dft(256)
    c["ccs"] = np.concatenate([cc, -sc], axis=1).astype(ml_dtypes.bfloat16)
    c["cl256"] = cc.astype(ml_dtypes.bfloat16)
    c["sl256"] = sc.astype(ml_dtypes.bfloat16)
    cL, sL = dft(SEQ)
    c["cl4096"] = cL.astype(ml_dtypes.bfloat16)
    c["sl4096"] = sL.astype(ml_dtypes.bfloat16)
    for L, name in ((SEQ, "icnt4096"), (CTX, "icnt256")):
        tt = np.arange(L)
        tab = np.zeros((2, 128, L), np.float32)
        for gi, w in enumerate((2, 4, 8, 16)):
            lo = np.clip(tt - w // 2, 0, L)
            hi = np.clip(tt - w // 2 + w, 0, L)
            tab[gi // 2, (gi % 2) * 64:(gi % 2) * 64 + 64, :] = (1.0 / (hi - lo).astype(np.float32))[None, :]
        c[name] = tab
    return c


CONST_SHAPES = {
    "ropecos": ([32, SEQ], F32), "ropesin": ([32, SEQ], F32),
    "ccs": ([256, 512], BF16), "cl256": ([256, 256], BF16), "sl256": ([256, 256], BF16),
    "cl4096": ([SEQ, SEQ], BF16), "sl4096": ([SEQ, SEQ], BF16),
    "icnt4096": ([2, 128, SEQ], F32), "icnt256": ([2, 128, CTX], F32),
}

PARAM_SHAPES = {
    "w_mod": [DEPTH, D, 6 * D], "b_mod": [DEPTH, 6 * D], "norm1_g": [DEPTH, D], "norm2_g": [DEPTH, D],
    "w_in": [DEPTH, D, IN_W], "b_gate": [DEPTH, 3 * D], "q_norm_g": [DEPTH, 256], "w_uq": [DEPTH, 256, 768],
    "kv_norm_g": [DEPTH, 128], "w_ukv": [DEPTH, 128, 1024], "w_oa": [DEPTH, 512, D], "w_ob": [DEPTH, 256, D],
    "w_grp": [DEPTH, 4, 64, 64], "pool_scale": [DEPTH, 256], "w_oc": [DEPTH, 256, D], "w_out": [DEPTH, D, D],
    "w_pq": [DEPTH, D, 2048], "peer_keys": [DEPTH, 8, 2, 128, 128], "peer_down0": [NEXP, D], "peer_down1": [NEXP, D],
    "peer_up0": [NEXP, D], "peer_up1": [NEXP, D], "final_g": [D],
}

GROUPS = [(0, CTX)] + [(CTX + 512 * g, 512) for g in range(SEQ // 512)]


def build(stop_after=None, dbg=(), peer_tiles=None):
    nc = bass.Bass("TRN2", target_bir_lowering=False)
    S = Sched(nc)

    def dram_in(name, shape, dt=F32):
        return nc.dram_tensor(name, shape, dt, kind="ExternalInput").ap()

    def dram_scratch(name, shape, dt):
        kind = "ExternalOutput" if name in dbg else "Internal"
        return nc.dram_tensor(name, shape, dt, kind=kind).ap()

    xin = dram_in("x", [SEQ, D])
    ctxin = dram_in("ctx", [CTX, D])
    cvec = dram_in("cvec", [2, D])
    P = {k: dram_in(k, v) for k, v in PARAM_SHAPES.items()}
    C = {k: dram_in(k, v[0], v[1]) for k, v in CONST_SHAPES.items()}
    out = nc.dram_tensor("out", [SEQ, D], F32, kind="ExternalOutput").ap()

    XT = dram_scratch("XT", [8, 128, T], F32)
    GT = dram_scratch("GT", [24, 128, T], BF16)
    QT = dram_scratch("QT", [NH, 96, T], BF16)
    KT = dram_scratch("KT", [NH, 96, T], BF16)
    VV = dram_scratch("VV", [T, 512], BF16)
    ZFT = dram_scratch("ZFT", [2, 128, T], BF16)
    ZPT = dram_scratch("ZPT", [2, 128, T], F32)
    ATT = dram_scratch("ATT", [4, 128, T], BF16)
    YFT = dram_scratch("YFT", [2, 128, T], BF16)
    YCT = dram_scratch("YCT", [2, 128, T], BF16)
    H2T = dram_scratch("H2T", [8, 128, T], BF16)
    PQT = dram_scratch("PQT", [16, 128, T], BF16)
    DUB = [dram_scratch("DUB%d" % i, [NEXP, 2 * D], BF16) for i in range(DEPTH)]

    V = lambda fn, r=(), w=(): S.op("vector", fn, r, w)
    A = lambda fn, r=(), w=(): S.op("scalar", fn, r, w)
    G = lambda fn, r=(), w=(): S.op("gpsimd", fn, r, w)
    TE = lambda fn, r=(), w=(): S.op("tensor", fn, r, w)
    DM = lambda fn, r=(), w=(), q="sync": S.dma(q, fn, r, w)

    rr = [0]

    def evac(out_ap, in_ap, r, w, scale=None):
        rr[0] += 1
        if rr[0] % 2 == 0:
            if scale is None:
                V(lambda e: e.tensor_copy(out_ap, in_ap), r, w)
            else:
                V(lambda e: e.tensor_single_scalar(out_ap, in_ap, float(scale), op=ALU.mult), r, w)
        else:
            A(lambda e: e.activation(out=out_ap, in_=in_ap, func=AF.Copy,
                                     scale=1.0 if scale is None else float(scale)), r, w)

    with contextlib.ExitStack() as top:
        uniq = [0]

        def sb(st, name, shape, dt):
            uniq[0] += 1
            return st.enter_context(nc.sbuf_tensor("%s_%d" % (name, uniq[0]), shape, dt))

        def pst(st, name, shape, dt):
            uniq[0] += 1
            return st.enter_context(nc.psum_tensor("%s_%d" % (name, uniq[0]), shape, dt))

        ident_f = sb(top, "ident_f", [128, 128], F32)
        ident_b = sb(top, "ident_b", [128, 128], BF16)
        ones_f = sb(top, "ones_f", [128, 128], F32)
        ones_b = sb(top, "ones_b", [128, 128], BF16)
        iot = sb(top, "iot", [128, 128], F32)
        gm1 = sb(top, "gm1", [128, 8, 2], F32)
        sh1 = sb(top, "sh1", [128, 8, 2], F32)
        gt1 = sb(top, "gt1", [128, 8, 2], F32)
        gm2 = sb(top, "gm2", [128, 8, 2], F32)
        sh2 = sb(top, "sh2", [128, 8, 2], F32)
        gt2 = sb(top, "gt2", [128, 8, 2], F32)
        epsb = sb(top, "epsb", [128, 1], F32)

        G(lambda e: e.iota(iot[:], [[1, 128]], base=0, channel_multiplier=-1,
                           allow_small_or_imprecise_dtypes=True), w=["iot"])
        V(lambda e: e.tensor_single_scalar(ident_f[:], iot[:], 0.0, op=ALU.is_equal), r=["iot"], w=["ident_f"])
        V(lambda e: e.tensor_single_scalar(ident_b[:], iot[:], 0.0, op=ALU.is_equal), r=["iot"], w=["ident_b"])
        V(lambda e: e.memset(ones_f[:], 1.0), w=["ones_f"])
        V(lambda e: e.memset(ones_b[:], 1.0), w=["ones_b"])
        V(lambda e: e.memset(epsb[:], EPS), w=["epsb"])

        def phase_load_x():
            with contextlib.ExitStack() as st:
                xt = [sb(st, "ld_x%d" % i, [128, D], F32) for i in range(2)]
                xo = [sb(st, "ld_o%d" % i, [128, 8, 128], F32) for i in range(2)]
                ps = [pst(st, "ld_ps%d" % i, [128, 4, 128], F32) for i in range(4)]
                ntile = T // 128
                for ti in range(ntile):
                    b = ti % 2
                    src = ctxin[ti * 128:(ti + 1) * 128, :] if ti < 2 else xin[(ti - 2) * 128:(ti - 1) * 128, :]
                    DM(lambda e, b=b, src=src: e.dma_start(out=xt[b][:], in_=src), w=["ld_x%d" % b])
                    for half in range(2):
                        pi = (ti * 2 + half) % 4
                        for j in range(4):
                            k = half * 4 + j
                            TE(lambda e, b=b, pi=pi, j=j, k=k: e.transpose(ps[pi][:, j, :], xt[b][:, k * 128:(k + 1) * 128], ident_f[:]),
                               r=["ld_x%d" % b, "ident_f"], w=["ld_ps%d" % pi])
                        evac(xo[b][:, half * 4:(half + 1) * 4, :], ps[pi][:], r=["ld_ps%d" % pi], w=[("ld_o", b, half)])
                    DM(lambda e, b=b, ti=ti: e.dma_start(out=XT[:, :, ti * 128:(ti + 1) * 128].rearrange("k p t -> p k t"), in_=xo[b][:]),
                       r=[("ld_o", b, 0), ("ld_o", b, 1)], w=[("XT", ti)])
            S.barrier()

        def phase_mod(l):
            with contextlib.ExitStack() as st:
                cT = sb(st, "md_c", [128, 2, 8], F32)
                scT = sb(st, "md_sc", [128, 8, 2], F32)
                bm = sb(st, "md_b", [128, 48], F32)
                g1 = sb(st, "md_g1", [128, 8], F32)
                g2 = sb(st, "md_g2", [128, 8], F32)
                modT = sb(st, "md_mod", [128, 48, 2], F32)
                wm = [sb(st, "md_w%d" % i, [128, 8, 512], F32) for i in range(2)]
                mps = pst(st, "md_ps", [128, 48, 2], F32)
                for r_ in range(2):
                    DM(lambda e, r_=r_: e.dma_start(out=cT[:, r_, :], in_=cvec[r_].rearrange("(k p) -> p k", p=128), allow_slow_non_contiguous=True), w=["md_c"])
                DM(lambda e: e.dma_start(out=bm[:], in_=P["b_mod"][l].rearrange("(n p) -> p n", p=128), allow_slow_non_contiguous=True), w=["md_b"])
                DM(lambda e: e.dma_start(out=g1[:], in_=P["norm1_g"][l].rearrange("(n p) -> p n", p=128), allow_slow_non_contiguous=True), w=["md_g1"])
                DM(lambda e: e.dma_start(out=g2[:], in_=P["norm2_g"][l].rearrange("(n p) -> p n", p=128), allow_slow_non_contiguous=True), w=["md_g2"])
                for r_ in range(2):
                    A(lambda e, r_=r_: e.activation(out=scT[:, :, r_], in_=cT[:, r_, :], func=AF.Silu), r=["md_c"], w=["md_sc"])
                for j in range(12):
                    b = j % 2
                    DM(lambda e, b=b, j=j: e.dma_start(out=wm[b][:], in_=P["w_mod"][l][:, j * 512:(j + 1) * 512].rearrange("(k p) n -> p k n", p=128)),
                       w=["md_w%d" % b])
                    for i in range(4):
                        n = j * 4 + i
                        for k in range(8):
                            TE(lambda e, b=b, i=i, k=k, n=n: e.matmul(mps[:, n, :], lhsT=wm[b][:, k, i * 128:(i + 1) * 128], rhs=scT[:, k, :],
                                                                      start=(k == 0), stop=(k == 7)),
                               r=["md_w%d" % b, "md_sc"], w=["md_ps"])
                V(lambda e: e.tensor_tensor(out=modT[:], in0=mps[:], in1=bm[:].unsqueeze(2).to_broadcast([128, 48, 2]), op=ALU.add),
                  r=["md_ps", "md_b"], w=["md_mod"])
                V(lambda e: e.tensor_copy(sh1[:], modT[:, 0:8, :]), r=["md_mod"], w=["sh1"])
                V(lambda e: e.scalar_tensor_tensor(out=gm1[:], in0=modT[:, 8:16, :], scalar=1.0, in1=g1[:].unsqueeze(2).to_broadcast([128, 8, 2]),
                                                   op0=ALU.add, op1=ALU.mult), r=["md_mod", "md_g1"], w=["gm1"])
                V(lambda e: e.tensor_copy(gt1[:], modT[:, 16:24, :]), r=["md_mod"], w=["gt1"])
                V(lambda e: e.tensor_copy(sh2[:], modT[:, 24:32, :]), r=["md_mod"], w=["sh2"])
                V(lambda e: e.scalar_tensor_tensor(out=gm2[:], in0=modT[:, 32:40, :], scalar=1.0, in1=g2[:].unsqueeze(2).to_broadcast([128, 8, 2]),
                                                   op0=ALU.add, op1=ALU.mult), r=["md_mod", "md_g2"], w=["gm2"])
                V(lambda e: e.tensor_copy(gt2[:], modT[:, 40:48, :]), r=["md_mod"], w=["gt2"])
            S.barrier()

        def rms_stats(src, nk, n, nfeat, sq, ps_stat, rstd, keys_r, ksq, kps, krstd):
            for k in range(nk):
                A(lambda e, k=k: e.activation(out=sq[:, k, :n], in_=src[:, k, :n], func=AF.Square), r=keys_r, w=[ksq])
            for k in range(nk):
                TE(lambda e, k=k: e.matmul(ps_stat[:, :n], lhsT=ones_f[:], rhs=sq[:, k, :n], start=(k == 0), stop=(k == nk - 1)),
                   r=[ksq, "ones_f"], w=[kps])
            A(lambda e: e.activation(out=rstd[:, :n], in_=ps_stat[:, :n], func=AF.Sqrt, scale=1.0 / nfeat, bias=epsb[:]),
              r=[kps, "epsb"], w=[krstd])
            V(lambda e: e.reciprocal(rstd[:, :n], rstd[:, :n]), r=[krstd], w=[krstd])

        def phase_proj(l, last):
            with contextlib.ExitStack() as st:
                w_in = sb(st, "pj_win", [128, 8, IN_W], BF16)
                w_krot = sb(st, "pj_wkrot", [128, 8, 32], BF16)
                wq_raw = sb(st, "pj_wqraw", [128, 2, 768], BF16)
                wq_rot = sb(st, "pj_wqrot", [128, 2, NH, 32], BF16)
                wkv = sb(st, "pj_wkv", [128, 1024], BF16)
                wv = sb(st, "pj_wv", [128, NH, 64], BF16)
                qg = sb(st, "pj_qg", [128, 2], F32)
                kvg = sb(st, "pj_kvg", [128, 1], F32)
                bg = sb(st, "pj_bg", [128, 24], F32)
                rcos = sb(st, "pj_cos", [32, SEQ], F32)
                rsin = sb(st, "pj_sin", [32, SEQ], F32)
                xg = sb(st, "pj_xg", [128, 8, 512], F32)
                sq = sb(st, "pj_sq", [128, 8, 512], F32)
                rstd = sb(st, "pj_rstd", [128, 512], F32)
                hT = sb(st, "pj_hT", [128, 8, 512], BF16)
                cq = sb(st, "pj_cq", [128, 2, 512], F32)
                cqn = sb(st, "pj_cqn", [128, 2, 512], BF16)
                ckv = sb(st, "pj_ckv", [128, 1, 512], F32)
                cn = sb(st, "pj_cn", [128, 512], BF16)
                rstd2 = sb(st, "pj_rstd2", [128, 512], F32)
                ob = [sb(st, "pj_ob%d" % i, [128, 512], BF16) for i in range(4)]
                of = [sb(st, "pj_of%d" % i, [128, 512], F32) for i in range(2)]
                t1 = sb(st, "pj_t1", [32, 512], F32)
                t2 = sb(st, "pj_t2", [32, 512], F32)
                ps = [pst(st, "pj_ps%d" % i, [128, 512], F32) for i in range(6)]
                ps_st = pst(st, "pj_pst", [128, 512], F32)
                ps_rot = pst(st, "pj_prot", [128, 512], F32)

                for k in range(8):
                    for hh in range(2):
                        DM(lambda e, k=k, hh=hh: e.dma_start(out=w_in[:, k, hh * 2000:(hh + 1) * 2000],
                                                             in_=P["w_in"][l][k * 128:(k + 1) * 128, hh * 2000:(hh + 1) * 2000]),
                           w=["pj_win"], q="gpsimd")
                DM(lambda e: e.dma_start(out=wq_raw[:], in_=P["w_uq"][l].rearrange("(k p) n -> p k n", p=128)), w=["pj_wqraw"], q="gpsimd")
                DM(lambda e: e.dma_start(out=wkv[:], in_=P["w_ukv"][l]), w=["pj_wkv"], q="gpsimd")
                DM(lambda e: e.dma_start(out=qg[:], in_=P["q_norm_g"][l].rearrange("(n p) -> p n", p=128), allow_slow_non_contiguous=True), w=["pj_qg"])
                DM(lambda e: e.dma_start(out=kvg[:], in_=P["kv_norm_g"][l].rearrange("(n p) -> p n", p=128), allow_slow_non_contiguous=True), w=["pj_kvg"])
                DM(lambda e: e.dma_start(out=bg[:], in_=P["b_gate"][l].rearrange("(n p) -> p n", p=128), allow_slow_non_contiguous=True), w=["pj_bg"])
                DM(lambda e: e.dma_start(out=rcos[:], in_=C["ropecos"]), w=["pj_cos"])
                DM(lambda e: e.dma_start(out=rsin[:], in_=C["ropesin"]), w=["pj_sin"])
                kr_cols = w_in[:, :, 384:416].rearrange("p k (rc x f) -> p k rc x f", rc=2, x=2)
                kro = w_krot[:].rearrange("p k (rc x f) -> p k rc x f", rc=2, x=2)
                V(lambda e: e.tensor_single_scalar(kro[:, :, :, 0, :], kr_cols[:, :, :, 1, :], -1.0, op=ALU.mult), r=["pj_win"], w=["pj_wkrot"])
                V(lambda e: e.tensor_copy(kro[:, :, :, 1, :], kr_cols[:, :, :, 0, :]), r=["pj_win"], w=["pj_wkrot"])
                for k in range(2):
                    qr = wq_raw[:, k, :].rearrange("p (h c) -> p h c", c=96)[:, :, 64:96].rearrange("p h (rc x f) -> p h rc x f", rc=2, x=2)
                    qo = wq_rot[:, k, :, :].rearrange("p h (rc x f) -> p h rc x f", rc=2, x=2)
                    for rc in range(2):
                        V(lambda e, qo=qo, qr=qr, rc=rc: e.tensor_single_scalar(qo[:, :, rc, 0, :], qr[:, :, rc, 1, :], -1.0, op=ALU.mult),
                          r=["pj_wqraw"], w=["pj_wqrot"])
                        V(lambda e, qo=qo, qr=qr, rc=rc: e.tensor_copy(qo[:, :, rc, 1, :], qr[:, :, rc, 0, :]), r=["pj_wqraw"], w=["pj_wqrot"])
                V(lambda e: e.tensor_copy(wv[:], wkv[:].rearrange("p (h c) -> p h c", c=128)[:, :, 64:128]), r=["pj_wkv"], w=["pj_wv"])

                obi = [0]

                def nxt_ob():
                    obi[0] += 1
                    return obi[0] % 4

                psi = [0]

                def nxt_ps():
                    psi[0] += 1
                    return psi[0] % 6

                def do_group(gi, t0, n):
                    is_ctx = gi == 0
                    r_ = 1 if is_ctx else 0
                    s0 = t0 - CTX
                    kv_only = last and is_ctx
                    DM(lambda e, t0=t0, n=n: e.dma_start(out=xg[:, :, :n], in_=XT[:, :, t0:t0 + n].rearrange("k p t -> p k t")), r=["XT"], w=["pj_xg"])
                    rms_stats(xg, 8, n, D, sq, ps_st, rstd, ["pj_xg"], "pj_sq", "pj_pst", "pj_rstd")
                    for k in range(8):
                        V(lambda e, k=k, n=n: e.tensor_tensor(out=sq[:, k, :n], in0=xg[:, k, :n], in1=rstd[:, :n], op=ALU.mult),
                          r=["pj_xg", "pj_rstd"], w=["pj_sq"])
                        V(lambda e, k=k, n=n, r_=r_: e.tensor_scalar(hT[:, k, :n], sq[:, k, :n], gm1[:, k, r_:r_ + 1], sh1[:, k, r_:r_ + 1],
                                                                     op0=ALU.mult, op1=ALU.add),
                          r=["pj_sq", "gm1", "sh1"], w=["pj_hT"])

                    p = nxt_ps()
                    for k in range(8):
                        TE(lambda e, k=k, p=p: e.matmul(ps[p][:, :n], lhsT=w_in[:, k, 256:384], rhs=hT[:, k, :n], start=(k == 0), stop=(k == 7)),
                           r=["pj_hT", "pj_win"], w=["pj_ps%d" % p])
                    V(lambda e, p=p: e.tensor_copy(ckv[:, 0, :n], ps[p][:, :n]), r=["pj_ps%d" % p], w=["pj_ckv"])
                    rms_stats(ckv, 1, n, 128, sq, ps_st, rstd2, ["pj_ckv"], "pj_sq", "pj_pst", "pj_rstd2")
                    V(lambda e: e.tensor_tensor(out=sq[:, 0, :n], in0=ckv[:, 0, :n], in1=rstd2[:, :n], op=ALU.mult),
                      r=["pj_ckv", "pj_rstd2"], w=["pj_sq"])
                    V(lambda e: e.tensor_scalar(cn[:, :n], sq[:, 0, :n], kvg[:, 0:1], None, op0=ALU.mult), r=["pj_sq", "pj_kvg"], w=["pj_cn"])
                    for h in range(NH):
                        p = nxt_ps()
                        TE(lambda e, p=p, h=h: e.matmul(ps[p][:64, :n], lhsT=wkv[:, h * 128:h * 128 + 64], rhs=cn[:, :n], start=True, stop=True),
                           r=["pj_cn", "pj_wkv"], w=["pj_ps%d" % p])
                        o = nxt_ob()
                        evac(ob[o][:64, :n], ps[p][:64, :n], r=["pj_ps%d" % p], w=["pj_ob%d" % o])
                        DM(lambda e, o=o, h=h: e.dma_start(out=KT[h, 32:96, t0:t0 + n], in_=ob[o][:64, :n]), r=["pj_ob%d" % o], w=[("KT", S.dma_next)])
                    for j in range(n // 128):
                        p = nxt_ps()
                        TE(lambda e, p=p, j=j: e.matmul(ps[p][:, :], lhsT=cn[:, j * 128:(j + 1) * 128], rhs=wv[:].rearrange("p h c -> p (h c)"), start=True, stop=True),
                           r=["pj_cn", "pj_wv"], w=["pj_ps%d" % p])
                        o = nxt_ob()
                        evac(ob[o][:, :], ps[p][:, :], r=["pj_ps%d" % p], w=["pj_ob%d" % o])
                        DM(lambda e, o=o, j=j: e.dma_start(out=VV[t0 + j * 128:t0 + (j + 1) * 128, :], in_=ob[o][:, :]), r=["pj_ob%d" % o], w=[("VV", S.dma_next)])
                    p = nxt_ps()
                    for k in range(8):
                        TE(lambda e, k=k, p=p: e.matmul(ps[p][:32, :n], lhsT=w_in[:, k, 384:416], rhs=hT[:, k, :n], start=(k == 0), stop=(k == 7)),
                           r=["pj_hT", "pj_win"], w=["pj_ps%d" % p])
                    o = nxt_ob()
                    if is_ctx:
                        evac(ob[o][:32, :n], ps[p][:32, :n], r=["pj_ps%d" % p], w=["pj_ob%d" % o])
                    else:
                        for k in range(8):
                            TE(lambda e, k=k: e.matmul(ps_rot[:32, :n], lhsT=w_krot[:, k, :], rhs=hT[:, k, :n], start=(k == 0), stop=(k == 7)),
                               r=["pj_hT", "pj_wkrot"], w=["pj_prot"])
                        V(lambda e, p=p: e.tensor_tensor(out=t1[:, :n], in0=ps[p][:32, :n], in1=rcos[:, s0:s0 + n], op=ALU.mult),
                          r=["pj_ps%d" % p, "pj_cos"], w=["pj_t1"])
                        V(lambda e: e.tensor_tensor(out=t2[:, :n], in0=ps_rot[:32, :n], in1=rsin[:, s0:s0 + n], op=ALU.mult),
                          r=["pj_prot", "pj_sin"], w=["pj_t2"])
                        V(lambda e, o=o: e.tensor_tensor(out=ob[o][:32, :n], in0=t1[:, :n], in1=t2[:, :n], op=ALU.add),
                          r=["pj_t1", "pj_t2"], w=["pj_ob%d" % o])
                    for h in range(NH):
                        DM(lambda e, o=o, h=h: e.dma_start(out=KT[h, 0:32, t0:t0 + n], in_=ob[o][:32, :n]), r=["pj_ob%d" % o], w=[("KT", S.dma_next)])
                    if kv_only:
                        return
                    for c in range(2):
                        p = nxt_ps()
                        for k in range(8):
                            TE(lambda e, k=k, p=p, c=c: e.matmul(ps[p][:, :n], lhsT=w_in[:, k, c * 128:(c + 1) * 128], rhs=hT[:, k, :n], start=(k == 0), stop=(k == 7)),
                               r=["pj_hT", "pj_win"], w=["pj_ps%d" % p])
                        V(lambda e, p=p, c=c: e.tensor_copy(cq[:, c, :n], ps[p][:, :n]), r=["pj_ps%d" % p], w=["pj_cq"])
                    rms_stats(cq, 2, n, 256, sq, ps_st, rstd2, ["pj_cq"], "pj_sq", "pj_pst", "pj_rstd2")
                    for c in range(2):
                        V(lambda e, c=c: e.tensor_tensor(out=sq[:, c, :n], in0=cq[:, c, :n], in1=rstd2[:, :n], op=ALU.mult),
                          r=["pj_cq", "pj_rstd2"], w=["pj_sq"])
                        V(lambda e, c=c: e.tensor_scalar(cqn[:, c, :n], sq[:, c, :n], qg[:, c:c + 1], None, op0=ALU.mult), r=["pj_sq", "pj_qg"], w=["pj_cqn"])
                    for h in range(NH):
                        p = nxt_ps()
                        for c in range(2):
                            TE(lambda e, p=p, c=c, h=h: e.matmul(ps[p][:64, :n], lhsT=wq_raw[:, c, h * 96:h * 96 + 64], rhs=cqn[:, c, :n], start=(c == 0), stop=(c == 1)),
                               r=["pj_cqn", "pj_wqraw"], w=["pj_ps%d" % p])
                        o = nxt_ob()
                        evac(ob[o][:64, :n], ps[p][:64, :n], r=["pj_ps%d" % p], w=["pj_ob%d" % o])
                        DM(lambda e, o=o, h=h: e.dma_start(out=QT[h, 32:96, t0:t0 + n], in_=ob[o][:64, :n]), r=["pj_ob%d" % o], w=[("QT", S.dma_next)])
                        p = nxt_ps()
                        for c in range(2):
                            TE(lambda e, p=p, c=c, h=h: e.matmul(ps[p][:32, :n], lhsT=wq_raw[:, c, h * 96 + 64:h * 96 + 96], rhs=cqn[:, c, :n], start=(c == 0), stop=(c == 1)),
                               r=["pj_cqn", "pj_wqraw"], w=["pj_ps%d" % p])
                        o = nxt_ob()
                        if is_ctx:
                            evac(ob[o][:32, :n], ps[p][:32, :n], r=["pj_ps%d" % p], w=["pj_ob%d" % o])
                        else:
                            for c in range(2):
                                TE(lambda e, c=c, h=h: e.matmul(ps_rot[:32, :n], lhsT=wq_rot[:, c, h, :], rhs=cqn[:, c, :n], start=(c == 0), stop=(c == 1)),
                                   r=["pj_cqn", "pj_wqrot"], w=["pj_prot"])
                            V(lambda e, p=p: e.tensor_tensor(out=t1[:, :n], in0=ps[p][:32, :n], in1=rcos[:, s0:s0 + n], op=ALU.mult),
                              r=["pj_ps%d" % p, "pj_cos"], w=["pj_t1"])
                            V(lambda e: e.tensor_tensor(out=t2[:, :n], in0=ps_rot[:32, :n], in1=rsin[:, s0:s0 + n], op=ALU.mult),
                              r=["pj_prot", "pj_sin"], w=["pj_t2"])
                            V(lambda e, o=o: e.tensor_tensor(out=ob[o][:32, :n], in0=t1[:, :n], in1=t2[:, :n], op=ALU.add),
                              r=["pj_t1", "pj_t2"], w=["pj_ob%d" % o])
                        DM(lambda e, o=o, h=h: e.dma_start(out=QT[h, 0:32, t0:t0 + n], in_=ob[o][:32, :n]), r=["pj_ob%d" % o], w=[("QT", S.dma_next)])
                    for c in range(2):
                        p = nxt_ps()
                        for k in range(8):
                            TE(lambda e, k=k, p=p, c=c: e.matmul(ps[p][:, :n], lhsT=w_in[:, k, 416 + c * 128:416 + (c + 1) * 128], rhs=hT[:, k, :n], start=(k == 0), stop=(k == 7)),
                               r=["pj_hT", "pj_win"], w=["pj_ps%d" % p])
                        o = nxt_ob()
                        evac(ob[o][:, :n], ps[p][:, :n], r=["pj_ps%d" % p], w=["pj_ob%d" % o])
                        DM(lambda e, o=o, c=c: e.dma_start(out=ZFT[c, :, t0:t0 + n], in_=ob[o][:, :n]), r=["pj_ob%d" % o], w=[("ZFT", S.dma_next)])
                    for c in range(2):
                        p = nxt_ps()
                        for k in range(8):
                            TE(lambda e, k=k, p=p, c=c: e.matmul(ps[p][:, :n], lhsT=w_in[:, k, 672 + c * 128:672 + (c + 1) * 128], rhs=hT[:, k, :n], start=(k == 0), stop=(k == 7)),
                               r=["pj_hT", "pj_win"], w=["pj_ps%d" % p])
                        evac(of[c][:, :n], ps[p][:, :n], r=["pj_ps%d" % p], w=["pj_of%d" % c])
                        DM(lambda e, c=c: e.dma_start(out=ZPT[c, :, t0:t0 + n], in_=of[c][:, :n]), r=["pj_of%d" % c], w=[("ZPT", S.dma_next)])
                    for c in range(24):
                        p = nxt_ps()
                        for k in range(8):
                            TE(lambda e, k=k, p=p, c=c: e.matmul(ps[p][:, :n], lhsT=w_in[:, k, 928 + c * 128:928 + (c + 1) * 128], rhs=hT[:, k, :n], start=(k == 0), stop=(k == 7)),
                               r=["pj_hT", "pj_win"], w=["pj_ps%d" % p])
                        o = nxt_ob()
                        A(lambda e, o=o, p=p, c=c: e.activation(out=ob[o][:, :n], in_=ps[p][:, :n], func=AF.Sigmoid, bias=bg[:, c:c + 1], scale=1.0),
                          r=["pj_ps%d" % p, "pj_bg"], w=["pj_ob%d" % o])
                        DM(lambda e, o=o, c=c: e.dma_start(out=GT[c, :, t0:t0 + n], in_=ob[o][:, :n]), r=["pj_ob%d" % o], w=[("GT", S.dma_next)])

                for gi, (t0, n) in enumerate(GROUPS):
                    do_group(gi, t0, n)
            S.barrier()

        def phase_attn(l, last):
            with contextlib.ExitStack() as st:
                v_sb = sb(st, "at_v", [128, T // 128, 512], BF16)
                v_ext = sb(st, "at_vx", [128, T // 128, NH, 65], BF16)
                sel = sb(st, "at_sel", [65, 64], F32)
                kth = [sb(st, "at_k%d" % i, [96, T], BF16) for i in range(2)]
                qg = [sb(st, "at_q%d" % i, [96, 512], BF16) for i in range(2)]
                pt = [sb(st, "at_p%d" % i, [128, 512], BF16) for i in range(3)]
                o_sb = sb(st, "at_osb", [65, 512], F32)
                rs = sb(st, "at_rs", [64, 512], F32)
                osb = [sb(st, "at_o%d" % i, [64, 512], BF16) for i in range(2)]
                ps_s = [pst(st, "at_pss%d" % i, [128, 512], F32) for i in range(3)]
                ps_o = [pst(st, "at_pso%d" % i, [128, 512], F32) for i in range(2)]
                ps_b = pst(st, "at_psb", [128, 512], F32)
                DM(lambda e: e.dma_start(out=v_sb[:], in_=VV.rearrange("(kt p) c -> p kt c", p=128)), w=["at_v"])
                V(lambda e: e.memset(v_ext[:], 1.0), w=["at_vx"])
                G(lambda e: e.tensor_copy(v_ext[:, :, :, 0:64], v_sb[:].rearrange("p k (h c) -> p k h c", c=64)), r=["at_v"], w=["at_vx"])
                V(lambda e: e.memset(sel[:], 0.0), w=["at_sel"])
                V(lambda e: e.memset(sel[64:65, :], 1.0), w=["at_sel"])
                cnt = [0]
                segs = ([] if last else [(0, CTX, 2)]) + [(t0, n, T // 128) for (t0, n) in GROUPS[1:]]

                def do_head(h):
                    kb = h % 2
                    DM(lambda e: e.dma_start(out=kth[kb][:], in_=KT[h]), w=["at_k%d" % kb])

                    def do_seg(t0, n, nkt):
                        cnt[0] += 1
                        qb = cnt[0] % 2
                        DM(lambda e: e.dma_start(out=qg[qb][:, :n], in_=QT[h, :, t0:t0 + n]), w=["at_q%d" % qb])

                        def s_mm(kt):
                            si = kt % 3
                            TE(lambda e: e.matmul(ps_s[si][:, :n], lhsT=kth[kb][:, kt * 128:(kt + 1) * 128], rhs=qg[qb][:, :n], start=True, stop=True),
                               r=["at_k%d" % kb, "at_q%d" % qb], w=["at_pss%d" % si])

                        def pv(kt):
                            si = kt % 3
                            A(lambda e: e.activation(out=pt[si][:, :n], in_=ps_s[si][:, :n], func=AF.Exp, scale=ATT_SCALE),
                              r=["at_pss%d" % si], w=["at_p%d" % si])
                            TE(lambda e: e.matmul(ps_o[qb][:65, :n], lhsT=v_ext[:, kt, h, :], rhs=pt[si][:, :n], start=(kt == 0), stop=(kt == nkt - 1)),
                               r=["at_vx", "at_p%d" % si], w=["at_pso%d" % qb])
                        s_mm(0)
                        if nkt > 1:
                            s_mm(1)
                        for kt in range(nkt):
                            if kt + 2 < nkt:
                                s_mm(kt + 2)
                            pv(kt)
                        A(lambda e: e.activation(out=o_sb[:, :n], in_=ps_o[qb][:65, :n], func=AF.Copy), r=["at_pso%d" % qb], w=["at_osb"])
                        TE(lambda e: e.matmul(ps_b[:64, :n], lhsT=sel[:, :], rhs=o_sb[:, :n], start=True, stop=True), r=["at_sel", "at_osb"], w=["at_psb"])
                        V(lambda e: e.reciprocal(rs[:, :n], ps_b[:64, :n]), r=["at_psb"], w=["at_rs"])
                        V(lambda e: e.tensor_tensor(out=osb[qb][:, :n], in0=o_sb[:64, :n], in1=rs[:, :n], op=ALU.mult),
                          r=["at_osb", "at_rs"], w=["at_o%d" % qb])
                        DM(lambda e: e.dma_start(out=ATT[h // 2, (h % 2) * 64:(h % 2) * 64 + 64, t0:t0 + n], in_=osb[qb][:, :n]),
                           r=["at_o%d" % qb], w=[("ATT", S.dma_next)])
                    for (t0, n, nkt) in segs:
                        do_seg(t0, n, nkt)
                for h in range(NH):
                    do_head(h)
            S.barrier()

        def phase_fourier(l, last):
            segs = ([] if last else [(0, CTX, "cl256", "sl256")]) + [(CTX, SEQ, "cl4096", "sl4096")]

            def do_seg(t0, L, cln, sln):
                ntt = L // 128
                KW = 256
                with contextlib.ExitStack() as st:
                    ccs = sb(st, "fo_ccs", [128, 2, 512], BF16)
                    zft = sb(st, "fo_z", [128, 2, L], BF16)
                    uv = sb(st, "fo_uv", [128, ntt, 512], BF16)
                    clb = [sb(st, "fo_cl%d" % i, [128, ntt, KW], BF16) for i in range(2)]
                    slb = [sb(st, "fo_sl%d" % i, [128, ntt, KW], BF16) for i in range(2)]
                    yo = [sb(st, "fo_y%d" % i, [128, KW], BF16) for i in range(2)]
                    ps = [pst(st, "fo_ps%d" % i, [128, 512], F32) for i in range(4)]
                    DM(lambda e: e.dma_start(out=ccs[:], in_=C["ccs"].rearrange("(c p) n -> p c n", p=128)), w=["fo_ccs"])
                    DM(lambda e: e.dma_start(out=zft[:], in_=ZFT[:, :, t0:t0 + L].rearrange("c p t -> p c t")), w=["fo_z"])

                    def do_tt(tt):
                        pi = tt % 4
                        for c in range(2):
                            TE(lambda e, c=c: e.matmul(ps[pi][:, :], lhsT=zft[:, c, tt * 128:(tt + 1) * 128], rhs=ccs[:, c, :], start=(c == 0), stop=(c == 1)),
                               r=["fo_z", "fo_ccs"], w=["fo_ps%d" % pi])
                        evac(uv[:, tt, :], ps[pi][:, :], r=["fo_ps%d" % pi], w=[("fo_uv", tt)])
                    for tt in range(ntt):
                        do_tt(tt)
                    uvkeys = [("fo_uv", tt) for tt in range(ntt)]
                    scale = 1.0 / math.sqrt(L * 256.0)

                    def do_kc(kc):
                        b = kc % 2
                        DM(lambda e: e.dma_start(out=clb[b][:], in_=C[cln][:, kc * KW:(kc + 1) * KW].rearrange("(tt p) k -> p tt k", p=128)), w=["fo_cl%d" % b])
                        DM(lambda e: e.dma_start(out=slb[b][:], in_=C[sln][:, kc * KW:(kc + 1) * KW].rearrange("(tt p) k -> p tt k", p=128)), w=["fo_sl%d" % b])

                        def do_m(m):
                            pi = (kc * 2 + m) % 4
                            for tt in range(ntt):
                                TE(lambda e, tt=tt: e.matmul(ps[pi][:, :KW], lhsT=uv[:, tt, m * 128:(m + 1) * 128], rhs=clb[b][:, tt, :], start=(tt == 0), stop=False),
                                   r=uvkeys + ["fo_cl%d" % b], w=["fo_ps%d" % pi])
                                TE(lambda e, tt=tt: e.matmul(ps[pi][:, :KW], lhsT=uv[:, tt, 256 + m * 128:256 + (m + 1) * 128], rhs=slb[b][:, tt, :], start=False, stop=(tt == ntt - 1)),
                                   r=uvkeys + ["fo_sl%d" % b], w=["fo_ps%d" % pi])
                            evac(yo[m][:, :], ps[pi][:, :KW], r=["fo_ps%d" % pi], w=["fo_y%d" % m], scale=scale)
                            DM(lambda e: e.dma_start(out=YFT[m, :, t0 + kc * KW:t0 + (kc + 1) * KW], in_=yo[m][:, :]), r=["fo_y%d" % m], w=[("YFT", S.dma_next)])
                        for m in range(2):
                            do_m(m)
                    for kc in range(L // KW):
                        do_kc(kc)
                S.barrier()
            for sg in segs:
                do_seg(*sg)

        def phase_pool(l, last):
            segs = ([] if last else [(0, CTX, "icnt256")]) + [(CTX, SEQ, "icnt4096")]

            def do_seg(t0, L, icn):
                with contextlib.ExitStack() as st:
                    zpp = sb(st, "po_z", [128, 2, L + 16], F32)
                    sA = sb(st, "po_sA", [128, L + 16], F32)
                    sB = sb(st, "po_sB", [128, L + 16], F32)
                    icnt = sb(st, "po_ic", [128, 2, L], F32)
                    tmp = sb(st, "po_tmp", [128, L], F32)
                    poolT = sb(st, "po_pT", [128, 2, L], BF16)
                    wbd = [sb(st, "po_w%d" % i, [128, 128], BF16) for i in range(2)]
                    psc = sb(st, "po_sc", [128, 2], F32)
                    yo = [sb(st, "po_y%d" % i, [128, 512], BF16) for i in range(2)]
                    ps = [pst(st, "po_ps%d" % i, [128, 512], F32) for i in range(2)]
                    V(lambda e: e.memset(zpp[:], 0.0), w=["po_z"])
                    for ch in range(2):
                        V(lambda e, ch=ch: e.memset(wbd[ch][:], 0.0), w=["po_w%d" % ch])
                    DM(lambda e: e.dma_start(out=zpp[:, :, 8:8 + L], in_=ZPT[:, :, t0:t0 + L].rearrange("c p t -> p c t")), w=["po_z"])
                    DM(lambda e: e.dma_start(out=icnt[:], in_=C[icn].rearrange("c p t -> p c t")), w=["po_ic"])
                    DM(lambda e: e.dma_start(out=psc[:], in_=P["pool_scale"][l].rearrange("(c p) -> p c", p=128), allow_slow_non_contiguous=True), w=["po_sc"])
                    for g in range(4):
                        o = (g % 2) * 64
                        DM(lambda e, g=g, o=o: e.dma_start(out=wbd[g // 2][o:o + 64, o:o + 64], in_=P["w_grp"][l, g]), w=["po_w%d" % (g // 2)], q="gpsimd")

                    def do_ch(ch):
                        V(lambda e: e.tensor_tensor(out=sA[:, 1:L + 16], in0=zpp[:, ch, 0:L + 15], in1=zpp[:, ch, 1:L + 16], op=ALU.add), r=["po_z"], w=["po_sA"])
                        V(lambda e: e.tensor_tensor(out=sB[:, 2:L + 15], in0=sA[:, 1:L + 14], in1=sA[:, 3:L + 16], op=ALU.add), r=["po_sA"], w=["po_sB"])
                        if ch == 1:
                            V(lambda e: e.tensor_tensor(out=sA[:, 4:L + 13], in0=sB[:, 2:L + 11], in1=sB[:, 6:L + 15], op=ALU.add), r=["po_sB"], w=["po_sA"])
                            V(lambda e: e.tensor_tensor(out=sB[:, 8:L + 9], in0=sA[:, 4:L + 5], in1=sA[:, 12:L + 13], op=ALU.add), r=["po_sA"], w=["po_sB"])
                        V(lambda e: e.tensor_tensor(out=tmp[0:64, :], in0=sA[0:64, 8:8 + L], in1=icnt[0:64, ch, :], op=ALU.mult), r=["po_sA", "po_ic"], w=["po_tmp"])
                        V(lambda e: e.tensor_tensor(out=tmp[64:128, :], in0=sB[64:128, 8:8 + L], in1=icnt[64:128, ch, :], op=ALU.mult), r=["po_sB", "po_ic"], w=["po_tmp"])
                        V(lambda e: e.tensor_tensor(out=poolT[:, ch, :], in0=tmp[:, :], in1=zpp[:, ch, 8:8 + L], op=ALU.subtract), r=["po_tmp", "po_z"], w=[("po_pT", ch)])

                        def do_tc(tc_):
                            n = min(512, L - tc_ * 512)
                            pi = tc_ % 2
                            TE(lambda e: e.matmul(ps[pi][:, :n], lhsT=wbd[ch][:], rhs=poolT[:, ch, tc_ * 512:tc_ * 512 + n], start=True, stop=True),
                               r=[("po_pT", ch), "po_w%d" % ch], w=["po_ps%d" % pi])
                            V(lambda e: e.tensor_scalar(yo[pi][:, :n], ps[pi][:, :n], psc[:, ch:ch + 1], None, op0=ALU.mult), r=["po_ps%d" % pi, "po_sc"], w=["po_y%d" % pi])
                            DM(lambda e: e.dma_start(out=YCT[ch, :, t0 + tc_ * 512:t0 + tc_ * 512 + n], in_=yo[pi][:, :n]), r=["po_y%d" % pi], w=[("YCT", S.dma_next)])
                        for tc_ in range((L + 511) // 512):
                            do_tc(tc_)
                    for ch in range(2):
                        do_ch(ch)
                S.barrier()
            for sg in segs:
                do_seg(*sg)

        def phase_combine(l, last):
            with contextlib.ExitStack() as st:
                w_oa = sb(st, "cb_woa", [128, 4, D], BF16)
                w_ob = sb(st, "cb_wob", [128, 2, D], BF16)
                w_oc = sb(st, "cb_woc", [128, 2, D], BF16)
                w_out = sb(st, "cb_wout", [128, 8, D], BF16)
                w_pq = sb(st, "cb_wpq", [128, 8, 2048], BF16)
                att = sb(st, "cb_att", [128, 4, 512], BF16)
                yf = sb(st, "cb_yf", [128, 2, 512], BF16)
                yc = sb(st, "cb_yc", [128, 2, 512], BF16)
                gT = sb(st, "cb_g", [128, 24, 512], BF16)
                xg = sb(st, "cb_xg", [128, 8, 512], F32)
                sq = sb(st, "cb_sq", [128, 8, 512], F32)
                rstd = sb(st, "cb_rstd", [128, 512], F32)
                u = sb(st, "cb_u", [128, 8, 512], BF16)
                h2 = sb(st, "cb_h2", [128, 8, 512], BF16)
                t1 = sb(st, "cb_t1", [128, 512], F32)
                t2 = sb(st, "cb_t2", [128, 512], F32)
                t3 = sb(st, "cb_t3", [128, 512], F32)
                ob = [sb(st, "cb_ob%d" % i, [128, 512], BF16) for i in range(3)]
                ps = [pst(st, "cb_ps%d" % i, [128, 512], F32) for i in range(6)]
                ps_st = pst(st, "cb_pst", [128, 512], F32)
                for k in range(4):
                    DM(lambda e, k=k: e.dma_start(out=w_oa[:, k, :], in_=P["w_oa"][l][k * 128:(k + 1) * 128, :]), w=["cb_woa"], q="gpsimd")
                for k in range(2):
                    DM(lambda e, k=k: e.dma_start(out=w_ob[:, k, :], in_=P["w_ob"][l][k * 128:(k + 1) * 128, :]), w=["cb_wob"], q="gpsimd")
                    DM(lambda e, k=k: e.dma_start(out=w_oc[:, k, :], in_=P["w_oc"][l][k * 128:(k + 1) * 128, :]), w=["cb_woc"], q="gpsimd")
                for k in range(8):
                    DM(lambda e, k=k: e.dma_start(out=w_out[:, k, :], in_=P["w_out"][l][k * 128:(k + 1) * 128, :]), w=["cb_wout"], q="gpsimd")
                    DM(lambda e, k=k: e.dma_start(out=w_pq[:, k, :], in_=P["w_pq"][l][k * 128:(k + 1) * 128, :]), w=["cb_wpq"], q="gpsimd")
                psi = [0]

                def do_group(gi, t0, n):
                    r_ = 1 if gi == 0 else 0
                    DM(lambda e: e.dma_start(out=att[:, :, :n], in_=ATT[:, :, t0:t0 + n].rearrange("k p t -> p k t")), w=["cb_att"])
                    DM(lambda e: e.dma_start(out=yf[:, :, :n], in_=YFT[:, :, t0:t0 + n].rearrange("k p t -> p k t")), w=["cb_yf"])
                    DM(lambda e: e.dma_start(out=yc[:, :, :n], in_=YCT[:, :, t0:t0 + n].rearrange("k p t -> p k t")), w=["cb_yc"])
                    DM(lambda e: e.dma_start(out=gT[:, :, :n], in_=GT[:, :, t0:t0 + n].rearrange("k p t -> p k t")), w=["cb_g"])
                    DM(lambda e: e.dma_start(out=xg[:, :, :n], in_=XT[:, :, t0:t0 + n].rearrange("k p t -> p k t")), w=["cb_xg"])

                    def do_dch(dc):
                        pa, pb, pc = psi[0] % 6, (psi[0] + 1) % 6, (psi[0] + 2) % 6
                        psi[0] += 3
                        for k in range(4):
                            TE(lambda e, k=k: e.matmul(ps[pa][:, :n], lhsT=w_oa[:, k, dc * 128:(dc + 1) * 128], rhs=att[:, k, :n], start=(k == 0), stop=(k == 3)),
                               r=["cb_woa", "cb_att"], w=["cb_ps%d" % pa])
                        for k in range(2):
                            TE(lambda e, k=k: e.matmul(ps[pb][:, :n], lhsT=w_ob[:, k, dc * 128:(dc + 1) * 128], rhs=yf[:, k, :n], start=(k == 0), stop=(k == 1)),
                               r=["cb_wob", "cb_yf"], w=["cb_ps%d" % pb])
                        for k in range(2):
                            TE(lambda e, k=k: e.matmul(ps[pc][:, :n], lhsT=w_oc[:, k, dc * 128:(dc + 1) * 128], rhs=yc[:, k, :n], start=(k == 0), stop=(k == 1)),
                               r=["cb_woc", "cb_yc"], w=["cb_ps%d" % pc])
                        V(lambda e: e.tensor_tensor(out=t1[:, :n], in0=ps[pa][:, :n], in1=gT[:, dc, :n], op=ALU.mult), r=["cb_ps%d" % pa, "cb_g"], w=["cb_t1"])
                        V(lambda e: e.tensor_tensor(out=t2[:, :n], in0=ps[pb][:, :n], in1=gT[:, 8 + dc, :n], op=ALU.mult), r=["cb_ps%d" % pb, "cb_g"], w=["cb_t2"])
                        V(lambda e: e.tensor_tensor(out=t3[:, :n], in0=ps[pc][:, :n], in1=gT[:, 16 + dc, :n], op=ALU.mult), r=["cb_ps%d" % pc, "cb_g"], w=["cb_t3"])
                        G(lambda e: e.tensor_tensor(out=t1[:, :n], in0=t1[:, :n], in1=t2[:, :n], op=ALU.add), r=["cb_t1", "cb_t2"], w=["cb_t1"])
                        G(lambda e: e.tensor_tensor(out=u[:, dc, :n], in0=t1[:, :n], in1=t3[:, :n], op=ALU.add), r=["cb_t1", "cb_t3"], w=[("cb_u", dc)])
                    for dc in range(8):
                        do_dch(dc)
                    ukeys = [("cb_u", dc) for dc in range(8)]

                    def do_out(dc):
                        pm = psi[0] % 6
                        psi[0] += 1
                        for k in range(8):
                            TE(lambda e, k=k: e.matmul(ps[pm][:, :n], lhsT=w_out[:, k, dc * 128:(dc + 1) * 128], rhs=u[:, k, :n], start=(k == 0), stop=(k == 7)),
                               r=ukeys + ["cb_wout"], w=["cb_ps%d" % pm])
                        V(lambda e: e.scalar_tensor_tensor(out=xg[:, dc, :n], in0=ps[pm][:, :n], scalar=gt1[:, dc, r_:r_ + 1], in1=xg[:, dc, :n], op0=ALU.mult, op1=ALU.add),
                          r=["cb_ps%d" % pm, "gt1", "cb_xg"], w=["cb_xg"])
                    for dc in range(8):
                        do_out(dc)
                    DM(lambda e: e.dma_start(out=XT[:, :, t0:t0 + n].rearrange("k p t -> p k t"), in_=xg[:, :, :n]), r=["cb_xg"], w=[("XTw", gi)])
                    rms_stats(xg, 8, n, D, sq, ps_st, rstd, ["cb_xg"], "cb_sq", "cb_pst", "cb_rstd")

                    def do_h2(k):
                        V(lambda e: e.tensor_tensor(out=sq[:, k, :n], in0=xg[:, k, :n], in1=rstd[:, :n], op=ALU.mult), r=["cb_xg", "cb_rstd"], w=["cb_sq"])
                        V(lambda e: e.tensor_scalar(h2[:, k, :n], sq[:, k, :n], gm2[:, k, r_:r_ + 1], sh2[:, k, r_:r_ + 1], op0=ALU.mult, op1=ALU.add),
                          r=["cb_sq", "gm2", "sh2"], w=["cb_h2"])
                    for k in range(8):
                        do_h2(k)
                    DM(lambda e: e.dma_start(out=H2T[:, :, t0:t0 + n].rearrange("k p t -> p k t"), in_=h2[:, :, :n]), r=["cb_h2"], w=[("H2T", gi)])

                    def do_pq(nc_):
                        pm = psi[0] % 6
                        psi[0] += 1
                        o = nc_ % 3
                        for k in range(8):
                            TE(lambda e, k=k: e.matmul(ps[pm][:, :n], lhsT=w_pq[:, k, nc_ * 128:(nc_ + 1) * 128], rhs=h2[:, k, :n], start=(k == 0), stop=(k == 7)),
                               r=["cb_h2", "cb_wpq"], w=["cb_ps%d" % pm])
                        evac(ob[o][:, :n], ps[pm][:, :n], r=["cb_ps%d" % pm], w=["cb_ob%d" % o])
                        DM(lambda e: e.dma_start(out=PQT[nc_, :, t0:t0 + n], in_=ob[o][:, :n]), r=["cb_ob%d" % o], w=[("PQT", S.dma_next)])
                    for nc_ in range(16):
                        do_pq(nc_)
                for gi, (t0, n) in enumerate(GROUPS):
                    if last and gi == 0:
                        continue
                    do_group(gi, t0, n)
            S.barrier()

        def phase_cast_tables():
            with contextlib.ExitStack() as st:
                tb = [sb(st, "ct_t%d" % i, [128, 8, D], BF16) for i in range(4)]
                cnt = [0]

                def do_chunk(l, src, o, c):
                    cnt[0] += 1
                    b = cnt[0] % 4
                    DM(lambda e: e.dma_start(out=tb[b][:], in_=src[c * 1024:(c + 1) * 1024, :].rearrange("(p r) d -> p r d", r=8)),
                       w=["ct_t%d" % b], q="gpsimd")
                    DM(lambda e: e.dma_start(out=DUB[l][c * 1024:(c + 1) * 1024, o:o + D].rearrange("(p r) d -> p r d", r=8), in_=tb[b][:]),
                       r=["ct_t%d" % b], w=[("cast", S.dma_next)])
                for l in range(DEPTH):
                    for (src, o) in ((P["peer_down%d" % l], 0), (P["peer_up%d" % l], D)):
                        for c in range(16):
                            do_chunk(l, src, o, c)
            S.barrier()

        def phase_peer(l, last):
            with contextlib.ExitStack() as st:
                keysT = sb(st, "pe_keysT", [128, 16, 128], BF16)
                iotf = sb(st, "pe_iotf", [128, 16], F32)
                pq = sb(st, "pe_pq", [128, 16, 128], BF16)
                s_sb = sb(st, "pe_s", [128, 16, 128], F32)
                s2 = sb(st, "pe_s2", [128, 16, 128], F32)
                kraw = s2
                sv = sb(st, "pe_sv", [128, 16, 16], F32)
                si = sb(st, "pe_si", [128, 16, 16], U32)
                sif = sb(st, "pe_sif", [128, 16, 16], F32)
                cand = sb(st, "pe_cand", [128, 8, 256], F32)
                cand2 = sb(st, "pe_cand2", [128, 8, 256], F32)
                eq = cand2[:].rearrange("p h (a b) -> p h a b", b=16)
                ts = sb(st, "pe_ts", [128, 8, 16], F32)
                pos = sb(st, "pe_pos", [128, 8, 16], U32)
                pa_i = sb(st, "pe_pai", [128, 8, 16], U32)
                pb_i = sb(st, "pe_pbi", [128, 8, 16], U32)
                pa_f = sb(st, "pe_paf", [128, 8, 16], F32)
                pb_f = sb(st, "pe_pbf", [128, 8, 16], F32)
                If = sb(st, "pe_If", [128, 8, 16], F32)
                Jf = sb(st, "pe_Jf", [128, 8, 16], F32)
                ef = sb(st, "pe_ef", [128, 8, 16], F32)
                gsum = sb(st, "pe_gsum", [128, 8], F32)
                h2T = sb(st, "pe_h2T", [128, 8, 128], BF16)
                eidx_ = [sb(st, "pe_eidx%d" % i, [128, 128], U32) for i in range(2)]
                gate_ = [sb(st, "pe_gate%d" % i, [128, 8, 16], F32) for i in range(2)]
                h2tok_ = [sb(st, "pe_h2tok%d" % i, [128, D], BF16) for i in range(2)]
                xt_ = [sb(st, "pe_xt%d" % i, [128, 8, 128], F32) for i in range(2)]
                NR = 24
                SG = 8
                rows = [sb(st, "pe_rows%d" % i, [128, 2 * D], BF16) for i in range(NR)]
                junk = [sb(st, "pe_junk%d" % i, [128, D], BF16) for i in range(3)]
                junk2 = sb(st, "pe_junk2", [128, D], BF16)
                diag8 = [sb(st, "pe_diag%d" % i, [128, SG, 128], BF16) for i in range(3)]
                a_sb = sb(st, "pe_a", [128, 128], F32)
                g1 = sb(st, "pe_g1", [128, 128], F32)
                g2 = sb(st, "pe_g2", [128, 128], F32)
                act = sb(st, "pe_act", [128, 128], F32)
                acc = sb(st, "pe_acc", [128, D], F32)
                ps_sc = [pst(st, "pe_psc%d" % i, [128, 4, 128], F32) for i in range(4)]
                ps_h2 = pst(st, "pe_ph2", [128, 8, 128], BF16)
                ps_acc = [pst(st, "pe_pacc%d" % i, [128, 512], F32) for i in range(2)]

                G(lambda e: e.iota(iotf[:], [[1, 16]], base=0, channel_multiplier=0, allow_small_or_imprecise_dtypes=True), w=["pe_iotf"])
                DM(lambda e: e.dma_start(out=kraw[:], in_=P["peer_keys"][l].rearrange("h p k d -> k (h p) d")), w=["pe_s2"])
                for hp in range(16):
                    TE(lambda e, hp=hp: e.transpose(ps_sc[hp // 4][:, hp % 4, :], kraw[:, hp, :], ident_f[:]), r=["pe_s2", "ident_f"], w=["pe_psc%d" % (hp // 4)])
                for b4 in range(4):
                    evac(keysT[:, b4 * 4:(b4 + 1) * 4, :], ps_sc[b4][:], r=["pe_psc%d" % b4], w=["pe_keysT"])
                rcnt = [0]
                gcnt = [0]

                def prep_tile(ti):
                    c0 = ti * 128
                    tb = ti % 2
                    eidx, gate, h2tok, xt = eidx_[tb], gate_[tb], h2tok_[tb], xt_[tb]
                    kE, kG, kH, kX = "pe_eidx%d" % tb, "pe_gate%d" % tb, "pe_h2tok%d" % tb, "pe_xt%d" % tb
                    DM(lambda e: e.dma_start(out=pq[:], in_=PQT[:, :, c0:c0 + 128].rearrange("n p t -> p n t")), w=["pe_pq"])
                    DM(lambda e: e.dma_start(out=h2T[:], in_=H2T[:, :, c0:c0 + 128].rearrange("k p t -> p k t")), w=["pe_h2T"])
                    DM(lambda e: e.dma_start(out=xt[:], in_=XT[:, :, c0:c0 + 128].rearrange("k p t -> p k t")), w=[kX])
                    for hp in range(16):
                        TE(lambda e, hp=hp: e.matmul(ps_sc[hp // 4][:, hp % 4, :], lhsT=pq[:, hp, :], rhs=keysT[:, hp, :], start=True, stop=True),
                           r=["pe_pq", "pe_keysT"], w=["pe_psc%d" % (hp // 4)])
                    for b4 in range(4):
                        evac(s_sb[:, b4 * 4:(b4 + 1) * 4, :], ps_sc[b4][:], r=["pe_psc%d" % b4], w=["pe_s"])
                    for k in range(8):
                        TE(lambda e, k=k: e.transpose(ps_h2[:, k, :], h2T[:, k, :], ident_b[:]), r=["pe_h2T", "ident_b"], w=["pe_ph2"])
                    A(lambda e: e.activation(out=h2tok[:], in_=ps_h2[:].rearrange("p k d -> p (k d)"), func=AF.Copy), r=["pe_ph2"], w=[kH])
                    for hp in range(16):
                        V(lambda e, hp=hp: e.max(out=sv[:, hp, 0:8], in_=s_sb[:, hp, :]), r=["pe_s"], w=["pe_sv"])
                        V(lambda e, hp=hp: e.max_index(out=si[:, hp, 0:8], in_max=sv[:, hp, 0:8], in_values=s_sb[:, hp, :]), r=["pe_s", "pe_sv"], w=["pe_si"])
                        V(lambda e, hp=hp: e.match_replace(out=s2[:, hp, :], in_to_replace=sv[:, hp, 0:8], in_values=s_sb[:, hp, :], imm_value=-1e30),
                          r=["pe_s", "pe_sv"], w=["pe_s2"])
                        V(lambda e, hp=hp: e.max(out=sv[:, hp, 8:16], in_=s2[:, hp, :]), r=["pe_s2"], w=["pe_sv"])
                        V(lambda e, hp=hp: e.max_index(out=si[:, hp, 8:16], in_max=sv[:, hp, 8:16], in_values=s2[:, hp, :]), r=["pe_s2", "pe_sv"], w=["pe_si"])
                    V(lambda e: e.tensor_copy(sif[:], si[:]), r=["pe_si"], w=["pe_sif"])
                    svv = sv[:].rearrange("p (h q) a -> p h q a", q=2)
                    sfv = sif[:].rearrange("p (h q) a -> p h q a", q=2)
                    candv = cand[:].rearrange("p h (a b) -> p h a b", b=16)
                    V(lambda e: e.tensor_tensor(out=candv, in0=svv[:, :, 0, :].unsqueeze(3).to_broadcast([128, 8, 16, 16]),
                                                in1=svv[:, :, 1, :].unsqueeze(2).to_broadcast([128, 8, 16, 16]), op=ALU.add), r=["pe_sv"], w=["pe_cand"])
                    for h in range(8):
                        V(lambda e, h=h: e.max(out=ts[:, h, 0:8], in_=cand[:, h, :]), r=["pe_cand"], w=["pe_ts"])
                        V(lambda e, h=h: e.max_index(out=pos[:, h, 0:8], in_max=ts[:, h, 0:8], in_values=cand[:, h, :]), r=["pe_cand", "pe_ts"], w=["pe_pos"])
                        V(lambda e, h=h: e.match_replace(out=cand2[:, h, :], in_to_replace=ts[:, h, 0:8], in_values=cand[:, h, :], imm_value=-1e30),
                          r=["pe_cand", "pe_ts"], w=["pe_cand2"])
                        V(lambda e, h=h: e.max(out=ts[:, h, 8:16], in_=cand2[:, h, :]), r=["pe_cand2"], w=["pe_ts"])
                        V(lambda e, h=h: e.max_index(out=pos[:, h, 8:16], in_max=ts[:, h, 8:16], in_values=cand2[:, h, :]), r=["pe_cand2", "pe_ts"], w=["pe_pos"])
                    V(lambda e: e.tensor_single_scalar(pa_i[:], pos[:], 4, op=ALU.logical_shift_right), r=["pe_pos"], w=["pe_pai"])
                    V(lambda e: e.tensor_single_scalar(pb_i[:], pos[:], 15, op=ALU.bitwise_and), r=["pe_pos"], w=["pe_pbi"])
                    V(lambda e: e.tensor_copy(pa_f[:], pa_i[:]), r=["pe_pai"], w=["pe_paf"])
                    V(lambda e: e.tensor_copy(pb_f[:], pb_i[:]), r=["pe_pbi"], w=["pe_pbf"])
                    iob = iotf[:].unsqueeze(1).unsqueeze(1).to_broadcast([128, 8, 16, 16])
                    for (pf, half, dst, kd) in ((pa_f, 0, If, "pe_If"), (pb_f, 1, Jf, "pe_Jf")):
                        V(lambda e, pf=pf: e.tensor_tensor(out=eq, in0=iob, in1=pf[:].unsqueeze(3).to_broadcast([128, 8, 16, 16]), op=ALU.is_equal),
                          r=["pe_iotf", "pe_paf", "pe_pbf"], w=["pe_cand2"])
                        V(lambda e, half=half: e.tensor_tensor(out=eq, in0=eq, in1=sfv[:, :, half, :].unsqueeze(2).to_broadcast([128, 8, 16, 16]), op=ALU.mult),
                          r=["pe_cand2", "pe_sif"], w=["pe_cand2"])
                        V(lambda e, dst=dst: e.tensor_reduce(out=dst[:], in_=eq, axis=AX.X, op=ALU.add), r=["pe_cand2"], w=[kd])
                    V(lambda e: e.scalar_tensor_tensor(out=ef[:], in0=If[:], scalar=128.0, in1=Jf[:], op0=ALU.mult, op1=ALU.add), r=["pe_If", "pe_Jf"], w=["pe_ef"])
                    V(lambda e: e.tensor_copy(eidx[:], ef[:].rearrange("p h k -> p (h k)")), r=["pe_ef"], w=[kE])
                    V(lambda e: e.tensor_tensor(out=gate[:], in0=ts[:], in1=ts[:, :, 0:1].to_broadcast([128, 8, 16]), op=ALU.subtract), r=["pe_ts"], w=[kG])
                    A(lambda e: e.activation(out=gate[:], in_=gate[:], func=AF.Exp), r=[kG], w=[kG])
                    V(lambda e: e.tensor_reduce(out=gsum[:], in_=gate[:], axis=AX.X, op=ALU.add), r=[kG], w=["pe_gsum"])
                    V(lambda e: e.reciprocal(gsum[:], gsum[:]), r=["pe_gsum"], w=["pe_gsum"])
                    V(lambda e: e.tensor_tensor(out=gate[:], in0=gate[:], in1=gsum[:].unsqueeze(2).to_broadcast([128, 8, 16]), op=ALU.mult), r=[kG, "pe_gsum"], w=[kG])

                def run_tile(ti):
                    c0 = ti * 128
                    r_ = 1 if ti < 2 else 0
                    tb = ti % 2
                    eidx, gate, h2tok, xt = eidx_[tb], gate_[tb], h2tok_[tb], xt_[tb]
                    kE, kG, kH, kX = "pe_eidx%d" % tb, "pe_gate%d" % tb, "pe_h2tok%d" % tb, "pe_xt%d" % tb
                    gatev = gate[:].rearrange("p h k -> p (h k)")

                    def do_slot_group(g0):
                        bufs = []
                        gcnt[0] += 1
                        dg = gcnt[0] % 3
                        for sl in range(g0, g0 + SG):
                            rcnt[0] += 1
                            b = rcnt[0] % NR
                            jb = rcnt[0] % 3
                            bufs.append(b)
                            DM(lambda e, b=b, sl=sl: e.indirect_dma_start(out=rows[b][:, :], out_offset=None, in_=DUB[l],
                                                                          in_offset=bass.IndirectOffsetOnAxis(ap=eidx[:, sl:sl + 1], axis=0)),
                               r=[kE], w=["pe_rows%d" % b], q="gpsimd")
                            V(lambda e, b=b, jb=jb: e.tensor_tensor(out=junk[jb][:], in0=rows[b][:, 0:D], in1=h2tok[:], op=ALU.mult),
                              r=["pe_rows%d" % b, kH], w=["pe_junk%d" % jb])
                            A(lambda e, jb=jb, sl=sl: e.activation(out=junk2[:], in_=junk[jb][:], func=AF.Copy, accum_out=a_sb[:, sl:sl + 1]),
                              r=["pe_junk%d" % jb], w=[("pe_a", g0), "pe_junk2"])
                        gs = slice(g0, g0 + SG)
                        V(lambda e: e.tensor_tensor(out=g1[:, gs], in0=a_sb[:, gs], in1=a_sb[:, gs], op=ALU.mult), r=[("pe_a", g0)], w=[("pe_g1", g0)])
                        V(lambda e: e.tensor_scalar(g1[:, gs], g1[:, gs], 0.044715, 1.0, op0=ALU.mult, op1=ALU.add), r=[("pe_g1", g0)], w=[("pe_g1", g0)])
                        V(lambda e: e.tensor_tensor(out=g1[:, gs], in0=g1[:, gs], in1=a_sb[:, gs], op=ALU.mult), r=[("pe_g1", g0), ("pe_a", g0)], w=[("pe_g1", g0)])
                        A(lambda e: e.activation(out=g2[:, gs], in_=g1[:, gs], func=AF.Sigmoid, scale=1.5957691216057308), r=[("pe_g1", g0)], w=[("pe_g2", g0)])
                        V(lambda e: e.tensor_tensor(out=g2[:, gs], in0=g2[:, gs], in1=a_sb[:, gs], op=ALU.mult), r=[("pe_g2", g0), ("pe_a", g0)], w=[("pe_g2", g0)])
                        V(lambda e: e.tensor_tensor(out=act[:, gs], in0=g2[:, gs], in1=gatev[:, gs], op=ALU.mult), r=[("pe_g2", g0), kG], w=[("pe_act", g0)])
                        V(lambda e: e.tensor_tensor(out=diag8[dg][:], in0=ident_b[:].unsqueeze(1).to_broadcast([128, SG, 128]),
                                                    in1=act[:, gs].unsqueeze(2).to_broadcast([128, SG, 128]), op=ALU.mult),
                          r=["ident_b", ("pe_act", g0)], w=["pe_diag%d" % dg])
                        for i_, sl in enumerate(range(g0, g0 + SG)):
                            b = bufs[i_]
                            for hf in range(2):
                                TE(lambda e, hf=hf, b=b, sl=sl, i_=i_: e.matmul(ps_acc[hf][:, :], lhsT=diag8[dg][:, i_, :], rhs=rows[b][:, D + hf * 512:D + (hf + 1) * 512],
                                                                                start=(sl == 0), stop=(sl == 127)),
                                   r=["pe_diag%d" % dg, "pe_rows%d" % b], w=["pe_pacc%d" % hf])
                    for g0 in range(0, 128, SG):
                        do_slot_group(g0)
                    for hf in range(2):
                        evac(acc[:, hf * 512:(hf + 1) * 512], ps_acc[hf][:, :], r=["pe_pacc%d" % hf], w=[("pe_acc", hf)])
                    for k in range(8):
                        TE(lambda e, k=k: e.transpose(ps_sc[k // 4][:, k % 4, :], acc[:, k * 128:(k + 1) * 128], ident_f[:]), r=[("pe_acc", k // 4), "ident_f"], w=["pe_psc%d" % (k // 4)])
                    for k in range(8):
                        V(lambda e, k=k: e.scalar_tensor_tensor(out=xt[:, k, :], in0=ps_sc[k // 4][:, k % 4, :], scalar=gt2[:, k, r_:r_ + 1], in1=xt[:, k, :],
                                                               op0=ALU.mult, op1=ALU.add),
                          r=["pe_psc%d" % (k // 4), "gt2", kX], w=[kX])
                    DM(lambda e: e.dma_start(out=XT[:, :, c0:c0 + 128].rearrange("k p t -> p k t"), in_=xt[:]), r=[kX], w=[("XTw", ti)])

                tiles = [ti for ti in range(2 if last else 0, T // 128) if peer_tiles is None or ti in peer_tiles]
                if tiles:
                    prep_tile(tiles[0])
                for i, ti in enumerate(tiles):
                    if i + 1 < len(tiles):
                        prep_tile(tiles[i + 1])
                    run_tile(ti)
            S.barrier()

        def phase_final():
            with contextlib.ExitStack() as st:
                fg = sb(st, "fn_g", [128, 8], F32)
                xg = sb(st, "fn_xg", [128, 8, 512], F32)
                sq = sb(st, "fn_sq", [128, 8, 512], F32)
                rstd = sb(st, "fn_rstd", [128, 512], F32)
                ot = [sb(st, "fn_o%d" % i, [128, D], F32) for i in range(2)]
                ps_st = pst(st, "fn_pst", [128, 512], F32)
                ps = [pst(st, "fn_ps%d" % i, [128, 4, 128], F32) for i in range(4)]
                DM(lambda e: e.dma_start(out=fg[:], in_=P["final_g"].rearrange("(n p) -> p n", p=128), allow_slow_non_contiguous=True), w=["fn_g"])
                cnt = [0]

                def do_group(gi, t0, n):
                    DM(lambda e: e.dma_start(out=xg[:, :, :n], in_=XT[:, :, t0:t0 + n].rearrange("k p t -> p k t")), w=["fn_xg"])
                    rms_stats(xg, 8, n, D, sq, ps_st, rstd, ["fn_xg"], "fn_sq", "fn_pst", "fn_rstd")

                    def do_k(k):
                        V(lambda e: e.tensor_tensor(out=sq[:, k, :n], in0=xg[:, k, :n], in1=rstd[:, :n], op=ALU.mult), r=["fn_xg", "fn_rstd"], w=["fn_sq"])
                        V(lambda e: e.tensor_scalar(xg[:, k, :n], sq[:, k, :n], fg[:, k:k + 1], None, op0=ALU.mult), r=["fn_sq", "fn_g"], w=["fn_xg"])
                    for k in range(8):
                        do_k(k)

                    def do_j(j):
                        cnt[0] += 1
                        ob_ = cnt[0] % 2
                        for half in range(2):
                            pi = (cnt[0] * 2 + half) % 4
                            for kk in range(4):
                                k = half * 4 + kk
                                TE(lambda e, k=k, kk=kk, pi=pi: e.transpose(ps[pi][:, kk, :], xg[:, k, j * 128:(j + 1) * 128], ident_f[:]),
                                   r=["fn_xg", "ident_f"], w=["fn_ps%d" % pi])
                            evac(ot[ob_][:, half * 512:(half + 1) * 512], ps[pi][:].rearrange("p a b -> p (a b)"), r=["fn_ps%d" % pi], w=[("fn_o", ob_, half)])
                        row0 = t0 - CTX + j * 128
                        DM(lambda e: e.dma_start(out=out[row0:row0 + 128, :], in_=ot[ob_][:]), r=[("fn_o", ob_, 0), ("fn_o", ob_, 1)], w=[("out", row0)])
                    for j in range(n // 128):
                        do_j(j)
                for gi, (t0, n) in enumerate(GROUPS):
                    if gi == 0:
                        continue
                    do_group(gi, t0, n)
            S.barrier()

        phase_load_x()
        if stop_after is None or stop_after[0] in ("peer", "castonly"):
            phase_cast_tables()
        for l in range(DEPTH):
            if stop_after is not None and stop_after[0] == "castonly":
                break
            last = l == DEPTH - 1
            phase_mod(l)
            phase_proj(l, last)
            if stop_after == ("proj", l):
                break
            phase_attn(l, last)
            phase_fourier(l, last)
            phase_pool(l, last)
            if stop_after == ("mix", l):
                break
            phase_combine(l, last)
            if stop_after == ("combine", l):
                break
            phase_peer(l, last)
            if stop_after == ("peer", l):
                break
        if stop_after is None:
            phase_final()
        S.emit()
    return nc, S


def make_in_maps(inputs, cores):
    consts = _const_tables()
    maps = []
    for b in cores:
        m = {"x": np.ascontiguousarray(inputs["x"][b]), "ctx": np.ascontiguousarray(inputs["ctx"][b]),
             "cvec": np.ascontiguousarray(np.stack([inputs["c"][b], inputs["c_ctx"]], axis=0))}
        for k in PARAM_SHAPES:
            if k.startswith("peer_down") or k.startswith("peer_up"):
                m[k] = np.ascontiguousarray(inputs[k[:-1]][int(k[-1])])
            else:
                m[k] = np.ascontiguousarray(inputs[k])
        m.update(consts)
        maps.append(m)
    return maps


def kernel(**inputs):
    inputs = {k: np.asarray(v) for k, v in inputs.items()}
    nc, _ = build()
    cores = list(range(8))
    res = run_bass_kernel_spmd(nc, make_in_maps(inputs, cores), core_ids=cores)
    return np.stack([res.results[i]["out"] for i in cores], axis=0).astype(np.float32)
```

```python
import contextlib
import math
import numpy as np
import ml_dtypes
import concourse.bass as bass
import concourse.mybir as mybir
from concourse.bass_utils import run_bass_kernel_spmd

F32 = mybir.dt.float32
BF16 = mybir.dt.bfloat16
U32 = mybir.dt.uint32
I32 = mybir.dt.int32
AF = mybir.ActivationFunctionType
ALU = mybir.AluOpType
AX = mybir.AxisListType

ENGS = ("sync", "scalar", "vector", "gpsimd", "tensor")
N_DMA_SEMS = 90

D = 1024
SEQ = 4096
CTX = 256
T = SEQ + CTX
DEPTH = 2
NH = 8
EPS = 1e-6
ATT_SCALE = 96 ** -0.5
IN_W = 4000
NEXP = 16384


class Sched:
    def __init__(self, nc):
        self.nc = nc
        self.ops = {e: [] for e in ENGS}
        self.count = {e: 0 for e in ENGS}
        self.known = {e: {} for e in ENGS}
        self.last_w = {}
        self.readers = {}
        self.dma_next = 0
        self.dma_target = [0] * N_DMA_SEMS
        self.pending = {e: [] for e in ENGS}
        self.n_instr = 0

    def _need(self, eng, tok, waits):
        if tok is None:
            return
        kind, src, val = tok
        if kind == "e" and src == eng and eng == "tensor":
            return
        k = (kind, src)
        if self.known[eng].get(k, 0) >= val:
            return
        self.known[eng][k] = val
        waits.append(tok)

    def _deps(self, eng, r, w):
        waits = []
        for tok in self.pending[eng]:
            self._need(eng, tok, waits)
        self.pending[eng] = []
        for key in r:
            self._need(eng, self.last_w.get(key), waits)
        for key in w:
            self._need(eng, self.last_w.get(key), waits)
            for tok in self.readers.get(key, ()):
                self._need(eng, tok, waits)
        best = {}
        for kind, src, val in waits:
            k = (kind, src)
            if best.get(k, 0) < val:
                best[k] = val
        return [(k[0], k[1], v) for k, v in best.items()]

    def _commit(self, tok, r, w):
        for key in r:
            self.readers.setdefault(key, []).append(tok)
        for key in w:
            self.last_w[key] = tok
            self.readers[key] = []

    def op(self, eng, fn, r=(), w=()):
        waits = self._deps(eng, r, w)
        self.count[eng] += 1
        tok = ("e", eng, self.count[eng])
        self._commit(tok, r, w)
        self.ops[eng].append(("op", fn, waits, None))
        self.n_instr += 1 + len(waits)
        return tok

    def dma(self, eng, fn, r=(), w=()):
        s = self.dma_next % N_DMA_SEMS
        self.dma_next += 1
        waits = self._deps(eng, r, w)
        prev = self.dma_target[s]
        if prev > 0:
            k = ("d", s)
            if self.known[eng].get(k, 0) < prev:
                self.known[eng][k] = prev
                waits.append(("d", s, prev))
        self.dma_target[s] = prev + 16
        tok = ("d", s, prev + 16)
        self._commit(tok, r, w)
        self.ops[eng].append(("dma", fn, waits, s))
        self.n_instr += 1 + len(waits)
        return tok

    def barrier(self):
        waits = []
        for e in ENGS:
            if e != "sync" and self.count[e] > 0:
                self._need("sync", ("e", e, self.count[e]), waits)
        for s in range(N_DMA_SEMS):
            if self.dma_target[s] > 0:
                self._need("sync", ("d", s, self.dma_target[s]), waits)
        self.count["sync"] += 1
        tok = ("e", "sync", self.count["sync"])
        self.ops["sync"].append(("op", lambda e: e.nop(), waits, None))
        self.n_instr += 1 + len(waits)
        for e in ENGS:
            if e != "sync":
                self.pending[e].append(tok)
                for e2 in ENGS:
                    self.known[e][("e", e2)] = max(self.known[e].get(("e", e2), 0),
                                                   self.count[e2] if e2 != "sync" else 0)
                for s in range(N_DMA_SEMS):
                    self.known[e][("d", s)] = self.dma_target[s]
                self.known[e].pop(("e", "sync"), None)
        self.last_w = {}
        self.readers = {}

    def emit(self):
        nc = self.nc
        with contextlib.ExitStack() as st:
            esem = {e: st.enter_context(nc.semaphore("c_" + e)) for e in ENGS}
            dsem = [st.enter_context(nc.semaphore("d%d" % i)) for i in range(N_DMA_SEMS)]
            block = st.enter_context(nc.Block())

            def mk(ename):
                def body(eng):
                    for kind, fn, waits, s in self.ops[ename]:
                        for wk, src, val in waits:
                            eng.wait_ge(esem[src] if wk == "e" else dsem[src], val)
                        ins = fn(eng)
                        if kind == "op":
                            ins.then_inc(esem[ename], 1)
                        else:
                            ins.then_inc(dsem[s], 16)
                    if ename == "sync":
                        for i in range(N_DMA_SEMS):
                            if self.dma_target[i] > 0:
                                eng.wait_ge(dsem[i], self.dma_target[i])
                        for e2 in ENGS:
                            if e2 != "sync" and self.count[e2] > 0:
                                eng.wait_ge(esem[e2], self.count[e2])
                return body

            block.sync(mk("sync"))
            block.scalar(mk("scalar"))
            block.vector(mk("vector"))
            block.gpsimd(mk("gpsimd"))
            block.tensor(mk("tensor"))


def _const_tables():
    c = {}
    half = 16
    inv_freq = (10000.0 ** (-np.arange(0, half, 2, dtype=np.float32) / half)).astype(np.float32)
    t = np.arange(SEQ)
    r = (t // 64).astype(np.float32)
    cl = (t % 64).astype(np.float32)
    ang_r = (r[None, :] * inv_freq[:, None]).astype(np.float32)
    ang_c = (cl[None, :] * inv_freq[:, None]).astype(np.float32)
    ang = np.concatenate([ang_r, ang_r, ang_c, ang_c], axis=0)
    c["ropecos"] = np.cos(ang).astype(np.float32)
    c["ropesin"] = np.sin(ang).astype(np.float32)
    def dft(n):
        k = np.arange(n, dtype=np.int64)
        kk = (k[:, None] * k[None, :]) % n
        a = 2.0 * np.pi * kk.astype(np.float64) / n
        return np.cos(a), np.sin(a)
    cc, sc = dft(256)
    c["ccs"] = np.concatenate([cc, -sc], axis=1).astype(ml_dtypes.bfloat16)
    c["cl256"] = cc.astype(ml_dtypes.bfloat16)
    c["sl256"] = sc.astype(ml_dtypes.bfloat16)
    cL, sL = dft(SEQ)
    c["cl4096"] = cL.astype(ml_dtypes.bfloat16)
    c["sl4096"] = sL.astype(ml_dtypes.bfloat16)
    for L, name in ((SEQ, "icnt4096"), (CTX, "icnt256")):
        tt = np.arange(L)
        tab = np.zeros((2, 128, L), np.float32)
        for gi, w in enumerate((2, 4, 8, 16)):
            lo = np.clip(tt - w // 2, 0, L)
            hi = np.clip(tt - w // 2 + w, 0, L)
            tab[gi // 2, (gi % 2) * 64:(gi % 2) * 64 + 64, :] = (1.0 / (hi - lo).astype(np.float32))[None, :]
        c[name] = tab
    return c


CONST_SHAPES = {
    "ropecos": ([32, SEQ], F32), "ropesin": ([32, SEQ], F32),
    "ccs": ([256, 512], BF16), "cl256": ([256, 256], BF16), "sl256": ([256, 256], BF16),
    "cl4096": ([SEQ, SEQ], BF16), "sl4096": ([SEQ, SEQ], BF16),
    "icnt4096": ([2, 128, SEQ], F32), "icnt256": ([2, 128, CTX], F32),
}

PARAM_SHAPES = {
    "w_mod": [DEPTH, D, 6 * D], "b_mod": [DEPTH, 6 * D], "norm1_g": [DEPTH, D], "norm2_g": [DEPTH, D],
    "w_in": [DEPTH, D, IN_W], "b_gate": [DEPTH, 3 * D], "q_norm_g": [DEPTH, 256], "w_uq": [DEPTH, 256, 768],
    "kv_norm_g": [DEPTH, 128], "w_ukv": [DEPTH, 128, 1024], "w_oa": [DEPTH, 512, D], "w_ob": [DEPTH, 256, D],
    "w_grp": [DEPTH, 4, 64, 64], "pool_scale": [DEPTH, 256], "w_oc": [DEPTH, 256, D], "w_out": [DEPTH, D, D],
    "w_pq": [DEPTH, D, 2048], "peer_keys": [DEPTH, 8, 2, 128, 128], "peer_down0": [NEXP, D], "peer_down1": [NEXP, D],
    "peer_up0": [NEXP, D], "peer_up1": [NEXP, D], "final_g": [D],
}

GROUPS = [(0, CTX)] + [(CTX + 512 * g, 512) for g in range(SEQ // 512)]


def build(stop_after=None, dbg=(), peer_tiles=None):
    nc = bass.Bass("TRN2", target_bir_lowering=False)
    S = Sched(nc)

    def dram_in(name, shape, dt=F32):
        return nc.dram_tensor(name, shape, dt, kind="ExternalInput").ap()

    def dram_scratch(name, shape, dt):
        kind = "ExternalOutput" if name in dbg else "Internal"
        return nc.dram_tensor(name, shape, dt, kind=kind).ap()

    xin = dram_in("x", [SEQ, D])
    ctxin = dram_in("ctx", [CTX, D])
    cvec = dram_in("cvec", [2, D])
    P = {k: dram_in(k, v) for k, v in PARAM_SHAPES.items()}
    C = {k: dram_in(k, v[0], v[1]) for k, v in CONST_SHAPES.items()}
    out = nc.dram_tensor("out", [SEQ, D], F32, kind="ExternalOutput").ap()

    XT = dram_scratch("XT", [8, 128, T], F32)
    GT = dram_scratch("GT", [24, 128, T], BF16)
    QT = dram_scratch("QT", [NH, 96, T], BF16)
    KT = dram_scratch("KT", [NH, 96, T], BF16)
    VV = dram_scratch("VV", [T, 512], BF16)
    ZFT = dram_scratch("ZFT", [2, 128, T], BF16)
    ZPT = dram_scratch("ZPT", [2, 128, T], F32)
    ATT = dram_scratch("ATT", [4, 128, T], BF16)
    YFT = dram_scratch("YFT", [2, 128, T], BF16)
    YCT = dram_scratch("YCT", [2, 128, T], BF16)
    H2T = dram_scratch("H2T", [8, 128, T], BF16)
    PQT = dram_scratch("PQT", [16, 128, T], BF16)
    DUB = [dram_scratch("DUB%d" % i, [NEXP, 2 * D], BF16) for i in range(DEPTH)]

    V = lambda fn, r=(), w=(): S.op("vector", fn, r, w)
    A = lambda fn, r=(), w=(): S.op("scalar", fn, r, w)
    G = lambda fn, r=(), w=(): S.op("gpsimd", fn, r, w)
    TE = lambda fn, r=(), w=(): S.op("tensor", fn, r, w)
    DM = lambda fn, r=(), w=(), q="sync": S.dma(q, fn, r, w)

    rr = [0]

    def evac(out_ap, in_ap, r, w, scale=None):
        rr[0] += 1
        if rr[0] % 2 == 0:
            if scale is None:
                V(lambda e: e.tensor_copy(out_ap, in_ap), r, w)
            else:
                V(lambda e: e.tensor_single_scalar(out_ap, in_ap, float(scale), op=ALU.mult), r, w)
        else:
            A(lambda e: e.activation(out=out_ap, in_=in_ap, func=AF.Copy,
                                     scale=1.0 if scale is None else float(scale)), r, w)

    with contextlib.ExitStack() as top:
        uniq = [0]

        def sb(st, name, shape, dt):
            uniq[0] += 1
            return st.enter_context(nc.sbuf_tensor("%s_%d" % (name, uniq[0]), shape, dt))

        def pst(st, name, shape, dt):
            uniq[0] += 1
            return st.enter_context(nc.psum_tensor("%s_%d" % (name, uniq[0]), shape, dt))

        ident_f = sb(top, "ident_f", [128, 128], F32)
        ident_b = sb(top, "ident_b", [128, 128], BF16)
        ones_f = sb(top, "ones_f", [128, 128], F32)
        ones_b = sb(top, "ones_b", [128, 128], BF16)
        iot = sb(top, "iot", [128, 128], F32)
        gm1 = sb(top, "gm1", [128, 8, 2], F32)
        sh1 = sb(top, "sh1", [128, 8, 2], F32)
        gt1 = sb(top, "gt1", [128, 8, 2], F32)
        gm2 = sb(top, "gm2", [128, 8, 2], F32)
        sh2 = sb(top, "sh2", [128, 8, 2], F32)
        gt2 = sb(top, "gt2", [128, 8, 2], F32)
        epsb = sb(top, "epsb", [128, 1], F32)

        G(lambda e: e.iota(iot[:], [[1, 128]], base=0, channel_multiplier=-1,
                           allow_small_or_imprecise_dtypes=True), w=["iot"])
        V(lambda e: e.tensor_single_scalar(ident_f[:], iot[:], 0.0, op=ALU.is_equal), r=["iot"], w=["ident_f"])
        V(lambda e: e.tensor_single_scalar(ident_b[:], iot[:], 0.0, op=ALU.is_equal), r=["iot"], w=["ident_b"])
        V(lambda e: e.memset(ones_f[:], 1.0), w=["ones_f"])
        V(lambda e: e.memset(ones_b[:], 1.0), w=["ones_b"])
        V(lambda e: e.memset(epsb[:], EPS), w=["epsb"])

        def phase_load_x():
            with contextlib.ExitStack() as st:
                xt = [sb(st, "ld_x%d" % i, [128, D], F32) for i in range(2)]
                xo = [sb(st, "ld_o%d" % i, [128, 8, 128], F32) for i in range(2)]
                ps = [pst(st, "ld_ps%d" % i, [128, 4, 128], F32) for i in range(4)]
                ntile = T // 128
                for ti in range(ntile):
                    b = ti % 2
                    src = ctxin[ti * 128:(ti + 1) * 128, :] if ti < 2 else xin[(ti - 2) * 128:(ti - 1) * 128, :]
                    DM(lambda e, b=b, src=src: e.dma_start(out=xt[b][:], in_=src), w=["ld_x%d" % b])
                    for half in range(2):
                        pi = (ti * 2 + half) % 4
                        for j in range(4):
                            k = half * 4 + j
                            TE(lambda e, b=b, pi=pi, j=j, k=k: e.transpose(ps[pi][:, j, :], xt[b][:, k * 128:(k + 1) * 128], ident_f[:]),
                               r=["ld_x%d" % b, "ident_f"], w=["ld_ps%d" % pi])
                        evac(xo[b][:, half * 4:(half + 1) * 4, :], ps[pi][:], r=["ld_ps%d" % pi], w=[("ld_o", b, half)])
                    DM(lambda e, b=b, ti=ti: e.dma_start(out=XT[:, :, ti * 128:(ti + 1) * 128].rearrange("k p t -> p k t"), in_=xo[b][:]),
                       r=[("ld_o", b, 0), ("ld_o", b, 1)], w=[("XT", ti)])
            S.barrier()

        def phase_mod(l):
            with contextlib.ExitStack() as st:
                cT = sb(st, "md_c", [128, 2, 8], F32)
                scT = sb(st, "md_sc", [128, 8, 2], F32)
                bm = sb(st, "md_b", [128, 48], F32)
                g1 = sb(st, "md_g1", [128, 8], F32)
                g2 = sb(st, "md_g2", [128, 8], F32)
                modT = sb(st, "md_mod", [128, 48, 2], F32)
                wm = [sb(st, "md_w%d" % i, [128, 8, 512], F32) for i in range(2)]
                mps = pst(st, "md_ps", [128, 48, 2], F32)
                for r_ in range(2):
                    DM(lambda e, r_=r_: e.dma_start(out=cT[:, r_, :], in_=cvec[r_].rearrange("(k p) -> p k", p=128), allow_slow_non_contiguous=True), w=["md_c"])
                DM(lambda e: e.dma_start(out=bm[:], in_=P["b_mod"][l].rearrange("(n p) -> p n", p=128), allow_slow_non_contiguous=True), w=["md_b"])
                DM(lambda e: e.dma_start(out=g1[:], in_=P["norm1_g"][l].rearrange("(n p) -> p n", p=128), allow_slow_non_contiguous=True), w=["md_g1"])
                DM(lambda e: e.dma_start(out=g2[:], in_=P["norm2_g"][l].rearrange("(n p) -> p n", p=128), allow_slow_non_contiguous=True), w=["md_g2"])
                for r_ in range(2):
                    A(lambda e, r_=r_: e.activation(out=scT[:, :, r_], in_=cT[:, r_, :], func=AF.Silu), r=["md_c"], w=["md_sc"])
                for j in range(12):
                    b = j % 2
                    DM(lambda e, b=b, j=j: e.dma_start(out=wm[b][:], in_=P["w_mod"][l][:, j * 512:(j + 1) * 512].rearrange("(k p) n -> p k n", p=128)),
                       w=["md_w%d" % b])
                    for i in range(4):
                        n = j * 4 + i
                        for k in range(8):
                            TE(lambda e, b=b, i=i, k=k, n=n: e.matmul(mps[:, n, :], lhsT=wm[b][:, k, i * 128:(i + 1) * 128], rhs=scT[:, k, :],
                                                                      start=(k == 0), stop=(k == 7)),
                               r=["md_w%d" % b, "md_sc"], w=["md_ps"])
                V(lambda e: e.tensor_tensor(out=modT[:], in0=mps[:], in1=bm[:].unsqueeze(2).to_broadcast([128, 48, 2]), op=ALU.add),
                  r=["md_ps", "md_b"], w=["md_mod"])
                V(lambda e: e.tensor_copy(sh1[:], modT[:, 0:8, :]), r=["md_mod"], w=["sh1"])
                V(lambda e: e.scalar_tensor_tensor(out=gm1[:], in0=modT[:, 8:16, :], scalar=1.0, in1=g1[:].unsqueeze(2).to_broadcast([128, 8, 2]),
                                                   op0=ALU.add, op1=ALU.mult), r=["md_mod", "md_g1"], w=["gm1"])
                V(lambda e: e.tensor_copy(gt1[:], modT[:, 16:24, :]), r=["md_mod"], w=["gt1"])
                V(lambda e: e.tensor_copy(sh2[:], modT[:, 24:32, :]), r=["md_mod"], w=["sh2"])
                V(lambda e: e.scalar_tensor_tensor(out=gm2[:], in0=modT[:, 32:40, :], scalar=1.0, in1=g2[:].unsqueeze(2).to_broadcast([128, 8, 2]),
                                                   op0=ALU.add, op1=ALU.mult), r=["md_mod", "md_g2"], w=["gm2"])
                V(lambda e: e.tensor_copy(gt2[:], modT[:, 40:48, :]), r=["md_mod"], w=["gt2"])
            S.barrier()

        def rms_stats(src, nk, n, nfeat, sq, ps_stat, rstd, keys_r, ksq, kps, krstd):
            for k in range(nk):
                A(lambda e, k=k: e.activation(out=sq[:, k, :n], in_=src[:, k, :n], func=AF.Square), r=keys_r, w=[ksq])
            for k in range(nk):
                TE(lambda e, k=k: e.matmul(ps_stat[:, :n], lhsT=ones_f[:], rhs=sq[:, k, :n], start=(k == 0), stop=(k == nk - 1)),
                   r=[ksq, "ones_f"], w=[kps])
            A(lambda e: e.activation(out=rstd[:, :n], in_=ps_stat[:, :n], func=AF.Sqrt, scale=1.0 / nfeat, bias=epsb[:]),
              r=[kps, "epsb"], w=[krstd])
            V(lambda e: e.reciprocal(rstd[:, :n], rstd[:, :n]), r=[krstd], w=[krstd])

        def phase_proj(l, last):
            with contextlib.ExitStack() as st:
                w_in = sb(st, "pj_win", [128, 8, IN_W], BF16)
                w_krot = sb(st, "pj_wkrot", [128, 8, 32], BF16)
                wq_raw = sb(st, "pj_wqraw", [128, 2, 768], BF16)
                wq_rot = sb(st, "pj_wqrot", [128, 2, NH, 32], BF16)
                wkv = sb(st, "pj_wkv", [128, 1024], BF16)
                wv = sb(st, "pj_wv", [128, NH, 64], BF16)
                qg = sb(st, "pj_qg", [128, 2], F32)
                kvg = sb(st, "pj_kvg", [128, 1], F32)
                bg = sb(st, "pj_bg", [128, 24], F32)
                rcos = sb(st, "pj_cos", [32, SEQ], F32)
                rsin = sb(st, "pj_sin", [32, SEQ], F32)
                xg = sb(st, "pj_xg", [128, 8, 512], F32)
                sq = sb(st, "pj_sq", [128, 8, 512], F32)
                rstd = sb(st, "pj_rstd", [128, 512], F32)
                hT = sb(st, "pj_hT", [128, 8, 512], BF16)
                cq = sb(st, "pj_cq", [128, 2, 512], F32)
                cqn = sb(st, "pj_cqn", [128, 2, 512], BF16)
                ckv = sb(st, "pj_ckv", [128, 1, 512], F32)
                cn = sb(st, "pj_cn", [128, 512], BF16)
                rstd2 = sb(st, "pj_rstd2", [128, 512], F32)
                ob = [sb(st, "pj_ob%d" % i, [128, 512], BF16) for i in range(4)]
                of = [sb(st, "pj_of%d" % i, [128, 512], F32) for i in range(2)]
                t1 = sb(st, "pj_t1", [32, 512], F32)
                t2 = sb(st, "pj_t2", [32, 512], F32)
                ps = [pst(st, "pj_ps%d" % i, [128, 512], F32) for i in range(6)]
                ps_st = pst(st, "pj_pst", [128, 512], F32)
                ps_rot = pst(st, "pj_prot", [128, 512], F32)

                for k in range(8):
                    for hh in range(2):
                        DM(lambda e, k=k, hh=hh: e.dma_start(out=w_in[:, k, hh * 2000:(hh + 1) * 2000],
                                                             in_=P["w_in"][l][k * 128:(k + 1) * 128, hh * 2000:(hh + 1) * 2000]),
                           w=["pj_win"], q="gpsimd")
                DM(lambda e: e.dma_start(out=wq_raw[:], in_=P["w_uq"][l].rearrange("(k p) n -> p k n", p=128)), w=["pj_wqraw"], q="gpsimd")
                DM(lambda e: e.dma_start(out=wkv[:], in_=P["w_ukv"][l]), w=["pj_wkv"], q="gpsimd")
                DM(lambda e: e.dma_start(out=qg[:], in_=P["q_norm_g"][l].rearrange("(n p) -> p n", p=128), allow_slow_non_contiguous=True), w=["pj_qg"])
                DM(lambda e: e.dma_start(out=kvg[:], in_=P["kv_norm_g"][l].rearrange("(n p) -> p n", p=128), allow_slow_non_contiguous=True), w=["pj_kvg"])
                DM(lambda e: e.dma_start(out=bg[:], in_=P["b_gate"][l].rearrange("(n p) -> p n", p=128), allow_slow_non_contiguous=True), w=["pj_bg"])
                DM(lambda e: e.dma_start(out=rcos[:], in_=C["ropecos"]), w=["pj_cos"])
                DM(lambda e: e.dma_start(out=rsin[:], in_=C["ropesin"]), w=["pj_sin"])
                kr_cols = w_in[:, :, 384:416].rearrange("p k (rc x f) -> p k rc x f", rc=2, x=2)
                kro = w_krot[:].rearrange("p k (rc x f) -> p k rc x f", rc=2, x=2)
                V(lambda e: e.tensor_single_scalar(kro[:, :, :, 0, :], kr_cols[:, :, :, 1, :], -1.0, op=ALU.mult), r=["pj_win"], w=["pj_wkrot"])
                V(lambda e: e.tensor_copy(kro[:, :, :, 1, :], kr_cols[:, :, :, 0, :]), r=["pj_win"], w=["pj_wkrot"])
                for k in range(2):
                    qr = wq_raw[:, k, :].rearrange("p (h c) -> p h c", c=96)[:, :, 64:96].rearrange("p h (rc x f) -> p h rc x f", rc=2, x=2)
                    qo = wq_rot[:, k, :, :].rearrange("p h (rc x f) -> p h rc x f", rc=2, x=2)
                    for rc in range(2):
                        V(lambda e, qo=qo, qr=qr, rc=rc: e.tensor_single_scalar(qo[:, :, rc, 0, :], qr[:, :, rc, 1, :], -1.0, op=ALU.mult),
                          r=["pj_wqraw"], w=["pj_wqrot"])
                        V(lambda e, qo=qo, qr=qr, rc=rc: e.tensor_copy(qo[:, :, rc, 1, :], qr[:, :, rc, 0, :]), r=["pj_wqraw"], w=["pj_wqrot"])
                V(lambda e: e.tensor_copy(wv[:], wkv[:].rearrange("p (h c) -> p h c", c=128)[:, :, 64:128]), r=["pj_wkv"], w=["pj_wv"])

                obi = [0]

                def nxt_ob():
                    obi[0] += 1
                    return obi[0] % 4

                psi = [0]

                def nxt_ps():
                    psi[0] += 1
                    return psi[0] % 6

                def do_group(gi, t0, n):
                    is_ctx = gi == 0
                    r_ = 1 if is_ctx else 0
                    s0 = t0 - CTX
                    kv_only = last and is_ctx
                    DM(lambda e, t0=t0, n=n: e.dma_start(out=xg[:, :, :n], in_=XT[:, :, t0:t0 + n].rearrange("k p t -> p k t")), r=["XT"], w=["pj_xg"])
                    rms_stats(xg, 8, n, D, sq, ps_st, rstd, ["pj_xg"], "pj_sq", "pj_pst", "pj_rstd")
                    for k in range(8):
                        V(lambda e, k=k, n=n: e.tensor_tensor(out=sq[:, k, :n], in0=xg[:, k, :n], in1=rstd[:, :n], op=ALU.mult),
                          r=["pj_xg", "pj_rstd"], w=["pj_sq"])
                        V(lambda e, k=k, n=n, r_=r_: e.tensor_scalar(hT[:, k, :n], sq[:, k, :n], gm1[:, k, r_:r_ + 1], sh1[:, k, r_:r_ + 1],
                                                                     op0=ALU.mult, op1=ALU.add),
                          r=["pj_sq", "gm1", "sh1"], w=["pj_hT"])

                    p = nxt_ps()
                    for k in range(8):
                        TE(lambda e, k=k, p=p: e.matmul(ps[p][:, :n], lhsT=w_in[:, k, 256:384], rhs=hT[:, k, :n], start=(k == 0), stop=(k == 7)),
                           r=["pj_hT", "pj_win"], w=["pj_ps%d" % p])
                    V(lambda e, p=p: e.tensor_copy(ckv[:, 0, :n], ps[p][:, :n]), r=["pj_ps%d" % p], w=["pj_ckv"])
                    rms_stats(ckv, 1, n, 128, sq, ps_st, rstd2, ["pj_ckv"], "pj_sq", "pj_pst", "pj_rstd2")
                    V(lambda e: e.tensor_tensor(out=sq[:, 0, :n], in0=ckv[:, 0, :n], in1=rstd2[:, :n], op=ALU.mult),
                      r=["pj_ckv", "pj_rstd2"], w=["pj_sq"])
                    V(lambda e: e.tensor_scalar(cn[:, :n], sq[:, 0, :n], kvg[:, 0:1], None, op0=ALU.mult), r=["pj_sq", "pj_kvg"], w=["pj_cn"])
                    for h in range(NH):
                        p = nxt_ps()
                        TE(lambda e, p=p, h=h: e.matmul(ps[p][:64, :n], lhsT=wkv[:, h * 128:h * 128 + 64], rhs=cn[:, :n], start=True, stop=True),
                           r=["pj_cn", "pj_wkv"], w=["pj_ps%d" % p])
                        o = nxt_ob()
                        evac(ob[o][:64, :n], ps[p][:64, :n], r=["pj_ps%d" % p], w=["pj_ob%d" % o])
                        DM(lambda e, o=o, h=h: e.dma_start(out=KT[h, 32:96, t0:t0 + n], in_=ob[o][:64, :n]), r=["pj_ob%d" % o], w=[("KT", S.dma_next)])
                    for j in range(n // 128):
                        p = nxt_ps()
                        TE(lambda e, p=p, j=j: e.matmul(ps[p][:, :], lhsT=cn[:, j * 128:(j + 1) * 128], rhs=wv[:].rearrange("p h c -> p (h c)"), start=True, stop=True),
                           r=["pj_cn", "pj_wv"], w=["pj_ps%d" % p])
                        o = nxt_ob()
                        evac(ob[o][:, :], ps[p][:, :], r=["pj_ps%d" % p], w=["pj_ob%d" % o])
                        DM(lambda e, o=o, j=j: e.dma_start(out=VV[t0 + j * 128:t0 + (j + 1) * 128, :], in_=ob[o][:, :]), r=["pj_ob%d" % o], w=[("VV", S.dma_next)])
                    p = nxt_ps()
                    for k in range(8):
                        TE(lambda e, k=k, p=p: e.matmul(ps[p][:32, :n], lhsT=w_in[:, k, 384:416], rhs=hT[:, k, :n], start=(k == 0), stop=(k == 7)),
                           r=["pj_hT", "pj_win"], w=["pj_ps%d" % p])
                    o = nxt_ob()
                    if is_ctx:
                        evac(ob[o][:32, :n], ps[p][:32, :n], r=["pj_ps%d" % p], w=["pj_ob%d" % o])
                    else:
                        for k in range(8):
                            TE(lambda e, k=k: e.matmul(ps_rot[:32, :n], lhsT=w_krot[:, k, :], rhs=hT[:, k, :n], start=(k == 0), stop=(k == 7)),
                               r=["pj_hT", "pj_wkrot"], w=["pj_prot"])
                        V(lambda e, p=p: e.tensor_tensor(out=t1[:, :n], in0=ps[p][:32, :n], in1=rcos[:, s0:s0 + n], op=ALU.mult),
                          r=["pj_ps%d" % p, "pj_cos"], w=["pj_t1"])
                        V(lambda e: e.tensor_tensor(out=t2[:, :n], in0=ps_rot[:32, :n], in1=rsin[:, s0:s0 + n], op=ALU.mult),
                          r=["pj_prot", "pj_sin"], w=["pj_t2"])
                        V(lambda e, o=o: e.tensor_tensor(out=ob[o][:32, :n], in0=t1[:, :n], in1=t2[:, :n], op=ALU.add),
                          r=["pj_t1", "pj_t2"], w=["pj_ob%d" % o])
                    for h in range(NH):
                        DM(lambda e, o=o, h=h: e.dma_start(out=KT[h, 0:32, t0:t0 + n], in_=ob[o][:32, :n]), r=["pj_ob%d" % o], w=[("KT", S.dma_next)])
                    if kv_only:
                        return
                    for c in range(2):
                        p = nxt_ps()
                        for k in range(8):
                            TE(lambda e, k=k, p=p, c=c: e.matmul(ps[p][:, :n], lhsT=w_in[:, k, c * 128:(c + 1) * 128], rhs=hT[:, k, :n], start=(k == 0), stop=(k == 7)),
                               r=["pj_hT", "pj_win"], w=["pj_ps%d" % p])
                        V(lambda e, p=p, c=c: e.tensor_copy(cq[:, c, :n], ps[p][:, :n]), r=["pj_ps%d" % p], w=["pj_cq"])
                    rms_stats(cq, 2, n, 256, sq, ps_st, rstd2, ["pj_cq"], "pj_sq", "pj_pst", "pj_rstd2")
                    for c in range(2):
                        V(lambda e, c=c: e.tensor_tensor(out=sq[:, c, :n], in0=cq[:, c, :n], in1=rstd2[:, :n], op=ALU.mult),
                          r=["pj_cq", "pj_rstd2"], w=["pj_sq"])
                        V(lambda e, c=c: e.tensor_scalar(cqn[:, c, :n], sq[:, c, :n], qg[:, c:c + 1], None, op0=ALU.mult), r=["pj_sq", "pj_qg"], w=["pj_cqn"])
                    for h in range(NH):
                        p = nxt_ps()
                        for c in range(2):
                            TE(lambda e, p=p, c=c, h=h: e.matmul(ps[p][:64, :n], lhsT=wq_raw[:, c, h * 96:h * 96 + 64], rhs=cqn[:, c, :n], start=(c == 0), stop=(c == 1)),
                               r=["pj_cqn", "pj_wqraw"], w=["pj_ps%d" % p])
                        o = nxt_ob()
                        evac(ob[o][:64, :n], ps[p][:64, :n], r=["pj_ps%d" % p], w=["pj_ob%d" % o])
                        DM(lambda e, o=o, h=h: e.dma_start(out=QT[h, 32:96, t0:t0 + n], in_=ob[o][:64, :n]), r=["pj_ob%d" % o], w=[("QT", S.dma_next)])
                        p = nxt_ps()
                        for c in range(2):
                            TE(lambda e, p=p, c=c, h=h: e.matmul(ps[p][:32, :n], lhsT=wq_raw[:, c, h * 96 + 64:h * 96 + 96], rhs=cqn[:, c, :n], start=(c == 0), stop=(c == 1)),
                               r=["pj_cqn", "pj_wqraw"], w=["pj_ps%d" % p])
                        o = nxt_ob()
                        if is_ctx:
                            evac(ob[o][:32, :n], ps[p][:32, :n], r=["pj_ps%d" % p], w=["pj_ob%d" % o])
                        else:
                            for c in range(2):
                                TE(lambda e, c=c, h=h: e.matmul(ps_rot[:32, :n], lhsT=wq_rot[:, c, h, :], rhs=cqn[:, c, :n], start=(c == 0), stop=(c == 1)),
                                   r=["pj_cqn", "pj_wqrot"], w=["pj_prot"])
                            V(lambda e, p=p: e.tensor_tensor(out=t1[:, :n], in0=ps[p][:32, :n], in1=rcos[:, s0:s0 + n], op=ALU.mult),
                              r=["pj_ps%d" % p, "pj_cos"], w=["pj_t1"])
                            V(lambda e: e.tensor_tensor(out=t2[:, :n], in0=ps_rot[:32, :n], in1=rsin[:, s0:s0 + n], op=ALU.mult),
                              r=["pj_prot", "pj_sin"], w=["pj_t2"])
                            V(lambda e, o=o: e.tensor_tensor(out=ob[o][:32, :n], in0=t1[:, :n], in1=t2[:, :n], op=ALU.add),
                              r=["pj_t1", "pj_t2"], w=["pj_ob%d" % o])
                        DM(lambda e, o=o, h=h: e.dma_start(out=QT[h, 0:32, t0:t0 + n], in_=ob[o][:32, :n]), r=["pj_ob%d" % o], w=[("QT", S.dma_next)])
                    for c in range(2):
                        p = nxt_ps()
                        for k in range(8):
                            TE(lambda e, k=k, p=p, c=c: e.matmul(ps[p][:, :n], lhsT=w_in[:, k, 416 + c * 128:416 + (c + 1) * 128], rhs=hT[:, k, :n], start=(k == 0), stop=(k == 7)),
                               r=["pj_hT", "pj_win"], w=["pj_ps%d" % p])
                        o = nxt_ob()
                        evac(ob[o][:, :n], ps[p][:, :n], r=["pj_ps%d" % p], w=["pj_ob%d" % o])
                        DM(lambda e, o=o, c=c: e.dma_start(out=ZFT[c, :, t0:t0 + n], in_=ob[o][:, :n]), r=["pj_ob%d" % o], w=[("ZFT", S.dma_next)])
                    for c in range(2):
                        p = nxt_ps()
                        for k in range(8):
                            TE(lambda e, k=k, p=p, c=c: e.matmul(ps[p][:, :n], lhsT=w_in[:, k, 672 + c * 128:672 + (c + 1) * 128], rhs=hT[:, k, :n], start=(k == 0), stop=(k == 7)),
                               r=["pj_hT", "pj_win"], w=["pj_ps%d" % p])
                        evac(of[c][:, :n], ps[p][:, :n], r=["pj_ps%d" % p], w=["pj_of%d" % c])
                        DM(lambda e, c=c: e.dma_start(out=ZPT[c, :, t0:t0 + n], in_=of[c][:, :n]), r=["pj_of%d" % c], w=[("ZPT", S.dma_next)])
                    for c in range(24):
                        p = nxt_ps()
                        for k in range(8):
                            TE(lambda e, k=k, p=p, c=c: e.matmul(ps[p][:, :n], lhsT=w_in[:, k, 928 + c * 128:928 + (c + 1) * 128], rhs=hT[:, k, :n], start=(k == 0), stop=(k == 7)),
                               r=["pj_hT", "pj_win"], w=["pj_ps%d" % p])
                        o = nxt_ob()
                        A(lambda e, o=o, p=p, c=c: e.activation(out=ob[o][:, :n], in_=ps[p][:, :n], func=AF.Sigmoid, bias=bg[:, c:c + 1], scale=1.0),
                          r=["pj_ps%d" % p, "pj_bg"], w=["pj_ob%d" % o])
                        DM(lambda e, o=o, c=c: e.dma_start(out=GT[c, :, t0:t0 + n], in_=ob[o][:, :n]), r=["pj_ob%d" % o], w=[("GT", S.dma_next)])

                for gi, (t0, n) in enumerate(GROUPS):
                    do_group(gi, t0, n)
            S.barrier()

        def phase_attn(l, last):
            with contextlib.ExitStack() as st:
                v_sb = sb(st, "at_v", [128, T // 128, 512], BF16)
                v_ext = sb(st, "at_vx", [128, T // 128, NH, 65], BF16)
                sel = sb(st, "at_sel", [65, 64], F32)
                kth = [sb(st, "at_k%d" % i, [96, T], BF16) for i in range(2)]
                qg = [sb(st, "at_q%d" % i, [96, 512], BF16) for i in range(2)]
                pt = [sb(st, "at_p%d" % i, [128, 512], BF16) for i in range(3)]
                o_sb = sb(st, "at_osb", [65, 512], F32)
                rs = sb(st, "at_rs", [64, 512], F32)
                osb = [sb(st, "at_o%d" % i, [64, 512], BF16) for i in range(2)]
                ps_s = [pst(st, "at_pss%d" % i, [128, 512], F32) for i in range(3)]
                ps_o = [pst(st, "at_pso%d" % i, [128, 512], F32) for i in range(2)]
                ps_b = pst(st, "at_psb", [128, 512], F32)
                DM(lambda e: e.dma_start(out=v_sb[:], in_=VV.rearrange("(kt p) c -> p kt c", p=128)), w=["at_v"])
                V(lambda e: e.memset(v_ext[:], 1.0), w=["at_vx"])
                G(lambda e: e.tensor_copy(v_ext[:, :, :, 0:64], v_sb[:].rearrange("p k (h c) -> p k h c", c=64)), r=["at_v"], w=["at_vx"])
                V(lambda e: e.memset(sel[:], 0.0), w=["at_sel"])
                V(lambda e: e.memset(sel[64:65, :], 1.0), w=["at_sel"])
                cnt = [0]
                segs = ([] if last else [(0, CTX, 2)]) + [(t0, n, T // 128) for (t0, n) in GROUPS[1:]]

                def do_head(h):
                    kb = h % 2
                    DM(lambda e: e.dma_start(out=kth[kb][:], in_=KT[h]), w=["at_k%d" % kb])

                    def do_seg(t0, n, nkt):
                        cnt[0] += 1
                        qb = cnt[0] % 2
                        DM(lambda e: e.dma_start(out=qg[qb][:, :n], in_=QT[h, :, t0:t0 + n]), w=["at_q%d" % qb])

                        def s_mm(kt):
                            si = kt % 3
                            TE(lambda e: e.matmul(ps_s[si][:, :n], lhsT=kth[kb][:, kt * 128:(kt + 1) * 128], rhs=qg[qb][:, :n], start=True, stop=True),
                               r=["at_k%d" % kb, "at_q%d" % qb], w=["at_pss%d" % si])

                        def pv(kt):
                            si = kt % 3
                            A(lambda e: e.activation(out=pt[si][:, :n], in_=ps_s[si][:, :n], func=AF.Exp, scale=ATT_SCALE),
                              r=["at_pss%d" % si], w=["at_p%d" % si])
                            TE(lambda e: e.matmul(ps_o[qb][:65, :n], lhsT=v_ext[:, kt, h, :], rhs=pt[si][:, :n], start=(kt == 0), stop=(kt == nkt - 1)),
                               r=["at_vx", "at_p%d" % si], w=["at_pso%d" % qb])
                        s_mm(0)
                        if nkt > 1:
                            s_mm(1)
                        for kt in range(nkt):
                            if kt + 2 < nkt:
                                s_mm(kt + 2)
                            pv(kt)
                        A(lambda e: e.activation(out=o_sb[:, :n], in_=ps_o[qb][:65, :n], func=AF.Copy), r=["at_pso%d" % qb], w=["at_osb"])
                        TE(lambda e: e.matmul(ps_b[:64, :n], lhsT=sel[:, :], rhs=o_sb[:, :n], start=True, stop=True), r=["at_sel", "at_osb"], w=["at_psb"])
                        V(lambda e: e.reciprocal(rs[:, :n], ps_b[:64, :n]), r=["at_psb"], w=["at_rs"])
                        V(lambda e: e.tensor_tensor(out=osb[qb][:, :n], in0=o_sb[:64, :n], in1=rs[:, :n], op=ALU.mult),
                          r=["at_osb", "at_rs"], w=["at_o%d" % qb])
                        DM(lambda e: e.dma_start(out=ATT[h // 2, (h % 2) * 64:(h % 2) * 64 + 64, t0:t0 + n], in_=osb[qb][:, :n]),
                           r=["at_o%d" % qb], w=[("ATT", S.dma_next)])
                    for (t0, n, nkt) in segs:
                        do_seg(t0, n, nkt)
                for h in range(NH):
                    do_head(h)
            S.barrier()

        def phase_fourier(l, last):
            segs = ([] if last else [(0, CTX, "cl256", "sl256")]) + [(CTX, SEQ, "cl4096", "sl4096")]

            def do_seg(t0, L, cln, sln):
                ntt = L // 128
                KW = 256
                with contextlib.ExitStack() as st:
                    ccs = sb(st, "fo_ccs", [128, 2, 512], BF16)
                    zft = sb(st, "fo_z", [128, 2, L], BF16)
                    uv = sb(st, "fo_uv", [128, ntt, 512], BF16)
                    clb = [sb(st, "fo_cl%d" % i, [128, ntt, KW], BF16) for i in range(2)]
                    slb = [sb(st, "fo_sl%d" % i, [128, ntt, KW], BF16) for i in range(2)]
                    yo = [sb(st, "fo_y%d" % i, [128, KW], BF16) for i in range(2)]
                    ps = [pst(st, "fo_ps%d" % i, [128, 512], F32) for i in range(4)]
                    DM(lambda e: e.dma_start(out=ccs[:], in_=C["ccs"].rearrange("(c p) n -> p c n", p=128)), w=["fo_ccs"])
                    DM(lambda e: e.dma_start(out=zft[:], in_=ZFT[:, :, t0:t0 + L].rearrange("c p t -> p c t")), w=["fo_z"])

                    def do_tt(tt):
                        pi = tt % 4
                        for c in range(2):
                            TE(lambda e, c=c: e.matmul(ps[pi][:, :], lhsT=zft[:, c, tt * 128:(tt + 1) * 128], rhs=ccs[:, c, :], start=(c == 0), stop=(c == 1)),
                               r=["fo_z", "fo_ccs"], w=["fo_ps%d" % pi])
                        evac(uv[:, tt, :], ps[pi][:, :], r=["fo_ps%d" % pi], w=[("fo_uv", tt)])
                    for tt in range(ntt):
                        do_tt(tt)
                    uvkeys = [("fo_uv", tt) for tt in range(ntt)]
                    scale = 1.0 / math.sqrt(L * 256.0)

                    def do_kc(kc):
                        b = kc % 2
                        DM(lambda e: e.dma_start(out=clb[b][:], in_=C[cln][:, kc * KW:(kc + 1) * KW].rearrange("(tt p) k -> p tt k", p=128)), w=["fo_cl%d" % b])
                        DM(lambda e: e.dma_start(out=slb[b][:], in_=C[sln][:, kc * KW:(kc + 1) * KW].rearrange("(tt p) k -> p tt k", p=128)), w=["fo_sl%d" % b])

                        def do_m(m):
                            pi = (kc * 2 + m) % 4
                            for tt in range(ntt):
                                TE(lambda e, tt=tt: e.matmul(ps[pi][:, :KW], lhsT=uv[:, tt, m * 128:(m + 1) * 128], rhs=clb[b][:, tt, :], start=(tt == 0), stop=False),
                                   r=uvkeys + ["fo_cl%d" % b], w=["fo_ps%d" % pi])
                                TE(lambda e, tt=tt: e.matmul(ps[pi][:, :KW], lhsT=uv[:, tt, 256 + m * 128:256 + (m + 1) * 128], rhs=slb[b][:, tt, :], start=False, stop=(tt == ntt - 1)),
                                   r=uvkeys + ["fo_sl%d" % b], w=["fo_ps%d" % pi])
                            evac(yo[m][:, :], ps[pi][:, :KW], r=["fo_ps%d" % pi], w=["fo_y%d" % m], scale=scale)
                            DM(lambda e: e.dma_start(out=YFT[m, :, t0 + kc * KW:t0 + (kc + 1) * KW], in_=yo[m][:, :]), r=["fo_y%d" % m], w=[("YFT", S.dma_next)])
                        for m in range(2):
                            do_m(m)
                    for kc in range(L // KW):
                        do_kc(kc)
                S.barrier()
            for sg in segs:
                do_seg(*sg)

        def phase_pool(l, last):
            segs = ([] if last else [(0, CTX, "icnt256")]) + [(CTX, SEQ, "icnt4096")]

            def do_seg(t0, L, icn):
                with contextlib.ExitStack() as st:
                    zpp = sb(st, "po_z", [128, 2, L + 16], F32)
                    sA = sb(st, "po_sA", [128, L + 16], F32)
                    sB = sb(st, "po_sB", [128, L + 16], F32)
                    icnt = sb(st, "po_ic", [128, 2, L], F32)
                    tmp = sb(st, "po_tmp", [128, L], F32)
                    poolT = sb(st, "po_pT", [128, 2, L], BF16)
                    wbd = [sb(st, "po_w%d" % i, [128, 128], BF16) for i in range(2)]
                    psc = sb(st, "po_sc", [128, 2], F32)
                    yo = [sb(st, "po_y%d" % i, [128, 512], BF16) for i in range(2)]
                    ps = [pst(st, "po_ps%d" % i, [128, 512], F32) for i in range(2)]
                    V(lambda e: e.memset(zpp[:], 0.0), w=["po_z"])
                    for ch in range(2):
                        V(lambda e, ch=ch: e.memset(wbd[ch][:], 0.0), w=["po_w%d" % ch])
                    DM(lambda e: e.dma_start(out=zpp[:, :, 8:8 + L], in_=ZPT[:, :, t0:t0 + L].rearrange("c p t -> p c t")), w=["po_z"])
                    DM(lambda e: e.dma_start(out=icnt[:], in_=C[icn].rearrange("c p t -> p c t")), w=["po_ic"])
                    DM(lambda e: e.dma_start(out=psc[:], in_=P["pool_scale"][l].rearrange("(c p) -> p c", p=128), allow_slow_non_contiguous=True), w=["po_sc"])
                    for g in range(4):
                        o = (g % 2) * 64
                        DM(lambda e, g=g, o=o: e.dma_start(out=wbd[g // 2][o:o + 64, o:o + 64], in_=P["w_grp"][l, g]), w=["po_w%d" % (g // 2)], q="gpsimd")

                    def do_ch(ch):
                        V(lambda e: e.tensor_tensor(out=sA[:, 1:L + 16], in0=zpp[:, ch, 0:L + 15], in1=zpp[:, ch, 1:L + 16], op=ALU.add), r=["po_z"], w=["po_sA"])
                        V(lambda e: e.tensor_tensor(out=sB[:, 2:L + 15], in0=sA[:, 1:L + 14], in1=sA[:, 3:L + 16], op=ALU.add), r=["po_sA"], w=["po_sB"])
                        if ch == 1:
                            V(lambda e: e.tensor_tensor(out=sA[:, 4:L + 13], in0=sB[:, 2:L + 11], in1=sB[:, 6:L + 15], op=ALU.add), r=["po_sB"], w=["po_sA"])
                            V(lambda e: e.tensor_tensor(out=sB[:, 8:L + 9], in0=sA[:, 4:L + 5], in1=sA[:, 12:L + 13], op=ALU.add), r=["po_sA"], w=["po_sB"])
                        V(lambda e: e.tensor_tensor(out=tmp[0:64, :], in0=sA[0:64, 8:8 + L], in1=icnt[0:64, ch, :], op=ALU.mult), r=["po_sA", "po_ic"], w=["po_tmp"])
                        V(lambda e: e.tensor_tensor(out=tmp[64:128, :], in0=sB[64:128, 8:8 + L], in1=icnt[64:128, ch, :], op=ALU.mult), r=["po_sB", "po_ic"], w=["po_tmp"])
                        V(lambda e: e.tensor_tensor(out=poolT[:, ch, :], in0=tmp[:, :], in1=zpp[:, ch, 8:8 + L], op=ALU.subtract), r=["po_tmp", "po_z"], w=[("po_pT", ch)])

                        def do_tc(tc_):
                            n = min(512, L - tc_ * 512)
                            pi = tc_ % 2
                            TE(lambda e: e.matmul(ps[pi][:, :n], lhsT=wbd[ch][:], rhs=poolT[:, ch, tc_ * 512:tc_ * 512 + n], start=True, stop=True),
                               r=[("po_pT", ch), "po_w%d" % ch], w=["po_ps%d" % pi])
                            V(lambda e: e.tensor_scalar(yo[pi][:, :n], ps[pi][:, :n], psc[:, ch:ch + 1], None, op0=ALU.mult), r=["po_ps%d" % pi, "po_sc"], w=["po_y%d" % pi])
                            DM(lambda e: e.dma_start(out=YCT[ch, :, t0 + tc_ * 512:t0 + tc_ * 512 + n], in_=yo[pi][:, :n]), r=["po_y%d" % pi], w=[("YCT", S.dma_next)])
                        for tc_ in range((L + 511) // 512):
                            do_tc(tc_)
                    for ch in range(2):
                        do_ch(ch)
                S.barrier()
            for sg in segs:
                do_seg(*sg)

        def phase_combine(l, last):
            with contextlib.ExitStack() as st:
                w_oa = sb(st, "cb_woa", [128, 4, D], BF16)
                w_ob = sb(st, "cb_wob", [128, 2, D], BF16)
                w_oc = sb(st, "cb_woc", [128, 2, D], BF16)
                w_out = sb(st, "cb_wout", [128, 8, D], BF16)
                w_pq = sb(st, "cb_wpq", [128, 8, 2048], BF16)
                att = sb(st, "cb_att", [128, 4, 512], BF16)
                yf = sb(st, "cb_yf", [128, 2, 512], BF16)
                yc = sb(st, "cb_yc", [128, 2, 512], BF16)
                gT = sb(st, "cb_g", [128, 24, 512], BF16)
                xg = sb(st, "cb_xg", [128, 8, 512], F32)
                sq = sb(st, "cb_sq", [128, 8, 512], F32)
                rstd = sb(st, "cb_rstd", [128, 512], F32)
                u = sb(st, "cb_u", [128, 8, 512], BF16)
                h2 = sb(st, "cb_h2", [128, 8, 512], BF16)
                t1 = sb(st, "cb_t1", [128, 512], F32)
                t2 = sb(st, "cb_t2", [128, 512], F32)
                t3 = sb(st, "cb_t3", [128, 512], F32)
                ob = [sb(st, "cb_ob%d" % i, [128, 512], BF16) for i in range(3)]
                ps = [pst(st, "cb_ps%d" % i, [128, 512], F32) for i in range(6)]
                ps_st = pst(st, "cb_pst", [128, 512], F32)
                for k in range(4):
                    DM(lambda e, k=k: e.dma_start(out=w_oa[:, k, :], in_=P["w_oa"][l][k * 128:(k + 1) * 128, :]), w=["cb_woa"], q="gpsimd")
                for k in range(2):
                    DM(lambda e, k=k: e.dma_start(out=w_ob[:, k, :], in_=P["w_ob"][l][k * 128:(k + 1) * 128, :]), w=["cb_wob"], q="gpsimd")
                    DM(lambda e, k=k: e.dma_start(out=w_oc[:, k, :], in_=P["w_oc"][l][k * 128:(k + 1) * 128, :]), w=["cb_woc"], q="gpsimd")
                for k in range(8):
                    DM(lambda e, k=k: e.dma_start(out=w_out[:, k, :], in_=P["w_out"][l][k * 128:(k + 1) * 128, :]), w=["cb_wout"], q="gpsimd")
                    DM(lambda e, k=k: e.dma_start(out=w_pq[:, k, :], in_=P["w_pq"][l][k * 128:(k + 1) * 128, :]), w=["cb_wpq"], q="gpsimd")
                psi = [0]

                def do_group(gi, t0, n):
                    r_ = 1 if gi == 0 else 0
                    DM(lambda e: e.dma_start(out=att[:, :, :n], in_=ATT[:, :, t0:t0 + n].rearrange("k p t -> p k t")), w=["cb_att"])
                    DM(lambda e: e.dma_start(out=yf[:, :, :n], in_=YFT[:, :, t0:t0 + n].rearrange("k p t -> p k t")), w=["cb_yf"])
                    DM(lambda e: e.dma_start(out=yc[:, :, :n], in_=YCT[:, :, t0:t0 + n].rearrange("k p t -> p k t")), w=["cb_yc"])
                    DM(lambda e: e.dma_start(out=gT[:, :, :n], in_=GT[:, :, t0:t0 + n].rearrange("k p t -> p k t")), w=["cb_g"])
                    DM(lambda e: e.dma_start(out=xg[:, :, :n], in_=XT[:, :, t0:t0 + n].rearrange("k p t -> p k t")), w=["cb_xg"])

                    def do_dch(dc):
                        pa, pb, pc = psi[0] % 6, (psi[0] + 1) % 6, (psi[0] + 2) % 6
                        psi[0] += 3
                        for k in range(4):
                            TE(lambda e, k=k: e.matmul(ps[pa][:, :n], lhsT=w_oa[:, k, dc * 128:(dc + 1) * 128], rhs=att[:, k, :n], start=(k == 0), stop=(k == 3)),
                               r=["cb_woa", "cb_att"], w=["cb_ps%d" % pa])
                        for k in range(2):
                            TE(lambda e, k=k: e.matmul(ps[pb][:, :n], lhsT=w_ob[:, k, dc * 128:(dc + 1) * 128], rhs=yf[:, k, :n], start=(k == 0), stop=(k == 1)),
                               r=["cb_wob", "cb_yf"], w=["cb_ps%d" % pb])
                        for k in range(2):
                            TE(lambda e, k=k: e.matmul(ps[pc][:, :n], lhsT=w_oc[:, k, dc * 128:(dc + 1) * 128], rhs=yc[:, k, :n], start=(k == 0), stop=(k == 1)),
                               r=["cb_woc", "cb_yc"], w=["cb_ps%d" % pc])
                        V(lambda e: e.tensor_tensor(out=t1[:, :n], in0=ps[pa][:, :n], in1=gT[:, dc, :n], op=ALU.mult), r=["cb_ps%d" % pa, "cb_g"], w=["cb_t1"])
                        V(lambda e: e.tensor_tensor(out=t2[:, :n], in0=ps[pb][:, :n], in1=gT[:, 8 + dc, :n], op=ALU.mult), r=["cb_ps%d" % pb, "cb_g"], w=["cb_t2"])
                        V(lambda e: e.tensor_tensor(out=t3[:, :n], in0=ps[pc][:, :n], in1=gT[:, 16 + dc, :n], op=ALU.mult), r=["cb_ps%d" % pc, "cb_g"], w=["cb_t3"])
                        G(lambda e: e.tensor_tensor(out=t1[:, :n], in0=t1[:, :n], in1=t2[:, :n], op=ALU.add), r=["cb_t1", "cb_t2"], w=["cb_t1"])
                        G(lambda e: e.tensor_tensor(out=u[:, dc, :n], in0=t1[:, :n], in1=t3[:, :n], op=ALU.add), r=["cb_t1", "cb_t3"], w=[("cb_u", dc)])
                    for dc in range(8):
                        do_dch(dc)
                    ukeys = [("cb_u", dc) for dc in range(8)]

                    def do_out(dc):
                        pm = psi[0] % 6
                        psi[0] += 1
                        for k in range(8):
                            TE(lambda e, k=k: e.matmul(ps[pm][:, :n], lhsT=w_out[:, k, dc * 128:(dc + 1) * 128], rhs=u[:, k, :n], start=(k == 0), stop=(k == 7)),
                               r=ukeys + ["cb_wout"], w=["cb_ps%d" % pm])
                        V(lambda e: e.scalar_tensor_tensor(out=xg[:, dc, :n], in0=ps[pm][:, :n], scalar=gt1[:, dc, r_:r_ + 1], in1=xg[:, dc, :n], op0=ALU.mult, op1=ALU.add),
                          r=["cb_ps%d" % pm, "gt1", "cb_xg"], w=["cb_xg"])
                    for dc in range(8):
                        do_out(dc)
                    DM(lambda e: e.dma_start(out=XT[:, :, t0:t0 + n].rearrange("k p t -> p k t"), in_=xg[:, :, :n]), r=["cb_xg"], w=[("XTw", gi)])
                    rms_stats(xg, 8, n, D, sq, ps_st, rstd, ["cb_xg"], "cb_sq", "cb_pst", "cb_rstd")

                    def do_h2(k):
                        V(lambda e: e.tensor_tensor(out=sq[:, k, :n], in0=xg[:, k, :n], in1=rstd[:, :n], op=ALU.mult), r=["cb_xg", "cb_rstd"], w=["cb_sq"])
                        V(lambda e: e.tensor_scalar(h2[:, k, :n], sq[:, k, :n], gm2[:, k, r_:r_ + 1], sh2[:, k, r_:r_ + 1], op0=ALU.mult, op1=ALU.add),
                          r=["cb_sq", "gm2", "sh2"], w=["cb_h2"])
                    for k in range(8):
                        do_h2(k)
                    DM(lambda e: e.dma_start(out=H2T[:, :, t0:t0 + n].rearrange("k p t -> p k t"), in_=h2[:, :, :n]), r=["cb_h2"], w=[("H2T", gi)])

                    def do_pq(nc_):
                        pm = psi[0] % 6
                        psi[0] += 1
                        o = nc_ % 3
                        for k in range(8):
                            TE(lambda e, k=k: e.matmul(ps[pm][:, :n], lhsT=w_pq[:, k, nc_ * 128:(nc_ + 1) * 128], rhs=h2[:, k, :n], start=(k == 0), stop=(k == 7)),
                               r=["cb_h2", "cb_wpq"], w=["cb_ps%d" % pm])
                        evac(ob[o][:, :n], ps[pm][:, :n], r=["cb_ps%d" % pm], w=["cb_ob%d" % o])
                        DM(lambda e: e.dma_start(out=PQT[nc_, :, t0:t0 + n], in_=ob[o][:, :n]), r=["cb_ob%d" % o], w=[("PQT", S.dma_next)])
                    for nc_ in range(16):
                        do_pq(nc_)
                for gi, (t0, n) in enumerate(GROUPS):
                    if last and gi == 0:
                        continue
                    do_group(gi, t0, n)
            S.barrier()

        def phase_cast_tables():
            with contextlib.ExitStack() as st:
                tb = [sb(st, "ct_t%d" % i, [128, 8, D], BF16) for i in range(4)]
                cnt = [0]

                def do_chunk(l, src, o, c):
                    cnt[0] += 1
                    b = cnt[0] % 4
                    DM(lambda e: e.dma_start(out=tb[b][:], in_=src[c * 1024:(c + 1) * 1024, :].rearrange("(p r) d -> p r d", r=8)),
                       w=["ct_t%d" % b], q="gpsimd")
                    DM(lambda e: e.dma_start(out=DUB[l][c * 1024:(c + 1) * 1024, o:o + D].rearrange("(p r) d -> p r d", r=8), in_=tb[b][:]),
                       r=["ct_t%d" % b], w=[("cast", S.dma_next)])
                for l in range(DEPTH):
                    for (src, o) in ((P["peer_down%d" % l], 0), (P["peer_up%d" % l], D)):
                        for c in range(16):
                            do_chunk(l, src, o, c)
            S.barrier()

        def phase_peer(l, last):
            with contextlib.ExitStack() as st:
                keysT = sb(st, "pe_keysT", [128, 16, 128], BF16)
                iotf = sb(st, "pe_iotf", [128, 16], F32)
                pq = sb(st, "pe_pq", [128, 16, 128], BF16)
                s_sb = sb(st, "pe_s", [128, 16, 128], F32)
                s2 = sb(st, "pe_s2", [128, 16, 128], F32)
                kraw = s2
                sv = sb(st, "pe_sv", [128, 16, 16], F32)
                si = sb(st, "pe_si", [128, 16, 16], U32)
                sif = sb(st, "pe_sif", [128, 16, 16], F32)
                cand = sb(st, "pe_cand", [128, 8, 256], F32)
                cand2 = sb(st, "pe_cand2", [128, 8, 256], F32)
                eq = cand2[:].rearrange("p h (a b) -> p h a b", b=16)
                ts = sb(st, "pe_ts", [128, 8, 16], F32)
                pos = sb(st, "pe_pos", [128, 8, 16], U32)
                pa_i = sb(st, "pe_pai", [128, 8, 16], U32)
                pb_i = sb(st, "pe_pbi", [128, 8, 16], U32)
                pa_f = sb(st, "pe_paf", [128, 8, 16], F32)
                pb_f = sb(st, "pe_pbf", [128, 8, 16], F32)
                If = sb(st, "pe_If", [128, 8, 16], F32)
                Jf = sb(st, "pe_Jf", [128, 8, 16], F32)
                ef = sb(st, "pe_ef", [128, 8, 16], F32)
                gsum = sb(st, "pe_gsum", [128, 8], F32)
                h2T = sb(st, "pe_h2T", [128, 8, 128], BF16)
                eidx_ = [sb(st, "pe_eidx%d" % i, [128, 128], U32) for i in range(2)]
                gate_ = [sb(st, "pe_gate%d" % i, [128, 8, 16], F32) for i in range(2)]
                h2tok_ = [sb(st, "pe_h2tok%d" % i, [128, D], BF16) for i in range(2)]
                xt_ = [sb(st, "pe_xt%d" % i, [128, 8, 128], F32) for i in range(2)]
                NR = 24
                SG = 8
                rows = [sb(st, "pe_rows%d" % i, [128, 2 * D], BF16) for i in range(NR)]
                junk = [sb(st, "pe_junk%d" % i, [128, D], BF16) for i in range(4)]
                junk2 = sb(st, "pe_junk2", [128, D], BF16)
                junk3 = sb(st, "pe_junk3", [128, D], BF16)
                diag8 = [sb(st, "pe_diag%d" % i, [128, SG, 128], BF16) for i in range(3)]
                a_sb = sb(st, "pe_a", [128, 128], F32)
                g1 = sb(st, "pe_g1", [128, 128], F32)
                g2 = sb(st, "pe_g2", [128, 128], F32)
                act = sb(st, "pe_act", [128, 128], F32)
                acc = sb(st, "pe_acc", [128, D], F32)
                ps_sc = [pst(st, "pe_psc%d" % i, [128, 4, 128], F32) for i in range(4)]
                ps_h2 = pst(st, "pe_ph2", [128, 8, 128], BF16)
                ps_acc = [pst(st, "pe_pacc%d" % i, [128, 512], F32) for i in range(2)]

                G(lambda e: e.iota(iotf[:], [[1, 16]], base=0, channel_multiplier=0, allow_small_or_imprecise_dtypes=True), w=["pe_iotf"])
                DM(lambda e: e.dma_start(out=kraw[:], in_=P["peer_keys"][l].rearrange("h p k d -> k (h p) d")), w=["pe_s2"])
                for hp in range(16):
                    TE(lambda e, hp=hp: e.transpose(ps_sc[hp // 4][:, hp % 4, :], kraw[:, hp, :], ident_f[:]), r=["pe_s2", "ident_f"], w=["pe_psc%d" % (hp // 4)])
                for b4 in range(4):
                    evac(keysT[:, b4 * 4:(b4 + 1) * 4, :], ps_sc[b4][:], r=["pe_psc%d" % b4], w=["pe_keysT"])
                rcnt = [0]
                gcnt = [0]

                def prep_tile(ti):
                    c0 = ti * 128
                    tb = ti % 2
                    eidx, gate, h2tok, xt = eidx_[tb], gate_[tb], h2tok_[tb], xt_[tb]
                    kE, kG, kH, kX = "pe_eidx%d" % tb, "pe_gate%d" % tb, "pe_h2tok%d" % tb, "pe_xt%d" % tb
                    DM(lambda e: e.dma_start(out=pq[:], in_=PQT[:, :, c0:c0 + 128].rearrange("n p t -> p n t")), w=["pe_pq"])
                    DM(lambda e: e.dma_start(out=h2T[:], in_=H2T[:, :, c0:c0 + 128].rearrange("k p t -> p k t")), w=["pe_h2T"])
                    DM(lambda e: e.dma_start(out=xt[:], in_=XT[:, :, c0:c0 + 128].rearrange("k p t -> p k t")), w=[kX])
                    for hp in range(16):
                        TE(lambda e, hp=hp: e.matmul(ps_sc[hp // 4][:, hp % 4, :], lhsT=pq[:, hp, :], rhs=keysT[:, hp, :], start=True, stop=True),
                           r=["pe_pq", "pe_keysT"], w=["pe_psc%d" % (hp // 4)])
                    for b4 in range(4):
                        evac(s_sb[:, b4 * 4:(b4 + 1) * 4, :], ps_sc[b4][:], r=["pe_psc%d" % b4], w=["pe_s"])
                    yield
                    for k in range(8):
                        TE(lambda e, k=k: e.transpose(ps_h2[:, k, :], h2T[:, k, :], ident_b[:]), r=["pe_h2T", "ident_b"], w=["pe_ph2"])
                    A(lambda e: e.activation(out=h2tok[:], in_=ps_h2[:].rearrange("p k d -> p (k d)"), func=AF.Copy), r=["pe_ph2"], w=[kH])
                    yield
                    for hp in range(16):
                        V(lambda e, hp=hp: e.max(out=sv[:, hp, 0:8], in_=s_sb[:, hp, :]), r=["pe_s"], w=["pe_sv"])
                        V(lambda e, hp=hp: e.max_index(out=si[:, hp, 0:8], in_max=sv[:, hp, 0:8], in_values=s_sb[:, hp, :]), r=["pe_s", "pe_sv"], w=["pe_si"])
                        V(lambda e, hp=hp: e.match_replace(out=s2[:, hp, :], in_to_replace=sv[:, hp, 0:8], in_values=s_sb[:, hp, :], imm_value=-1e30),
                          r=["pe_s", "pe_sv"], w=["pe_s2"])
                        V(lambda e, hp=hp: e.max(out=sv[:, hp, 8:16], in_=s2[:, hp, :]), r=["pe_s2"], w=["pe_sv"])
                        V(lambda e, hp=hp: e.max_index(out=si[:, hp, 8:16], in_max=sv[:, hp, 8:16], in_values=s2[:, hp, :]), r=["pe_s2", "pe_sv"], w=["pe_si"])
                        yield
                    V(lambda e: e.tensor_copy(sif[:], si[:]), r=["pe_si"], w=["pe_sif"])
                    svv = sv[:].rearrange("p (h q) a -> p h q a", q=2)
                    sfv = sif[:].rearrange("p (h q) a -> p h q a", q=2)
                    candv = cand[:].rearrange("p h (a b) -> p h a b", b=16)
                    V(lambda e: e.tensor_tensor(out=candv, in0=svv[:, :, 0, :].unsqueeze(3).to_broadcast([128, 8, 16, 16]),
                                                in1=svv[:, :, 1, :].unsqueeze(2).to_broadcast([128, 8, 16, 16]), op=ALU.add), r=["pe_sv"], w=["pe_cand"])
                    for h in range(8):
                        V(lambda e, h=h: e.max(out=ts[:, h, 0:8], in_=cand[:, h, :]), r=["pe_cand"], w=["pe_ts"])
                        V(lambda e, h=h: e.max_index(out=pos[:, h, 0:8], in_max=ts[:, h, 0:8], in_values=cand[:, h, :]), r=["pe_cand", "pe_ts"], w=["pe_pos"])
                        V(lambda e, h=h: e.match_replace(out=cand2[:, h, :], in_to_replace=ts[:, h, 0:8], in_values=cand[:, h, :], imm_value=-1e30),
                          r=["pe_cand", "pe_ts"], w=["pe_cand2"])
                        V(lambda e, h=h: e.max(out=ts[:, h, 8:16], in_=cand2[:, h, :]), r=["pe_cand2"], w=["pe_ts"])
                        V(lambda e, h=h: e.max_index(out=pos[:, h, 8:16], in_max=ts[:, h, 8:16], in_values=cand2[:, h, :]), r=["pe_cand2", "pe_ts"], w=["pe_pos"])
                        yield
                    V(lambda e: e.tensor_single_scalar(pa_i[:], pos[:], 4, op=ALU.logical_shift_right), r=["pe_pos"], w=["pe_pai"])
                    V(lambda e: e.tensor_single_scalar(pb_i[:], pos[:], 15, op=ALU.bitwise_and), r=["pe_pos"], w=["pe_pbi"])
                    V(lambda e: e.tensor_copy(pa_f[:], pa_i[:]), r=["pe_pai"], w=["pe_paf"])
                    V(lambda e: e.tensor_copy(pb_f[:], pb_i[:]), r=["pe_pbi"], w=["pe_pbf"])
                    iob = iotf[:].unsqueeze(1).unsqueeze(1).to_broadcast([128, 8, 16, 16])
                    for (pf, half, dst, kd) in ((pa_f, 0, If, "pe_If"), (pb_f, 1, Jf, "pe_Jf")):
                        V(lambda e, pf=pf: e.tensor_tensor(out=eq, in0=iob, in1=pf[:].unsqueeze(3).to_broadcast([128, 8, 16, 16]), op=ALU.is_equal),
                          r=["pe_iotf", "pe_paf", "pe_pbf"], w=["pe_cand2"])
                        V(lambda e, half=half: e.tensor_tensor(out=eq, in0=eq, in1=sfv[:, :, half, :].unsqueeze(2).to_broadcast([128, 8, 16, 16]), op=ALU.mult),
                          r=["pe_cand2", "pe_sif"], w=["pe_cand2"])
                        V(lambda e, dst=dst: e.tensor_reduce(out=dst[:], in_=eq, axis=AX.X, op=ALU.add), r=["pe_cand2"], w=[kd])
                        yield
                    V(lambda e: e.scalar_tensor_tensor(out=ef[:], in0=If[:], scalar=128.0, in1=Jf[:], op0=ALU.mult, op1=ALU.add), r=["pe_If", "pe_Jf"], w=["pe_ef"])
                    V(lambda e: e.tensor_copy(eidx[:], ef[:].rearrange("p h k -> p (h k)")), r=["pe_ef"], w=[kE])
                    yield
                    V(lambda e: e.tensor_tensor(out=gate[:], in0=ts[:], in1=ts[:, :, 0:1].to_broadcast([128, 8, 16]), op=ALU.subtract), r=["pe_ts"], w=[kG])
                    A(lambda e: e.activation(out=gate[:], in_=gate[:], func=AF.Exp), r=[kG], w=[kG])
                    V(lambda e: e.tensor_reduce(out=gsum[:], in_=gate[:], axis=AX.X, op=ALU.add), r=[kG], w=["pe_gsum"])
                    V(lambda e: e.reciprocal(gsum[:], gsum[:]), r=["pe_gsum"], w=["pe_gsum"])
                    V(lambda e: e.tensor_tensor(out=gate[:], in0=gate[:], in1=gsum[:].unsqueeze(2).to_broadcast([128, 8, 16]), op=ALU.mult), r=[kG, "pe_gsum"], w=[kG])

                def run_tile(ti, nxt=None):
                    c0 = ti * 128
                    r_ = 1 if ti < 2 else 0
                    tb = ti % 2
                    eidx, gate, h2tok, xt = eidx_[tb], gate_[tb], h2tok_[tb], xt_[tb]
                    kE, kG, kH, kX = "pe_eidx%d" % tb, "pe_gate%d" % tb, "pe_h2tok%d" % tb, "pe_xt%d" % tb
                    gatev = gate[:].rearrange("p h k -> p (h k)")

                    gbufs = {}

                    def stage1(g0):
                        bufs = []
                        for sl in range(g0, g0 + SG):
                            rcnt[0] += 1
                            b = rcnt[0] % NR
                            jb = rcnt[0] % 4
                            bufs.append(b)
                            DM(lambda e, b=b, sl=sl: e.indirect_dma_start(out=rows[b][:, :], out_offset=None, in_=DUB[l],
                                                                          in_offset=bass.IndirectOffsetOnAxis(ap=eidx[:, sl:sl + 1], axis=0)),
                               r=[kE], w=["pe_rows%d" % b], q="gpsimd")
                            V(lambda e, b=b, jb=jb: e.tensor_tensor(out=junk[jb][:], in0=rows[b][:, 0:D], in1=h2tok[:], op=ALU.mult),
                              r=["pe_rows%d" % b, kH], w=["pe_junk%d" % jb])
                            A(lambda e, jb=jb, sl=sl: e.activation(out=junk2[:], in_=junk[jb][:], func=AF.Copy, accum_out=a_sb[:, sl:sl + 1]),
                              r=["pe_junk%d" % jb], w=[("pe_a", g0), "pe_junk2"])
                        gbufs[g0] = bufs

                    def stage2(g0):
                        bufs = gbufs[g0]
                        gcnt[0] += 1
                        dg = gcnt[0] % 3
                        gs = slice(g0, g0 + SG)
                        V(lambda e: e.tensor_tensor(out=g1[:, gs], in0=a_sb[:, gs], in1=a_sb[:, gs], op=ALU.mult), r=[("pe_a", g0)], w=[("pe_g1", g0)])
                        V(lambda e: e.tensor_scalar(g1[:, gs], g1[:, gs], 0.044715, 1.0, op0=ALU.mult, op1=ALU.add), r=[("pe_g1", g0)], w=[("pe_g1", g0)])
                        V(lambda e: e.tensor_tensor(out=g1[:, gs], in0=g1[:, gs], in1=a_sb[:, gs], op=ALU.mult), r=[("pe_g1", g0), ("pe_a", g0)], w=[("pe_g1", g0)])
                        A(lambda e: e.activation(out=g2[:, gs], in_=g1[:, gs], func=AF.Sigmoid, scale=1.5957691216057308), r=[("pe_g1", g0)], w=[("pe_g2", g0)])
                        V(lambda e: e.tensor_tensor(out=g2[:, gs], in0=g2[:, gs], in1=a_sb[:, gs], op=ALU.mult), r=[("pe_g2", g0), ("pe_a", g0)], w=[("pe_g2", g0)])
                        V(lambda e: e.tensor_tensor(out=act[:, gs], in0=g2[:, gs], in1=gatev[:, gs], op=ALU.mult), r=[("pe_g2", g0), kG], w=[("pe_act", g0)])
                        V(lambda e: e.tensor_tensor(out=diag8[dg][:], in0=ident_b[:].unsqueeze(1).to_broadcast([128, SG, 128]),
                                                    in1=act[:, gs].unsqueeze(2).to_broadcast([128, SG, 128]), op=ALU.mult),
                          r=["ident_b", ("pe_act", g0)], w=["pe_diag%d" % dg])
                        for i_, sl in enumerate(range(g0, g0 + SG)):
                            b = bufs[i_]
                            for hf in range(2):
                                TE(lambda e, hf=hf, b=b, sl=sl, i_=i_: e.matmul(ps_acc[hf][:, :], lhsT=diag8[dg][:, i_, :], rhs=rows[b][:, D + hf * 512:D + (hf + 1) * 512],
                                                                                start=(sl == 0), stop=(sl == 127)),
                                   r=["pe_diag%d" % dg, "pe_rows%d" % b], w=["pe_pacc%d" % hf])
                    glist = list(range(0, 128, SG))

                    def advance(nsteps):
                        if nxt is not None:
                            for _ in range(nsteps):
                                next(nxt, None)
                    stage1(glist[0])
                    for gi_, g0 in enumerate(glist):
                        if gi_ + 1 < len(glist):
                            stage1(glist[gi_ + 1])
                        stage2(g0)
                        advance(3)
                    if nxt is not None:
                        for _ in nxt:
                            pass
                    for hf in range(2):
                        evac(acc[:, hf * 512:(hf + 1) * 512], ps_acc[hf][:, :], r=["pe_pacc%d" % hf], w=[("pe_acc", hf)])
                    for k in range(8):
                        TE(lambda e, k=k: e.transpose(ps_sc[k // 4][:, k % 4, :], acc[:, k * 128:(k + 1) * 128], ident_f[:]), r=[("pe_acc", k // 4), "ident_f"], w=["pe_psc%d" % (k // 4)])
                    for k in range(8):
                        V(lambda e, k=k: e.scalar_tensor_tensor(out=xt[:, k, :], in0=ps_sc[k // 4][:, k % 4, :], scalar=gt2[:, k, r_:r_ + 1], in1=xt[:, k, :],
                                                               op0=ALU.mult, op1=ALU.add),
                          r=["pe_psc%d" % (k // 4), "gt2", kX], w=[kX])
                    DM(lambda e: e.dma_start(out=XT[:, :, c0:c0 + 128].rearrange("k p t -> p k t"), in_=xt[:]), r=[kX], w=[("XTw", ti)])

                tiles = [ti for ti in range(2 if last else 0, T // 128) if peer_tiles is None or ti in peer_tiles]
                if tiles:
                    for _ in prep_tile(tiles[0]):
                        pass
                for i, ti in enumerate(tiles):
                    run_tile(ti, prep_tile(tiles[i + 1]) if i + 1 < len(tiles) else None)
            S.barrier()

        def phase_final():
            with contextlib.ExitStack() as st:
                fg = sb(st, "fn_g", [128, 8], F32)
                xg = sb(st, "fn_xg", [128, 8, 512], F32)
                sq = sb(st, "fn_sq", [128, 8, 512], F32)
                rstd = sb(st, "fn_rstd", [128, 512], F32)
                ot = [sb(st, "fn_o%d" % i, [128, D], F32) for i in range(2)]
                ps_st = pst(st, "fn_pst", [128, 512], F32)
                ps = [pst(st, "fn_ps%d" % i, [128, 4, 128], F32) for i in range(4)]
                DM(lambda e: e.dma_start(out=fg[:], in_=P["final_g"].rearrange("(n p) -> p n", p=128), allow_slow_non_contiguous=True), w=["fn_g"])
                cnt = [0]

                def do_group(gi, t0, n):
                    DM(lambda e: e.dma_start(out=xg[:, :, :n], in_=XT[:, :, t0:t0 + n].rearrange("k p t -> p k t")), w=["fn_xg"])
                    rms_stats(xg, 8, n, D, sq, ps_st, rstd, ["fn_xg"], "fn_sq", "fn_pst", "fn_rstd")

                    def do_k(k):
                        V(lambda e: e.tensor_tensor(out=sq[:, k, :n], in0=xg[:, k, :n], in1=rstd[:, :n], op=ALU.mult), r=["fn_xg", "fn_rstd"], w=["fn_sq"])
                        V(lambda e: e.tensor_scalar(xg[:, k, :n], sq[:, k, :n], fg[:, k:k + 1], None, op0=ALU.mult), r=["fn_sq", "fn_g"], w=["fn_xg"])
                    for k in range(8):
                        do_k(k)

                    def do_j(j):
                        cnt[0] += 1
                        ob_ = cnt[0] % 2
                        for half in range(2):
                            pi = (cnt[0] * 2 + half) % 4
                            for kk in range(4):
                                k = half * 4 + kk
                                TE(lambda e, k=k, kk=kk, pi=pi: e.transpose(ps[pi][:, kk, :], xg[:, k, j * 128:(j + 1) * 128], ident_f[:]),
                                   r=["fn_xg", "ident_f"], w=["fn_ps%d" % pi])
                            evac(ot[ob_][:, half * 512:(half + 1) * 512], ps[pi][:].rearrange("p a b -> p (a b)"), r=["fn_ps%d" % pi], w=[("fn_o", ob_, half)])
                        row0 = t0 - CTX + j * 128
                        DM(lambda e: e.dma_start(out=out[row0:row0 + 128, :], in_=ot[ob_][:]), r=[("fn_o", ob_, 0), ("fn_o", ob_, 1)], w=[("out", row0)])
                    for j in range(n // 128):
                        do_j(j)
                for gi, (t0, n) in enumerate(GROUPS):
                    if gi == 0:
                        continue
                    do_group(gi, t0, n)
            S.barrier()

        phase_load_x()
        if stop_after is None or stop_after[0] in ("peer", "castonly"):
            phase_cast_tables()
        for l in range(DEPTH):
            if stop_after is not None and stop_after[0] == "castonly":
                break
            last = l == DEPTH - 1
            phase_mod(l)
            phase_proj(l, last)
            if stop_after == ("proj", l):
                break
            phase_attn(l, last)
            phase_fourier(l, last)
            phase_pool(l, last)
            if stop_after == ("mix", l):
                break
            phase_combine(l, last)
            if stop_after == ("combine", l):
                break
            phase_peer(l, last)
            if stop_after == ("peer", l):
                break
        if stop_after is None:
            phase_final()
        S.emit()
    return nc, S


def make_in_maps(inputs, cores):
    consts = _const_tables()
    maps = []
    for b in cores:
        m = {"x": np.ascontiguousarray(inputs["x"][b]), "ctx": np.ascontiguousarray(inputs["ctx"][b]),
             "cvec": np.ascontiguousarray(np.stack([inputs["c"][b], inputs["c_ctx"]], axis=0))}
        for k in PARAM_SHAPES:
            if k.startswith("peer_down") or k.startswith("peer_up"):
                m[k] = np.ascontiguousarray(inputs[k[:-1]][int(k[-1])])
            else:
                m[k] = np.ascontiguousarray(inputs[k])
        m.update(consts)
        maps.append(m)
    return maps


def kernel(**inputs):
    inputs = {k: np.asarray(v) for k, v in inputs.items()}
    nc, _ = build()
    cores = list(range(8))
    res = run_bass_kernel_spmd(nc, make_in_maps(inputs, cores), core_ids=cores)
    return np.stack([res.results[i]["out"] for i in cores], axis=0).astype(np.float32)
```

```python
import contextlib
import math
import numpy as np
import ml_dtypes
import concourse.bass as bass
import concourse.mybir as mybir
from concourse.bass_utils import run_bass_kernel_spmd

F32 = mybir.dt.float32
BF16 = mybir.dt.bfloat16
U32 = mybir.dt.uint32
I32 = mybir.dt.int32
AF = mybir.ActivationFunctionType
ALU = mybir.AluOpType
AX = mybir.AxisListType

ENGS = ("sync", "scalar", "vector", "gpsimd", "tensor")
N_DMA_SEMS = 90

D = 1024
SEQ = 4096
CTX = 256
T = SEQ + CTX
DEPTH = 2
NH = 8
EPS = 1e-6
ATT_SCALE = 96 ** -0.5
IN_W = 4000
NEXP = 16384


class Sched:
    def __init__(self, nc):
        self.nc = nc
        self.ops = {e: [] for e in ENGS}
        self.count = {e: 0 for e in ENGS}
        self.known = {e: {} for e in ENGS}
        self.last_w = {}
        self.readers = {}
        self.dma_next = 0
        self.dma_target = [0] * N_DMA_SEMS
        self.pending = {e: [] for e in ENGS}
        self.n_instr = 0

    def _need(self, eng, tok, waits):
        if tok is None:
            return
        kind, src, val = tok
        if kind == "e" and src == eng and eng == "tensor":
            return
        k = (kind, src)
        if self.known[eng].get(k, 0) >= val:
            return
        self.known[eng][k] = val
        waits.append(tok)

    def _deps(self, eng, r, w):
        waits = []
        for tok in self.pending[eng]:
            self._need(eng, tok, waits)
        self.pending[eng] = []
        for key in r:
            self._need(eng, self.last_w.get(key), waits)
        for key in w:
            self._need(eng, self.last_w.get(key), waits)
            for tok in self.readers.get(key, ()):
                self._need(eng, tok, waits)
        best = {}
        for kind, src, val in waits:
            k = (kind, src)
            if best.get(k, 0) < val:
                best[k] = val
        return [(k[0], k[1], v) for k, v in best.items()]

    def _commit(self, tok, r, w):
        for key in r:
            self.readers.setdefault(key, []).append(tok)
        for key in w:
            self.last_w[key] = tok
            self.readers[key] = []

    def op(self, eng, fn, r=(), w=()):
        waits = self._deps(eng, r, w)
        self.count[eng] += 1
        tok = ("e", eng, self.count[eng])
        self._commit(tok, r, w)
        self.ops[eng].append(("op", fn, waits, None))
        self.n_instr += 1 + len(waits)
        return tok

    def dma(self, eng, fn, r=(), w=()):
        s = self.dma_next % N_DMA_SEMS
        self.dma_next += 1
        waits = self._deps(eng, r, w)
        prev = self.dma_target[s]
        if prev > 0:
            k = ("d", s)
            if self.known[eng].get(k, 0) < prev:
                self.known[eng][k] = prev
                waits.append(("d", s, prev))
        self.dma_target[s] = prev + 16
        tok = ("d", s, prev + 16)
        self._commit(tok, r, w)
        self.ops[eng].append(("dma", fn, waits, s))
        self.n_instr += 1 + len(waits)
        return tok

    def barrier(self):
        waits = []
        for e in ENGS:
            if e != "sync" and self.count[e] > 0:
                self._need("sync", ("e", e, self.count[e]), waits)
        for s in range(N_DMA_SEMS):
            if self.dma_target[s] > 0:
                self._need("sync", ("d", s, self.dma_target[s]), waits)
        self.count["sync"] += 1
        tok = ("e", "sync", self.count["sync"])
        self.ops["sync"].append(("op", lambda e: e.nop(), waits, None))
        self.n_instr += 1 + len(waits)
        for e in ENGS:
            if e != "sync":
                self.pending[e].append(tok)
                for e2 in ENGS:
                    self.known[e][("e", e2)] = max(self.known[e].get(("e", e2), 0),
                                                   self.count[e2] if e2 != "sync" else 0)
                for s in range(N_DMA_SEMS):
                    self.known[e][("d", s)] = self.dma_target[s]
                self.known[e].pop(("e", "sync"), None)
        self.last_w = {}
        self.readers = {}

    def emit(self):
        nc = self.nc
        with contextlib.ExitStack() as st:
            esem = {e: st.enter_context(nc.semaphore("c_" + e)) for e in ENGS}
            dsem = [st.enter_context(nc.semaphore("d%d" % i)) for i in range(N_DMA_SEMS)]
            block = st.enter_context(nc.Block())

            def mk(ename):
                def body(eng):
                    for kind, fn, waits, s in self.ops[ename]:
                        for wk, src, val in waits:
                            eng.wait_ge(esem[src] if wk == "e" else dsem[src], val)
                        ins = fn(eng)
                        if kind == "op":
                            ins.then_inc(esem[ename], 1)
                        else:
                            ins.then_inc(dsem[s], 16)
                    if ename == "sync":
                        for i in range(N_DMA_SEMS):
                            if self.dma_target[i] > 0:
                                eng.wait_ge(dsem[i], self.dma_target[i])
                        for e2 in ENGS:
                            if e2 != "sync" and self.count[e2] > 0:
                                eng.wait_ge(esem[e2], self.count[e2])
                return body

            block.sync(mk("sync"))
            block.scalar(mk("scalar"))
            block.vector(mk("vector"))
            block.gpsimd(mk("gpsimd"))
            block.tensor(mk("tensor"))


def _const_tables():
    c = {}
    half = 16
    inv_freq = (10000.0 ** (-np.arange(0, half, 2, dtype=np.float32) / half)).astype(np.float32)
    t = np.arange(SEQ)
    r = (t // 64).astype(np.float32)
    cl = (t % 64).astype(np.float32)
    ang_r = (r[None, :] * inv_freq[:, None]).astype(np.float32)
    ang_c = (cl[None, :] * inv_freq[:, None]).astype(np.float32)
    ang = np.concatenate([ang_r, ang_r, ang_c, ang_c], axis=0)
    c["ropecos"] = np.cos(ang).astype(np.float32)
    c["ropesin"] = np.sin(ang).astype(np.float32)
    def dft(n):
        k = np.arange(n, dtype=np.int64)
        kk = (k[:, None] * k[None, :]) % n
        a = 2.0 * np.pi * kk.astype(np.float64) / n
        return np.cos(a), np.sin(a)
    cc, sc = dft(256)
    c["ccs"] = np.concatenate([cc, -sc], axis=1).astype(ml_dtypes.bfloat16)
    c["cl256"] = cc.astype(ml_dtypes.bfloat16)
    c["sl256"] = sc.astype(ml_dtypes.bfloat16)
    cL, sL = dft(SEQ)
    c["cl4096"] = cL.astype(ml_dtypes.bfloat16)
    c["sl4096"] = sL.astype(ml_dtypes.bfloat16)
    for L, name in ((SEQ, "icnt4096"), (CTX, "icnt256")):
        tt = np.arange(L)
        tab = np.zeros((2, 128, L), np.float32)
        for gi, w in enumerate((2, 4, 8, 16)):
            lo = np.clip(tt - w // 2, 0, L)
            hi = np.clip(tt - w // 2 + w, 0, L)
            tab[gi // 2, (gi % 2) * 64:(gi % 2) * 64 + 64, :] = (1.0 / (hi - lo).astype(np.float32))[None, :]
        c[name] = tab
    return c


CONST_SHAPES = {
    "ropecos": ([32, SEQ], F32), "ropesin": ([32, SEQ], F32),
    "ccs": ([256, 512], BF16), "cl256": ([256, 256], BF16), "sl256": ([256, 256], BF16),
    "cl4096": ([SEQ, SEQ], BF16), "sl4096": ([SEQ, SEQ], BF16),
    "icnt4096": ([2, 128, SEQ], F32), "icnt256": ([2, 128, CTX], F32),
}

PARAM_SHAPES = {
    "w_mod": [DEPTH, D, 6 * D], "b_mod": [DEPTH, 6 * D], "norm1_g": [DEPTH, D], "norm2_g": [DEPTH, D],
    "w_in": [DEPTH, D, IN_W], "b_gate": [DEPTH, 3 * D], "q_norm_g": [DEPTH, 256], "w_uq": [DEPTH, 256, 768],
    "kv_norm_g": [DEPTH, 128], "w_ukv": [DEPTH, 128, 1024], "w_oa": [DEPTH, 512, D], "w_ob": [DEPTH, 256, D],
    "w_grp": [DEPTH, 4, 64, 64], "pool_scale": [DEPTH, 256], "w_oc": [DEPTH, 256, D], "w_out": [DEPTH, D, D],
    "w_pq": [DEPTH, D, 2048], "peer_keys": [DEPTH, 8, 2, 128, 128], "peer_down0": [NEXP, D], "peer_down1": [NEXP, D],
    "peer_up0": [NEXP, D], "peer_up1": [NEXP, D], "final_g": [D],
}

GROUPS = [(0, CTX)] + [(CTX + 512 * g, 512) for g in range(SEQ // 512)]


def build(stop_after=None, dbg=(), peer_tiles=None):
    nc = bass.Bass("TRN2", target_bir_lowering=False)
    S = Sched(nc)

    def dram_in(name, shape, dt=F32):
        return nc.dram_tensor(name, shape, dt, kind="ExternalInput").ap()

    def dram_scratch(name, shape, dt):
        kind = "ExternalOutput" if name in dbg else "Internal"
        return nc.dram_tensor(name, shape, dt, kind=kind).ap()

    xin = dram_in("x", [SEQ, D])
    ctxin = dram_in("ctx", [CTX, D])
    cvec = dram_in("cvec", [2, D])
    P = {k: dram_in(k, v) for k, v in PARAM_SHAPES.items()}
    C = {k: dram_in(k, v[0], v[1]) for k, v in CONST_SHAPES.items()}
    out = nc.dram_tensor("out", [SEQ, D], F32, kind="ExternalOutput").ap()

    XT = dram_scratch("XT", [8, 128, T], F32)
    GT = dram_scratch("GT", [24, 128, T], BF16)
    QT = dram_scratch("QT", [NH, 96, T], BF16)
    KT = dram_scratch("KT", [NH, 96, T], BF16)
    VV = dram_scratch("VV", [T, 512], BF16)
    ZFT = dram_scratch("ZFT", [2, 128, T], BF16)
    ZPT = dram_scratch("ZPT", [2, 128, T], F32)
    ATT = dram_scratch("ATT", [4, 128, T], BF16)
    YFT = dram_scratch("YFT", [2, 128, T], BF16)
    YCT = dram_scratch("YCT", [2, 128, T], BF16)
    H2T = dram_scratch("H2T", [8, 128, T], BF16)
    PQT = dram_scratch("PQT", [16, 128, T], BF16)
    DUB = [dram_scratch("DUB%d" % i, [NEXP, 2 * D], BF16) for i in range(DEPTH)]

    V = lambda fn, r=(), w=(): S.op("vector", fn, r, w)
    A = lambda fn, r=(), w=(): S.op("scalar", fn, r, w)
    G = lambda fn, r=(), w=(): S.op("gpsimd", fn, r, w)
    TE = lambda fn, r=(), w=(): S.op("tensor", fn, r, w)
    DM = lambda fn, r=(), w=(), q="sync": S.dma(q, fn, r, w)

    rr = [0]

    def evac(out_ap, in_ap, r, w, scale=None):
        rr[0] += 1
        if rr[0] % 2 == 0:
            if scale is None:
                V(lambda e: e.tensor_copy(out_ap, in_ap), r, w)
            else:
                V(lambda e: e.tensor_single_scalar(out_ap, in_ap, float(scale), op=ALU.mult), r, w)
        else:
            A(lambda e: e.activation(out=out_ap, in_=in_ap, func=AF.Copy,
                                     scale=1.0 if scale is None else float(scale)), r, w)

    with contextlib.ExitStack() as top:
        uniq = [0]

        def sb(st, name, shape, dt):
            uniq[0] += 1
            return st.enter_context(nc.sbuf_tensor("%s_%d" % (name, uniq[0]), shape, dt))

        def pst(st, name, shape, dt):
            uniq[0] += 1
            return st.enter_context(nc.psum_tensor("%s_%d" % (name, uniq[0]), shape, dt))

        ident_f = sb(top, "ident_f", [128, 128], F32)
        ident_b = sb(top, "ident_b", [128, 128], BF16)
        ones_f = sb(top, "ones_f", [128, 128], F32)
        ones_b = sb(top, "ones_b", [128, 128], BF16)
        iot = sb(top, "iot", [128, 128], F32)
        gm1 = sb(top, "gm1", [128, 8, 2], F32)
        sh1 = sb(top, "sh1", [128, 8, 2], F32)
        gt1 = sb(top, "gt1", [128, 8, 2], F32)
        gm2 = sb(top, "gm2", [128, 8, 2], F32)
        sh2 = sb(top, "sh2", [128, 8, 2], F32)
        gt2 = sb(top, "gt2", [128, 8, 2], F32)
        epsb = sb(top, "epsb", [128, 1], F32)

        G(lambda e: e.iota(iot[:], [[1, 128]], base=0, channel_multiplier=-1,
                           allow_small_or_imprecise_dtypes=True), w=["iot"])
        V(lambda e: e.tensor_single_scalar(ident_f[:], iot[:], 0.0, op=ALU.is_equal), r=["iot"], w=["ident_f"])
        V(lambda e: e.tensor_single_scalar(ident_b[:], iot[:], 0.0, op=ALU.is_equal), r=["iot"], w=["ident_b"])
        V(lambda e: e.memset(ones_f[:], 1.0), w=["ones_f"])
        V(lambda e: e.memset(ones_b[:], 1.0), w=["ones_b"])
        V(lambda e: e.memset(epsb[:], EPS), w=["epsb"])

        def phase_load_x():
            with contextlib.ExitStack() as st:
                xt = [sb(st, "ld_x%d" % i, [128, D], F32) for i in range(2)]
                xo = [sb(st, "ld_o%d" % i, [128, 8, 128], F32) for i in range(2)]
                ps = [pst(st, "ld_ps%d" % i, [128, 4, 128], F32) for i in range(4)]
                ntile = T // 128
                for ti in range(ntile):
                    b = ti % 2
                    src = ctxin[ti * 128:(ti + 1) * 128, :] if ti < 2 else xin[(ti - 2) * 128:(ti - 1) * 128, :]
                    DM(lambda e, b=b, src=src: e.dma_start(out=xt[b][:], in_=src), w=["ld_x%d" % b])
                    for half in range(2):
                        pi = (ti * 2 + half) % 4
                        for j in range(4):
                            k = half * 4 + j
                            TE(lambda e, b=b, pi=pi, j=j, k=k: e.transpose(ps[pi][:, j, :], xt[b][:, k * 128:(k + 1) * 128], ident_f[:]),
                               r=["ld_x%d" % b, "ident_f"], w=["ld_ps%d" % pi])
                        evac(xo[b][:, half * 4:(half + 1) * 4, :], ps[pi][:], r=["ld_ps%d" % pi], w=[("ld_o", b, half)])
                    DM(lambda e, b=b, ti=ti: e.dma_start(out=XT[:, :, ti * 128:(ti + 1) * 128].rearrange("k p t -> p k t"), in_=xo[b][:]),
                       r=[("ld_o", b, 0), ("ld_o", b, 1)], w=[("XT", ti)])
            S.barrier()

        def phase_mod(l):
            with contextlib.ExitStack() as st:
                cT = sb(st, "md_c", [128, 2, 8], F32)
                scT = sb(st, "md_sc", [128, 8, 2], F32)
                bm = sb(st, "md_b", [128, 48], F32)
                g1 = sb(st, "md_g1", [128, 8], F32)
                g2 = sb(st, "md_g2", [128, 8], F32)
                modT = sb(st, "md_mod", [128, 48, 2], F32)
                wm = [sb(st, "md_w%d" % i, [128, 8, 512], F32) for i in range(2)]
                mps = pst(st, "md_ps", [128, 48, 2], F32)
                for r_ in range(2):
                    DM(lambda e, r_=r_: e.dma_start(out=cT[:, r_, :], in_=cvec[r_].rearrange("(k p) -> p k", p=128), allow_slow_non_contiguous=True), w=["md_c"])
                DM(lambda e: e.dma_start(out=bm[:], in_=P["b_mod"][l].rearrange("(n p) -> p n", p=128), allow_slow_non_contiguous=True), w=["md_b"])
                DM(lambda e: e.dma_start(out=g1[:], in_=P["norm1_g"][l].rearrange("(n p) -> p n", p=128), allow_slow_non_contiguous=True), w=["md_g1"])
                DM(lambda e: e.dma_start(out=g2[:], in_=P["norm2_g"][l].rearrange("(n p) -> p n", p=128), allow_slow_non_contiguous=True), w=["md_g2"])
                for r_ in range(2):
                    A(lambda e, r_=r_: e.activation(out=scT[:, :, r_], in_=cT[:, r_, :], func=AF.Silu), r=["md_c"], w=["md_sc"])
                for j in range(12):
                    b = j % 2
                    DM(lambda e, b=b, j=j: e.dma_start(out=wm[b][:], in_=P["w_mod"][l][:, j * 512:(j + 1) * 512].rearrange("(k p) n -> p k n", p=128)),
                       w=["md_w%d" % b])
                    for i in range(4):
                        n = j * 4 + i
                        for k in range(8):
                            TE(lambda e, b=b, i=i, k=k, n=n: e.matmul(mps[:, n, :], lhsT=wm[b][:, k, i * 128:(i + 1) * 128], rhs=scT[:, k, :],
                                                                      start=(k == 0), stop=(k == 7)),
                               r=["md_w%d" % b, "md_sc"], w=["md_ps"])
                V(lambda e: e.tensor_tensor(out=modT[:], in0=mps[:], in1=bm[:].unsqueeze(2).to_broadcast([128, 48, 2]), op=ALU.add),
                  r=["md_ps", "md_b"], w=["md_mod"])
                V(lambda e: e.tensor_copy(sh1[:], modT[:, 0:8, :]), r=["md_mod"], w=["sh1"])
                V(lambda e: e.scalar_tensor_tensor(out=gm1[:], in0=modT[:, 8:16, :], scalar=1.0, in1=g1[:].unsqueeze(2).to_broadcast([128, 8, 2]),
                                                   op0=ALU.add, op1=ALU.mult), r=["md_mod", "md_g1"], w=["gm1"])
                V(lambda e: e.tensor_copy(gt1[:], modT[:, 16:24, :]), r=["md_mod"], w=["gt1"])
                V(lambda e: e.tensor_copy(sh2[:], modT[:, 24:32, :]), r=["md_mod"], w=["sh2"])
                V(lambda e: e.scalar_tensor_tensor(out=gm2[:], in0=modT[:, 32:40, :], scalar=1.0, in1=g2[:].unsqueeze(2).to_broadcast([128, 8, 2]),
                                                   op0=ALU.add, op1=ALU.mult), r=["md_mod", "md_g2"], w=["gm2"])
                V(lambda e: e.tensor_copy(gt2[:], modT[:, 40:48, :]), r=["md_mod"], w=["gt2"])
            S.barrier()

        def rms_stats(src, nk, n, nfeat, sq, ps_stat, rstd, keys_r, ksq, kps, krstd):
            for k in range(nk):
                A(lambda e, k=k: e.activation(out=sq[:, k, :n], in_=src[:, k, :n], func=AF.Square), r=keys_r, w=[ksq])
            for k in range(nk):
                TE(lambda e, k=k: e.matmul(ps_stat[:, :n], lhsT=ones_f[:], rhs=sq[:, k, :n], start=(k == 0), stop=(k == nk - 1)),
                   r=[ksq, "ones_f"], w=[kps])
            A(lambda e: e.activation(out=rstd[:, :n], in_=ps_stat[:, :n], func=AF.Sqrt, scale=1.0 / nfeat, bias=epsb[:]),
              r=[kps, "epsb"], w=[krstd])
            V(lambda e: e.reciprocal(rstd[:, :n], rstd[:, :n]), r=[krstd], w=[krstd])

        def phase_proj(l, last):
            with contextlib.ExitStack() as st:
                w_in = sb(st, "pj_win", [128, 8, IN_W], BF16)
                w_krot = sb(st, "pj_wkrot", [128, 8, 32], BF16)
                wq_raw = sb(st, "pj_wqraw", [128, 2, 768], BF16)
                wq_rot = sb(st, "pj_wqrot", [128, 2, NH, 32], BF16)
                wkv = sb(st, "pj_wkv", [128, 1024], BF16)
                wv = sb(st, "pj_wv", [128, NH, 64], BF16)
                qg = sb(st, "pj_qg", [128, 2], F32)
                kvg = sb(st, "pj_kvg", [128, 1], F32)
                bg = sb(st, "pj_bg", [128, 24], F32)
                rcos = sb(st, "pj_cos", [32, SEQ], F32)
                rsin = sb(st, "pj_sin", [32, SEQ], F32)
                xg = sb(st, "pj_xg", [128, 8, 512], F32)
                sq = sb(st, "pj_sq", [128, 8, 512], F32)
                rstd = sb(st, "pj_rstd", [128, 512], F32)
                hT = sb(st, "pj_hT", [128, 8, 512], BF16)
                cq = sb(st, "pj_cq", [128, 2, 512], F32)
                cqn = sb(st, "pj_cqn", [128, 2, 512], BF16)
                ckv = sb(st, "pj_ckv", [128, 1, 512], F32)
                cn = sb(st, "pj_cn", [128, 512], BF16)
                rstd2 = sb(st, "pj_rstd2", [128, 512], F32)
                ob = [sb(st, "pj_ob%d" % i, [128, 512], BF16) for i in range(4)]
                of = [sb(st, "pj_of%d" % i, [128, 512], F32) for i in range(2)]
                t1 = sb(st, "pj_t1", [32, 512], F32)
                t2 = sb(st, "pj_t2", [32, 512], F32)
                ps = [pst(st, "pj_ps%d" % i, [128, 512], F32) for i in range(6)]
                ps_st = pst(st, "pj_pst", [128, 512], F32)
                ps_rot = pst(st, "pj_prot", [128, 512], F32)

                for k in range(8):
                    for hh in range(2):
                        DM(lambda e, k=k, hh=hh: e.dma_start(out=w_in[:, k, hh * 2000:(hh + 1) * 2000],
                                                             in_=P["w_in"][l][k * 128:(k + 1) * 128, hh * 2000:(hh + 1) * 2000]),
                           w=["pj_win"], q="gpsimd")
                DM(lambda e: e.dma_start(out=wq_raw[:], in_=P["w_uq"][l].rearrange("(k p) n -> p k n", p=128)), w=["pj_wqraw"], q="gpsimd")
                DM(lambda e: e.dma_start(out=wkv[:], in_=P["w_ukv"][l]), w=["pj_wkv"], q="gpsimd")
                DM(lambda e: e.dma_start(out=qg[:], in_=P["q_norm_g"][l].rearrange("(n p) -> p n", p=128), allow_slow_non_contiguous=True), w=["pj_qg"])
                DM(lambda e: e.dma_start(out=kvg[:], in_=P["kv_norm_g"][l].rearrange("(n p) -> p n", p=128), allow_slow_non_contiguous=True), w=["pj_kvg"])
                DM(lambda e: e.dma_start(out=bg[:], in_=P["b_gate"][l].rearrange("(n p) -> p n", p=128), allow_slow_non_contiguous=True), w=["pj_bg"])
                DM(lambda e: e.dma_start(out=rcos[:], in_=C["ropecos"]), w=["pj_cos"])
                DM(lambda e: e.dma_start(out=rsin[:], in_=C["ropesin"]), w=["pj_sin"])
                kr_cols = w_in[:, :, 384:416].rearrange("p k (rc x f) -> p k rc x f", rc=2, x=2)
                kro = w_krot[:].rearrange("p k (rc x f) -> p k rc x f", rc=2, x=2)
                V(lambda e: e.tensor_single_scalar(kro[:, :, :, 0, :], kr_cols[:, :, :, 1, :], -1.0, op=ALU.mult), r=["pj_win"], w=["pj_wkrot"])
                V(lambda e: e.tensor_copy(kro[:, :, :, 1, :], kr_cols[:, :, :, 0, :]), r=["pj_win"], w=["pj_wkrot"])
                for k in range(2):
                    qr = wq_raw[:, k, :].rearrange("p (h c) -> p h c", c=96)[:, :, 64:96].rearrange("p h (rc x f) -> p h rc x f", rc=2, x=2)
                    qo = wq_rot[:, k, :, :].rearrange("p h (rc x f) -> p h rc x f", rc=2, x=2)
                    for rc in range(2):
                        V(lambda e, qo=qo, qr=qr, rc=rc: e.tensor_single_scalar(qo[:, :, rc, 0, :], qr[:, :, rc, 1, :], -1.0, op=ALU.mult),
                          r=["pj_wqraw"], w=["pj_wqrot"])
                        V(lambda e, qo=qo, qr=qr, rc=rc: e.tensor_copy(qo[:, :, rc, 1, :], qr[:, :, rc, 0, :]), r=["pj_wqraw"], w=["pj_wqrot"])
                V(lambda e: e.tensor_copy(wv[:], wkv[:].rearrange("p (h c) -> p h c", c=128)[:, :, 64:128]), r=["pj_wkv"], w=["pj_wv"])

                obi = [0]

                def nxt_ob():
                    obi[0] += 1
                    return obi[0] % 4

                psi = [0]

                def nxt_ps():
                    psi[0] += 1
                    return psi[0] % 6

                def do_group(gi, t0, n):
                    is_ctx = gi == 0
                    r_ = 1 if is_ctx else 0
                    s0 = t0 - CTX
                    kv_only = last and is_ctx
                    DM(lambda e, t0=t0, n=n: e.dma_start(out=xg[:, :, :n], in_=XT[:, :, t0:t0 + n].rearrange("k p t -> p k t")), r=["XT"], w=["pj_xg"])
                    rms_stats(xg, 8, n, D, sq, ps_st, rstd, ["pj_xg"], "pj_sq", "pj_pst", "pj_rstd")
                    for k in range(8):
                        V(lambda e, k=k, n=n: e.tensor_tensor(out=sq[:, k, :n], in0=xg[:, k, :n], in1=rstd[:, :n], op=ALU.mult),
                          r=["pj_xg", "pj_rstd"], w=["pj_sq"])
                        V(lambda e, k=k, n=n, r_=r_: e.tensor_scalar(hT[:, k, :n], sq[:, k, :n], gm1[:, k, r_:r_ + 1], sh1[:, k, r_:r_ + 1],
                                                                     op0=ALU.mult, op1=ALU.add),
                          r=["pj_sq", "gm1", "sh1"], w=["pj_hT"])

                    p = nxt_ps()
                    for k in range(8):
                        TE(lambda e, k=k, p=p: e.matmul(ps[p][:, :n], lhsT=w_in[:, k, 256:384], rhs=hT[:, k, :n], start=(k == 0), stop=(k == 7)),
                           r=["pj_hT", "pj_win"], w=["pj_ps%d" % p])
                    V(lambda e, p=p: e.tensor_copy(ckv[:, 0, :n], ps[p][:, :n]), r=["pj_ps%d" % p], w=["pj_ckv"])
                    rms_stats(ckv, 1, n, 128, sq, ps_st, rstd2, ["pj_ckv"], "pj_sq", "pj_pst", "pj_rstd2")
                    V(lambda e: e.tensor_tensor(out=sq[:, 0, :n], in0=ckv[:, 0, :n], in1=rstd2[:, :n], op=ALU.mult),
                      r=["pj_ckv", "pj_rstd2"], w=["pj_sq"])
                    V(lambda e: e.tensor_scalar(cn[:, :n], sq[:, 0, :n], kvg[:, 0:1], None, op0=ALU.mult), r=["pj_sq", "pj_kvg"], w=["pj_cn"])
                    for h in range(NH):
                        p = nxt_ps()
                        TE(lambda e, p=p, h=h: e.matmul(ps[p][:64, :n], lhsT=wkv[:, h * 128:h * 128 + 64], rhs=cn[:, :n], start=True, stop=True),
                           r=["pj_cn", "pj_wkv"], w=["pj_ps%d" % p])
                        o = nxt_ob()
                        evac(ob[o][:64, :n], ps[p][:64, :n], r=["pj_ps%d" % p], w=["pj_ob%d" % o])
                        DM(lambda e, o=o, h=h: e.dma_start(out=KT[h, 32:96, t0:t0 + n], in_=ob[o][:64, :n]), r=["pj_ob%d" % o], w=[("KT", S.dma_next)])
                    for j in range(n // 128):
                        p = nxt_ps()
                        TE(lambda e, p=p, j=j: e.matmul(ps[p][:, :], lhsT=cn[:, j * 128:(j + 1) * 128], rhs=wv[:].rearrange("p h c -> p (h c)"), start=True, stop=True),
                           r=["pj_cn", "pj_wv"], w=["pj_ps%d" % p])
                        o = nxt_ob()
                        evac(ob[o][:, :], ps[p][:, :], r=["pj_ps%d" % p], w=["pj_ob%d" % o])
                        DM(lambda e, o=o, j=j: e.dma_start(out=VV[t0 + j * 128:t0 + (j + 1) * 128, :], in_=ob[o][:, :]), r=["pj_ob%d" % o], w=[("VV", S.dma_next)])
                    p = nxt_ps()
                    for k in range(8):
                        TE(lambda e, k=k, p=p: e.matmul(ps[p][:32, :n], lhsT=w_in[:, k, 384:416], rhs=hT[:, k, :n], start=(k == 0), stop=(k == 7)),
                           r=["pj_hT", "pj_win"], w=["pj_ps%d" % p])
                    o = nxt_ob()
                    if is_ctx:
                        evac(ob[o][:32, :n], ps[p][:32, :n], r=["pj_ps%d" % p], w=["pj_ob%d" % o])
                    else:
                        for k in range(8):
                            TE(lambda e, k=k: e.matmul(ps_rot[:32, :n], lhsT=w_krot[:, k, :], rhs=hT[:, k, :n], start=(k == 0), stop=(k == 7)),
                               r=["pj_hT", "pj_wkrot"], w=["pj_prot"])
                        V(lambda e, p=p: e.tensor_tensor(out=t1[:, :n], in0=ps[p][:32, :n], in1=rcos[:, s0:s0 + n], op=ALU.mult),
                          r=["pj_ps%d" % p, "pj_cos"], w=["pj_t1"])
                        V(lambda e: e.tensor_tensor(out=t2[:, :n], in0=ps_rot[:32, :n], in1=rsin[:, s0:s0 + n], op=ALU.mult),
                          r=["pj_prot", "pj_sin"], w=["pj_t2"])
                        V(lambda e, o=o: e.tensor_tensor(out=ob[o][:32, :n], in0=t1[:, :n], in1=t2[:, :n], op=ALU.add),
                          r=["pj_t1", "pj_t2"], w=["pj_ob%d" % o])
                    for h in range(NH):
                        DM(lambda e, o=o, h=h: e.dma_start(out=KT[h, 0:32, t0:t0 + n], in_=ob[o][:32, :n]), r=["pj_ob%d" % o], w=[("KT", S.dma_next)])
                    if kv_only:
                        return
                    for c in range(2):
                        p = nxt_ps()
                        for k in range(8):
                            TE(lambda e, k=k, p=p, c=c: e.matmul(ps[p][:, :n], lhsT=w_in[:, k, c * 128:(c + 1) * 128], rhs=hT[:, k, :n], start=(k == 0), stop=(k == 7)),
                               r=["pj_hT", "pj_win"], w=["pj_ps%d" % p])
                        V(lambda e, p=p, c=c: e.tensor_copy(cq[:, c, :n], ps[p][:, :n]), r=["pj_ps%d" % p], w=["pj_cq"])
                    rms_stats(cq, 2, n, 256, sq, ps_st, rstd2, ["pj_cq"], "pj_sq", "pj_pst", "pj_rstd2")
                    for c in range(2):
                        V(lambda e, c=c: e.tensor_tensor(out=sq[:, c, :n], in0=cq[:, c, :n], in1=rstd2[:, :n], op=ALU.mult),
                          r=["pj_cq", "pj_rstd2"], w=["pj_sq"])
                        V(lambda e, c=c: e.tensor_scalar(cqn[:, c, :n], sq[:, c, :n], qg[:, c:c + 1], None, op0=ALU.mult), r=["pj_sq", "pj_qg"], w=["pj_cqn"])
                    for h in range(NH):
                        p = nxt_ps()
                        for c in range(2):
                            TE(lambda e, p=p, c=c, h=h: e.matmul(ps[p][:64, :n], lhsT=wq_raw[:, c, h * 96:h * 96 + 64], rhs=cqn[:, c, :n], start=(c == 0), stop=(c == 1)),
                               r=["pj_cqn", "pj_wqraw"], w=["pj_ps%d" % p])
                        o = nxt_ob()
                        evac(ob[o][:64, :n], ps[p][:64, :n], r=["pj_ps%d" % p], w=["pj_ob%d" % o])
                        DM(lambda e, o=o, h=h: e.dma_start(out=QT[h, 32:96, t0:t0 + n], in_=ob[o][:64, :n]), r=["pj_ob%d" % o], w=[("QT", S.dma_next)])
                        p = nxt_ps()
                        for c in range(2):
                            TE(lambda e, p=p, c=c, h=h: e.matmul(ps[p][:32, :n], lhsT=wq_raw[:, c, h * 96 + 64:h * 96 + 96], rhs=cqn[:, c, :n], start=(c == 0), stop=(c == 1)),
                               r=["pj_cqn", "pj_wqraw"], w=["pj_ps%d" % p])
                        o = nxt_ob()
                        if is_ctx:
                            evac(ob[o][:32, :n], ps[p][:32, :n], r=["pj_ps%d" % p], w=["pj_ob%d" % o])
                        else:
                            for c in range(2):
                                TE(lambda e, c=c, h=h: e.matmul(ps_rot[:32, :n], lhsT=wq_rot[:, c, h, :], rhs=cqn[:, c, :n], start=(c == 0), stop=(c == 1)),
                                   r=["pj_cqn", "pj_wqrot"], w=["pj_prot"])
                            V(lambda e, p=p: e.tensor_tensor(out=t1[:, :n], in0=ps[p][:32, :n], in1=rcos[:, s0:s0 + n], op=ALU.mult),
                              r=["pj_ps%d" % p, "pj_cos"], w=["pj_t1"])
                            V(lambda e: e.tensor_tensor(out=t2[:, :n], in0=ps_rot[:32, :n], in1=rsin[:, s0:s0 + n], op=ALU.mult),
                              r=["pj_prot", "pj_sin"], w=["pj_t2"])
                            V(lambda e, o=o: e.tensor_tensor(out=ob[o][:32, :n], in0=t1[:, :n], in1=t2[:, :n], op=ALU.add),
                              r=["pj_t1", "pj_t2"], w=["pj_ob%d" % o])
                        DM(lambda e, o=o, h=h: e.dma_start(out=QT[h, 0:32, t0:t0 + n], in_=ob[o][:32, :n]), r=["pj_ob%d" % o], w=[("QT", S.dma_next)])
                    for c in range(2):
                        p = nxt_ps()
                        for k in range(8):
                            TE(lambda e, k=k, p=p, c=c: e.matmul(ps[p][:, :n], lhsT=w_in[:, k, 416 + c * 128:416 + (c + 1) * 128], rhs=hT[:, k, :n], start=(k == 0), stop=(k == 7)),
                               r=["pj_hT", "pj_win"], w=["pj_ps%d" % p])
                        o = nxt_ob()
                        evac(ob[o][:, :n], ps[p][:, :n], r=["pj_ps%d" % p], w=["pj_ob%d" % o])
                        DM(lambda e, o=o, c=c: e.dma_start(out=ZFT[c, :, t0:t0 + n], in_=ob[o][:, :n]), r=["pj_ob%d" % o], w=[("ZFT", S.dma_next)])
                    for c in range(2):
                        p = nxt_ps()
                        for k in range(8):
                            TE(lambda e, k=k, p=p, c=c: e.matmul(ps[p][:, :n], lhsT=w_in[:, k, 672 + c * 128:672 + (c + 1) * 128], rhs=hT[:, k, :n], start=(k == 0), stop=(k == 7)),
                               r=["pj_hT", "pj_win"], w=["pj_ps%d" % p])
                        evac(of[c][:, :n], ps[p][:, :n], r=["pj_ps%d" % p], w=["pj_of%d" % c])
                        DM(lambda e, c=c: e.dma_start(out=ZPT[c, :, t0:t0 + n], in_=of[c][:, :n]), r=["pj_of%d" % c], w=[("ZPT", S.dma_next)])
                    for c in range(24):
                        p = nxt_ps()
                        for k in range(8):
                            TE(lambda e, k=k, p=p, c=c: e.matmul(ps[p][:, :n], lhsT=w_in[:, k, 928 + c * 128:928 + (c + 1) * 128], rhs=hT[:, k, :n], start=(k == 0), stop=(k == 7)),
                               r=["pj_hT", "pj_win"], w=["pj_ps%d" % p])
                        o = nxt_ob()
                        A(lambda e, o=o, p=p, c=c: e.activation(out=ob[o][:, :n], in_=ps[p][:, :n], func=AF.Sigmoid, bias=bg[:, c:c + 1], scale=1.0),
                          r=["pj_ps%d" % p, "pj_bg"], w=["pj_ob%d" % o])
                        DM(lambda e, o=o, c=c: e.dma_start(out=GT[c, :, t0:t0 + n], in_=ob[o][:, :n]), r=["pj_ob%d" % o], w=[("GT", S.dma_next)])

                for gi, (t0, n) in enumerate(GROUPS):
                    do_group(gi, t0, n)
            S.barrier()

        def phase_attn(l, last):
            with contextlib.ExitStack() as st:
                v_sb = sb(st, "at_v", [128, T // 128, 512], BF16)
                v_ext = sb(st, "at_vx", [128, T // 128, NH, 65], BF16)
                sel = sb(st, "at_sel", [65, 64], F32)
                kth = [sb(st, "at_k%d" % i, [96, T], BF16) for i in range(2)]
                qg = [sb(st, "at_q%d" % i, [96, 512], BF16) for i in range(2)]
                pt = [sb(st, "at_p%d" % i, [128, 512], BF16) for i in range(3)]
                o_sb = sb(st, "at_osb", [65, 512], F32)
                rs = sb(st, "at_rs", [64, 512], F32)
                osb = [sb(st, "at_o%d" % i, [64, 512], BF16) for i in range(2)]
                ps_s = [pst(st, "at_pss%d" % i, [128, 512], F32) for i in range(3)]
                ps_o = [pst(st, "at_pso%d" % i, [128, 512], F32) for i in range(2)]
                ps_b = pst(st, "at_psb", [128, 512], F32)
                DM(lambda e: e.dma_start(out=v_sb[:], in_=VV.rearrange("(kt p) c -> p kt c", p=128)), w=["at_v"])
                V(lambda e: e.memset(v_ext[:], 1.0), w=["at_vx"])
                G(lambda e: e.tensor_copy(v_ext[:, :, :, 0:64], v_sb[:].rearrange("p k (h c) -> p k h c", c=64)), r=["at_v"], w=["at_vx"])
                V(lambda e: e.memset(sel[:], 0.0), w=["at_sel"])
                V(lambda e: e.memset(sel[64:65, :], 1.0), w=["at_sel"])
                cnt = [0]
                segs = ([] if last else [(0, CTX, 2)]) + [(t0, n, T // 128) for (t0, n) in GROUPS[1:]]

                def do_head(h):
                    kb = h % 2
                    DM(lambda e: e.dma_start(out=kth[kb][:], in_=KT[h]), w=["at_k%d" % kb])

                    def do_seg(t0, n, nkt):
                        cnt[0] += 1
                        qb = cnt[0] % 2
                        DM(lambda e: e.dma_start(out=qg[qb][:, :n], in_=QT[h, :, t0:t0 + n]), w=["at_q%d" % qb])

                        def s_mm(kt):
                            si = kt % 3
                            TE(lambda e: e.matmul(ps_s[si][:, :n], lhsT=kth[kb][:, kt * 128:(kt + 1) * 128], rhs=qg[qb][:, :n], start=True, stop=True),
                               r=["at_k%d" % kb, "at_q%d" % qb], w=["at_pss%d" % si])

                        def pv(kt):
                            si = kt % 3
                            A(lambda e: e.activation(out=pt[si][:, :n], in_=ps_s[si][:, :n], func=AF.Exp, scale=ATT_SCALE),
                              r=["at_pss%d" % si], w=["at_p%d" % si])
                            TE(lambda e: e.matmul(ps_o[qb][:65, :n], lhsT=v_ext[:, kt, h, :], rhs=pt[si][:, :n], start=(kt == 0), stop=(kt == nkt - 1)),
                               r=["at_vx", "at_p%d" % si], w=["at_pso%d" % qb])
                        s_mm(0)
                        if nkt > 1:
                            s_mm(1)
                        for kt in range(nkt):
                            if kt + 2 < nkt:
                                s_mm(kt + 2)
                            pv(kt)
                        A(lambda e: e.activation(out=o_sb[:, :n], in_=ps_o[qb][:65, :n], func=AF.Copy), r=["at_pso%d" % qb], w=["at_osb"])
                        TE(lambda e: e.matmul(ps_b[:64, :n], lhsT=sel[:, :], rhs=o_sb[:, :n], start=True, stop=True), r=["at_sel", "at_osb"], w=["at_psb"])
                        V(lambda e: e.reciprocal(rs[:, :n], ps_b[:64, :n]), r=["at_psb"], w=["at_rs"])
                        V(lambda e: e.tensor_tensor(out=osb[qb][:, :n], in0=o_sb[:64, :n], in1=rs[:, :n], op=ALU.mult),
                          r=["at_osb", "at_rs"], w=["at_o%d" % qb])
                        DM(lambda e: e.dma_start(out=ATT[h // 2, (h % 2) * 64:(h % 2) * 64 + 64, t0:t0 + n], in_=osb[qb][:, :n]),
                           r=["at_o%d" % qb], w=[("ATT", S.dma_next)])
                    for (t0, n, nkt) in segs:
                        do_seg(t0, n, nkt)
                for h in range(NH):
                    do_head(h)
            S.barrier()

        def phase_fourier(l, last):
            segs = ([] if last else [(0, CTX, "cl256", "sl256")]) + [(CTX, SEQ, "cl4096", "sl4096")]

            def do_seg(t0, L, cln, sln):
                ntt = L // 128
                KW = 256
                with contextlib.ExitStack() as st:
                    ccs = sb(st, "fo_ccs", [128, 2, 512], BF16)
                    zft = sb(st, "fo_z", [128, 2, L], BF16)
                    uv = sb(st, "fo_uv", [128, ntt, 512], BF16)
                    clb = [sb(st, "fo_cl%d" % i, [128, ntt, KW], BF16) for i in range(2)]
                    slb = [sb(st, "fo_sl%d" % i, [128, ntt, KW], BF16) for i in range(2)]
                    yo = [sb(st, "fo_y%d" % i, [128, KW], BF16) for i in range(2)]
                    ps = [pst(st, "fo_ps%d" % i, [128, 512], F32) for i in range(4)]
                    DM(lambda e: e.dma_start(out=ccs[:], in_=C["ccs"].rearrange("(c p) n -> p c n", p=128)), w=["fo_ccs"])
                    DM(lambda e: e.dma_start(out=zft[:], in_=ZFT[:, :, t0:t0 + L].rearrange("c p t -> p c t")), w=["fo_z"])

                    def do_tt(tt):
                        pi = tt % 4
                        for c in range(2):
                            TE(lambda e, c=c: e.matmul(ps[pi][:, :], lhsT=zft[:, c, tt * 128:(tt + 1) * 128], rhs=ccs[:, c, :], start=(c == 0), stop=(c == 1)),
                               r=["fo_z", "fo_ccs"], w=["fo_ps%d" % pi])
                        evac(uv[:, tt, :], ps[pi][:, :], r=["fo_ps%d" % pi], w=[("fo_uv", tt)])
                    for tt in range(ntt):
                        do_tt(tt)
                    uvkeys = [("fo_uv", tt) for tt in range(ntt)]
                    scale = 1.0 / math.sqrt(L * 256.0)

                    def do_kc(kc):
                        b = kc % 2
                        DM(lambda e: e.dma_start(out=clb[b][:], in_=C[cln][:, kc * KW:(kc + 1) * KW].rearrange("(tt p) k -> p tt k", p=128)), w=["fo_cl%d" % b])
                        DM(lambda e: e.dma_start(out=slb[b][:], in_=C[sln][:, kc * KW:(kc + 1) * KW].rearrange("(tt p) k -> p tt k", p=128)), w=["fo_sl%d" % b])

                        def do_m(m):
                            pi = (kc * 2 + m) % 4
                            for tt in range(ntt):
                                TE(lambda e, tt=tt: e.matmul(ps[pi][:, :KW], lhsT=uv[:, tt, m * 128:(m + 1) * 128], rhs=clb[b][:, tt, :], start=(tt == 0), stop=False),
                                   r=uvkeys + ["fo_cl%d" % b], w=["fo_ps%d" % pi])
                                TE(lambda e, tt=tt: e.matmul(ps[pi][:, :KW], lhsT=uv[:, tt, 256 + m * 128:256 + (m + 1) * 128], rhs=slb[b][:, tt, :], start=False, stop=(tt == ntt - 1)),
                                   r=uvkeys + ["fo_sl%d" % b], w=["fo_ps%d" % pi])
                            evac(yo[m][:, :], ps[pi][:, :KW], r=["fo_ps%d" % pi], w=["fo_y%d" % m], scale=scale)
                            DM(lambda e: e.dma_start(out=YFT[m, :, t0 + kc * KW:t0 + (kc + 1) * KW], in_=yo[m][:, :]), r=["fo_y%d" % m], w=[("YFT", S.dma_next)])
                        for m in range(2):
                            do_m(m)
                    for kc in range(L // KW):
                        do_kc(kc)
                S.barrier()
            for sg in segs:
                do_seg(*sg)

        def phase_pool(l, last):
            segs = ([] if last else [(0, CTX, "icnt256")]) + [(CTX, SEQ, "icnt4096")]

            def do_seg(t0, L, icn):
                with contextlib.ExitStack() as st:
                    zpp = sb(st, "po_z", [128, 2, L + 16], F32)
                    sA = sb(st, "po_sA", [128, L + 16], F32)
                    sB = sb(st, "po_sB", [128, L + 16], F32)
                    icnt = sb(st, "po_ic", [128, 2, L], F32)
                    tmp = sb(st, "po_tmp", [128, L], F32)
                    poolT = sb(st, "po_pT", [128, 2, L], BF16)
                    wbd = [sb(st, "po_w%d" % i, [128, 128], BF16) for i in range(2)]
                    psc = sb(st, "po_sc", [128, 2], F32)
                    yo = [sb(st, "po_y%d" % i, [128, 512], BF16) for i in range(2)]
                    ps = [pst(st, "po_ps%d" % i, [128, 512], F32) for i in range(2)]
                    V(lambda e: e.memset(zpp[:], 0.0), w=["po_z"])
                    for ch in range(2):
                        V(lambda e, ch=ch: e.memset(wbd[ch][:], 0.0), w=["po_w%d" % ch])
                    DM(lambda e: e.dma_start(out=zpp[:, :, 8:8 + L], in_=ZPT[:, :, t0:t0 + L].rearrange("c p t -> p c t")), w=["po_z"])
                    DM(lambda e: e.dma_start(out=icnt[:], in_=C[icn].rearrange("c p t -> p c t")), w=["po_ic"])
                    DM(lambda e: e.dma_start(out=psc[:], in_=P["pool_scale"][l].rearrange("(c p) -> p c", p=128), allow_slow_non_contiguous=True), w=["po_sc"])
                    for g in range(4):
                        o = (g % 2) * 64
                        DM(lambda e, g=g, o=o: e.dma_start(out=wbd[g // 2][o:o + 64, o:o + 64], in_=P["w_grp"][l, g]), w=["po_w%d" % (g // 2)], q="gpsimd")

                    def do_ch(ch):
                        V(lambda e: e.tensor_tensor(out=sA[:, 1:L + 16], in0=zpp[:, ch, 0:L + 15], in1=zpp[:, ch, 1:L + 16], op=ALU.add), r=["po_z"], w=["po_sA"])
                        V(lambda e: e.tensor_tensor(out=sB[:, 2:L + 15], in0=sA[:, 1:L + 14], in1=sA[:, 3:L + 16], op=ALU.add), r=["po_sA"], w=["po_sB"])
                        if ch == 1:
                            V(lambda e: e.tensor_tensor(out=sA[:, 4:L + 13], in0=sB[:, 2:L + 11], in1=sB[:, 6:L + 15], op=ALU.add), r=["po_sB"], w=["po_sA"])
                            V(lambda e: e.tensor_tensor(out=sB[:, 8:L + 9], in0=sA[:, 4:L + 5], in1=sA[:, 12:L + 13], op=ALU.add), r=["po_sA"], w=["po_sB"])
                        V(lambda e: e.tensor_tensor(out=tmp[0:64, :], in0=sA[0:64, 8:8 + L], in1=icnt[0:64, ch, :], op=ALU.mult), r=["po_sA", "po_ic"], w=["po_tmp"])
                        V(lambda e: e.tensor_tensor(out=tmp[64:128, :], in0=sB[64:128, 8:8 + L], in1=icnt[64:128, ch, :], op=ALU.mult), r=["po_sB", "po_ic"], w=["po_tmp"])
                        V(lambda e: e.tensor_tensor(out=poolT[:, ch, :], in0=tmp[:, :], in1=zpp[:, ch, 8:8 + L], op=ALU.subtract), r=["po_tmp", "po_z"], w=[("po_pT", ch)])

                        def do_tc(tc_):
                            n = min(512, L - tc_ * 512)
                            pi = tc_ % 2
                            TE(lambda e: e.matmul(ps[pi][:, :n], lhsT=wbd[ch][:], rhs=poolT[:, ch, tc_ * 512:tc_ * 512 + n], start=True, stop=True),
                               r=[("po_pT", ch), "po_w%d" % ch], w=["po_ps%d" % pi])
                            V(lambda e: e.tensor_scalar(yo[pi][:, :n], ps[pi][:, :n], psc[:, ch:ch + 1], None, op0=ALU.mult), r=["po_ps%d" % pi, "po_sc"], w=["po_y%d" % pi])
                            DM(lambda e: e.dma_start(out=YCT[ch, :, t0 + tc_ * 512:t0 + tc_ * 512 + n], in_=yo[pi][:, :n]), r=["po_y%d" % pi], w=[("YCT", S.dma_next)])
                        for tc_ in range((L + 511) // 512):
                            do_tc(tc_)
                    for ch in range(2):
                        do_ch(ch)
                S.barrier()
            for sg in segs:
                do_seg(*sg)

        def phase_combine(l, last):
            with contextlib.ExitStack() as st:
                w_oa = sb(st, "cb_woa", [128, 4, D], BF16)
                w_ob = sb(st, "cb_wob", [128, 2, D], BF16)
                w_oc = sb(st, "cb_woc", [128, 2, D], BF16)
                w_out = sb(st, "cb_wout", [128, 8, D], BF16)
                w_pq = sb(st, "cb_wpq", [128, 8, 2048], BF16)
                att = sb(st, "cb_att", [128, 4, 512], BF16)
                yf = sb(st, "cb_yf", [128, 2, 512], BF16)
                yc = sb(st, "cb_yc", [128, 2, 512], BF16)
                gT = sb(st, "cb_g", [128, 24, 512], BF16)
                xg = sb(st, "cb_xg", [128, 8, 512], F32)
                sq = sb(st, "cb_sq", [128, 8, 512], F32)
                rstd = sb(st, "cb_rstd", [128, 512], F32)
                u = sb(st, "cb_u", [128, 8, 512], BF16)
                h2 = sb(st, "cb_h2", [128, 8, 512], BF16)
                t1 = sb(st, "cb_t1", [128, 512], F32)
                t2 = sb(st, "cb_t2", [128, 512], F32)
                t3 = sb(st, "cb_t3", [128, 512], F32)
                ob = [sb(st, "cb_ob%d" % i, [128, 512], BF16) for i in range(3)]
                ps = [pst(st, "cb_ps%d" % i, [128, 512], F32) for i in range(6)]
                ps_st = pst(st, "cb_pst", [128, 512], F32)
                for k in range(4):
                    DM(lambda e, k=k: e.dma_start(out=w_oa[:, k, :], in_=P["w_oa"][l][k * 128:(k + 1) * 128, :]), w=["cb_woa"], q="gpsimd")
                for k in range(2):
                    DM(lambda e, k=k: e.dma_start(out=w_ob[:, k, :], in_=P["w_ob"][l][k * 128:(k + 1) * 128, :]), w=["cb_wob"], q="gpsimd")
                    DM(lambda e, k=k: e.dma_start(out=w_oc[:, k, :], in_=P["w_oc"][l][k * 128:(k + 1) * 128, :]), w=["cb_woc"], q="gpsimd")
                for k in range(8):
                    DM(lambda e, k=k: e.dma_start(out=w_out[:, k, :], in_=P["w_out"][l][k * 128:(k + 1) * 128, :]), w=["cb_wout"], q="gpsimd")
                    DM(lambda e, k=k: e.dma_start(out=w_pq[:, k, :], in_=P["w_pq"][l][k * 128:(k + 1) * 128, :]), w=["cb_wpq"], q="gpsimd")
                psi = [0]

                def do_group(gi, t0, n):
                    r_ = 1 if gi == 0 else 0
                    DM(lambda e: e.dma_start(out=att[:, :, :n], in_=ATT[:, :, t0:t0 + n].rearrange("k p t -> p k t")), w=["cb_att"])
                    DM(lambda e: e.dma_start(out=yf[:, :, :n], in_=YFT[:, :, t0:t0 + n].rearrange("k p t -> p k t")), w=["cb_yf"])
                    DM(lambda e: e.dma_start(out=yc[:, :, :n], in_=YCT[:, :, t0:t0 + n].rearrange("k p t -> p k t")), w=["cb_yc"])
                    DM(lambda e: e.dma_start(out=gT[:, :, :n], in_=GT[:, :, t0:t0 + n].rearrange("k p t -> p k t")), w=["cb_g"])
                    DM(lambda e: e.dma_start(out=xg[:, :, :n], in_=XT[:, :, t0:t0 + n].rearrange("k p t -> p k t")), w=["cb_xg"])

                    def do_dch(dc):
                        pa, pb, pc = psi[0] % 6, (psi[0] + 1) % 6, (psi[0] + 2) % 6
                        psi[0] += 3
                        for k in range(4):
                            TE(lambda e, k=k: e.matmul(ps[pa][:, :n], lhsT=w_oa[:, k, dc * 128:(dc + 1) * 128], rhs=att[:, k, :n], start=(k == 0), stop=(k == 3)),
                               r=["cb_woa", "cb_att"], w=["cb_ps%d" % pa])
                        for k in range(2):
                            TE(lambda e, k=k: e.matmul(ps[pb][:, :n], lhsT=w_ob[:, k, dc * 128:(dc + 1) * 128], rhs=yf[:, k, :n], start=(k == 0), stop=(k == 1)),
                               r=["cb_wob", "cb_yf"], w=["cb_ps%d" % pb])
                        for k in range(2):
                            TE(lambda e, k=k: e.matmul(ps[pc][:, :n], lhsT=w_oc[:, k, dc * 128:(dc + 1) * 128], rhs=yc[:, k, :n], start=(k == 0), stop=(k == 1)),
                               r=["cb_woc", "cb_yc"], w=["cb_ps%d" % pc])
                        V(lambda e: e.tensor_tensor(out=t1[:, :n], in0=ps[pa][:, :n], in1=gT[:, dc, :n], op=ALU.mult), r=["cb_ps%d" % pa, "cb_g"], w=["cb_t1"])
                        V(lambda e: e.tensor_tensor(out=t2[:, :n], in0=ps[pb][:, :n], in1=gT[:, 8 + dc, :n], op=ALU.mult), r=["cb_ps%d" % pb, "cb_g"], w=["cb_t2"])
                        V(lambda e: e.tensor_tensor(out=t3[:, :n], in0=ps[pc][:, :n], in1=gT[:, 16 + dc, :n], op=ALU.mult), r=["cb_ps%d" % pc, "cb_g"], w=["cb_t3"])
                        G(lambda e: e.tensor_tensor(out=t1[:, :n], in0=t1[:, :n], in1=t2[:, :n], op=ALU.add), r=["cb_t1", "cb_t2"], w=["cb_t1"])
                        G(lambda e: e.tensor_tensor(out=u[:, dc, :n], in0=t1[:, :n], in1=t3[:, :n], op=ALU.add), r=["cb_t1", "cb_t3"], w=[("cb_u", dc)])
                    for dc in range(8):
                        do_dch(dc)
                    ukeys = [("cb_u", dc) for dc in range(8)]

                    def do_out(dc):
                        pm = psi[0] % 6
                        psi[0] += 1
                        for k in range(8):
                            TE(lambda e, k=k: e.matmul(ps[pm][:, :n], lhsT=w_out[:, k, dc * 128:(dc + 1) * 128], rhs=u[:, k, :n], start=(k == 0), stop=(k == 7)),
                               r=ukeys + ["cb_wout"], w=["cb_ps%d" % pm])
                        V(lambda e: e.scalar_tensor_tensor(out=xg[:, dc, :n], in0=ps[pm][:, :n], scalar=gt1[:, dc, r_:r_ + 1], in1=xg[:, dc, :n], op0=ALU.mult, op1=ALU.add),
                          r=["cb_ps%d" % pm, "gt1", "cb_xg"], w=["cb_xg"])
                    for dc in range(8):
                        do_out(dc)
                    DM(lambda e: e.dma_start(out=XT[:, :, t0:t0 + n].rearrange("k p t -> p k t"), in_=xg[:, :, :n]), r=["cb_xg"], w=[("XTw", gi)])
                    rms_stats(xg, 8, n, D, sq, ps_st, rstd, ["cb_xg"], "cb_sq", "cb_pst", "cb_rstd")

                    def do_h2(k):
                        V(lambda e: e.tensor_tensor(out=sq[:, k, :n], in0=xg[:, k, :n], in1=rstd[:, :n], op=ALU.mult), r=["cb_xg", "cb_rstd"], w=["cb_sq"])
                        V(lambda e: e.tensor_scalar(h2[:, k, :n], sq[:, k, :n], gm2[:, k, r_:r_ + 1], sh2[:, k, r_:r_ + 1], op0=ALU.mult, op1=ALU.add),
                          r=["cb_sq", "gm2", "sh2"], w=["cb_h2"])
                    for k in range(8):
                        do_h2(k)
                    DM(lambda e: e.dma_start(out=H2T[:, :, t0:t0 + n].rearrange("k p t -> p k t"), in_=h2[:, :, :n]), r=["cb_h2"], w=[("H2T", gi)])

                    def do_pq(nc_):
                        pm = psi[0] % 6
                        psi[0] += 1
                        o = nc_ % 3
                        for k in range(8):
                            TE(lambda e, k=k: e.matmul(ps[pm][:, :n], lhsT=w_pq[:, k, nc_ * 128:(nc_ + 1) * 128], rhs=h2[:, k, :n], start=(k == 0), stop=(k == 7)),
                               r=["cb_h2", "cb_wpq"], w=["cb_ps%d" % pm])
                        evac(ob[o][:, :n], ps[pm][:, :n], r=["cb_ps%d" % pm], w=["cb_ob%d" % o])
                        DM(lambda e: e.dma_start(out=PQT[nc_, :, t0:t0 + n], in_=ob[o][:, :n]), r=["cb_ob%d" % o], w=[("PQT", S.dma_next)])
                    for nc_ in range(16):
                        do_pq(nc_)
                for gi, (t0, n) in enumerate(GROUPS):
                    if last and gi == 0:
                        continue
                    do_group(gi, t0, n)
            S.barrier()

        def phase_cast_tables():
            with contextlib.ExitStack() as st:
                tb = [sb(st, "ct_t%d" % i, [128, 8, D], BF16) for i in range(4)]
                cnt = [0]

                def do_chunk(l, src, o, c):
                    cnt[0] += 1
                    b = cnt[0] % 4
                    DM(lambda e: e.dma_start(out=tb[b][:], in_=src[c * 1024:(c + 1) * 1024, :].rearrange("(p r) d -> p r d", r=8)),
                       w=["ct_t%d" % b], q="gpsimd")
                    DM(lambda e: e.dma_start(out=DUB[l][c * 1024:(c + 1) * 1024, o:o + D].rearrange("(p r) d -> p r d", r=8), in_=tb[b][:]),
                       r=["ct_t%d" % b], w=[("cast", S.dma_next)])
                for l in range(DEPTH):
                    for (src, o) in ((P["peer_down%d" % l], 0), (P["peer_up%d" % l], D)):
                        for c in range(16):
                            do_chunk(l, src, o, c)
            S.barrier()

        def phase_peer(l, last):
            with contextlib.ExitStack() as st:
                keysT = sb(st, "pe_keysT", [128, 16, 128], BF16)
                iotf = sb(st, "pe_iotf", [128, 16], F32)
                pq = sb(st, "pe_pq", [128, 16, 128], BF16)
                s_sb = sb(st, "pe_s", [128, 16, 128], F32)
                s2 = sb(st, "pe_s2", [128, 16, 128], F32)
                kraw = s2
                sv = sb(st, "pe_sv", [128, 16, 16], F32)
                si = sb(st, "pe_si", [128, 16, 16], U32)
                sif = sb(st, "pe_sif", [128, 16, 16], F32)
                cand = sb(st, "pe_cand", [128, 8, 256], F32)
                cand2 = sb(st, "pe_cand2", [128, 8, 256], F32)
                eq = cand2[:].rearrange("p h (a b) -> p h a b", b=16)
                ts = sb(st, "pe_ts", [128, 8, 16], F32)
                pos = sb(st, "pe_pos", [128, 8, 16], U32)
                pa_i = sb(st, "pe_pai", [128, 8, 16], U32)
                pb_i = sb(st, "pe_pbi", [128, 8, 16], U32)
                pa_f = sb(st, "pe_paf", [128, 8, 16], F32)
                pb_f = sb(st, "pe_pbf", [128, 8, 16], F32)
                If = sb(st, "pe_If", [128, 8, 16], F32)
                Jf = sb(st, "pe_Jf", [128, 8, 16], F32)
                ef = sb(st, "pe_ef", [128, 8, 16], F32)
                gsum = sb(st, "pe_gsum", [128, 8], F32)
                h2T = sb(st, "pe_h2T", [128, 8, 128], BF16)
                eidx_ = [sb(st, "pe_eidx%d" % i, [128, 128], U32) for i in range(2)]
                gate_ = [sb(st, "pe_gate%d" % i, [128, 8, 16], F32) for i in range(2)]
                h2tok_ = [sb(st, "pe_h2tok%d" % i, [128, D], BF16) for i in range(2)]
                xt_ = [sb(st, "pe_xt%d" % i, [128, 8, 128], F32) for i in range(2)]
                NR = 24
                SG = 8
                rows = [sb(st, "pe_rows%d" % i, [128, 2 * D], BF16) for i in range(NR)]
                junk = [sb(st, "pe_junk%d" % i, [128, D], BF16) for i in range(4)]
                junk2 = sb(st, "pe_junk2", [128, D], BF16)
                junk3 = sb(st, "pe_junk3", [128, D], BF16)
                diag8 = [sb(st, "pe_diag%d" % i, [128, SG, 128], BF16) for i in range(3)]
                a_sb = sb(st, "pe_a", [128, 128], F32)
                g1 = sb(st, "pe_g1", [128, 128], F32)
                g2 = sb(st, "pe_g2", [128, 128], F32)
                act = sb(st, "pe_act", [128, 128], F32)
                acc = sb(st, "pe_acc", [128, D], F32)
                ps_sc = [pst(st, "pe_psc%d" % i, [128, 4, 128], F32) for i in range(2)]
                ps_part = [pst(st, "pe_ppart%d" % i, [128, 4, 128], F32) for i in range(2)]
                ps_a = pst(st, "pe_pa", [128, 128], F32)
                part_sb = [sb(st, "pe_partsb%d" % i, [128, 4, 128], F32) for i in range(2)]
                ps_h2 = pst(st, "pe_ph2", [128, 8, 128], BF16)
                ps_acc = [pst(st, "pe_pacc%d" % i, [128, 512], F32) for i in range(2)]

                G(lambda e: e.iota(iotf[:], [[1, 16]], base=0, channel_multiplier=0, allow_small_or_imprecise_dtypes=True), w=["pe_iotf"])
                DM(lambda e: e.dma_start(out=kraw[:], in_=P["peer_keys"][l].rearrange("h p k d -> k (h p) d")), w=["pe_s2"])
                for hp in range(16):
                    bk = (hp // 4) % 2
                    TE(lambda e, hp=hp, bk=bk: e.transpose(ps_sc[bk][:, hp % 4, :], kraw[:, hp, :], ident_f[:]), r=["pe_s2", "ident_f"], w=["pe_psc%d" % bk])
                    if hp % 4 == 3:
                        b4 = hp // 4
                        evac(keysT[:, b4 * 4:(b4 + 1) * 4, :], ps_sc[bk][:], r=["pe_psc%d" % bk], w=["pe_keysT"])
                rcnt = [0]
                gcnt = [0]
                jcnt = [0]
                pcnt = [0]

                def prep_tile(ti):
                    c0 = ti * 128
                    tb = ti % 2
                    eidx, gate, h2tok, xt = eidx_[tb], gate_[tb], h2tok_[tb], xt_[tb]
                    kE, kG, kH, kX = "pe_eidx%d" % tb, "pe_gate%d" % tb, "pe_h2tok%d" % tb, "pe_xt%d" % tb
                    DM(lambda e: e.dma_start(out=pq[:], in_=PQT[:, :, c0:c0 + 128].rearrange("n p t -> p n t")), w=["pe_pq"])
                    DM(lambda e: e.dma_start(out=h2T[:], in_=H2T[:, :, c0:c0 + 128].rearrange("k p t -> p k t")), w=["pe_h2T"])
                    DM(lambda e: e.dma_start(out=xt[:], in_=XT[:, :, c0:c0 + 128].rearrange("k p t -> p k t")), w=[kX])
                    for hp in range(16):
                        bk = (hp // 4) % 2
                        TE(lambda e, hp=hp, bk=bk: e.matmul(ps_sc[bk][:, hp % 4, :], lhsT=pq[:, hp, :], rhs=keysT[:, hp, :], start=True, stop=True),
                           r=["pe_pq", "pe_keysT"], w=["pe_psc%d" % bk])
                        if hp % 4 == 3:
                            b4 = hp // 4
                            evac(s_sb[:, b4 * 4:(b4 + 1) * 4, :], ps_sc[bk][:], r=["pe_psc%d" % bk], w=["pe_s"])
                            yield
                    for k in range(8):
                        TE(lambda e, k=k: e.transpose(ps_h2[:, k, :], h2T[:, k, :], ident_b[:]), r=["pe_h2T", "ident_b"], w=["pe_ph2"])
                    A(lambda e: e.activation(out=h2tok[:], in_=ps_h2[:].rearrange("p k d -> p (k d)"), func=AF.Copy), r=["pe_ph2"], w=[kH])
                    yield
                    for hp in range(16):
                        V(lambda e, hp=hp: e.max(out=sv[:, hp, 0:8], in_=s_sb[:, hp, :]), r=["pe_s"], w=["pe_sv"])
                        V(lambda e, hp=hp: e.max_index(out=si[:, hp, 0:8], in_max=sv[:, hp, 0:8], in_values=s_sb[:, hp, :]), r=["pe_s", "pe_sv"], w=["pe_si"])
                        V(lambda e, hp=hp: e.match_replace(out=s2[:, hp, :], in_to_replace=sv[:, hp, 0:8], in_values=s_sb[:, hp, :], imm_value=-1e30),
                          r=["pe_s", "pe_sv"], w=["pe_s2"])
                        V(lambda e, hp=hp: e.max(out=sv[:, hp, 8:16], in_=s2[:, hp, :]), r=["pe_s2"], w=["pe_sv"])
                        V(lambda e, hp=hp: e.max_index(out=si[:, hp, 8:16], in_max=sv[:, hp, 8:16], in_values=s2[:, hp, :]), r=["pe_s2", "pe_sv"], w=["pe_si"])
                        yield
                    V(lambda e: e.tensor_copy(sif[:], si[:]), r=["pe_si"], w=["pe_sif"])
                    svv = sv[:].rearrange("p (h q) a -> p h q a", q=2)
                    sfv = sif[:].rearrange("p (h q) a -> p h q a", q=2)
                    candv = cand[:].rearrange("p h (a b) -> p h a b", b=16)
                    V(lambda e: e.tensor_tensor(out=candv, in0=svv[:, :, 0, :].unsqueeze(3).to_broadcast([128, 8, 16, 16]),
                                                in1=svv[:, :, 1, :].unsqueeze(2).to_broadcast([128, 8, 16, 16]), op=ALU.add), r=["pe_sv"], w=["pe_cand"])
                    for h in range(8):
                        V(lambda e, h=h: e.max(out=ts[:, h, 0:8], in_=cand[:, h, :]), r=["pe_cand"], w=["pe_ts"])
                        V(lambda e, h=h: e.max_index(out=pos[:, h, 0:8], in_max=ts[:, h, 0:8], in_values=cand[:, h, :]), r=["pe_cand", "pe_ts"], w=["pe_pos"])
                        V(lambda e, h=h: e.match_replace(out=cand2[:, h, :], in_to_replace=ts[:, h, 0:8], in_values=cand[:, h, :], imm_value=-1e30),
                          r=["pe_cand", "pe_ts"], w=["pe_cand2"])
                        V(lambda e, h=h: e.max(out=ts[:, h, 8:16], in_=cand2[:, h, :]), r=["pe_cand2"], w=["pe_ts"])
                        V(lambda e, h=h: e.max_index(out=pos[:, h, 8:16], in_max=ts[:, h, 8:16], in_values=cand2[:, h, :]), r=["pe_cand2", "pe_ts"], w=["pe_pos"])
                        yield
                    V(lambda e: e.tensor_single_scalar(pa_i[:], pos[:], 4, op=ALU.logical_shift_right), r=["pe_pos"], w=["pe_pai"])
                    V(lambda e: e.tensor_single_scalar(pb_i[:], pos[:], 15, op=ALU.bitwise_and), r=["pe_pos"], w=["pe_pbi"])
                    V(lambda e: e.tensor_copy(pa_f[:], pa_i[:]), r=["pe_pai"], w=["pe_paf"])
                    V(lambda e: e.tensor_copy(pb_f[:], pb_i[:]), r=["pe_pbi"], w=["pe_pbf"])
                    iob = iotf[:].unsqueeze(1).unsqueeze(1).to_broadcast([128, 8, 16, 16])
                    for (pf, half, dst, kd) in ((pa_f, 0, If, "pe_If"), (pb_f, 1, Jf, "pe_Jf")):
                        V(lambda e, pf=pf: e.tensor_tensor(out=eq, in0=iob, in1=pf[:].unsqueeze(3).to_broadcast([128, 8, 16, 16]), op=ALU.is_equal),
                          r=["pe_iotf", "pe_paf", "pe_pbf"], w=["pe_cand2"])
                        V(lambda e, half=half: e.tensor_tensor(out=eq, in0=eq, in1=sfv[:, :, half, :].unsqueeze(2).to_broadcast([128, 8, 16, 16]), op=ALU.mult),
                          r=["pe_cand2", "pe_sif"], w=["pe_cand2"])
                        V(lambda e, dst=dst: e.tensor_reduce(out=dst[:], in_=eq, axis=AX.X, op=ALU.add), r=["pe_cand2"], w=[kd])
                        yield
                    V(lambda e: e.scalar_tensor_tensor(out=ef[:], in0=If[:], scalar=128.0, in1=Jf[:], op0=ALU.mult, op1=ALU.add), r=["pe_If", "pe_Jf"], w=["pe_ef"])
                    V(lambda e: e.tensor_copy(eidx[:], ef[:].rearrange("p h k -> p (h k)")), r=["pe_ef"], w=[kE])
                    yield
                    V(lambda e: e.tensor_tensor(out=gate[:], in0=ts[:], in1=ts[:, :, 0:1].to_broadcast([128, 8, 16]), op=ALU.subtract), r=["pe_ts"], w=[kG])
                    A(lambda e: e.activation(out=gate[:], in_=gate[:], func=AF.Exp), r=[kG], w=[kG])
                    V(lambda e: e.tensor_reduce(out=gsum[:], in_=gate[:], axis=AX.X, op=ALU.add), r=[kG], w=["pe_gsum"])
                    V(lambda e: e.reciprocal(gsum[:], gsum[:]), r=["pe_gsum"], w=["pe_gsum"])
                    V(lambda e: e.tensor_tensor(out=gate[:], in0=gate[:], in1=gsum[:].unsqueeze(2).to_broadcast([128, 8, 16]), op=ALU.mult), r=[kG, "pe_gsum"], w=[kG])

                def run_tile(ti, nxt=None):
                    c0 = ti * 128
                    r_ = 1 if ti < 2 else 0
                    tb = ti % 2
                    eidx, gate, h2tok, xt = eidx_[tb], gate_[tb], h2tok_[tb], xt_[tb]
                    kE, kG, kH, kX = "pe_eidx%d" % tb, "pe_gate%d" % tb, "pe_h2tok%d" % tb, "pe_xt%d" % tb
                    gatev = gate[:].rearrange("p h k -> p (h k)")

                    gbufs = {}

                    def stage1a(g0):
                        bufs = []
                        for sl in range(g0, g0 + SG):
                            rcnt[0] += 1
                            b = rcnt[0] % NR
                            bufs.append(b)
                            DM(lambda e, b=b, sl=sl: e.indirect_dma_start(out=rows[b][:, :], out_offset=None, in_=DUB[l],
                                                                          in_offset=bass.IndirectOffsetOnAxis(ap=eidx[:, sl:sl + 1], axis=0)),
                               r=[kE], w=["pe_rows%d" % b], q="gpsimd")
                        gbufs[g0] = bufs

                    def stage1b(g0):
                        for q4 in range(SG // 4):
                            pcnt[0] += 1
                            pb = pcnt[0] % 2
                            for i4 in range(4):
                                i_ = q4 * 4 + i4
                                b = gbufs[g0][i_]
                                jcnt[0] += 1
                                jb = jcnt[0] % 4
                                V(lambda e, b=b, jb=jb: e.tensor_tensor(out=junk[jb][:], in0=rows[b][:, 0:D], in1=h2tok[:], op=ALU.mult),
                                  r=["pe_rows%d" % b, kH], w=["pe_junk%d" % jb])
                                for c in range(8):
                                    TE(lambda e, jb=jb, c=c, i4=i4, pb=pb: e.matmul(ps_part[pb][:, i4, :], lhsT=junk[jb][:, c * 128:(c + 1) * 128], rhs=ident_b[:],
                                                                                   start=(c == 0), stop=(c == 7)),
                                       r=["pe_junk%d" % jb, "ident_b"], w=["pe_ppart%d" % pb])
                            A(lambda e, pb=pb: e.activation(out=part_sb[pb][:], in_=ps_part[pb][:], func=AF.Copy), r=["pe_ppart%d" % pb], w=["pe_partsb%d" % pb])
                            for i4 in range(4):
                                sl = g0 + q4 * 4 + i4
                                TE(lambda e, i4=i4, sl=sl, pb=pb: e.matmul(ps_a[:, sl:sl + 1], lhsT=part_sb[pb][:, i4, :], rhs=ones_f[:, 0:1], start=True, stop=True),
                                   r=["pe_partsb%d" % pb, "ones_f"], w=["pe_pa"])
                        A(lambda e: e.activation(out=a_sb[:, g0:g0 + SG], in_=ps_a[:, g0:g0 + SG], func=AF.Copy), r=["pe_pa"], w=[("pe_a", g0)])

                    def stage2(g0):
                        bufs = gbufs[g0]
                        gcnt[0] += 1
                        dg = gcnt[0] % 3
                        gs = slice(g0, g0 + SG)
                        V(lambda e: e.tensor_tensor(out=g1[:, gs], in0=a_sb[:, gs], in1=a_sb[:, gs], op=ALU.mult), r=[("pe_a", g0)], w=[("pe_g1", g0)])
                        V(lambda e: e.tensor_scalar(g1[:, gs], g1[:, gs], 0.044715, 1.0, op0=ALU.mult, op1=ALU.add), r=[("pe_g1", g0)], w=[("pe_g1", g0)])
                        V(lambda e: e.tensor_tensor(out=g1[:, gs], in0=g1[:, gs], in1=a_sb[:, gs], op=ALU.mult), r=[("pe_g1", g0), ("pe_a", g0)], w=[("pe_g1", g0)])
                        A(lambda e: e.activation(out=g2[:, gs], in_=g1[:, gs], func=AF.Sigmoid, scale=1.5957691216057308), r=[("pe_g1", g0)], w=[("pe_g2", g0)])
                        V(lambda e: e.tensor_tensor(out=g2[:, gs], in0=g2[:, gs], in1=a_sb[:, gs], op=ALU.mult), r=[("pe_g2", g0), ("pe_a", g0)], w=[("pe_g2", g0)])
                        V(lambda e: e.tensor_tensor(out=act[:, gs], in0=g2[:, gs], in1=gatev[:, gs], op=ALU.mult), r=[("pe_g2", g0), kG], w=[("pe_act", g0)])
                        V(lambda e: e.tensor_tensor(out=diag8[dg][:], in0=ident_b[:].unsqueeze(1).to_broadcast([128, SG, 128]),
                                                    in1=act[:, gs].unsqueeze(2).to_broadcast([128, SG, 128]), op=ALU.mult),
                          r=["ident_b", ("pe_act", g0)], w=["pe_diag%d" % dg])
                        for i_, sl in enumerate(range(g0, g0 + SG)):
                            b = bufs[i_]
                            for hf in range(2):
                                TE(lambda e, hf=hf, b=b, sl=sl, i_=i_: e.matmul(ps_acc[hf][:, :], lhsT=diag8[dg][:, i_, :], rhs=rows[b][:, D + hf * 512:D + (hf + 1) * 512],
                                                                                start=(sl == 0), stop=(sl == 127)),
                                   r=["pe_diag%d" % dg, "pe_rows%d" % b], w=["pe_pacc%d" % hf])
                    glist = list(range(0, 128, SG))

                    def advance(nsteps):
                        if nxt is not None:
                            for _ in range(nsteps):
                                next(nxt, None)
                    stage1a(glist[0])
                    if len(glist) > 1:
                        stage1a(glist[1])
                    stage1b(glist[0])
                    for gi_, g0 in enumerate(glist):
                        if gi_ + 2 < len(glist):
                            stage1a(glist[gi_ + 2])
                        if gi_ + 1 < len(glist):
                            stage1b(glist[gi_ + 1])
                        stage2(g0)
                        advance(3)
                    if nxt is not None:
                        for _ in nxt:
                            pass
                    for hf in range(2):
                        evac(acc[:, hf * 512:(hf + 1) * 512], ps_acc[hf][:, :], r=["pe_pacc%d" % hf], w=[("pe_acc", hf)])
                    for k in range(8):
                        TE(lambda e, k=k: e.transpose(ps_sc[k // 4][:, k % 4, :], acc[:, k * 128:(k + 1) * 128], ident_f[:]), r=[("pe_acc", k // 4), "ident_f"], w=["pe_psc%d" % (k // 4)])
                    for k in range(8):
                        V(lambda e, k=k: e.scalar_tensor_tensor(out=xt[:, k, :], in0=ps_sc[k // 4][:, k % 4, :], scalar=gt2[:, k, r_:r_ + 1], in1=xt[:, k, :],
                                                               op0=ALU.mult, op1=ALU.add),
                          r=["pe_psc%d" % (k // 4), "gt2", kX], w=[kX])
                    DM(lambda e: e.dma_start(out=XT[:, :, c0:c0 + 128].rearrange("k p t -> p k t"), in_=xt[:]), r=[kX], w=[("XTw", ti)])

                tiles = [ti for ti in range(2 if last else 0, T // 128) if peer_tiles is None or ti in peer_tiles]
                if tiles:
                    for _ in prep_tile(tiles[0]):
                        pass
                for i, ti in enumerate(tiles):
                    run_tile(ti, prep_tile(tiles[i + 1]) if i + 1 < len(tiles) else None)
            S.barrier()

        def phase_final():
            with contextlib.ExitStack() as st:
                fg = sb(st, "fn_g", [128, 8], F32)
                xg = sb(st, "fn_xg", [128, 8, 512], F32)
                sq = sb(st, "fn_sq", [128, 8, 512], F32)
                rstd = sb(st, "fn_rstd", [128, 512], F32)
                ot = [sb(st, "fn_o%d" % i, [128, D], F32) for i in range(2)]
                ps_st = pst(st, "fn_pst", [128, 512], F32)
                ps = [pst(st, "fn_ps%d" % i, [128, 4, 128], F32) for i in range(4)]
                DM(lambda e: e.dma_start(out=fg[:], in_=P["final_g"].rearrange("(n p) -> p n", p=128), allow_slow_non_contiguous=True), w=["fn_g"])
                cnt = [0]

                def do_group(gi, t0, n):
                    DM(lambda e: e.dma_start(out=xg[:, :, :n], in_=XT[:, :, t0:t0 + n].rearrange("k p t -> p k t")), w=["fn_xg"])
                    rms_stats(xg, 8, n, D, sq, ps_st, rstd, ["fn_xg"], "fn_sq", "fn_pst", "fn_rstd")

                    def do_k(k):
                        V(lambda e: e.tensor_tensor(out=sq[:, k, :n], in0=xg[:, k, :n], in1=rstd[:, :n], op=ALU.mult), r=["fn_xg", "fn_rstd"], w=["fn_sq"])
                        V(lambda e: e.tensor_scalar(xg[:, k, :n], sq[:, k, :n], fg[:, k:k + 1], None, op0=ALU.mult), r=["fn_sq", "fn_g"], w=["fn_xg"])
                    for k in range(8):
                        do_k(k)

                    def do_j(j):
                        cnt[0] += 1
                        ob_ = cnt[0] % 2
                        for half in range(2):
                            pi = (cnt[0] * 2 + half) % 4
                            for kk in range(4):
                                k = half * 4 + kk
                                TE(lambda e, k=k, kk=kk, pi=pi: e.transpose(ps[pi][:, kk, :], xg[:, k, j * 128:(j + 1) * 128], ident_f[:]),
                                   r=["fn_xg", "ident_f"], w=["fn_ps%d" % pi])
                            evac(ot[ob_][:, half * 512:(half + 1) * 512], ps[pi][:].rearrange("p a b -> p (a b)"), r=["fn_ps%d" % pi], w=[("fn_o", ob_, half)])
                        row0 = t0 - CTX + j * 128
                        DM(lambda e: e.dma_start(out=out[row0:row0 + 128, :], in_=ot[ob_][:]), r=[("fn_o", ob_, 0), ("fn_o", ob_, 1)], w=[("out", row0)])
                    for j in range(n // 128):
                        do_j(j)
                for gi, (t0, n) in enumerate(GROUPS):
                    if gi == 0:
                        continue
                    do_group(gi, t0, n)
            S.barrier()

        phase_load_x()
        if stop_after is None or stop_after[0] in ("peer", "castonly"):
            phase_cast_tables()
        for l in range(DEPTH):
            if stop_after is not None and stop_after[0] == "castonly":
                break
            last = l == DEPTH - 1
            phase_mod(l)
            phase_proj(l, last)
            if stop_after == ("proj", l):
                break
            phase_attn(l, last)
            phase_fourier(l, last)
            phase_pool(l, last)
            if stop_after == ("mix", l):
                break
            phase_combine(l, last)
            if stop_after == ("combine", l):
                break
            phase_peer(l, last)
            if stop_after == ("peer", l):
                break
        if stop_after is None:
            phase_final()
        S.emit()
    return nc, S


def make_in_maps(inputs, cores):
    consts = _const_tables()
    maps = []
    for b in cores:
        m = {"x": np.ascontiguousarray(inputs["x"][b]), "ctx": np.ascontiguousarray(inputs["ctx"][b]),
             "cvec": np.ascontiguousarray(np.stack([inputs["c"][b], inputs["c_ctx"]], axis=0))}
        for k in PARAM_SHAPES:
            if k.startswith("peer_down") or k.startswith("peer_up"):
                m[k] = np.ascontiguousarray(inputs[k[:-1]][int(k[-1])])
            else:
                m[k] = np.ascontiguousarray(inputs[k])
        m.update(consts)
        maps.append(m)
    return maps


def kernel(**inputs):
    inputs = {k: np.asarray(v) for k, v in inputs.items()}
    nc, _ = build()
    cores = list(range(8))
    res = run_bass_kernel_spmd(nc, make_in_maps(inputs, cores), core_ids=cores)
    return np.stack([res.results[i]["out"] for i in cores], axis=0).astype(np.float32)
```

```python
import contextlib
import math
import numpy as np
import ml_dtypes
import concourse.bass as bass
import concourse.mybir as mybir
from concourse.bass_utils import run_bass_kernel_spmd

F32 = mybir.dt.float32
BF16 = mybir.dt.bfloat16
U32 = mybir.dt.uint32
I32 = mybir.dt.int32
AF = mybir.ActivationFunctionType
ALU = mybir.AluOpType
AX = mybir.AxisListType

ENGS = ("sync", "scalar", "vector", "gpsimd", "tensor")
N_DMA_SEMS = 90

D = 1024
SEQ = 4096
CTX = 256
T = SEQ + CTX
DEPTH = 2
NH = 8
EPS = 1e-6
ATT_SCALE = 96 ** -0.5
IN_W = 4000
NEXP = 16384


class Sched:
    def __init__(self, nc):
        self.nc = nc
        self.ops = {e: [] for e in ENGS}
        self.count = {e: 0 for e in ENGS}
        self.known = {e: {} for e in ENGS}
        self.last_w = {}
        self.readers = {}
        self.dma_next = 0
        self.dma_target = [0] * N_DMA_SEMS
        self.pending = {e: [] for e in ENGS}
        self.n_instr = 0

    def _need(self, eng, tok, waits):
        if tok is None:
            return
        kind, src, val = tok
        if kind == "e" and src == eng and eng == "tensor":
            return
        k = (kind, src)
        if self.known[eng].get(k, 0) >= val:
            return
        self.known[eng][k] = val
        waits.append(tok)

    def _deps(self, eng, r, w):
        waits = []
        for tok in self.pending[eng]:
            self._need(eng, tok, waits)
        self.pending[eng] = []
        for key in r:
            self._need(eng, self.last_w.get(key), waits)
        for key in w:
            self._need(eng, self.last_w.get(key), waits)
            for tok in self.readers.get(key, ()):
                self._need(eng, tok, waits)
        best = {}
        for kind, src, val in waits:
            k = (kind, src)
            if best.get(k, 0) < val:
                best[k] = val
        return [(k[0], k[1], v) for k, v in best.items()]

    def _commit(self, tok, r, w):
        for key in r:
            self.readers.setdefault(key, []).append(tok)
        for key in w:
            self.last_w[key] = tok
            self.readers[key] = []

    def op(self, eng, fn, r=(), w=()):
        waits = self._deps(eng, r, w)
        self.count[eng] += 1
        tok = ("e", eng, self.count[eng])
        self._commit(tok, r, w)
        self.ops[eng].append(("op", fn, waits, None))
        self.n_instr += 1 + len(waits)
        return tok

    def dma(self, eng, fn, r=(), w=()):
        s = self.dma_next % N_DMA_SEMS
        self.dma_next += 1
        waits = self._deps(eng, r, w)
        prev = self.dma_target[s]
        if prev > 0:
            k = ("d", s)
            if self.known[eng].get(k, 0) < prev:
                self.known[eng][k] = prev
                waits.append(("d", s, prev))
        self.dma_target[s] = prev + 16
        tok = ("d", s, prev + 16)
        self._commit(tok, r, w)
        self.ops[eng].append(("dma", fn, waits, s))
        self.n_instr += 1 + len(waits)
        return tok

    def barrier(self):
        waits = []
        for e in ENGS:
            if e != "sync" and self.count[e] > 0:
                self._need("sync", ("e", e, self.count[e]), waits)
        for s in range(N_DMA_SEMS):
            if self.dma_target[s] > 0:
                self._need("sync", ("d", s, self.dma_target[s]), waits)
        self.count["sync"] += 1
        tok = ("e", "sync", self.count["sync"])
        self.ops["sync"].append(("op", lambda e: e.nop(), waits, None))
        self.n_instr += 1 + len(waits)
        for e in ENGS:
            if e != "sync":
                self.pending[e].append(tok)
                for e2 in ENGS:
                    self.known[e][("e", e2)] = max(self.known[e].get(("e", e2), 0),
                                                   self.count[e2] if e2 != "sync" else 0)
                for s in range(N_DMA_SEMS):
                    self.known[e][("d", s)] = self.dma_target[s]
                self.known[e].pop(("e", "sync"), None)
        self.last_w = {}
        self.readers = {}

    def emit(self):
        nc = self.nc
        with contextlib.ExitStack() as st:
            esem = {e: st.enter_context(nc.semaphore("c_" + e)) for e in ENGS}
            dsem = [st.enter_context(nc.semaphore("d%d" % i)) for i in range(N_DMA_SEMS)]
            block = st.enter_context(nc.Block())

            def mk(ename):
                def body(eng):
                    for kind, fn, waits, s in self.ops[ename]:
                        for wk, src, val in waits:
                            eng.wait_ge(esem[src] if wk == "e" else dsem[src], val)
                        ins = fn(eng)
                        if kind == "op":
                            ins.then_inc(esem[ename], 1)
                        else:
                            ins.then_inc(dsem[s], 16)
                    if ename == "sync":
                        for i in range(N_DMA_SEMS):
                            if self.dma_target[i] > 0:
                                eng.wait_ge(dsem[i], self.dma_target[i])
                        for e2 in ENGS:
                            if e2 != "sync" and self.count[e2] > 0:
                                eng.wait_ge(esem[e2], self.count[e2])
                return body

            block.sync(mk("sync"))
            block.scalar(mk("scalar"))
            block.vector(mk("vector"))
            block.gpsimd(mk("gpsimd"))
            block.tensor(mk("tensor"))


def _const_tables():
    c = {}
    half = 16
    inv_freq = (10000.0 ** (-np.arange(0, half, 2, dtype=np.float32) / half)).astype(np.float32)
    t = np.arange(SEQ)
    r = (t // 64).astype(np.float32)
    cl = (t % 64).astype(np.float32)
    ang_r = (r[None, :] * inv_freq[:, None]).astype(np.float32)
    ang_c = (cl[None, :] * inv_freq[:, None]).astype(np.float32)
    ang = np.concatenate([ang_r, ang_r, ang_c, ang_c], axis=0)
    c["ropecos"] = np.cos(ang).astype(np.float32)
    c["ropesin"] = np.sin(ang).astype(np.float32)
    def dft(n):
        k = np.arange(n, dtype=np.int64)
        kk = (k[:, None] * k[None, :]) % n
        a = 2.0 * np.pi * kk.astype(np.float64) / n
        return np.cos(a), np.sin(a)
    cc, sc = dft(256)
    c["ccs"] = np.concatenate([cc, -sc], axis=1).astype(ml_dtypes.bfloat16)
    c["cl256"] = cc.astype(ml_dtypes.bfloat16)
    c["sl256"] = sc.astype(ml_dtypes.bfloat16)
    cL, sL = dft(SEQ)
    c["cl4096"] = cL.astype(ml_dtypes.bfloat16)
    c["sl4096"] = sL.astype(ml_dtypes.bfloat16)
    for L, name in ((SEQ, "icnt4096"), (CTX, "icnt256")):
        tt = np.arange(L)
        tab = np.zeros((2, 128, L), np.float32)
        for gi, w in enumerate((2, 4, 8, 16)):
            lo = np.clip(tt - w // 2, 0, L)
            hi = np.clip(tt - w // 2 + w, 0, L)
            tab[gi // 2, (gi % 2) * 64:(gi % 2) * 64 + 64, :] = (1.0 / (hi - lo).astype(np.float32))[None, :]
        c[name] = tab
    return c


CONST_SHAPES = {
    "ropecos": ([32, SEQ], F32), "ropesin": ([32, SEQ], F32),
    "ccs": ([256, 512], BF16), "cl256": ([256, 256], BF16), "sl256": ([256, 256], BF16),
    "cl4096": ([SEQ, SEQ], BF16), "sl4096": ([SEQ, SEQ], BF16),
    "icnt4096": ([2, 128, SEQ], F32), "icnt256": ([2, 128, CTX], F32),
}

PARAM_SHAPES = {
    "w_mod": [DEPTH, D, 6 * D], "b_mod": [DEPTH, 6 * D], "norm1_g": [DEPTH, D], "norm2_g": [DEPTH, D],
    "w_in": [DEPTH, D, IN_W], "b_gate": [DEPTH, 3 * D], "q_norm_g": [DEPTH, 256], "w_uq": [DEPTH, 256, 768],
    "kv_norm_g": [DEPTH, 128], "w_ukv": [DEPTH, 128, 1024], "w_oa": [DEPTH, 512, D], "w_ob": [DEPTH, 256, D],
    "w_grp": [DEPTH, 4, 64, 64], "pool_scale": [DEPTH, 256], "w_oc": [DEPTH, 256, D], "w_out": [DEPTH, D, D],
    "w_pq": [DEPTH, D, 2048], "peer_keys": [DEPTH, 8, 2, 128, 128], "peer_down0": [NEXP, D], "peer_down1": [NEXP, D],
    "peer_up0": [NEXP, D], "peer_up1": [NEXP, D], "final_g": [D],
}

GROUPS = [(0, CTX)] + [(CTX + 512 * g, 512) for g in range(SEQ // 512)]


def build(stop_after=None, dbg=(), peer_tiles=None):
    nc = bass.Bass("TRN2", target_bir_lowering=False)
    S = Sched(nc)

    def dram_in(name, shape, dt=F32):
        return nc.dram_tensor(name, shape, dt, kind="ExternalInput").ap()

    def dram_scratch(name, shape, dt):
        kind = "ExternalOutput" if name in dbg else "Internal"
        return nc.dram_tensor(name, shape, dt, kind=kind).ap()

    xin = dram_in("x", [SEQ, D])
    ctxin = dram_in("ctx", [CTX, D])
    cvec = dram_in("cvec", [2, D])
    P = {k: dram_in(k, v) for k, v in PARAM_SHAPES.items()}
    C = {k: dram_in(k, v[0], v[1]) for k, v in CONST_SHAPES.items()}
    out = nc.dram_tensor("out", [SEQ, D], F32, kind="ExternalOutput").ap()

    XT = dram_scratch("XT", [8, 128, T], F32)
    GT = dram_scratch("GT", [24, 128, T], BF16)
    QT = dram_scratch("QT", [NH, 96, T], BF16)
    KT = dram_scratch("KT", [NH, 96, T], BF16)
    VV = dram_scratch("VV", [T, 512], BF16)
    ZFT = dram_scratch("ZFT", [2, 128, T], BF16)
    ZPT = dram_scratch("ZPT", [2, 128, T], F32)
    ATT = dram_scratch("ATT", [4, 128, T], BF16)
    YFT = dram_scratch("YFT", [2, 128, T], BF16)
    YCT = dram_scratch("YCT", [2, 128, T], BF16)
    H2T = dram_scratch("H2T", [8, 128, T], BF16)
    PQT = dram_scratch("PQT", [16, 128, T], BF16)
    DUB = [dram_scratch("DUB%d" % i, [NEXP, 2 * D], BF16) for i in range(DEPTH)]

    V = lambda fn, r=(), w=(): S.op("vector", fn, r, w)
    A = lambda fn, r=(), w=(): S.op("scalar", fn, r, w)
    G = lambda fn, r=(), w=(): S.op("gpsimd", fn, r, w)
    TE = lambda fn, r=(), w=(): S.op("tensor", fn, r, w)
    DM = lambda fn, r=(), w=(), q="sync": S.dma(q, fn, r, w)

    rr = [0]

    def evac(out_ap, in_ap, r, w, scale=None):
        rr[0] += 1
        if rr[0] % 2 == 0:
            if scale is None:
                V(lambda e: e.tensor_copy(out_ap, in_ap), r, w)
            else:
                V(lambda e: e.tensor_single_scalar(out_ap, in_ap, float(scale), op=ALU.mult), r, w)
        else:
            A(lambda e: e.activation(out=out_ap, in_=in_ap, func=AF.Copy,
                                     scale=1.0 if scale is None else float(scale)), r, w)

    with contextlib.ExitStack() as top:
        uniq = [0]

        def sb(st, name, shape, dt):
            uniq[0] += 1
            return st.enter_context(nc.sbuf_tensor("%s_%d" % (name, uniq[0]), shape, dt))

        def pst(st, name, shape, dt):
            uniq[0] += 1
            return st.enter_context(nc.psum_tensor("%s_%d" % (name, uniq[0]), shape, dt))

        ident_f = sb(top, "ident_f", [128, 128], F32)
        ident_b = sb(top, "ident_b", [128, 128], BF16)
        ones_f = sb(top, "ones_f", [128, 128], F32)
        ones_b = sb(top, "ones_b", [128, 128], BF16)
        iot = sb(top, "iot", [128, 128], F32)
        gm1 = sb(top, "gm1", [128, 8, 2], F32)
        sh1 = sb(top, "sh1", [128, 8, 2], F32)
        gt1 = sb(top, "gt1", [128, 8, 2], F32)
        gm2 = sb(top, "gm2", [128, 8, 2], F32)
        sh2 = sb(top, "sh2", [128, 8, 2], F32)
        gt2 = sb(top, "gt2", [128, 8, 2], F32)
        epsb = sb(top, "epsb", [128, 1], F32)

        G(lambda e: e.iota(iot[:], [[1, 128]], base=0, channel_multiplier=-1,
                           allow_small_or_imprecise_dtypes=True), w=["iot"])
        V(lambda e: e.tensor_single_scalar(ident_f[:], iot[:], 0.0, op=ALU.is_equal), r=["iot"], w=["ident_f"])
        V(lambda e: e.tensor_single_scalar(ident_b[:], iot[:], 0.0, op=ALU.is_equal), r=["iot"], w=["ident_b"])
        V(lambda e: e.memset(ones_f[:], 1.0), w=["ones_f"])
        V(lambda e: e.memset(ones_b[:], 1.0), w=["ones_b"])
        V(lambda e: e.memset(epsb[:], EPS), w=["epsb"])

        def phase_load_x():
            with contextlib.ExitStack() as st:
                xt = [sb(st, "ld_x%d" % i, [128, D], F32) for i in range(2)]
                xo = [sb(st, "ld_o%d" % i, [128, 8, 128], F32) for i in range(2)]
                ps = [pst(st, "ld_ps%d" % i, [128, 4, 128], F32) for i in range(4)]
                ntile = T // 128
                for ti in range(ntile):
                    b = ti % 2
                    src = ctxin[ti * 128:(ti + 1) * 128, :] if ti < 2 else xin[(ti - 2) * 128:(ti - 1) * 128, :]
                    DM(lambda e, b=b, src=src: e.dma_start(out=xt[b][:], in_=src), w=["ld_x%d" % b])
                    for half in range(2):
                        pi = (ti * 2 + half) % 4
                        for j in range(4):
                            k = half * 4 + j
                            TE(lambda e, b=b, pi=pi, j=j, k=k: e.transpose(ps[pi][:, j, :], xt[b][:, k * 128:(k + 1) * 128], ident_f[:]),
                               r=["ld_x%d" % b, "ident_f"], w=["ld_ps%d" % pi])
                        evac(xo[b][:, half * 4:(half + 1) * 4, :], ps[pi][:], r=["ld_ps%d" % pi], w=[("ld_o", b, half)])
                    DM(lambda e, b=b, ti=ti: e.dma_start(out=XT[:, :, ti * 128:(ti + 1) * 128].rearrange("k p t -> p k t"), in_=xo[b][:]),
                       r=[("ld_o", b, 0), ("ld_o", b, 1)], w=[("XT", ti)])
            S.barrier()

        def phase_mod(l):
            with contextlib.ExitStack() as st:
                cT = sb(st, "md_c", [128, 2, 8], F32)
                scT = sb(st, "md_sc", [128, 8, 2], F32)
                bm = sb(st, "md_b", [128, 48], F32)
                g1 = sb(st, "md_g1", [128, 8], F32)
                g2 = sb(st, "md_g2", [128, 8], F32)
                modT = sb(st, "md_mod", [128, 48, 2], F32)
                wm = [sb(st, "md_w%d" % i, [128, 8, 512], F32) for i in range(2)]
                mps = pst(st, "md_ps", [128, 48, 2], F32)
                for r_ in range(2):
                    DM(lambda e, r_=r_: e.dma_start(out=cT[:, r_, :], in_=cvec[r_].rearrange("(k p) -> p k", p=128), allow_slow_non_contiguous=True), w=["md_c"])
                DM(lambda e: e.dma_start(out=bm[:], in_=P["b_mod"][l].rearrange("(n p) -> p n", p=128), allow_slow_non_contiguous=True), w=["md_b"])
                DM(lambda e: e.dma_start(out=g1[:], in_=P["norm1_g"][l].rearrange("(n p) -> p n", p=128), allow_slow_non_contiguous=True), w=["md_g1"])
                DM(lambda e: e.dma_start(out=g2[:], in_=P["norm2_g"][l].rearrange("(n p) -> p n", p=128), allow_slow_non_contiguous=True), w=["md_g2"])
                for r_ in range(2):
                    A(lambda e, r_=r_: e.activation(out=scT[:, :, r_], in_=cT[:, r_, :], func=AF.Silu), r=["md_c"], w=["md_sc"])
                for j in range(12):
                    b = j % 2
                    DM(lambda e, b=b, j=j: e.dma_start(out=wm[b][:], in_=P["w_mod"][l][:, j * 512:(j + 1) * 512].rearrange("(k p) n -> p k n", p=128)),
                       w=["md_w%d" % b])
                    for i in range(4):
                        n = j * 4 + i
                        for k in range(8):
                            TE(lambda e, b=b, i=i, k=k, n=n: e.matmul(mps[:, n, :], lhsT=wm[b][:, k, i * 128:(i + 1) * 128], rhs=scT[:, k, :],
                                                                      start=(k == 0), stop=(k == 7)),
                               r=["md_w%d" % b, "md_sc"], w=["md_ps"])
                V(lambda e: e.tensor_tensor(out=modT[:], in0=mps[:], in1=bm[:].unsqueeze(2).to_broadcast([128, 48, 2]), op=ALU.add),
                  r=["md_ps", "md_b"], w=["md_mod"])
                V(lambda e: e.tensor_copy(sh1[:], modT[:, 0:8, :]), r=["md_mod"], w=["sh1"])
                V(lambda e: e.scalar_tensor_tensor(out=gm1[:], in0=modT[:, 8:16, :], scalar=1.0, in1=g1[:].unsqueeze(2).to_broadcast([128, 8, 2]),
                                                   op0=ALU.add, op1=ALU.mult), r=["md_mod", "md_g1"], w=["gm1"])
                V(lambda e: e.tensor_copy(gt1[:], modT[:, 16:24, :]), r=["md_mod"], w=["gt1"])
                V(lambda e: e.tensor_copy(sh2[:], modT[:, 24:32, :]), r=["md_mod"], w=["sh2"])
                V(lambda e: e.scalar_tensor_tensor(out=gm2[:], in0=modT[:, 32:40, :], scalar=1.0, in1=g2[:].unsqueeze(2).to_broadcast([128, 8, 2]),
                                                   op0=ALU.add, op1=ALU.mult), r=["md_mod", "md_g2"], w=["gm2"])
                V(lambda e: e.tensor_copy(gt2[:], modT[:, 40:48, :]), r=["md_mod"], w=["gt2"])
            S.barrier()

        def rms_stats(src, nk, n, nfeat, sq, ps_stat, rstd, keys_r, ksq, kps, krstd):
            for k in range(nk):
                A(lambda e, k=k: e.activation(out=sq[:, k, :n], in_=src[:, k, :n], func=AF.Square), r=keys_r, w=[ksq])
            for k in range(nk):
                TE(lambda e, k=k: e.matmul(ps_stat[:, :n], lhsT=ones_b[:], rhs=sq[:, k, :n], start=(k == 0), stop=(k == nk - 1)),
                   r=[ksq, "ones_b"], w=[kps])
            A(lambda e: e.activation(out=rstd[:, :n], in_=ps_stat[:, :n], func=AF.Sqrt, scale=1.0 / nfeat, bias=epsb[:]),
              r=[kps, "epsb"], w=[krstd])
            V(lambda e: e.reciprocal(rstd[:, :n], rstd[:, :n]), r=[krstd], w=[krstd])

        def phase_proj(l, last):
            with contextlib.ExitStack() as st:
                w_in = sb(st, "pj_win", [128, 8, IN_W], BF16)
                w_krot = sb(st, "pj_wkrot", [128, 8, 32], BF16)
                wq_raw = sb(st, "pj_wqraw", [128, 2, 768], BF16)
                wq_rot = sb(st, "pj_wqrot", [128, 2, NH, 32], BF16)
                wkv = sb(st, "pj_wkv", [128, 1024], BF16)
                wv = sb(st, "pj_wv", [128, NH, 64], BF16)
                qg = sb(st, "pj_qg", [128, 2], F32)
                kvg = sb(st, "pj_kvg", [128, 1], F32)
                bg = sb(st, "pj_bg", [128, 24], F32)
                rcos = sb(st, "pj_cos", [32, SEQ], F32)
                rsin = sb(st, "pj_sin", [32, SEQ], F32)
                xg = sb(st, "pj_xg", [128, 8, 512], F32)
                sq = sb(st, "pj_sq", [128, 8, 512], F32)
                sqb = sb(st, "pj_sqb", [128, 8, 512], BF16)
                rstd = sb(st, "pj_rstd", [128, 512], F32)
                hT = sb(st, "pj_hT", [128, 8, 512], BF16)
                cq = sb(st, "pj_cq", [128, 2, 512], F32)
                cqn = sb(st, "pj_cqn", [128, 2, 512], BF16)
                ckv = sb(st, "pj_ckv", [128, 1, 512], F32)
                cn = sb(st, "pj_cn", [128, 512], BF16)
                rstd2 = sb(st, "pj_rstd2", [128, 512], F32)
                ob = [sb(st, "pj_ob%d" % i, [128, 512], BF16) for i in range(4)]
                of = [sb(st, "pj_of%d" % i, [128, 512], F32) for i in range(2)]
                t1 = sb(st, "pj_t1", [32, 512], F32)
                t2 = sb(st, "pj_t2", [32, 512], F32)
                ps = [pst(st, "pj_ps%d" % i, [128, 512], F32) for i in range(6)]
                ps_st = pst(st, "pj_pst", [128, 512], F32)
                ps_rot = pst(st, "pj_prot", [128, 512], F32)

                for k in range(8):
                    for hh in range(2):
                        DM(lambda e, k=k, hh=hh: e.dma_start(out=w_in[:, k, hh * 2000:(hh + 1) * 2000],
                                                             in_=P["w_in"][l][k * 128:(k + 1) * 128, hh * 2000:(hh + 1) * 2000]),
                           w=["pj_win"], q="gpsimd")
                DM(lambda e: e.dma_start(out=wq_raw[:], in_=P["w_uq"][l].rearrange("(k p) n -> p k n", p=128)), w=["pj_wqraw"], q="gpsimd")
                DM(lambda e: e.dma_start(out=wkv[:], in_=P["w_ukv"][l]), w=["pj_wkv"], q="gpsimd")
                DM(lambda e: e.dma_start(out=qg[:], in_=P["q_norm_g"][l].rearrange("(n p) -> p n", p=128), allow_slow_non_contiguous=True), w=["pj_qg"])
                DM(lambda e: e.dma_start(out=kvg[:], in_=P["kv_norm_g"][l].rearrange("(n p) -> p n", p=128), allow_slow_non_contiguous=True), w=["pj_kvg"])
                DM(lambda e: e.dma_start(out=bg[:], in_=P["b_gate"][l].rearrange("(n p) -> p n", p=128), allow_slow_non_contiguous=True), w=["pj_bg"])
                DM(lambda e: e.dma_start(out=rcos[:], in_=C["ropecos"]), w=["pj_cos"])
                DM(lambda e: e.dma_start(out=rsin[:], in_=C["ropesin"]), w=["pj_sin"])
                kr_cols = w_in[:, :, 384:416].rearrange("p k (rc x f) -> p k rc x f", rc=2, x=2)
                kro = w_krot[:].rearrange("p k (rc x f) -> p k rc x f", rc=2, x=2)
                V(lambda e: e.tensor_single_scalar(kro[:, :, :, 0, :], kr_cols[:, :, :, 1, :], -1.0, op=ALU.mult), r=["pj_win"], w=["pj_wkrot"])
                V(lambda e: e.tensor_copy(kro[:, :, :, 1, :], kr_cols[:, :, :, 0, :]), r=["pj_win"], w=["pj_wkrot"])
                for k in range(2):
                    qr = wq_raw[:, k, :].rearrange("p (h c) -> p h c", c=96)[:, :, 64:96].rearrange("p h (rc x f) -> p h rc x f", rc=2, x=2)
                    qo = wq_rot[:, k, :, :].rearrange("p h (rc x f) -> p h rc x f", rc=2, x=2)
                    for rc in range(2):
                        V(lambda e, qo=qo, qr=qr, rc=rc: e.tensor_single_scalar(qo[:, :, rc, 0, :], qr[:, :, rc, 1, :], -1.0, op=ALU.mult),
                          r=["pj_wqraw"], w=["pj_wqrot"])
                        V(lambda e, qo=qo, qr=qr, rc=rc: e.tensor_copy(qo[:, :, rc, 1, :], qr[:, :, rc, 0, :]), r=["pj_wqraw"], w=["pj_wqrot"])
                V(lambda e: e.tensor_copy(wv[:], wkv[:].rearrange("p (h c) -> p h c", c=128)[:, :, 64:128]), r=["pj_wkv"], w=["pj_wv"])

                obi = [0]

                def nxt_ob():
                    obi[0] += 1
                    return obi[0] % 4

                psi = [0]

                def nxt_ps():
                    psi[0] += 1
                    return psi[0] % 6

                def do_group(gi, t0, n):
                    is_ctx = gi == 0
                    r_ = 1 if is_ctx else 0
                    s0 = t0 - CTX
                    kv_only = last and is_ctx
                    DM(lambda e, t0=t0, n=n: e.dma_start(out=xg[:, :, :n], in_=XT[:, :, t0:t0 + n].rearrange("k p t -> p k t")), r=["XT"], w=["pj_xg"])
                    rms_stats(xg, 8, n, D, sqb, ps_st, rstd, ["pj_xg"], "pj_sqb", "pj_pst", "pj_rstd")
                    for k in range(8):
                        V(lambda e, k=k, n=n: e.tensor_tensor(out=sq[:, k, :n], in0=xg[:, k, :n], in1=rstd[:, :n], op=ALU.mult),
                          r=["pj_xg", "pj_rstd"], w=["pj_sq"])
                        V(lambda e, k=k, n=n, r_=r_: e.tensor_scalar(hT[:, k, :n], sq[:, k, :n], gm1[:, k, r_:r_ + 1], sh1[:, k, r_:r_ + 1],
                                                                     op0=ALU.mult, op1=ALU.add),
                          r=["pj_sq", "gm1", "sh1"], w=["pj_hT"])

                    p = nxt_ps()
                    for k in range(8):
                        TE(lambda e, k=k, p=p: e.matmul(ps[p][:, :n], lhsT=w_in[:, k, 256:384], rhs=hT[:, k, :n], start=(k == 0), stop=(k == 7)),
                           r=["pj_hT", "pj_win"], w=["pj_ps%d" % p])
                    V(lambda e, p=p: e.tensor_copy(ckv[:, 0, :n], ps[p][:, :n]), r=["pj_ps%d" % p], w=["pj_ckv"])
                    rms_stats(ckv, 1, n, 128, sqb, ps_st, rstd2, ["pj_ckv"], "pj_sqb", "pj_pst", "pj_rstd2")
                    V(lambda e: e.tensor_tensor(out=sq[:, 0, :n], in0=ckv[:, 0, :n], in1=rstd2[:, :n], op=ALU.mult),
                      r=["pj_ckv", "pj_rstd2"], w=["pj_sq"])
                    V(lambda e: e.tensor_scalar(cn[:, :n], sq[:, 0, :n], kvg[:, 0:1], None, op0=ALU.mult), r=["pj_sq", "pj_kvg"], w=["pj_cn"])
                    for h in range(NH):
                        p = nxt_ps()
                        TE(lambda e, p=p, h=h: e.matmul(ps[p][:64, :n], lhsT=wkv[:, h * 128:h * 128 + 64], rhs=cn[:, :n], start=True, stop=True),
                           r=["pj_cn", "pj_wkv"], w=["pj_ps%d" % p])
                        o = nxt_ob()
                        evac(ob[o][:64, :n], ps[p][:64, :n], r=["pj_ps%d" % p], w=["pj_ob%d" % o])
                        DM(lambda e, o=o, h=h: e.dma_start(out=KT[h, 32:96, t0:t0 + n], in_=ob[o][:64, :n]), r=["pj_ob%d" % o], w=[("KT", S.dma_next)])
                    for j in range(n // 128):
                        p = nxt_ps()
                        TE(lambda e, p=p, j=j: e.matmul(ps[p][:, :], lhsT=cn[:, j * 128:(j + 1) * 128], rhs=wv[:].rearrange("p h c -> p (h c)"), start=True, stop=True),
                           r=["pj_cn", "pj_wv"], w=["pj_ps%d" % p])
                        o = nxt_ob()
                        evac(ob[o][:, :], ps[p][:, :], r=["pj_ps%d" % p], w=["pj_ob%d" % o])
                        DM(lambda e, o=o, j=j: e.dma_start(out=VV[t0 + j * 128:t0 + (j + 1) * 128, :], in_=ob[o][:, :]), r=["pj_ob%d" % o], w=[("VV", S.dma_next)])
                    p = nxt_ps()
                    for k in range(8):
                        TE(lambda e, k=k, p=p: e.matmul(ps[p][:32, :n], lhsT=w_in[:, k, 384:416], rhs=hT[:, k, :n], start=(k == 0), stop=(k == 7)),
                           r=["pj_hT", "pj_win"], w=["pj_ps%d" % p])
                    o = nxt_ob()
                    if is_ctx:
                        evac(ob[o][:32, :n], ps[p][:32, :n], r=["pj_ps%d" % p], w=["pj_ob%d" % o])
                    else:
                        for k in range(8):
                            TE(lambda e, k=k: e.matmul(ps_rot[:32, :n], lhsT=w_krot[:, k, :], rhs=hT[:, k, :n], start=(k == 0), stop=(k == 7)),
                               r=["pj_hT", "pj_wkrot"], w=["pj_prot"])
                        V(lambda e, p=p: e.tensor_tensor(out=t1[:, :n], in0=ps[p][:32, :n], in1=rcos[:, s0:s0 + n], op=ALU.mult),
                          r=["pj_ps%d" % p, "pj_cos"], w=["pj_t1"])
                        V(lambda e: e.tensor_tensor(out=t2[:, :n], in0=ps_rot[:32, :n], in1=rsin[:, s0:s0 + n], op=ALU.mult),
                          r=["pj_prot", "pj_sin"], w=["pj_t2"])
                        V(lambda e, o=o: e.tensor_tensor(out=ob[o][:32, :n], in0=t1[:, :n], in1=t2[:, :n], op=ALU.add),
                          r=["pj_t1", "pj_t2"], w=["pj_ob%d" % o])
                    for h in range(NH):
                        DM(lambda e, o=o, h=h: e.dma_start(out=KT[h, 0:32, t0:t0 + n], in_=ob[o][:32, :n]), r=["pj_ob%d" % o], w=[("KT", S.dma_next)])
                    if kv_only:
                        return
                    for c in range(2):
                        p = nxt_ps()
                        for k in range(8):
                            TE(lambda e, k=k, p=p, c=c: e.matmul(ps[p][:, :n], lhsT=w_in[:, k, c * 128:(c + 1) * 128], rhs=hT[:, k, :n], start=(k == 0), stop=(k == 7)),
                               r=["pj_hT", "pj_win"], w=["pj_ps%d" % p])
                        V(lambda e, p=p, c=c: e.tensor_copy(cq[:, c, :n], ps[p][:, :n]), r=["pj_ps%d" % p], w=["pj_cq"])
                    rms_stats(cq, 2, n, 256, sqb, ps_st, rstd2, ["pj_cq"], "pj_sqb", "pj_pst", "pj_rstd2")
                    for c in range(2):
                        V(lambda e, c=c: e.tensor_tensor(out=sq[:, c, :n], in0=cq[:, c, :n], in1=rstd2[:, :n], op=ALU.mult),
                          r=["pj_cq", "pj_rstd2"], w=["pj_sq"])
                        V(lambda e, c=c: e.tensor_scalar(cqn[:, c, :n], sq[:, c, :n], qg[:, c:c + 1], None, op0=ALU.mult), r=["pj_sq", "pj_qg"], w=["pj_cqn"])
                    for h in range(NH):
                        p = nxt_ps()
                        for c in range(2):
                            TE(lambda e, p=p, c=c, h=h: e.matmul(ps[p][:64, :n], lhsT=wq_raw[:, c, h * 96:h * 96 + 64], rhs=cqn[:, c, :n], start=(c == 0), stop=(c == 1)),
                               r=["pj_cqn", "pj_wqraw"], w=["pj_ps%d" % p])
                        o = nxt_ob()
                        evac(ob[o][:64, :n], ps[p][:64, :n], r=["pj_ps%d" % p], w=["pj_ob%d" % o])
                        DM(lambda e, o=o, h=h: e.dma_start(out=QT[h, 32:96, t0:t0 + n], in_=ob[o][:64, :n]), r=["pj_ob%d" % o], w=[("QT", S.dma_next)])
                        p = nxt_ps()
                        for c in range(2):
                            TE(lambda e, p=p, c=c, h=h: e.matmul(ps[p][:32, :n], lhsT=wq_raw[:, c, h * 96 + 64:h * 96 + 96], rhs=cqn[:, c, :n], start=(c == 0), stop=(c == 1)),
                               r=["pj_cqn", "pj_wqraw"], w=["pj_ps%d" % p])
                        o = nxt_ob()
                        if is_ctx:
                            evac(ob[o][:32, :n], ps[p][:32, :n], r=["pj_ps%d" % p], w=["pj_ob%d" % o])
                        else:
                            for c in range(2):
                                TE(lambda e, c=c, h=h: e.matmul(ps_rot[:32, :n], lhsT=wq_rot[:, c, h, :], rhs=cqn[:, c, :n], start=(c == 0), stop=(c == 1)),
                                   r=["pj_cqn", "pj_wqrot"], w=["pj_prot"])
                            V(lambda e, p=p: e.tensor_tensor(out=t1[:, :n], in0=ps[p][:32, :n], in1=rcos[:, s0:s0 + n], op=ALU.mult),
                              r=["pj_ps%d" % p, "pj_cos"], w=["pj_t1"])
                            V(lambda e: e.tensor_tensor(out=t2[:, :n], in0=ps_rot[:32, :n], in1=rsin[:, s0:s0 + n], op=ALU.mult),
                              r=["pj_prot", "pj_sin"], w=["pj_t2"])
                            V(lambda e, o=o: e.tensor_tensor(out=ob[o][:32, :n], in0=t1[:, :n], in1=t2[:, :n], op=ALU.add),
                              r=["pj_t1", "pj_t2"], w=["pj_ob%d" % o])
                        DM(lambda e, o=o, h=h: e.dma_start(out=QT[h, 0:32, t0:t0 + n], in_=ob[o][:32, :n]), r=["pj_ob%d" % o], w=[("QT", S.dma_next)])
                    for c in range(2):
                        p = nxt_ps()
                        for k in range(8):
                            TE(lambda e, k=k, p=p, c=c: e.matmul(ps[p][:, :n], lhsT=w_in[:, k, 416 + c * 128:416 + (c + 1) * 128], rhs=hT[:, k, :n], start=(k == 0), stop=(k == 7)),
                               r=["pj_hT", "pj_win"], w=["pj_ps%d" % p])
                        o = nxt_ob()
                        evac(ob[o][:, :n], ps[p][:, :n], r=["pj_ps%d" % p], w=["pj_ob%d" % o])
                        DM(lambda e, o=o, c=c: e.dma_start(out=ZFT[c, :, t0:t0 + n], in_=ob[o][:, :n]), r=["pj_ob%d" % o], w=[("ZFT", S.dma_next)])
                    for c in range(2):
                        p = nxt_ps()
                        for k in range(8):
                            TE(lambda e, k=k, p=p, c=c: e.matmul(ps[p][:, :n], lhsT=w_in[:, k, 672 + c * 128:672 + (c + 1) * 128], rhs=hT[:, k, :n], start=(k == 0), stop=(k == 7)),
                               r=["pj_hT", "pj_win"], w=["pj_ps%d" % p])
                        evac(of[c][:, :n], ps[p][:, :n], r=["pj_ps%d" % p], w=["pj_of%d" % c])
                        DM(lambda e, c=c: e.dma_start(out=ZPT[c, :, t0:t0 + n], in_=of[c][:, :n]), r=["pj_of%d" % c], w=[("ZPT", S.dma_next)])
                    for c in range(24):
                        p = nxt_ps()
                        for k in range(8):
                            TE(lambda e, k=k, p=p, c=c: e.matmul(ps[p][:, :n], lhsT=w_in[:, k, 928 + c * 128:928 + (c + 1) * 128], rhs=hT[:, k, :n], start=(k == 0), stop=(k == 7)),
                               r=["pj_hT", "pj_win"], w=["pj_ps%d" % p])
                        o = nxt_ob()
                        A(lambda e, o=o, p=p, c=c: e.activation(out=ob[o][:, :n], in_=ps[p][:, :n], func=AF.Sigmoid, bias=bg[:, c:c + 1], scale=1.0),
                          r=["pj_ps%d" % p, "pj_bg"], w=["pj_ob%d" % o])
                        DM(lambda e, o=o, c=c: e.dma_start(out=GT[c, :, t0:t0 + n], in_=ob[o][:, :n]), r=["pj_ob%d" % o], w=[("GT", S.dma_next)])

                for gi, (t0, n) in enumerate(GROUPS):
                    do_group(gi, t0, n)
            S.barrier()

        def phase_attn(l, last):
            with contextlib.ExitStack() as st:
                v_sb = sb(st, "at_v", [128, T // 128, 512], BF16)
                v_ext = sb(st, "at_vx", [128, T // 128, NH, 65], BF16)
                sel = sb(st, "at_sel", [65, 64], F32)
                kth = [sb(st, "at_k%d" % i, [96, T], BF16) for i in range(2)]
                qg = [sb(st, "at_q%d" % i, [96, 512], BF16) for i in range(2)]
                pt = [sb(st, "at_p%d" % i, [128, 512], BF16) for i in range(3)]
                o_sb = sb(st, "at_osb", [65, 512], F32)
                rs = sb(st, "at_rs", [64, 512], F32)
                osb = [sb(st, "at_o%d" % i, [64, 512], BF16) for i in range(2)]
                ps_s = [pst(st, "at_pss%d" % i, [128, 512], F32) for i in range(3)]
                ps_o = [pst(st, "at_pso%d" % i, [128, 512], F32) for i in range(2)]
                ps_b = pst(st, "at_psb", [128, 512], F32)
                DM(lambda e: e.dma_start(out=v_sb[:], in_=VV.rearrange("(kt p) c -> p kt c", p=128)), w=["at_v"])
                V(lambda e: e.memset(v_ext[:], 1.0), w=["at_vx"])
                G(lambda e: e.tensor_copy(v_ext[:, :, :, 0:64], v_sb[:].rearrange("p k (h c) -> p k h c", c=64)), r=["at_v"], w=["at_vx"])
                V(lambda e: e.memset(sel[:], 0.0), w=["at_sel"])
                V(lambda e: e.memset(sel[64:65, :], 1.0), w=["at_sel"])
                cnt = [0]
                segs = ([] if last else [(0, CTX, 2)]) + [(t0, n, T // 128) for (t0, n) in GROUPS[1:]]
                if l == 0:
                    emit_cast_tables(st)

                def do_head(h):
                    kb = h % 2
                    DM(lambda e: e.dma_start(out=kth[kb][:], in_=KT[h]), w=["at_k%d" % kb])

                    def do_seg(t0, n, nkt):
                        cnt[0] += 1
                        qb = cnt[0] % 2
                        DM(lambda e: e.dma_start(out=qg[qb][:, :n], in_=QT[h, :, t0:t0 + n]), w=["at_q%d" % qb])

                        def s_mm(kt):
                            si = kt % 3
                            TE(lambda e: e.matmul(ps_s[si][:, :n], lhsT=kth[kb][:, kt * 128:(kt + 1) * 128], rhs=qg[qb][:, :n], start=True, stop=True),
                               r=["at_k%d" % kb, "at_q%d" % qb], w=["at_pss%d" % si])

                        def pv(kt):
                            si = kt % 3
                            A(lambda e: e.activation(out=pt[si][:, :n], in_=ps_s[si][:, :n], func=AF.Exp, scale=ATT_SCALE),
                              r=["at_pss%d" % si], w=["at_p%d" % si])
                            TE(lambda e: e.matmul(ps_o[qb][:65, :n], lhsT=v_ext[:, kt, h, :], rhs=pt[si][:, :n], start=(kt == 0), stop=(kt == nkt - 1)),
                               r=["at_vx", "at_p%d" % si], w=["at_pso%d" % qb])
                        s_mm(0)
                        if nkt > 1:
                            s_mm(1)
                        for kt in range(nkt):
                            if kt + 2 < nkt:
                                s_mm(kt + 2)
                            pv(kt)
                        A(lambda e: e.activation(out=o_sb[:, :n], in_=ps_o[qb][:65, :n], func=AF.Copy), r=["at_pso%d" % qb], w=["at_osb"])
                        TE(lambda e: e.matmul(ps_b[:64, :n], lhsT=sel[:, :], rhs=o_sb[:, :n], start=True, stop=True), r=["at_sel", "at_osb"], w=["at_psb"])
                        V(lambda e: e.reciprocal(rs[:, :n], ps_b[:64, :n]), r=["at_psb"], w=["at_rs"])
                        V(lambda e: e.tensor_tensor(out=osb[qb][:, :n], in0=o_sb[:64, :n], in1=rs[:, :n], op=ALU.mult),
                          r=["at_osb", "at_rs"], w=["at_o%d" % qb])
                        DM(lambda e: e.dma_start(out=ATT[h // 2, (h % 2) * 64:(h % 2) * 64 + 64, t0:t0 + n], in_=osb[qb][:, :n]),
                           r=["at_o%d" % qb], w=[("ATT", S.dma_next)])
                    for (t0, n, nkt) in segs:
                        do_seg(t0, n, nkt)
                for h in range(NH):
                    do_head(h)
            S.barrier()

        def phase_fourier(l, last):
            segs = ([] if last else [(0, CTX, "cl256", "sl256")]) + [(CTX, SEQ, "cl4096", "sl4096")]

            def do_seg(t0, L, cln, sln):
                ntt = L // 128
                KW = 256
                with contextlib.ExitStack() as st:
                    ccs = sb(st, "fo_ccs", [128, 2, 512], BF16)
                    zft = sb(st, "fo_z", [128, 2, L], BF16)
                    uv = sb(st, "fo_uv", [128, ntt, 512], BF16)
                    clb = [sb(st, "fo_cl%d" % i, [128, ntt, KW], BF16) for i in range(2)]
                    slb = [sb(st, "fo_sl%d" % i, [128, ntt, KW], BF16) for i in range(2)]
                    yo = [sb(st, "fo_y%d" % i, [128, KW], BF16) for i in range(2)]
                    ps = [pst(st, "fo_ps%d" % i, [128, 512], F32) for i in range(4)]
                    DM(lambda e: e.dma_start(out=ccs[:], in_=C["ccs"].rearrange("(c p) n -> p c n", p=128)), w=["fo_ccs"])
                    DM(lambda e: e.dma_start(out=zft[:], in_=ZFT[:, :, t0:t0 + L].rearrange("c p t -> p c t")), w=["fo_z"])

                    def do_tt(tt):
                        pi = tt % 4
                        for c in range(2):
                            TE(lambda e, c=c: e.matmul(ps[pi][:, :], lhsT=zft[:, c, tt * 128:(tt + 1) * 128], rhs=ccs[:, c, :], start=(c == 0), stop=(c == 1)),
                               r=["fo_z", "fo_ccs"], w=["fo_ps%d" % pi])
                        evac(uv[:, tt, :], ps[pi][:, :], r=["fo_ps%d" % pi], w=[("fo_uv", tt)])
                    for tt in range(ntt):
                        do_tt(tt)
                    uvkeys = [("fo_uv", tt) for tt in range(ntt)]
                    scale = 1.0 / math.sqrt(L * 256.0)

                    def do_kc(kc):
                        b = kc % 2
                        DM(lambda e: e.dma_start(out=clb[b][:], in_=C[cln][:, kc * KW:(kc + 1) * KW].rearrange("(tt p) k -> p tt k", p=128)), w=["fo_cl%d" % b])
                        DM(lambda e: e.dma_start(out=slb[b][:], in_=C[sln][:, kc * KW:(kc + 1) * KW].rearrange("(tt p) k -> p tt k", p=128)), w=["fo_sl%d" % b])

                        def do_m(m):
                            pi = (kc * 2 + m) % 4
                            for tt in range(ntt):
                                TE(lambda e, tt=tt: e.matmul(ps[pi][:, :KW], lhsT=uv[:, tt, m * 128:(m + 1) * 128], rhs=clb[b][:, tt, :], start=(tt == 0), stop=False),
                                   r=uvkeys + ["fo_cl%d" % b], w=["fo_ps%d" % pi])
                                TE(lambda e, tt=tt: e.matmul(ps[pi][:, :KW], lhsT=uv[:, tt, 256 + m * 128:256 + (m + 1) * 128], rhs=slb[b][:, tt, :], start=False, stop=(tt == ntt - 1)),
                                   r=uvkeys + ["fo_sl%d" % b], w=["fo_ps%d" % pi])
                            evac(yo[m][:, :], ps[pi][:, :KW], r=["fo_ps%d" % pi], w=["fo_y%d" % m], scale=scale)
                            DM(lambda e: e.dma_start(out=YFT[m, :, t0 + kc * KW:t0 + (kc + 1) * KW], in_=yo[m][:, :]), r=["fo_y%d" % m], w=[("YFT", S.dma_next)])
                        for m in range(2):
                            do_m(m)
                    for kc in range(L // KW):
                        do_kc(kc)
                S.barrier()
            for sg in segs:
                do_seg(*sg)

        def phase_pool(l, last):
            segs = ([] if last else [(0, CTX, "icnt256")]) + [(CTX, SEQ, "icnt4096")]

            def do_seg(t0, L, icn):
                with contextlib.ExitStack() as st:
                    zpp = sb(st, "po_z", [128, 2, L + 16], F32)
                    sA = sb(st, "po_sA", [128, L + 16], F32)
                    sB = sb(st, "po_sB", [128, L + 16], F32)
                    icnt = sb(st, "po_ic", [128, 2, L], F32)
                    tmp = sb(st, "po_tmp", [128, L], F32)
                    poolT = sb(st, "po_pT", [128, 2, L], BF16)
                    wbd = [sb(st, "po_w%d" % i, [128, 128], BF16) for i in range(2)]
                    psc = sb(st, "po_sc", [128, 2], F32)
                    yo = [sb(st, "po_y%d" % i, [128, 512], BF16) for i in range(2)]
                    ps = [pst(st, "po_ps%d" % i, [128, 512], F32) for i in range(2)]
                    V(lambda e: e.memset(zpp[:], 0.0), w=["po_z"])
                    for ch in range(2):
                        V(lambda e, ch=ch: e.memset(wbd[ch][:], 0.0), w=["po_w%d" % ch])
                    DM(lambda e: e.dma_start(out=zpp[:, :, 8:8 + L], in_=ZPT[:, :, t0:t0 + L].rearrange("c p t -> p c t")), w=["po_z"])
                    DM(lambda e: e.dma_start(out=icnt[:], in_=C[icn].rearrange("c p t -> p c t")), w=["po_ic"])
                    DM(lambda e: e.dma_start(out=psc[:], in_=P["pool_scale"][l].rearrange("(c p) -> p c", p=128), allow_slow_non_contiguous=True), w=["po_sc"])
                    for g in range(4):
                        o = (g % 2) * 64
                        DM(lambda e, g=g, o=o: e.dma_start(out=wbd[g // 2][o:o + 64, o:o + 64], in_=P["w_grp"][l, g]), w=["po_w%d" % (g // 2)], q="gpsimd")

                    def do_ch(ch):
                        V(lambda e: e.tensor_tensor(out=sA[:, 1:L + 16], in0=zpp[:, ch, 0:L + 15], in1=zpp[:, ch, 1:L + 16], op=ALU.add), r=["po_z"], w=["po_sA"])
                        V(lambda e: e.tensor_tensor(out=sB[:, 2:L + 15], in0=sA[:, 1:L + 14], in1=sA[:, 3:L + 16], op=ALU.add), r=["po_sA"], w=["po_sB"])
                        if ch == 1:
                            V(lambda e: e.tensor_tensor(out=sA[:, 4:L + 13], in0=sB[:, 2:L + 11], in1=sB[:, 6:L + 15], op=ALU.add), r=["po_sB"], w=["po_sA"])
                            V(lambda e: e.tensor_tensor(out=sB[:, 8:L + 9], in0=sA[:, 4:L + 5], in1=sA[:, 12:L + 13], op=ALU.add), r=["po_sA"], w=["po_sB"])
                        V(lambda e: e.tensor_tensor(out=tmp[0:64, :], in0=sA[0:64, 8:8 + L], in1=icnt[0:64, ch, :], op=ALU.mult), r=["po_sA", "po_ic"], w=["po_tmp"])
                        V(lambda e: e.tensor_tensor(out=tmp[64:128, :], in0=sB[64:128, 8:8 + L], in1=icnt[64:128, ch, :], op=ALU.mult), r=["po_sB", "po_ic"], w=["po_tmp"])
                        V(lambda e: e.tensor_tensor(out=poolT[:, ch, :], in0=tmp[:, :], in1=zpp[:, ch, 8:8 + L], op=ALU.subtract), r=["po_tmp", "po_z"], w=[("po_pT", ch)])

                        def do_tc(tc_):
                            n = min(512, L - tc_ * 512)
                            pi = tc_ % 2
                            TE(lambda e: e.matmul(ps[pi][:, :n], lhsT=wbd[ch][:], rhs=poolT[:, ch, tc_ * 512:tc_ * 512 + n], start=True, stop=True),
                               r=[("po_pT", ch), "po_w%d" % ch], w=["po_ps%d" % pi])
                            V(lambda e: e.tensor_scalar(yo[pi][:, :n], ps[pi][:, :n], psc[:, ch:ch + 1], None, op0=ALU.mult), r=["po_ps%d" % pi, "po_sc"], w=["po_y%d" % pi])
                            DM(lambda e: e.dma_start(out=YCT[ch, :, t0 + tc_ * 512:t0 + tc_ * 512 + n], in_=yo[pi][:, :n]), r=["po_y%d" % pi], w=[("YCT", S.dma_next)])
                        for tc_ in range((L + 511) // 512):
                            do_tc(tc_)
                    for ch in range(2):
                        do_ch(ch)
                S.barrier()
            for sg in segs:
                do_seg(*sg)

        def phase_combine(l, last):
            with contextlib.ExitStack() as st:
                w_oa = sb(st, "cb_woa", [128, 4, D], BF16)
                w_ob = sb(st, "cb_wob", [128, 2, D], BF16)
                w_oc = sb(st, "cb_woc", [128, 2, D], BF16)
                w_out = sb(st, "cb_wout", [128, 8, D], BF16)
                w_pq = sb(st, "cb_wpq", [128, 8, 2048], BF16)
                att = sb(st, "cb_att", [128, 4, 512], BF16)
                yf = sb(st, "cb_yf", [128, 2, 512], BF16)
                yc = sb(st, "cb_yc", [128, 2, 512], BF16)
                gT = sb(st, "cb_g", [128, 24, 512], BF16)
                xg = sb(st, "cb_xg", [128, 8, 512], F32)
                sq = sb(st, "cb_sq", [128, 8, 512], F32)
                sqb = sb(st, "cb_sqb", [128, 8, 512], BF16)
                rstd = sb(st, "cb_rstd", [128, 512], F32)
                u = sb(st, "cb_u", [128, 8, 512], BF16)
                h2 = sb(st, "cb_h2", [128, 8, 512], BF16)
                t1 = sb(st, "cb_t1", [128, 512], F32)
                t2 = sb(st, "cb_t2", [128, 512], F32)
                t3 = sb(st, "cb_t3", [128, 512], F32)
                ob = [sb(st, "cb_ob%d" % i, [128, 512], BF16) for i in range(3)]
                ps = [pst(st, "cb_ps%d" % i, [128, 512], F32) for i in range(6)]
                ps_st = pst(st, "cb_pst", [128, 512], F32)
                for k in range(4):
                    DM(lambda e, k=k: e.dma_start(out=w_oa[:, k, :], in_=P["w_oa"][l][k * 128:(k + 1) * 128, :]), w=["cb_woa"], q="gpsimd")
                for k in range(2):
                    DM(lambda e, k=k: e.dma_start(out=w_ob[:, k, :], in_=P["w_ob"][l][k * 128:(k + 1) * 128, :]), w=["cb_wob"], q="gpsimd")
                    DM(lambda e, k=k: e.dma_start(out=w_oc[:, k, :], in_=P["w_oc"][l][k * 128:(k + 1) * 128, :]), w=["cb_woc"], q="gpsimd")
                for k in range(8):
                    DM(lambda e, k=k: e.dma_start(out=w_out[:, k, :], in_=P["w_out"][l][k * 128:(k + 1) * 128, :]), w=["cb_wout"], q="gpsimd")
                    DM(lambda e, k=k: e.dma_start(out=w_pq[:, k, :], in_=P["w_pq"][l][k * 128:(k + 1) * 128, :]), w=["cb_wpq"], q="gpsimd")
                psi = [0]

                def do_group(gi, t0, n):
                    r_ = 1 if gi == 0 else 0
                    DM(lambda e: e.dma_start(out=att[:, :, :n], in_=ATT[:, :, t0:t0 + n].rearrange("k p t -> p k t")), w=["cb_att"])
                    DM(lambda e: e.dma_start(out=yf[:, :, :n], in_=YFT[:, :, t0:t0 + n].rearrange("k p t -> p k t")), w=["cb_yf"])
                    DM(lambda e: e.dma_start(out=yc[:, :, :n], in_=YCT[:, :, t0:t0 + n].rearrange("k p t -> p k t")), w=["cb_yc"])
                    DM(lambda e: e.dma_start(out=gT[:, :, :n], in_=GT[:, :, t0:t0 + n].rearrange("k p t -> p k t")), w=["cb_g"])
                    DM(lambda e: e.dma_start(out=xg[:, :, :n], in_=XT[:, :, t0:t0 + n].rearrange("k p t -> p k t")), w=["cb_xg"])

                    def do_dch(dc):
                        pa, pb, pc = psi[0] % 6, (psi[0] + 1) % 6, (psi[0] + 2) % 6
                        psi[0] += 3
                        for k in range(4):
                            TE(lambda e, k=k: e.matmul(ps[pa][:, :n], lhsT=w_oa[:, k, dc * 128:(dc + 1) * 128], rhs=att[:, k, :n], start=(k == 0), stop=(k == 3)),
                               r=["cb_woa", "cb_att"], w=["cb_ps%d" % pa])
                        for k in range(2):
                            TE(lambda e, k=k: e.matmul(ps[pb][:, :n], lhsT=w_ob[:, k, dc * 128:(dc + 1) * 128], rhs=yf[:, k, :n], start=(k == 0), stop=(k == 1)),
                               r=["cb_wob", "cb_yf"], w=["cb_ps%d" % pb])
                        for k in range(2):
                            TE(lambda e, k=k: e.matmul(ps[pc][:, :n], lhsT=w_oc[:, k, dc * 128:(dc + 1) * 128], rhs=yc[:, k, :n], start=(k == 0), stop=(k == 1)),
                               r=["cb_woc", "cb_yc"], w=["cb_ps%d" % pc])
                        V(lambda e: e.tensor_tensor(out=t1[:, :n], in0=ps[pa][:, :n], in1=gT[:, dc, :n], op=ALU.mult), r=["cb_ps%d" % pa, "cb_g"], w=["cb_t1"])
                        V(lambda e: e.tensor_tensor(out=t2[:, :n], in0=ps[pb][:, :n], in1=gT[:, 8 + dc, :n], op=ALU.mult), r=["cb_ps%d" % pb, "cb_g"], w=["cb_t2"])
                        V(lambda e: e.tensor_tensor(out=t3[:, :n], in0=ps[pc][:, :n], in1=gT[:, 16 + dc, :n], op=ALU.mult), r=["cb_ps%d" % pc, "cb_g"], w=["cb_t3"])
                        G(lambda e: e.tensor_tensor(out=t1[:, :n], in0=t1[:, :n], in1=t2[:, :n], op=ALU.add), r=["cb_t1", "cb_t2"], w=["cb_t1"])
                        G(lambda e: e.tensor_tensor(out=u[:, dc, :n], in0=t1[:, :n], in1=t3[:, :n], op=ALU.add), r=["cb_t1", "cb_t3"], w=[("cb_u", dc)])
                    for dc in range(8):
                        do_dch(dc)
                    ukeys = [("cb_u", dc) for dc in range(8)]

                    def do_out(dc):
                        pm = psi[0] % 6
                        psi[0] += 1
                        for k in range(8):
                            TE(lambda e, k=k: e.matmul(ps[pm][:, :n], lhsT=w_out[:, k, dc * 128:(dc + 1) * 128], rhs=u[:, k, :n], start=(k == 0), stop=(k == 7)),
                               r=ukeys + ["cb_wout"], w=["cb_ps%d" % pm])
                        V(lambda e: e.scalar_tensor_tensor(out=xg[:, dc, :n], in0=ps[pm][:, :n], scalar=gt1[:, dc, r_:r_ + 1], in1=xg[:, dc, :n], op0=ALU.mult, op1=ALU.add),
                          r=["cb_ps%d" % pm, "gt1", "cb_xg"], w=["cb_xg"])
                    for dc in range(8):
                        do_out(dc)
                    DM(lambda e: e.dma_start(out=XT[:, :, t0:t0 + n].rearrange("k p t -> p k t"), in_=xg[:, :, :n]), r=["cb_xg"], w=[("XTw", gi)])
                    rms_stats(xg, 8, n, D, sqb, ps_st, rstd, ["cb_xg"], "cb_sqb", "cb_pst", "cb_rstd")

                    def do_h2(k):
                        V(lambda e: e.tensor_tensor(out=sq[:, k, :n], in0=xg[:, k, :n], in1=rstd[:, :n], op=ALU.mult), r=["cb_xg", "cb_rstd"], w=["cb_sq"])
                        V(lambda e: e.tensor_scalar(h2[:, k, :n], sq[:, k, :n], gm2[:, k, r_:r_ + 1], sh2[:, k, r_:r_ + 1], op0=ALU.mult, op1=ALU.add),
                          r=["cb_sq", "gm2", "sh2"], w=["cb_h2"])
                    for k in range(8):
                        do_h2(k)
                    DM(lambda e: e.dma_start(out=H2T[:, :, t0:t0 + n].rearrange("k p t -> p k t"), in_=h2[:, :, :n]), r=["cb_h2"], w=[("H2T", gi)])

                    def do_pq(nc_):
                        pm = psi[0] % 6
                        psi[0] += 1
                        o = nc_ % 3
                        for k in range(8):
                            TE(lambda e, k=k: e.matmul(ps[pm][:, :n], lhsT=w_pq[:, k, nc_ * 128:(nc_ + 1) * 128], rhs=h2[:, k, :n], start=(k == 0), stop=(k == 7)),
                               r=["cb_h2", "cb_wpq"], w=["cb_ps%d" % pm])
                        evac(ob[o][:, :n], ps[pm][:, :n], r=["cb_ps%d" % pm], w=["cb_ob%d" % o])
                        DM(lambda e: e.dma_start(out=PQT[nc_, :, t0:t0 + n], in_=ob[o][:, :n]), r=["cb_ob%d" % o], w=[("PQT", S.dma_next)])
                    for nc_ in range(16):
                        do_pq(nc_)
                for gi, (t0, n) in enumerate(GROUPS):
                    if last and gi == 0:
                        continue
                    do_group(gi, t0, n)
            S.barrier()

        def emit_cast_tables(st):
            tb = [sb(st, "ct_t%d" % i, [128, 8, D], BF16) for i in range(4)]
            cnt = [0]

            def do_chunk(l, src, o, c):
                cnt[0] += 1
                b = cnt[0] % 4
                DM(lambda e: e.dma_start(out=tb[b][:], in_=src[c * 1024:(c + 1) * 1024, :].rearrange("(p r) d -> p r d", r=8)),
                   w=["ct_t%d" % b], q="gpsimd")
                DM(lambda e: e.dma_start(out=DUB[l][c * 1024:(c + 1) * 1024, o:o + D].rearrange("(p r) d -> p r d", r=8), in_=tb[b][:]),
                   r=["ct_t%d" % b], w=[("cast", S.dma_next)], q="gpsimd")
            for l in range(DEPTH):
                for (src, o) in ((P["peer_down%d" % l], 0), (P["peer_up%d" % l], D)):
                    for c in range(16):
                        do_chunk(l, src, o, c)

        def phase_peer(l, last):
            with contextlib.ExitStack() as st:
                keysT = sb(st, "pe_keysT", [128, 16, 128], BF16)
                iotf = sb(st, "pe_iotf", [128, 16], F32)
                pq = sb(st, "pe_pq", [128, 16, 128], BF16)
                s_sb = sb(st, "pe_s", [128, 16, 128], F32)
                s2 = sb(st, "pe_s2", [128, 16, 128], F32)
                kraw = s2
                sv = sb(st, "pe_sv", [128, 16, 16], F32)
                si = sb(st, "pe_si", [128, 16, 16], U32)
                sif = sb(st, "pe_sif", [128, 16, 16], F32)
                cand = sb(st, "pe_cand", [128, 8, 256], F32)
                cand2 = sb(st, "pe_cand2", [128, 8, 256], F32)
                eq = cand2[:].rearrange("p h (a b) -> p h a b", b=16)
                ts = sb(st, "pe_ts", [128, 8, 16], F32)
                pos = sb(st, "pe_pos", [128, 8, 16], U32)
                pa_i = sb(st, "pe_pai", [128, 8, 16], U32)
                pb_i = sb(st, "pe_pbi", [128, 8, 16], U32)
                pa_f = sb(st, "pe_paf", [128, 8, 16], F32)
                pb_f = sb(st, "pe_pbf", [128, 8, 16], F32)
                If = sb(st, "pe_If", [128, 8, 16], F32)
                Jf = sb(st, "pe_Jf", [128, 8, 16], F32)
                ef = sb(st, "pe_ef", [128, 8, 16], F32)
                gsum = sb(st, "pe_gsum", [128, 8], F32)
                h2T = sb(st, "pe_h2T", [128, 8, 128], BF16)
                eidx_ = [sb(st, "pe_eidx%d" % i, [128, 128], U32) for i in range(2)]
                gate_ = [sb(st, "pe_gate%d" % i, [128, 8, 16], F32) for i in range(2)]
                h2tok_ = [sb(st, "pe_h2tok%d" % i, [128, D], BF16) for i in range(2)]
                xt_ = [sb(st, "pe_xt%d" % i, [128, 8, 128], F32) for i in range(2)]
                NR = 24
                SG = 8
                rows = [sb(st, "pe_rows%d" % i, [128, 2 * D], BF16) for i in range(NR)]
                junk = [sb(st, "pe_junk%d" % i, [128, D], BF16) for i in range(4)]
                junk2 = sb(st, "pe_junk2", [128, D], BF16)
                junk3 = sb(st, "pe_junk3", [128, D], BF16)
                diag8 = [sb(st, "pe_diag%d" % i, [128, SG, 128], BF16) for i in range(3)]
                a_sb = sb(st, "pe_a", [128, 128], F32)
                g1 = sb(st, "pe_g1", [128, 128], F32)
                g2 = sb(st, "pe_g2", [128, 128], F32)
                act = sb(st, "pe_act", [128, 128], F32)
                acc = sb(st, "pe_acc", [128, D], F32)
                ps_sc = [pst(st, "pe_psc%d" % i, [128, 4, 128], F32) for i in range(2)]
                ps_part = [pst(st, "pe_ppart%d" % i, [128, 4, 128], F32) for i in range(2)]
                ps_a = pst(st, "pe_pa", [128, 128], F32)
                part_sb = [sb(st, "pe_partsb%d" % i, [128, 4, 128], F32) for i in range(2)]
                ps_h2 = pst(st, "pe_ph2", [128, 8, 128], BF16)
                ps_acc = [pst(st, "pe_pacc%d" % i, [128, 512], F32) for i in range(2)]

                G(lambda e: e.iota(iotf[:], [[1, 16]], base=0, channel_multiplier=0, allow_small_or_imprecise_dtypes=True), w=["pe_iotf"])
                DM(lambda e: e.dma_start(out=kraw[:], in_=P["peer_keys"][l].rearrange("h p k d -> k (h p) d")), w=["pe_s2"])
                for hp in range(16):
                    bk = (hp // 4) % 2
                    TE(lambda e, hp=hp, bk=bk: e.transpose(ps_sc[bk][:, hp % 4, :], kraw[:, hp, :], ident_f[:]), r=["pe_s2", "ident_f"], w=["pe_psc%d" % bk])
                    if hp % 4 == 3:
                        b4 = hp // 4
                        evac(keysT[:, b4 * 4:(b4 + 1) * 4, :], ps_sc[bk][:], r=["pe_psc%d" % bk], w=["pe_keysT"])
                rcnt = [0]
                gcnt = [0]
                jcnt = [0]
                pcnt = [0]

                def prep_tile(ti):
                    c0 = ti * 128
                    tb = ti % 2
                    eidx, gate, h2tok, xt = eidx_[tb], gate_[tb], h2tok_[tb], xt_[tb]
                    kE, kG, kH, kX = "pe_eidx%d" % tb, "pe_gate%d" % tb, "pe_h2tok%d" % tb, "pe_xt%d" % tb
                    DM(lambda e: e.dma_start(out=pq[:], in_=PQT[:, :, c0:c0 + 128].rearrange("n p t -> p n t")), w=["pe_pq"])
                    DM(lambda e: e.dma_start(out=h2T[:], in_=H2T[:, :, c0:c0 + 128].rearrange("k p t -> p k t")), w=["pe_h2T"])
                    DM(lambda e: e.dma_start(out=xt[:], in_=XT[:, :, c0:c0 + 128].rearrange("k p t -> p k t")), w=[kX])
                    for hp in range(16):
                        bk = (hp // 4) % 2
                        TE(lambda e, hp=hp, bk=bk: e.matmul(ps_sc[bk][:, hp % 4, :], lhsT=pq[:, hp, :], rhs=keysT[:, hp, :], start=True, stop=True),
                           r=["pe_pq", "pe_keysT"], w=["pe_psc%d" % bk])
                        if hp % 4 == 3:
                            b4 = hp // 4
                            evac(s_sb[:, b4 * 4:(b4 + 1) * 4, :], ps_sc[bk][:], r=["pe_psc%d" % bk], w=["pe_s"])
                            yield
                    for k in range(8):
                        TE(lambda e, k=k: e.transpose(ps_h2[:, k, :], h2T[:, k, :], ident_b[:]), r=["pe_h2T", "ident_b"], w=["pe_ph2"])
                    A(lambda e: e.activation(out=h2tok[:], in_=ps_h2[:].rearrange("p k d -> p (k d)"), func=AF.Copy), r=["pe_ph2"], w=[kH])
                    yield
                    for hp in range(16):
                        V(lambda e, hp=hp: e.max(out=sv[:, hp, 0:8], in_=s_sb[:, hp, :]), r=["pe_s"], w=["pe_sv"])
                        V(lambda e, hp=hp: e.max_index(out=si[:, hp, 0:8], in_max=sv[:, hp, 0:8], in_values=s_sb[:, hp, :]), r=["pe_s", "pe_sv"], w=["pe_si"])
                        V(lambda e, hp=hp: e.match_replace(out=s2[:, hp, :], in_to_replace=sv[:, hp, 0:8], in_values=s_sb[:, hp, :], imm_value=-1e30),
                          r=["pe_s", "pe_sv"], w=["pe_s2"])
                        V(lambda e, hp=hp: e.max(out=sv[:, hp, 8:16], in_=s2[:, hp, :]), r=["pe_s2"], w=["pe_sv"])
                        V(lambda e, hp=hp: e.max_index(out=si[:, hp, 8:16], in_max=sv[:, hp, 8:16], in_values=s2[:, hp, :]), r=["pe_s2", "pe_sv"], w=["pe_si"])
                        yield
                    V(lambda e: e.tensor_copy(sif[:], si[:]), r=["pe_si"], w=["pe_sif"])
                    svv = sv[:].rearrange("p (h q) a -> p h q a", q=2)
                    sfv = sif[:].rearrange("p (h q) a -> p h q a", q=2)
                    candv = cand[:].rearrange("p h (a b) -> p h a b", b=16)
                    V(lambda e: e.tensor_tensor(out=candv, in0=svv[:, :, 0, :].unsqueeze(3).to_broadcast([128, 8, 16, 16]),
                                                in1=svv[:, :, 1, :].unsqueeze(2).to_broadcast([128, 8, 16, 16]), op=ALU.add), r=["pe_sv"], w=["pe_cand"])
                    for h in range(8):
                        V(lambda e, h=h: e.max(out=ts[:, h, 0:8], in_=cand[:, h, :]), r=["pe_cand"], w=["pe_ts"])
                        V(lambda e, h=h: e.max_index(out=pos[:, h, 0:8], in_max=ts[:, h, 0:8], in_values=cand[:, h, :]), r=["pe_cand", "pe_ts"], w=["pe_pos"])
                        V(lambda e, h=h: e.match_replace(out=cand2[:, h, :], in_to_replace=ts[:, h, 0:8], in_values=cand[:, h, :], imm_value=-1e30),
                          r=["pe_cand", "pe_ts"], w=["pe_cand2"])
                        V(lambda e, h=h: e.max(out=ts[:, h, 8:16], in_=cand2[:, h, :]), r=["pe_cand2"], w=["pe_ts"])
                        V(lambda e, h=h: e.max_index(out=pos[:, h, 8:16], in_max=ts[:, h, 8:16], in_values=cand2[:, h, :]), r=["pe_cand2", "pe_ts"], w=["pe_pos"])
                        yield
                    V(lambda e: e.tensor_single_scalar(pa_i[:], pos[:], 4, op=ALU.logical_shift_right), r=["pe_pos"], w=["pe_pai"])
                    V(lambda e: e.tensor_single_scalar(pb_i[:], pos[:], 15, op=ALU.bitwise_and), r=["pe_pos"], w=["pe_pbi"])
                    V(lambda e: e.tensor_copy(pa_f[:], pa_i[:]), r=["pe_pai"], w=["pe_paf"])
                    V(lambda e: e.tensor_copy(pb_f[:], pb_i[:]), r=["pe_pbi"], w=["pe_pbf"])
                    iob = iotf[:].unsqueeze(1).unsqueeze(1).to_broadcast([128, 8, 16, 16])
                    for (pf, half, dst, kd) in ((pa_f, 0, If, "pe_If"), (pb_f, 1, Jf, "pe_Jf")):
                        V(lambda e, pf=pf: e.tensor_tensor(out=eq, in0=iob, in1=pf[:].unsqueeze(3).to_broadcast([128, 8, 16, 16]), op=ALU.is_equal),
                          r=["pe_iotf", "pe_paf", "pe_pbf"], w=["pe_cand2"])
                        V(lambda e, half=half: e.tensor_tensor(out=eq, in0=eq, in1=sfv[:, :, half, :].unsqueeze(2).to_broadcast([128, 8, 16, 16]), op=ALU.mult),
                          r=["pe_cand2", "pe_sif"], w=["pe_cand2"])
                        V(lambda e, dst=dst: e.tensor_reduce(out=dst[:], in_=eq, axis=AX.X, op=ALU.add), r=["pe_cand2"], w=[kd])
                        yield
                    V(lambda e: e.scalar_tensor_tensor(out=ef[:], in0=If[:], scalar=128.0, in1=Jf[:], op0=ALU.mult, op1=ALU.add), r=["pe_If", "pe_Jf"], w=["pe_ef"])
                    V(lambda e: e.tensor_copy(eidx[:], ef[:].rearrange("p h k -> p (h k)")), r=["pe_ef"], w=[kE])
                    yield
                    V(lambda e: e.tensor_tensor(out=gate[:], in0=ts[:], in1=ts[:, :, 0:1].to_broadcast([128, 8, 16]), op=ALU.subtract), r=["pe_ts"], w=[kG])
                    A(lambda e: e.activation(out=gate[:], in_=gate[:], func=AF.Exp), r=[kG], w=[kG])
                    V(lambda e: e.tensor_reduce(out=gsum[:], in_=gate[:], axis=AX.X, op=ALU.add), r=[kG], w=["pe_gsum"])
                    V(lambda e: e.reciprocal(gsum[:], gsum[:]), r=["pe_gsum"], w=["pe_gsum"])
                    V(lambda e: e.tensor_tensor(out=gate[:], in0=gate[:], in1=gsum[:].unsqueeze(2).to_broadcast([128, 8, 16]), op=ALU.mult), r=[kG, "pe_gsum"], w=[kG])

                def run_tile(ti, nxt=None):
                    c0 = ti * 128
                    r_ = 1 if ti < 2 else 0
                    tb = ti % 2
                    eidx, gate, h2tok, xt = eidx_[tb], gate_[tb], h2tok_[tb], xt_[tb]
                    kE, kG, kH, kX = "pe_eidx%d" % tb, "pe_gate%d" % tb, "pe_h2tok%d" % tb, "pe_xt%d" % tb
                    gatev = gate[:].rearrange("p h k -> p (h k)")

                    gbufs = {}

                    def stage1a(g0):
                        bufs = []
                        for sl in range(g0, g0 + SG):
                            rcnt[0] += 1
                            b = rcnt[0] % NR
                            bufs.append(b)
                            DM(lambda e, b=b, sl=sl: e.indirect_dma_start(out=rows[b][:, :], out_offset=None, in_=DUB[l],
                                                                          in_offset=bass.IndirectOffsetOnAxis(ap=eidx[:, sl:sl + 1], axis=0)),
                               r=[kE], w=["pe_rows%d" % b], q="gpsimd")
                        gbufs[g0] = bufs

                    def stage1b(g0):
                        for q4 in range(SG // 4):
                            pcnt[0] += 1
                            pb = pcnt[0] % 2
                            for i4 in range(4):
                                i_ = q4 * 4 + i4
                                b = gbufs[g0][i_]
                                jcnt[0] += 1
                                jb = jcnt[0] % 4
                                V(lambda e, b=b, jb=jb: e.tensor_tensor(out=junk[jb][:], in0=rows[b][:, 0:D], in1=h2tok[:], op=ALU.mult),
                                  r=["pe_rows%d" % b, kH], w=["pe_junk%d" % jb])
                                for c in range(8):
                                    TE(lambda e, jb=jb, c=c, i4=i4, pb=pb: e.matmul(ps_part[pb][:, i4, :], lhsT=junk[jb][:, c * 128:(c + 1) * 128], rhs=ident_b[:],
                                                                                   start=(c == 0), stop=(c == 7)),
                                       r=["pe_junk%d" % jb, "ident_b"], w=["pe_ppart%d" % pb])
                            A(lambda e, pb=pb: e.activation(out=part_sb[pb][:], in_=ps_part[pb][:], func=AF.Copy), r=["pe_ppart%d" % pb], w=["pe_partsb%d" % pb])
                            for i4 in range(4):
                                sl = g0 + q4 * 4 + i4
                                TE(lambda e, i4=i4, sl=sl, pb=pb: e.matmul(ps_a[:, sl:sl + 1], lhsT=part_sb[pb][:, i4, :], rhs=ones_f[:, 0:1], start=True, stop=True),
                                   r=["pe_partsb%d" % pb, "ones_f"], w=["pe_pa"])
                        A(lambda e: e.activation(out=a_sb[:, g0:g0 + SG], in_=ps_a[:, g0:g0 + SG], func=AF.Copy), r=["pe_pa"], w=[("pe_a", g0)])

                    def stage2(g0):
                        bufs = gbufs[g0]
                        gcnt[0] += 1
                        dg = gcnt[0] % 3
                        gs = slice(g0, g0 + SG)
                        V(lambda e: e.tensor_tensor(out=g1[:, gs], in0=a_sb[:, gs], in1=a_sb[:, gs], op=ALU.mult), r=[("pe_a", g0)], w=[("pe_g1", g0)])
                        V(lambda e: e.tensor_scalar(g1[:, gs], g1[:, gs], 0.044715, 1.0, op0=ALU.mult, op1=ALU.add), r=[("pe_g1", g0)], w=[("pe_g1", g0)])
                        V(lambda e: e.tensor_tensor(out=g1[:, gs], in0=g1[:, gs], in1=a_sb[:, gs], op=ALU.mult), r=[("pe_g1", g0), ("pe_a", g0)], w=[("pe_g1", g0)])
                        A(lambda e: e.activation(out=g2[:, gs], in_=g1[:, gs], func=AF.Sigmoid, scale=1.5957691216057308), r=[("pe_g1", g0)], w=[("pe_g2", g0)])
                        V(lambda e: e.tensor_tensor(out=g2[:, gs], in0=g2[:, gs], in1=a_sb[:, gs], op=ALU.mult), r=[("pe_g2", g0), ("pe_a", g0)], w=[("pe_g2", g0)])
                        V(lambda e: e.tensor_tensor(out=act[:, gs], in0=g2[:, gs], in1=gatev[:, gs], op=ALU.mult), r=[("pe_g2", g0), kG], w=[("pe_act", g0)])
                        V(lambda e: e.tensor_tensor(out=diag8[dg][:], in0=ident_b[:].unsqueeze(1).to_broadcast([128, SG, 128]),
                                                    in1=act[:, gs].unsqueeze(2).to_broadcast([128, SG, 128]), op=ALU.mult),
                          r=["ident_b", ("pe_act", g0)], w=["pe_diag%d" % dg])
                        for i_, sl in enumerate(range(g0, g0 + SG)):
                            b = bufs[i_]
                            for hf in range(2):
                                TE(lambda e, hf=hf, b=b, sl=sl, i_=i_: e.matmul(ps_acc[hf][:, :], lhsT=diag8[dg][:, i_, :], rhs=rows[b][:, D + hf * 512:D + (hf + 1) * 512],
                                                                                start=(sl == 0), stop=(sl == 127)),
                                   r=["pe_diag%d" % dg, "pe_rows%d" % b], w=["pe_pacc%d" % hf])
                    glist = list(range(0, 128, SG))

                    def advance(nsteps):
                        if nxt is not None:
                            for _ in range(nsteps):
                                next(nxt, None)
                    stage1a(glist[0])
                    if len(glist) > 1:
                        stage1a(glist[1])
                    stage1b(glist[0])
                    for gi_, g0 in enumerate(glist):
                        if gi_ + 2 < len(glist):
                            stage1a(glist[gi_ + 2])
                        if gi_ + 1 < len(glist):
                            stage1b(glist[gi_ + 1])
                        stage2(g0)
                        advance(3)
                    if nxt is not None:
                        for _ in nxt:
                            pass
                    for hf in range(2):
                        evac(acc[:, hf * 512:(hf + 1) * 512], ps_acc[hf][:, :], r=["pe_pacc%d" % hf], w=[("pe_acc", hf)])
                    for k in range(8):
                        TE(lambda e, k=k: e.transpose(ps_sc[k // 4][:, k % 4, :], acc[:, k * 128:(k + 1) * 128], ident_f[:]), r=[("pe_acc", k // 4), "ident_f"], w=["pe_psc%d" % (k // 4)])
                    for k in range(8):
                        V(lambda e, k=k: e.scalar_tensor_tensor(out=xt[:, k, :], in0=ps_sc[k // 4][:, k % 4, :], scalar=gt2[:, k, r_:r_ + 1], in1=xt[:, k, :],
                                                               op0=ALU.mult, op1=ALU.add),
                          r=["pe_psc%d" % (k // 4), "gt2", kX], w=[kX])
                    DM(lambda e: e.dma_start(out=XT[:, :, c0:c0 + 128].rearrange("k p t -> p k t"), in_=xt[:]), r=[kX], w=[("XTw", ti)])

                tiles = [ti for ti in range(2 if last else 0, T // 128) if peer_tiles is None or ti in peer_tiles]
                if tiles:
                    for _ in prep_tile(tiles[0]):
                        pass
                for i, ti in enumerate(tiles):
                    run_tile(ti, prep_tile(tiles[i + 1]) if i + 1 < len(tiles) else None)
            S.barrier()

        def phase_final():
            with contextlib.ExitStack() as st:
                fg = sb(st, "fn_g", [128, 8], F32)
                xg = sb(st, "fn_xg", [128, 8, 512], F32)
                sq = sb(st, "fn_sq", [128, 8, 512], F32)
                sqb = sb(st, "fn_sqb", [128, 8, 512], BF16)
                rstd = sb(st, "fn_rstd", [128, 512], F32)
                ot = [sb(st, "fn_o%d" % i, [128, D], F32) for i in range(2)]
                ps_st = pst(st, "fn_pst", [128, 512], F32)
                ps = [pst(st, "fn_ps%d" % i, [128, 4, 128], F32) for i in range(4)]
                DM(lambda e: e.dma_start(out=fg[:], in_=P["final_g"].rearrange("(n p) -> p n", p=128), allow_slow_non_contiguous=True), w=["fn_g"])
                cnt = [0]

                def do_group(gi, t0, n):
                    DM(lambda e: e.dma_start(out=xg[:, :, :n], in_=XT[:, :, t0:t0 + n].rearrange("k p t -> p k t")), w=["fn_xg"])
                    rms_stats(xg, 8, n, D, sqb, ps_st, rstd, ["fn_xg"], "fn_sqb", "fn_pst", "fn_rstd")

                    def do_k(k):
                        V(lambda e: e.tensor_tensor(out=sq[:, k, :n], in0=xg[:, k, :n], in1=rstd[:, :n], op=ALU.mult), r=["fn_xg", "fn_rstd"], w=["fn_sq"])
                        V(lambda e: e.tensor_scalar(xg[:, k, :n], sq[:, k, :n], fg[:, k:k + 1], None, op0=ALU.mult), r=["fn_sq", "fn_g"], w=["fn_xg"])
                    for k in range(8):
                        do_k(k)

                    def do_j(j):
                        cnt[0] += 1
                        ob_ = cnt[0] % 2
                        for half in range(2):
                            pi = (cnt[0] * 2 + half) % 4
                            for kk in range(4):
                                k = half * 4 + kk
                                TE(lambda e, k=k, kk=kk, pi=pi: e.transpose(ps[pi][:, kk, :], xg[:, k, j * 128:(j + 1) * 128], ident_f[:]),
                                   r=["fn_xg", "ident_f"], w=["fn_ps%d" % pi])
                            evac(ot[ob_][:, half * 512:(half + 1) * 512], ps[pi][:].rearrange("p a b -> p (a b)"), r=["fn_ps%d" % pi], w=[("fn_o", ob_, half)])
                        row0 = t0 - CTX + j * 128
                        DM(lambda e: e.dma_start(out=out[row0:row0 + 128, :], in_=ot[ob_][:]), r=[("fn_o", ob_, 0), ("fn_o", ob_, 1)], w=[("out", row0)])
                    for j in range(n // 128):
                        do_j(j)
                for gi, (t0, n) in enumerate(GROUPS):
                    if gi == 0:
                        continue
                    do_group(gi, t0, n)
            S.barrier()

        phase_load_x()
        for l in range(DEPTH):
            if stop_after is not None and stop_after[0] == "castonly":
                break
            last = l == DEPTH - 1
            phase_mod(l)
            phase_proj(l, last)
            if stop_after == ("proj", l):
                break
            phase_attn(l, last)
            phase_fourier(l, last)
            phase_pool(l, last)
            if stop_after == ("mix", l):
                break
            phase_combine(l, last)
            if stop_after == ("combine", l):
                break
            phase_peer(l, last)
            if stop_after == ("peer", l):
                break
        if stop_after is None:
            phase_final()
        S.emit()
    return nc, S


def make_in_maps(inputs, cores):
    consts = _const_tables()
    maps = []
    for b in cores:
        m = {"x": np.ascontiguousarray(inputs["x"][b]), "ctx": np.ascontiguousarray(inputs["ctx"][b]),
             "cvec": np.ascontiguousarray(np.stack([inputs["c"][b], inputs["c_ctx"]], axis=0))}
        for k in PARAM_SHAPES:
            if k.startswith("peer_down") or k.startswith("peer_up"):
                m[k] = np.ascontiguousarray(inputs[k[:-1]][int(k[-1])])
            else:
                m[k] = np.ascontiguousarray(inputs[k])
        m.update(consts)
        maps.append(m)
    return maps


def kernel(**inputs):
    inputs = {k: np.asarray(v) for k, v in inputs.items()}
    nc, _ = build()
    cores = list(range(8))
    res = run_bass_kernel_spmd(nc, make_in_maps(inputs, cores), core_ids=cores)
    return np.stack([res.results[i]["out"] for i in cores], axis=0).astype(np.float32)
```

```python
import contextlib
import math
import numpy as np
import ml_dtypes
import concourse.bass as bass
import concourse.mybir as mybir
from concourse.bass_utils import run_bass_kernel_spmd

F32 = mybir.dt.float32
BF16 = mybir.dt.bfloat16
U32 = mybir.dt.uint32
I32 = mybir.dt.int32
AF = mybir.ActivationFunctionType
ALU = mybir.AluOpType
AX = mybir.AxisListType

ENGS = ("sync", "scalar", "vector", "gpsimd", "tensor")
N_DMA_SEMS = 90

D = 1024
SEQ = 4096
CTX = 256
T = SEQ + CTX
DEPTH = 2
NH = 8
EPS = 1e-6
ATT_SCALE = 96 ** -0.5
IN_W = 4000
NEXP = 16384


class Sched:
    def __init__(self, nc):
        self.nc = nc
        self.ops = {e: [] for e in ENGS}
        self.count = {e: 0 for e in ENGS}
        self.known = {e: {} for e in ENGS}
        self.last_w = {}
        self.readers = {}
        self.dma_next = 0
        self.dma_target = [0] * N_DMA_SEMS
        self.pending = {e: [] for e in ENGS}
        self.n_instr = 0

    def _need(self, eng, tok, waits):
        if tok is None:
            return
        kind, src, val = tok
        if kind == "e" and src == eng and eng == "tensor":
            return
        k = (kind, src)
        if self.known[eng].get(k, 0) >= val:
            return
        self.known[eng][k] = val
        waits.append(tok)

    def _deps(self, eng, r, w):
        waits = []
        for tok in self.pending[eng]:
            self._need(eng, tok, waits)
        self.pending[eng] = []
        for key in r:
            self._need(eng, self.last_w.get(key), waits)
        for key in w:
            self._need(eng, self.last_w.get(key), waits)
            for tok in self.readers.get(key, ()):
                self._need(eng, tok, waits)
        best = {}
        for kind, src, val in waits:
            k = (kind, src)
            if best.get(k, 0) < val:
                best[k] = val
        return [(k[0], k[1], v) for k, v in best.items()]

    def _commit(self, tok, r, w):
        for key in r:
            self.readers.setdefault(key, []).append(tok)
        for key in w:
            self.last_w[key] = tok
            self.readers[key] = []

    def op(self, eng, fn, r=(), w=()):
        waits = self._deps(eng, r, w)
        self.count[eng] += 1
        tok = ("e", eng, self.count[eng])
        self._commit(tok, r, w)
        self.ops[eng].append(("op", fn, waits, None))
        self.n_instr += 1 + len(waits)
        return tok

    def dma(self, eng, fn, r=(), w=()):
        s = self.dma_next % N_DMA_SEMS
        self.dma_next += 1
        waits = self._deps(eng, r, w)
        prev = self.dma_target[s]
        if prev > 0:
            k = ("d", s)
            if self.known[eng].get(k, 0) < prev:
                self.known[eng][k] = prev
                waits.append(("d", s, prev))
        self.dma_target[s] = prev + 16
        tok = ("d", s, prev + 16)
        self._commit(tok, r, w)
        self.ops[eng].append(("dma", fn, waits, s))
        self.n_instr += 1 + len(waits)
        return tok

    def barrier(self):
        waits = []
        for e in ENGS:
            if e != "sync" and self.count[e] > 0:
                self._need("sync", ("e", e, self.count[e]), waits)
        for s in range(N_DMA_SEMS):
            if self.dma_target[s] > 0:
                self._need("sync", ("d", s, self.dma_target[s]), waits)
        self.count["sync"] += 1
        tok = ("e", "sync", self.count["sync"])
        self.ops["sync"].append(("op", lambda e: e.nop(), waits, None))
        self.n_instr += 1 + len(waits)
        for e in ENGS:
            if e != "sync":
                self.pending[e].append(tok)
                for e2 in ENGS:
                    self.known[e][("e", e2)] = max(self.known[e].get(("e", e2), 0),
                                                   self.count[e2] if e2 != "sync" else 0)
                for s in range(N_DMA_SEMS):
                    self.known[e][("d", s)] = self.dma_target[s]
                self.known[e].pop(("e", "sync"), None)
        self.last_w = {}
        self.readers = {}

    def emit(self):
        nc = self.nc
        with contextlib.ExitStack() as st:
            esem = {e: st.enter_context(nc.semaphore("c_" + e)) for e in ENGS}
            dsem = [st.enter_context(nc.semaphore("d%d" % i)) for i in range(N_DMA_SEMS)]
            block = st.enter_context(nc.Block())

            def mk(ename):
                def body(eng):
                    for kind, fn, waits, s in self.ops[ename]:
                        for wk, src, val in waits:
                            eng.wait_ge(esem[src] if wk == "e" else dsem[src], val)
                        ins = fn(eng)
                        if kind == "op":
                            ins.then_inc(esem[ename], 1)
                        else:
                            ins.then_inc(dsem[s], 16)
                    if ename == "sync":
                        for i in range(N_DMA_SEMS):
                            if self.dma_target[i] > 0:
                                eng.wait_ge(dsem[i], self.dma_target[i])
                        for e2 in ENGS:
                            if e2 != "sync" and self.count[e2] > 0:
                                eng.wait_ge(esem[e2], self.count[e2])
                return body

            block.sync(mk("sync"))
            block.scalar(mk("scalar"))
            block.vector(mk("vector"))
            block.gpsimd(mk("gpsimd"))
            block.tensor(mk("tensor"))


def _const_tables():
    c = {}
    half = 16
    inv_freq = (10000.0 ** (-np.arange(0, half, 2, dtype=np.float32) / half)).astype(np.float32)
    t = np.arange(SEQ)
    r = (t // 64).astype(np.float32)
    cl = (t % 64).astype(np.float32)
    ang_r = (r[None, :] * inv_freq[:, None]).astype(np.float32)
    ang_c = (cl[None, :] * inv_freq[:, None]).astype(np.float32)
    ang = np.concatenate([ang_r, ang_r, ang_c, ang_c], axis=0)
    c["ropecos"] = np.cos(ang).astype(np.float32)
    c["ropesin"] = np.sin(ang).astype(np.float32)
    def dft(n):
        k = np.arange(n, dtype=np.int64)
        kk = (k[:, None] * k[None, :]) % n
        a = 2.0 * np.pi * kk.astype(np.float64) / n
        return np.cos(a), np.sin(a)
    cc, sc = dft(256)
    c["ccs"] = np.concatenate([cc, -sc], axis=1).astype(ml_dtypes.bfloat16)
    c["cl256"] = cc.astype(ml_dtypes.bfloat16)
    c["sl256"] = sc.astype(ml_dtypes.bfloat16)
    cL, sL = dft(SEQ)
    c["cl4096"] = cL.astype(ml_dtypes.bfloat16)
    c["sl4096"] = sL.astype(ml_dtypes.bfloat16)
    for L, name in ((SEQ, "icnt4096"), (CTX, "icnt256")):
        tt = np.arange(L)
        tab = np.zeros((2, 128, L), np.float32)
        for gi, w in enumerate((2, 4, 8, 16)):
            lo = np.clip(tt - w // 2, 0, L)
            hi = np.clip(tt - w // 2 + w, 0, L)
            tab[gi // 2, (gi % 2) * 64:(gi % 2) * 64 + 64, :] = (1.0 / (hi - lo).astype(np.float32))[None, :]
        c[name] = tab
    return c


CONST_SHAPES = {
    "ropecos": ([32, SEQ], F32), "ropesin": ([32, SEQ], F32),
    "ccs": ([256, 512], BF16), "cl256": ([256, 256], BF16), "sl256": ([256, 256], BF16),
    "cl4096": ([SEQ, SEQ], BF16), "sl4096": ([SEQ, SEQ], BF16),
    "icnt4096": ([2, 128, SEQ], F32), "icnt256": ([2, 128, CTX], F32),
}

PARAM_SHAPES = {
    "w_mod": [DEPTH, D, 6 * D], "b_mod": [DEPTH, 6 * D], "norm1_g": [DEPTH, D], "norm2_g": [DEPTH, D],
    "w_in": [DEPTH, D, IN_W], "b_gate": [DEPTH, 3 * D], "q_norm_g": [DEPTH, 256], "w_uq": [DEPTH, 256, 768],
    "kv_norm_g": [DEPTH, 128], "w_ukv": [DEPTH, 128, 1024], "w_oa": [DEPTH, 512, D], "w_ob": [DEPTH, 256, D],
    "w_grp": [DEPTH, 4, 64, 64], "pool_scale": [DEPTH, 256], "w_oc": [DEPTH, 256, D], "w_out": [DEPTH, D, D],
    "w_pq": [DEPTH, D, 2048], "peer_keys": [DEPTH, 8, 2, 128, 128], "peer_down0": [NEXP, D], "peer_down1": [NEXP, D],
    "peer_up0": [NEXP, D], "peer_up1": [NEXP, D], "final_g": [D],
}

GROUPS = [(0, CTX)] + [(CTX + 512 * g, 512) for g in range(SEQ // 512)]


def build(stop_after=None, dbg=(), peer_tiles=None):
    nc = bass.Bass("TRN2", target_bir_lowering=False)
    S = Sched(nc)

    def dram_in(name, shape, dt=F32):
        return nc.dram_tensor(name, shape, dt, kind="ExternalInput").ap()

    def dram_scratch(name, shape, dt):
        kind = "ExternalOutput" if name in dbg else "Internal"
        return nc.dram_tensor(name, shape, dt, kind=kind).ap()

    xin = dram_in("x", [SEQ, D])
    ctxin = dram_in("ctx", [CTX, D])
    cvec = dram_in("cvec", [2, D])
    P = {k: dram_in(k, v) for k, v in PARAM_SHAPES.items()}
    C = {k: dram_in(k, v[0], v[1]) for k, v in CONST_SHAPES.items()}
    out = nc.dram_tensor("out", [SEQ, D], F32, kind="ExternalOutput").ap()

    XT = dram_scratch("XT", [8, 128, T], F32)
    GT = dram_scratch("GT", [24, 128, T], BF16)
    QT = dram_scratch("QT", [NH, 96, T], BF16)
    KT = dram_scratch("KT", [NH, 96, T], BF16)
    VV = dram_scratch("VV", [T, 512], BF16)
    ZFT = dram_scratch("ZFT", [2, 128, T], BF16)
    ZPT = dram_scratch("ZPT", [2, 128, T], F32)
    ATT = dram_scratch("ATT", [4, 128, T], BF16)
    YFT = dram_scratch("YFT", [2, 128, T], BF16)
    YCT = dram_scratch("YCT", [2, 128, T], BF16)
    H2T = dram_scratch("H2T", [8, 128, T], BF16)
    PQT = dram_scratch("PQT", [16, 128, T], BF16)
    DUB = [dram_scratch("DUB%d" % i, [NEXP, 2 * D], BF16) for i in range(DEPTH)]

    V = lambda fn, r=(), w=(): S.op("vector", fn, r, w)
    A = lambda fn, r=(), w=(): S.op("scalar", fn, r, w)
    G = lambda fn, r=(), w=(): S.op("gpsimd", fn, r, w)
    TE = lambda fn, r=(), w=(): S.op("tensor", fn, r, w)
    DM = lambda fn, r=(), w=(), q="sync": S.dma(q, fn, r, w)

    rr = [0]

    def evac(out_ap, in_ap, r, w, scale=None):
        rr[0] += 1
        if rr[0] % 2 == 0:
            if scale is None:
                V(lambda e: e.tensor_copy(out_ap, in_ap), r, w)
            else:
                V(lambda e: e.tensor_single_scalar(out_ap, in_ap, float(scale), op=ALU.mult), r, w)
        else:
            A(lambda e: e.activation(out=out_ap, in_=in_ap, func=AF.Copy,
                                     scale=1.0 if scale is None else float(scale)), r, w)

    with contextlib.ExitStack() as top:
        uniq = [0]

        def sb(st, name, shape, dt):
            uniq[0] += 1
            return st.enter_context(nc.sbuf_tensor("%s_%d" % (name, uniq[0]), shape, dt))

        def pst(st, name, shape, dt):
            uniq[0] += 1
            return st.enter_context(nc.psum_tensor("%s_%d" % (name, uniq[0]), shape, dt))

        ident_f = sb(top, "ident_f", [128, 128], F32)
        ident_b = sb(top, "ident_b", [128, 128], BF16)
        ones_f = sb(top, "ones_f", [128, 128], F32)
        ones_b = sb(top, "ones_b", [128, 128], BF16)
        iot = sb(top, "iot", [128, 128], F32)
        gm1 = sb(top, "gm1", [128, 8, 2], F32)
        sh1 = sb(top, "sh1", [128, 8, 2], F32)
        gt1 = sb(top, "gt1", [128, 8, 2], F32)
        gm2 = sb(top, "gm2", [128, 8, 2], F32)
        sh2 = sb(top, "sh2", [128, 8, 2], F32)
        gt2 = sb(top, "gt2", [128, 8, 2], F32)
        epsb = sb(top, "epsb", [128, 1], F32)

        G(lambda e: e.iota(iot[:], [[1, 128]], base=0, channel_multiplier=-1,
                           allow_small_or_imprecise_dtypes=True), w=["iot"])
        V(lambda e: e.tensor_single_scalar(ident_f[:], iot[:], 0.0, op=ALU.is_equal), r=["iot"], w=["ident_f"])
        V(lambda e: e.tensor_single_scalar(ident_b[:], iot[:], 0.0, op=ALU.is_equal), r=["iot"], w=["ident_b"])
        V(lambda e: e.memset(ones_f[:], 1.0), w=["ones_f"])
        V(lambda e: e.memset(ones_b[:], 1.0), w=["ones_b"])
        V(lambda e: e.memset(epsb[:], EPS), w=["epsb"])

        def phase_load_x():
            with contextlib.ExitStack() as st:
                xt = [sb(st, "ld_x%d" % i, [128, D], F32) for i in range(2)]
                xo = [sb(st, "ld_o%d" % i, [128, 8, 128], F32) for i in range(2)]
                ps = [pst(st, "ld_ps%d" % i, [128, 4, 128], F32) for i in range(4)]
                ntile = T // 128
                for ti in range(ntile):
                    b = ti % 2
                    src = ctxin[ti * 128:(ti + 1) * 128, :] if ti < 2 else xin[(ti - 2) * 128:(ti - 1) * 128, :]
                    DM(lambda e, b=b, src=src: e.dma_start(out=xt[b][:], in_=src), w=["ld_x%d" % b])
                    for half in range(2):
                        pi = (ti * 2 + half) % 4
                        for j in range(4):
                            k = half * 4 + j
                            TE(lambda e, b=b, pi=pi, j=j, k=k: e.transpose(ps[pi][:, j, :], xt[b][:, k * 128:(k + 1) * 128], ident_f[:]),
                               r=["ld_x%d" % b, "ident_f"], w=["ld_ps%d" % pi])
                        evac(xo[b][:, half * 4:(half + 1) * 4, :], ps[pi][:], r=["ld_ps%d" % pi], w=[("ld_o", b, half)])
                    DM(lambda e, b=b, ti=ti: e.dma_start(out=XT[:, :, ti * 128:(ti + 1) * 128].rearrange("k p t -> p k t"), in_=xo[b][:]),
                       r=[("ld_o", b, 0), ("ld_o", b, 1)], w=[("XT", ti)])
            S.barrier()

        def phase_mod(l):
            with contextlib.ExitStack() as st:
                cT = sb(st, "md_c", [128, 2, 8], F32)
                scT = sb(st, "md_sc", [128, 8, 2], F32)
                bm = sb(st, "md_b", [128, 48], F32)
                g1 = sb(st, "md_g1", [128, 8], F32)
                g2 = sb(st, "md_g2", [128, 8], F32)
                modT = sb(st, "md_mod", [128, 48, 2], F32)
                wm = [sb(st, "md_w%d" % i, [128, 8, 512], F32) for i in range(2)]
                mps = pst(st, "md_ps", [128, 48, 2], F32)
                for r_ in range(2):
                    DM(lambda e, r_=r_: e.dma_start(out=cT[:, r_, :], in_=cvec[r_].rearrange("(k p) -> p k", p=128), allow_slow_non_contiguous=True), w=["md_c"])
                DM(lambda e: e.dma_start(out=bm[:], in_=P["b_mod"][l].rearrange("(n p) -> p n", p=128), allow_slow_non_contiguous=True), w=["md_b"])
                DM(lambda e: e.dma_start(out=g1[:], in_=P["norm1_g"][l].rearrange("(n p) -> p n", p=128), allow_slow_non_contiguous=True), w=["md_g1"])
                DM(lambda e: e.dma_start(out=g2[:], in_=P["norm2_g"][l].rearrange("(n p) -> p n", p=128), allow_slow_non_contiguous=True), w=["md_g2"])
                for r_ in range(2):
                    A(lambda e, r_=r_: e.activation(out=scT[:, :, r_], in_=cT[:, r_, :], func=AF.Silu), r=["md_c"], w=["md_sc"])
                for j in range(12):
                    b = j % 2
                    DM(lambda e, b=b, j=j: e.dma_start(out=wm[b][:], in_=P["w_mod"][l][:, j * 512:(j + 1) * 512].rearrange("(k p) n -> p k n", p=128)),
                       w=["md_w%d" % b])
                    for i in range(4):
                        n = j * 4 + i
                        for k in range(8):
                            TE(lambda e, b=b, i=i, k=k, n=n: e.matmul(mps[:, n, :], lhsT=wm[b][:, k, i * 128:(i + 1) * 128], rhs=scT[:, k, :],
                                                                      start=(k == 0), stop=(k == 7)),
                               r=["md_w%d" % b, "md_sc"], w=["md_ps"])
                V(lambda e: e.tensor_tensor(out=modT[:], in0=mps[:], in1=bm[:].unsqueeze(2).to_broadcast([128, 48, 2]), op=ALU.add),
                  r=["md_ps", "md_b"], w=["md_mod"])
                V(lambda e: e.tensor_copy(sh1[:], modT[:, 0:8, :]), r=["md_mod"], w=["sh1"])
                V(lambda e: e.scalar_tensor_tensor(out=gm1[:], in0=modT[:, 8:16, :], scalar=1.0, in1=g1[:].unsqueeze(2).to_broadcast([128, 8, 2]),
                                                   op0=ALU.add, op1=ALU.mult), r=["md_mod", "md_g1"], w=["gm1"])
                V(lambda e: e.tensor_copy(gt1[:], modT[:, 16:24, :]), r=["md_mod"], w=["gt1"])
                V(lambda e: e.tensor_copy(sh2[:], modT[:, 24:32, :]), r=["md_mod"], w=["sh2"])
                V(lambda e: e.scalar_tensor_tensor(out=gm2[:], in0=modT[:, 32:40, :], scalar=1.0, in1=g2[:].unsqueeze(2).to_broadcast([128, 8, 2]),
                                                   op0=ALU.add, op1=ALU.mult), r=["md_mod", "md_g2"], w=["gm2"])
                V(lambda e: e.tensor_copy(gt2[:], modT[:, 40:48, :]), r=["md_mod"], w=["gt2"])
            S.barrier()

        def rms_stats(src, nk, n, nfeat, sq, ps_stat, rstd, keys_r, ksq, kps, krstd):
            for k in range(nk):
                A(lambda e, k=k: e.activation(out=sq[:, k, :n], in_=src[:, k, :n], func=AF.Square), r=keys_r, w=[ksq])
            for k in range(nk):
                TE(lambda e, k=k: e.matmul(ps_stat[:, :n], lhsT=ones_b[:], rhs=sq[:, k, :n], start=(k == 0), stop=(k == nk - 1)),
                   r=[ksq, "ones_b"], w=[kps])
            A(lambda e: e.activation(out=rstd[:, :n], in_=ps_stat[:, :n], func=AF.Sqrt, scale=1.0 / nfeat, bias=epsb[:]),
              r=[kps, "epsb"], w=[krstd])
            V(lambda e: e.reciprocal(rstd[:, :n], rstd[:, :n]), r=[krstd], w=[krstd])

        def phase_proj(l, last):
            with contextlib.ExitStack() as st:
                w_in = sb(st, "pj_win", [128, 8, IN_W], BF16)
                w_krot = sb(st, "pj_wkrot", [128, 8, 32], BF16)
                wq_raw = sb(st, "pj_wqraw", [128, 2, 768], BF16)
                wq_rot = sb(st, "pj_wqrot", [128, 2, NH, 32], BF16)
                wkv = sb(st, "pj_wkv", [128, 1024], BF16)
                wv = sb(st, "pj_wv", [128, NH, 64], BF16)
                qg = sb(st, "pj_qg", [128, 2], F32)
                kvg = sb(st, "pj_kvg", [128, 1], F32)
                bg = sb(st, "pj_bg", [128, 24], F32)
                rcos = sb(st, "pj_cos", [32, SEQ], F32)
                rsin = sb(st, "pj_sin", [32, SEQ], F32)
                xg = sb(st, "pj_xg", [128, 8, 512], F32)
                sq = sb(st, "pj_sq", [128, 8, 512], F32)
                sqb = sb(st, "pj_sqb", [128, 8, 512], BF16)
                rstd = sb(st, "pj_rstd", [128, 512], F32)
                hT = sb(st, "pj_hT", [128, 8, 512], BF16)
                cq = sb(st, "pj_cq", [128, 2, 512], F32)
                cqn = sb(st, "pj_cqn", [128, 2, 512], BF16)
                ckv = sb(st, "pj_ckv", [128, 1, 512], F32)
                cn = sb(st, "pj_cn", [128, 512], BF16)
                rstd2 = sb(st, "pj_rstd2", [128, 512], F32)
                ob = [sb(st, "pj_ob%d" % i, [128, 512], BF16) for i in range(4)]
                of = [sb(st, "pj_of%d" % i, [128, 512], F32) for i in range(2)]
                t1 = sb(st, "pj_t1", [32, 512], F32)
                t2 = sb(st, "pj_t2", [32, 512], F32)
                ps = [pst(st, "pj_ps%d" % i, [128, 512], F32) for i in range(6)]
                ps_st = pst(st, "pj_pst", [128, 512], F32)
                ps_rot = pst(st, "pj_prot", [128, 512], F32)

                for k in range(8):
                    for hh in range(2):
                        DM(lambda e, k=k, hh=hh: e.dma_start(out=w_in[:, k, hh * 2000:(hh + 1) * 2000],
                                                             in_=P["w_in"][l][k * 128:(k + 1) * 128, hh * 2000:(hh + 1) * 2000]),
                           w=["pj_win"], q="gpsimd")
                DM(lambda e: e.dma_start(out=wq_raw[:], in_=P["w_uq"][l].rearrange("(k p) n -> p k n", p=128)), w=["pj_wqraw"], q="gpsimd")
                DM(lambda e: e.dma_start(out=wkv[:], in_=P["w_ukv"][l]), w=["pj_wkv"], q="gpsimd")
                DM(lambda e: e.dma_start(out=qg[:], in_=P["q_norm_g"][l].rearrange("(n p) -> p n", p=128), allow_slow_non_contiguous=True), w=["pj_qg"])
                DM(lambda e: e.dma_start(out=kvg[:], in_=P["kv_norm_g"][l].rearrange("(n p) -> p n", p=128), allow_slow_non_contiguous=True), w=["pj_kvg"])
                DM(lambda e: e.dma_start(out=bg[:], in_=P["b_gate"][l].rearrange("(n p) -> p n", p=128), allow_slow_non_contiguous=True), w=["pj_bg"])
                DM(lambda e: e.dma_start(out=rcos[:], in_=C["ropecos"]), w=["pj_cos"])
                DM(lambda e: e.dma_start(out=rsin[:], in_=C["ropesin"]), w=["pj_sin"])
                kr_cols = w_in[:, :, 384:416].rearrange("p k (rc x f) -> p k rc x f", rc=2, x=2)
                kro = w_krot[:].rearrange("p k (rc x f) -> p k rc x f", rc=2, x=2)
                V(lambda e: e.tensor_single_scalar(kro[:, :, :, 0, :], kr_cols[:, :, :, 1, :], -1.0, op=ALU.mult), r=["pj_win"], w=["pj_wkrot"])
                V(lambda e: e.tensor_copy(kro[:, :, :, 1, :], kr_cols[:, :, :, 0, :]), r=["pj_win"], w=["pj_wkrot"])
                for k in range(2):
                    qr = wq_raw[:, k, :].rearrange("p (h c) -> p h c", c=96)[:, :, 64:96].rearrange("p h (rc x f) -> p h rc x f", rc=2, x=2)
                    qo = wq_rot[:, k, :, :].rearrange("p h (rc x f) -> p h rc x f", rc=2, x=2)
                    for rc in range(2):
                        V(lambda e, qo=qo, qr=qr, rc=rc: e.tensor_single_scalar(qo[:, :, rc, 0, :], qr[:, :, rc, 1, :], -1.0, op=ALU.mult),
                          r=["pj_wqraw"], w=["pj_wqrot"])
                        V(lambda e, qo=qo, qr=qr, rc=rc: e.tensor_copy(qo[:, :, rc, 1, :], qr[:, :, rc, 0, :]), r=["pj_wqraw"], w=["pj_wqrot"])
                V(lambda e: e.tensor_copy(wv[:], wkv[:].rearrange("p (h c) -> p h c", c=128)[:, :, 64:128]), r=["pj_wkv"], w=["pj_wv"])

                obi = [0]

                def nxt_ob():
                    obi[0] += 1
                    return obi[0] % 4

                psi = [0]

                def nxt_ps():
                    psi[0] += 1
                    return psi[0] % 6

                def do_group(gi, t0, n):
                    is_ctx = gi == 0
                    r_ = 1 if is_ctx else 0
                    s0 = t0 - CTX
                    kv_only = last and is_ctx
                    DM(lambda e, t0=t0, n=n: e.dma_start(out=xg[:, :, :n], in_=XT[:, :, t0:t0 + n].rearrange("k p t -> p k t")), r=["XT"], w=["pj_xg"], q="gpsimd")
                    rms_stats(xg, 8, n, D, sqb, ps_st, rstd, ["pj_xg"], "pj_sqb", "pj_pst", "pj_rstd")
                    for k in range(8):
                        V(lambda e, k=k, n=n: e.tensor_tensor(out=sq[:, k, :n], in0=xg[:, k, :n], in1=rstd[:, :n], op=ALU.mult),
                          r=["pj_xg", "pj_rstd"], w=["pj_sq"])
                        V(lambda e, k=k, n=n, r_=r_: e.tensor_scalar(hT[:, k, :n], sq[:, k, :n], gm1[:, k, r_:r_ + 1], sh1[:, k, r_:r_ + 1],
                                                                     op0=ALU.mult, op1=ALU.add),
                          r=["pj_sq", "gm1", "sh1"], w=["pj_hT"])

                    p = nxt_ps()
                    for k in range(8):
                        TE(lambda e, k=k, p=p: e.matmul(ps[p][:, :n], lhsT=w_in[:, k, 256:384], rhs=hT[:, k, :n], start=(k == 0), stop=(k == 7)),
                           r=["pj_hT", "pj_win"], w=["pj_ps%d" % p])
                    V(lambda e, p=p: e.tensor_copy(ckv[:, 0, :n], ps[p][:, :n]), r=["pj_ps%d" % p], w=["pj_ckv"])
                    rms_stats(ckv, 1, n, 128, sqb, ps_st, rstd2, ["pj_ckv"], "pj_sqb", "pj_pst", "pj_rstd2")
                    V(lambda e: e.tensor_tensor(out=sq[:, 0, :n], in0=ckv[:, 0, :n], in1=rstd2[:, :n], op=ALU.mult),
                      r=["pj_ckv", "pj_rstd2"], w=["pj_sq"])
                    V(lambda e: e.tensor_scalar(cn[:, :n], sq[:, 0, :n], kvg[:, 0:1], None, op0=ALU.mult), r=["pj_sq", "pj_kvg"], w=["pj_cn"])
                    for h in range(NH):
                        p = nxt_ps()
                        TE(lambda e, p=p, h=h: e.matmul(ps[p][:64, :n], lhsT=wkv[:, h * 128:h * 128 + 64], rhs=cn[:, :n], start=True, stop=True),
                           r=["pj_cn", "pj_wkv"], w=["pj_ps%d" % p])
                        o = nxt_ob()
                        evac(ob[o][:64, :n], ps[p][:64, :n], r=["pj_ps%d" % p], w=["pj_ob%d" % o])
                        DM(lambda e, o=o, h=h: e.dma_start(out=KT[h, 32:96, t0:t0 + n], in_=ob[o][:64, :n]), r=["pj_ob%d" % o], w=[("KT", S.dma_next)])
                    for j in range(n // 128):
                        p = nxt_ps()
                        TE(lambda e, p=p, j=j: e.matmul(ps[p][:, :], lhsT=cn[:, j * 128:(j + 1) * 128], rhs=wv[:].rearrange("p h c -> p (h c)"), start=True, stop=True),
                           r=["pj_cn", "pj_wv"], w=["pj_ps%d" % p])
                        o = nxt_ob()
                        evac(ob[o][:, :], ps[p][:, :], r=["pj_ps%d" % p], w=["pj_ob%d" % o])
                        DM(lambda e, o=o, j=j: e.dma_start(out=VV[t0 + j * 128:t0 + (j + 1) * 128, :], in_=ob[o][:, :]), r=["pj_ob%d" % o], w=[("VV", S.dma_next)])
                    p = nxt_ps()
                    for k in range(8):
                        TE(lambda e, k=k, p=p: e.matmul(ps[p][:32, :n], lhsT=w_in[:, k, 384:416], rhs=hT[:, k, :n], start=(k == 0), stop=(k == 7)),
                           r=["pj_hT", "pj_win"], w=["pj_ps%d" % p])
                    o = nxt_ob()
                    if is_ctx:
                        evac(ob[o][:32, :n], ps[p][:32, :n], r=["pj_ps%d" % p], w=["pj_ob%d" % o])
                    else:
                        for k in range(8):
                            TE(lambda e, k=k: e.matmul(ps_rot[:32, :n], lhsT=w_krot[:, k, :], rhs=hT[:, k, :n], start=(k == 0), stop=(k == 7)),
                               r=["pj_hT", "pj_wkrot"], w=["pj_prot"])
                        V(lambda e, p=p: e.tensor_tensor(out=t1[:, :n], in0=ps[p][:32, :n], in1=rcos[:, s0:s0 + n], op=ALU.mult),
                          r=["pj_ps%d" % p, "pj_cos"], w=["pj_t1"])
                        V(lambda e: e.tensor_tensor(out=t2[:, :n], in0=ps_rot[:32, :n], in1=rsin[:, s0:s0 + n], op=ALU.mult),
                          r=["pj_prot", "pj_sin"], w=["pj_t2"])
                        V(lambda e, o=o: e.tensor_tensor(out=ob[o][:32, :n], in0=t1[:, :n], in1=t2[:, :n], op=ALU.add),
                          r=["pj_t1", "pj_t2"], w=["pj_ob%d" % o])
                    for h in range(NH):
                        DM(lambda e, o=o, h=h: e.dma_start(out=KT[h, 0:32, t0:t0 + n], in_=ob[o][:32, :n]), r=["pj_ob%d" % o], w=[("KT", S.dma_next)])
                    if kv_only:
                        return
                    for c in range(2):
                        p = nxt_ps()
                        for k in range(8):
                            TE(lambda e, k=k, p=p, c=c: e.matmul(ps[p][:, :n], lhsT=w_in[:, k, c * 128:(c + 1) * 128], rhs=hT[:, k, :n], start=(k == 0), stop=(k == 7)),
                               r=["pj_hT", "pj_win"], w=["pj_ps%d" % p])
                        V(lambda e, p=p, c=c: e.tensor_copy(cq[:, c, :n], ps[p][:, :n]), r=["pj_ps%d" % p], w=["pj_cq"])
                    rms_stats(cq, 2, n, 256, sqb, ps_st, rstd2, ["pj_cq"], "pj_sqb", "pj_pst", "pj_rstd2")
                    for c in range(2):
                        V(lambda e, c=c: e.tensor_tensor(out=sq[:, c, :n], in0=cq[:, c, :n], in1=rstd2[:, :n], op=ALU.mult),
                          r=["pj_cq", "pj_rstd2"], w=["pj_sq"])
                        V(lambda e, c=c: e.tensor_scalar(cqn[:, c, :n], sq[:, c, :n], qg[:, c:c + 1], None, op0=ALU.mult), r=["pj_sq", "pj_qg"], w=["pj_cqn"])
                    for h in range(NH):
                        p = nxt_ps()
                        for c in range(2):
                            TE(lambda e, p=p, c=c, h=h: e.matmul(ps[p][:64, :n], lhsT=wq_raw[:, c, h * 96:h * 96 + 64], rhs=cqn[:, c, :n], start=(c == 0), stop=(c == 1)),
                               r=["pj_cqn", "pj_wqraw"], w=["pj_ps%d" % p])
                        o = nxt_ob()
                        evac(ob[o][:64, :n], ps[p][:64, :n], r=["pj_ps%d" % p], w=["pj_ob%d" % o])
                        DM(lambda e, o=o, h=h: e.dma_start(out=QT[h, 32:96, t0:t0 + n], in_=ob[o][:64, :n]), r=["pj_ob%d" % o], w=[("QT", S.dma_next)])
                        p = nxt_ps()
                        for c in range(2):
                            TE(lambda e, p=p, c=c, h=h: e.matmul(ps[p][:32, :n], lhsT=wq_raw[:, c, h * 96 + 64:h * 96 + 96], rhs=cqn[:, c, :n], start=(c == 0), stop=(c == 1)),
                               r=["pj_cqn", "pj_wqraw"], w=["pj_ps%d" % p])
                        o = nxt_ob()
                        if is_ctx:
                            evac(ob[o][:32, :n], ps[p][:32, :n], r=["pj_ps%d" % p], w=["pj_ob%d" % o])
                        else:
                            for c in range(2):
                                TE(lambda e, c=c, h=h: e.matmul(ps_rot[:32, :n], lhsT=wq_rot[:, c, h, :], rhs=cqn[:, c, :n], start=(c == 0), stop=(c == 1)),
                                   r=["pj_cqn", "pj_wqrot"], w=["pj_prot"])
                            V(lambda e, p=p: e.tensor_tensor(out=t1[:, :n], in0=ps[p][:32, :n], in1=rcos[:, s0:s0 + n], op=ALU.mult),
                              r=["pj_ps%d" % p, "pj_cos"], w=["pj_t1"])
                            V(lambda e: e.tensor_tensor(out=t2[:, :n], in0=ps_rot[:32, :n], in1=rsin[:, s0:s0 + n], op=ALU.mult),
                              r=["pj_prot", "pj_sin"], w=["pj_t2"])
                            V(lambda e, o=o: e.tensor_tensor(out=ob[o][:32, :n], in0=t1[:, :n], in1=t2[:, :n], op=ALU.add),
                              r=["pj_t1", "pj_t2"], w=["pj_ob%d" % o])
                        DM(lambda e, o=o, h=h: e.dma_start(out=QT[h, 0:32, t0:t0 + n], in_=ob[o][:32, :n]), r=["pj_ob%d" % o], w=[("QT", S.dma_next)])
                    for c in range(2):
                        p = nxt_ps()
                        for k in range(8):
                            TE(lambda e, k=k, p=p, c=c: e.matmul(ps[p][:, :n], lhsT=w_in[:, k, 416 + c * 128:416 + (c + 1) * 128], rhs=hT[:, k, :n], start=(k == 0), stop=(k == 7)),
                               r=["pj_hT", "pj_win"], w=["pj_ps%d" % p])
                        o = nxt_ob()
                        evac(ob[o][:, :n], ps[p][:, :n], r=["pj_ps%d" % p], w=["pj_ob%d" % o])
                        DM(lambda e, o=o, c=c: e.dma_start(out=ZFT[c, :, t0:t0 + n], in_=ob[o][:, :n]), r=["pj_ob%d" % o], w=[("ZFT", S.dma_next)])
                    for c in range(2):
                        p = nxt_ps()
                        for k in range(8):
                            TE(lambda e, k=k, p=p, c=c: e.matmul(ps[p][:, :n], lhsT=w_in[:, k, 672 + c * 128:672 + (c + 1) * 128], rhs=hT[:, k, :n], start=(k == 0), stop=(k == 7)),
                               r=["pj_hT", "pj_win"], w=["pj_ps%d" % p])
                        evac(of[c][:, :n], ps[p][:, :n], r=["pj_ps%d" % p], w=["pj_of%d" % c])
                        DM(lambda e, c=c: e.dma_start(out=ZPT[c, :, t0:t0 + n], in_=of[c][:, :n]), r=["pj_of%d" % c], w=[("ZPT", S.dma_next)])
                    for c in range(24):
                        p = nxt_ps()
                        for k in range(8):
                            TE(lambda e, k=k, p=p, c=c: e.matmul(ps[p][:, :n], lhsT=w_in[:, k, 928 + c * 128:928 + (c + 1) * 128], rhs=hT[:, k, :n], start=(k == 0), stop=(k == 7)),
                               r=["pj_hT", "pj_win"], w=["pj_ps%d" % p])
                        o = nxt_ob()
                        A(lambda e, o=o, p=p, c=c: e.activation(out=ob[o][:, :n], in_=ps[p][:, :n], func=AF.Sigmoid, bias=bg[:, c:c + 1], scale=1.0),
                          r=["pj_ps%d" % p, "pj_bg"], w=["pj_ob%d" % o])
                        DM(lambda e, o=o, c=c: e.dma_start(out=GT[c, :, t0:t0 + n], in_=ob[o][:, :n]), r=["pj_ob%d" % o], w=[("GT", S.dma_next)])

                for gi, (t0, n) in enumerate(GROUPS):
                    do_group(gi, t0, n)
            S.barrier()

        def phase_attn(l, last):
            with contextlib.ExitStack() as st:
                v_sb = sb(st, "at_v", [128, T // 128, 512], BF16)
                v_ext = sb(st, "at_vx", [128, T // 128, NH, 65], BF16)
                sel = sb(st, "at_sel", [65, 64], F32)
                kth = [sb(st, "at_k%d" % i, [96, T], BF16) for i in range(2)]
                qg = [sb(st, "at_q%d" % i, [96, 512], BF16) for i in range(2)]
                pt = [sb(st, "at_p%d" % i, [128, 512], BF16) for i in range(3)]
                o_sb = sb(st, "at_osb", [65, 512], F32)
                rs = sb(st, "at_rs", [64, 512], F32)
                osb = [sb(st, "at_o%d" % i, [64, 512], BF16) for i in range(2)]
                ps_s = [pst(st, "at_pss%d" % i, [128, 512], F32) for i in range(3)]
                ps_o = [pst(st, "at_pso%d" % i, [128, 512], F32) for i in range(2)]
                ps_b = pst(st, "at_psb", [128, 512], F32)
                DM(lambda e: e.dma_start(out=v_sb[:], in_=VV.rearrange("(kt p) c -> p kt c", p=128)), w=["at_v"])
                V(lambda e: e.memset(v_ext[:], 1.0), w=["at_vx"])
                G(lambda e: e.tensor_copy(v_ext[:, :, :, 0:64], v_sb[:].rearrange("p k (h c) -> p k h c", c=64)), r=["at_v"], w=["at_vx"])
                V(lambda e: e.memset(sel[:], 0.0), w=["at_sel"])
                V(lambda e: e.memset(sel[64:65, :], 1.0), w=["at_sel"])
                cnt = [0]
                segs = ([] if last else [(0, CTX, 2)]) + [(t0, n, T // 128) for (t0, n) in GROUPS[1:]]
                if l == 0:
                    emit_cast_tables(st)

                def do_head(h):
                    kb = h % 2
                    DM(lambda e: e.dma_start(out=kth[kb][:], in_=KT[h]), w=["at_k%d" % kb])

                    def do_seg(t0, n, nkt):
                        cnt[0] += 1
                        qb = cnt[0] % 2
                        DM(lambda e: e.dma_start(out=qg[qb][:, :n], in_=QT[h, :, t0:t0 + n]), w=["at_q%d" % qb])

                        def s_mm(kt):
                            si = kt % 3
                            TE(lambda e: e.matmul(ps_s[si][:, :n], lhsT=kth[kb][:, kt * 128:(kt + 1) * 128], rhs=qg[qb][:, :n], start=True, stop=True),
                               r=["at_k%d" % kb, "at_q%d" % qb], w=["at_pss%d" % si])

                        def pv(kt):
                            si = kt % 3
                            A(lambda e: e.activation(out=pt[si][:, :n], in_=ps_s[si][:, :n], func=AF.Exp, scale=ATT_SCALE),
                              r=["at_pss%d" % si], w=["at_p%d" % si])
                            TE(lambda e: e.matmul(ps_o[qb][:65, :n], lhsT=v_ext[:, kt, h, :], rhs=pt[si][:, :n], start=(kt == 0), stop=(kt == nkt - 1)),
                               r=["at_vx", "at_p%d" % si], w=["at_pso%d" % qb])
                        s_mm(0)
                        if nkt > 1:
                            s_mm(1)
                        for kt in range(nkt):
                            if kt + 2 < nkt:
                                s_mm(kt + 2)
                            pv(kt)
                        A(lambda e: e.activation(out=o_sb[:, :n], in_=ps_o[qb][:65, :n], func=AF.Copy), r=["at_pso%d" % qb], w=["at_osb"])
                        TE(lambda e: e.matmul(ps_b[:64, :n], lhsT=sel[:, :], rhs=o_sb[:, :n], start=True, stop=True), r=["at_sel", "at_osb"], w=["at_psb"])
                        V(lambda e: e.reciprocal(rs[:, :n], ps_b[:64, :n]), r=["at_psb"], w=["at_rs"])
                        V(lambda e: e.tensor_tensor(out=osb[qb][:, :n], in0=o_sb[:64, :n], in1=rs[:, :n], op=ALU.mult),
                          r=["at_osb", "at_rs"], w=["at_o%d" % qb])
                        DM(lambda e: e.dma_start(out=ATT[h // 2, (h % 2) * 64:(h % 2) * 64 + 64, t0:t0 + n], in_=osb[qb][:, :n]),
                           r=["at_o%d" % qb], w=[("ATT", S.dma_next)])
                    for (t0, n, nkt) in segs:
                        do_seg(t0, n, nkt)
                for h in range(NH):
                    do_head(h)
            S.barrier()

        def phase_fourier(l, last):
            segs = ([] if last else [(0, CTX, "cl256", "sl256")]) + [(CTX, SEQ, "cl4096", "sl4096")]

            def do_seg(t0, L, cln, sln):
                ntt = L // 128
                KW = 256
                with contextlib.ExitStack() as st:
                    ccs = sb(st, "fo_ccs", [128, 2, 512], BF16)
                    zft = sb(st, "fo_z", [128, 2, L], BF16)
                    uv = sb(st, "fo_uv", [128, ntt, 512], BF16)
                    clb = [sb(st, "fo_cl%d" % i, [128, ntt, KW], BF16) for i in range(2)]
                    slb = [sb(st, "fo_sl%d" % i, [128, ntt, KW], BF16) for i in range(2)]
                    yo = [sb(st, "fo_y%d" % i, [128, KW], BF16) for i in range(2)]
                    ps = [pst(st, "fo_ps%d" % i, [128, 512], F32) for i in range(4)]
                    DM(lambda e: e.dma_start(out=ccs[:], in_=C["ccs"].rearrange("(c p) n -> p c n", p=128)), w=["fo_ccs"])
                    DM(lambda e: e.dma_start(out=zft[:], in_=ZFT[:, :, t0:t0 + L].rearrange("c p t -> p c t")), w=["fo_z"])

                    def do_tt(tt):
                        pi = tt % 4
                        for c in range(2):
                            TE(lambda e, c=c: e.matmul(ps[pi][:, :], lhsT=zft[:, c, tt * 128:(tt + 1) * 128], rhs=ccs[:, c, :], start=(c == 0), stop=(c == 1)),
                               r=["fo_z", "fo_ccs"], w=["fo_ps%d" % pi])
                        evac(uv[:, tt, :], ps[pi][:, :], r=["fo_ps%d" % pi], w=[("fo_uv", tt)])
                    for tt in range(ntt):
                        do_tt(tt)
                    uvkeys = [("fo_uv", tt) for tt in range(ntt)]
                    scale = 1.0 / math.sqrt(L * 256.0)

                    def do_kc(kc):
                        b = kc % 2
                        DM(lambda e: e.dma_start(out=clb[b][:], in_=C[cln][:, kc * KW:(kc + 1) * KW].rearrange("(tt p) k -> p tt k", p=128)), w=["fo_cl%d" % b])
                        DM(lambda e: e.dma_start(out=slb[b][:], in_=C[sln][:, kc * KW:(kc + 1) * KW].rearrange("(tt p) k -> p tt k", p=128)), w=["fo_sl%d" % b])

                        def do_m(m):
                            pi = (kc * 2 + m) % 4
                            for tt in range(ntt):
                                TE(lambda e, tt=tt: e.matmul(ps[pi][:, :KW], lhsT=uv[:, tt, m * 128:(m + 1) * 128], rhs=clb[b][:, tt, :], start=(tt == 0), stop=False),
                                   r=uvkeys + ["fo_cl%d" % b], w=["fo_ps%d" % pi])
                                TE(lambda e, tt=tt: e.matmul(ps[pi][:, :KW], lhsT=uv[:, tt, 256 + m * 128:256 + (m + 1) * 128], rhs=slb[b][:, tt, :], start=False, stop=(tt == ntt - 1)),
                                   r=uvkeys + ["fo_sl%d" % b], w=["fo_ps%d" % pi])
                            evac(yo[m][:, :], ps[pi][:, :KW], r=["fo_ps%d" % pi], w=["fo_y%d" % m], scale=scale)
                            DM(lambda e: e.dma_start(out=YFT[m, :, t0 + kc * KW:t0 + (kc + 1) * KW], in_=yo[m][:, :]), r=["fo_y%d" % m], w=[("YFT", S.dma_next)])
                        for m in range(2):
                            do_m(m)
                    for kc in range(L // KW):
                        do_kc(kc)
                S.barrier()
            for sg in segs:
                do_seg(*sg)

        def phase_pool(l, last):
            segs = ([] if last else [(0, CTX, "icnt256")]) + [(CTX, SEQ, "icnt4096")]

            def do_seg(t0, L, icn):
                with contextlib.ExitStack() as st:
                    zpp = sb(st, "po_z", [128, 2, L + 16], F32)
                    sA = sb(st, "po_sA", [128, L + 16], F32)
                    sB = sb(st, "po_sB", [128, L + 16], F32)
                    icnt = sb(st, "po_ic", [128, 2, L], F32)
                    tmp = sb(st, "po_tmp", [128, L], F32)
                    poolT = sb(st, "po_pT", [128, 2, L], BF16)
                    wbd = [sb(st, "po_w%d" % i, [128, 128], BF16) for i in range(2)]
                    psc = sb(st, "po_sc", [128, 2], F32)
                    yo = [sb(st, "po_y%d" % i, [128, 512], BF16) for i in range(2)]
                    ps = [pst(st, "po_ps%d" % i, [128, 512], F32) for i in range(2)]
                    V(lambda e: e.memset(zpp[:], 0.0), w=["po_z"])
                    for ch in range(2):
                        V(lambda e, ch=ch: e.memset(wbd[ch][:], 0.0), w=["po_w%d" % ch])
                    DM(lambda e: e.dma_start(out=zpp[:, :, 8:8 + L], in_=ZPT[:, :, t0:t0 + L].rearrange("c p t -> p c t")), w=["po_z"])
                    DM(lambda e: e.dma_start(out=icnt[:], in_=C[icn].rearrange("c p t -> p c t")), w=["po_ic"])
                    DM(lambda e: e.dma_start(out=psc[:], in_=P["pool_scale"][l].rearrange("(c p) -> p c", p=128), allow_slow_non_contiguous=True), w=["po_sc"])
                    for g in range(4):
                        o = (g % 2) * 64
                        DM(lambda e, g=g, o=o: e.dma_start(out=wbd[g // 2][o:o + 64, o:o + 64], in_=P["w_grp"][l, g]), w=["po_w%d" % (g // 2)], q="gpsimd")

                    def do_ch(ch):
                        V(lambda e: e.tensor_tensor(out=sA[:, 1:L + 16], in0=zpp[:, ch, 0:L + 15], in1=zpp[:, ch, 1:L + 16], op=ALU.add), r=["po_z"], w=["po_sA"])
                        V(lambda e: e.tensor_tensor(out=sB[:, 2:L + 15], in0=sA[:, 1:L + 14], in1=sA[:, 3:L + 16], op=ALU.add), r=["po_sA"], w=["po_sB"])
                        if ch == 1:
                            V(lambda e: e.tensor_tensor(out=sA[:, 4:L + 13], in0=sB[:, 2:L + 11], in1=sB[:, 6:L + 15], op=ALU.add), r=["po_sB"], w=["po_sA"])
                            V(lambda e: e.tensor_tensor(out=sB[:, 8:L + 9], in0=sA[:, 4:L + 5], in1=sA[:, 12:L + 13], op=ALU.add), r=["po_sA"], w=["po_sB"])
                        V(lambda e: e.tensor_tensor(out=tmp[0:64, :], in0=sA[0:64, 8:8 + L], in1=icnt[0:64, ch, :], op=ALU.mult), r=["po_sA", "po_ic"], w=["po_tmp"])
                        V(lambda e: e.tensor_tensor(out=tmp[64:128, :], in0=sB[64:128, 8:8 + L], in1=icnt[64:128, ch, :], op=ALU.mult), r=["po_sB", "po_ic"], w=["po_tmp"])
                        V(lambda e: e.tensor_tensor(out=poolT[:, ch, :], in0=tmp[:, :], in1=zpp[:, ch, 8:8 + L], op=ALU.subtract), r=["po_tmp", "po_z"], w=[("po_pT", ch)])

                        def do_tc(tc_):
                            n = min(512, L - tc_ * 512)
                            pi = tc_ % 2
                            TE(lambda e: e.matmul(ps[pi][:, :n], lhsT=wbd[ch][:], rhs=poolT[:, ch, tc_ * 512:tc_ * 512 + n], start=True, stop=True),
                               r=[("po_pT", ch), "po_w%d" % ch], w=["po_ps%d" % pi])
                            V(lambda e: e.tensor_scalar(yo[pi][:, :n], ps[pi][:, :n], psc[:, ch:ch + 1], None, op0=ALU.mult), r=["po_ps%d" % pi, "po_sc"], w=["po_y%d" % pi])
                            DM(lambda e: e.dma_start(out=YCT[ch, :, t0 + tc_ * 512:t0 + tc_ * 512 + n], in_=yo[pi][:, :n]), r=["po_y%d" % pi], w=[("YCT", S.dma_next)])
                        for tc_ in range((L + 511) // 512):
                            do_tc(tc_)
                    for ch in range(2):
                        do_ch(ch)
                S.barrier()
            for sg in segs:
                do_seg(*sg)

        def phase_combine(l, last):
            with contextlib.ExitStack() as st:
                w_oa = sb(st, "cb_woa", [128, 4, D], BF16)
                w_ob = sb(st, "cb_wob", [128, 2, D], BF16)
                w_oc = sb(st, "cb_woc", [128, 2, D], BF16)
                w_out = sb(st, "cb_wout", [128, 8, D], BF16)
                w_pq = sb(st, "cb_wpq", [128, 8, 2048], BF16)
                att = sb(st, "cb_att", [128, 4, 512], BF16)
                yf = sb(st, "cb_yf", [128, 2, 512], BF16)
                yc = sb(st, "cb_yc", [128, 2, 512], BF16)
                gT = sb(st, "cb_g", [128, 24, 512], BF16)
                xg = sb(st, "cb_xg", [128, 8, 512], F32)
                sq = sb(st, "cb_sq", [128, 8, 512], F32)
                sqb = sb(st, "cb_sqb", [128, 8, 512], BF16)
                rstd = sb(st, "cb_rstd", [128, 512], F32)
                u = sb(st, "cb_u", [128, 8, 512], BF16)
                h2 = sb(st, "cb_h2", [128, 8, 512], BF16)
                t1 = sb(st, "cb_t1", [128, 512], F32)
                t2 = sb(st, "cb_t2", [128, 512], F32)
                t3 = sb(st, "cb_t3", [128, 512], F32)
                ob = [sb(st, "cb_ob%d" % i, [128, 512], BF16) for i in range(3)]
                ps = [pst(st, "cb_ps%d" % i, [128, 512], F32) for i in range(6)]
                ps_st = pst(st, "cb_pst", [128, 512], F32)
                for k in range(4):
                    DM(lambda e, k=k: e.dma_start(out=w_oa[:, k, :], in_=P["w_oa"][l][k * 128:(k + 1) * 128, :]), w=["cb_woa"], q="gpsimd")
                for k in range(2):
                    DM(lambda e, k=k: e.dma_start(out=w_ob[:, k, :], in_=P["w_ob"][l][k * 128:(k + 1) * 128, :]), w=["cb_wob"], q="gpsimd")
                    DM(lambda e, k=k: e.dma_start(out=w_oc[:, k, :], in_=P["w_oc"][l][k * 128:(k + 1) * 128, :]), w=["cb_woc"], q="gpsimd")
                for k in range(8):
                    DM(lambda e, k=k: e.dma_start(out=w_out[:, k, :], in_=P["w_out"][l][k * 128:(k + 1) * 128, :]), w=["cb_wout"], q="gpsimd")
                    DM(lambda e, k=k: e.dma_start(out=w_pq[:, k, :], in_=P["w_pq"][l][k * 128:(k + 1) * 128, :]), w=["cb_wpq"], q="gpsimd")
                psi = [0]

                def do_group(gi, t0, n):
                    r_ = 1 if gi == 0 else 0
                    DM(lambda e: e.dma_start(out=att[:, :, :n], in_=ATT[:, :, t0:t0 + n].rearrange("k p t -> p k t")), w=["cb_att"], q="gpsimd")
                    DM(lambda e: e.dma_start(out=yf[:, :, :n], in_=YFT[:, :, t0:t0 + n].rearrange("k p t -> p k t")), w=["cb_yf"], q="gpsimd")
                    DM(lambda e: e.dma_start(out=yc[:, :, :n], in_=YCT[:, :, t0:t0 + n].rearrange("k p t -> p k t")), w=["cb_yc"], q="gpsimd")
                    DM(lambda e: e.dma_start(out=gT[:, :, :n], in_=GT[:, :, t0:t0 + n].rearrange("k p t -> p k t")), w=["cb_g"], q="gpsimd")
                    DM(lambda e: e.dma_start(out=xg[:, :, :n], in_=XT[:, :, t0:t0 + n].rearrange("k p t -> p k t")), w=["cb_xg"], q="gpsimd")

                    def do_dch(dc):
                        pa, pb, pc = psi[0] % 6, (psi[0] + 1) % 6, (psi[0] + 2) % 6
                        psi[0] += 3
                        for k in range(4):
                            TE(lambda e, k=k: e.matmul(ps[pa][:, :n], lhsT=w_oa[:, k, dc * 128:(dc + 1) * 128], rhs=att[:, k, :n], start=(k == 0), stop=(k == 3)),
                               r=["cb_woa", "cb_att"], w=["cb_ps%d" % pa])
                        for k in range(2):
                            TE(lambda e, k=k: e.matmul(ps[pb][:, :n], lhsT=w_ob[:, k, dc * 128:(dc + 1) * 128], rhs=yf[:, k, :n], start=(k == 0), stop=(k == 1)),
                               r=["cb_wob", "cb_yf"], w=["cb_ps%d" % pb])
                        for k in range(2):
                            TE(lambda e, k=k: e.matmul(ps[pc][:, :n], lhsT=w_oc[:, k, dc * 128:(dc + 1) * 128], rhs=yc[:, k, :n], start=(k == 0), stop=(k == 1)),
                               r=["cb_woc", "cb_yc"], w=["cb_ps%d" % pc])
                        V(lambda e: e.tensor_tensor(out=t1[:, :n], in0=ps[pa][:, :n], in1=gT[:, dc, :n], op=ALU.mult), r=["cb_ps%d" % pa, "cb_g"], w=["cb_t1"])
                        V(lambda e: e.tensor_tensor(out=t2[:, :n], in0=ps[pb][:, :n], in1=gT[:, 8 + dc, :n], op=ALU.mult), r=["cb_ps%d" % pb, "cb_g"], w=["cb_t2"])
                        V(lambda e: e.tensor_tensor(out=t3[:, :n], in0=ps[pc][:, :n], in1=gT[:, 16 + dc, :n], op=ALU.mult), r=["cb_ps%d" % pc, "cb_g"], w=["cb_t3"])
                        G(lambda e: e.tensor_tensor(out=t1[:, :n], in0=t1[:, :n], in1=t2[:, :n], op=ALU.add), r=["cb_t1", "cb_t2"], w=["cb_t1"])
                        G(lambda e: e.tensor_tensor(out=u[:, dc, :n], in0=t1[:, :n], in1=t3[:, :n], op=ALU.add), r=["cb_t1", "cb_t3"], w=[("cb_u", dc)])
                    for dc in range(8):
                        do_dch(dc)
                    ukeys = [("cb_u", dc) for dc in range(8)]

                    def do_out(dc):
                        pm = psi[0] % 6
                        psi[0] += 1
                        for k in range(8):
                            TE(lambda e, k=k: e.matmul(ps[pm][:, :n], lhsT=w_out[:, k, dc * 128:(dc + 1) * 128], rhs=u[:, k, :n], start=(k == 0), stop=(k == 7)),
                               r=ukeys + ["cb_wout"], w=["cb_ps%d" % pm])
                        V(lambda e: e.scalar_tensor_tensor(out=xg[:, dc, :n], in0=ps[pm][:, :n], scalar=gt1[:, dc, r_:r_ + 1], in1=xg[:, dc, :n], op0=ALU.mult, op1=ALU.add),
                          r=["cb_ps%d" % pm, "gt1", "cb_xg"], w=["cb_xg"])
                    for dc in range(8):
                        do_out(dc)
                    DM(lambda e: e.dma_start(out=XT[:, :, t0:t0 + n].rearrange("k p t -> p k t"), in_=xg[:, :, :n]), r=["cb_xg"], w=[("XTw", gi)])
                    rms_stats(xg, 8, n, D, sqb, ps_st, rstd, ["cb_xg"], "cb_sqb", "cb_pst", "cb_rstd")

                    def do_h2(k):
                        V(lambda e: e.tensor_tensor(out=sq[:, k, :n], in0=xg[:, k, :n], in1=rstd[:, :n], op=ALU.mult), r=["cb_xg", "cb_rstd"], w=["cb_sq"])
                        V(lambda e: e.tensor_scalar(h2[:, k, :n], sq[:, k, :n], gm2[:, k, r_:r_ + 1], sh2[:, k, r_:r_ + 1], op0=ALU.mult, op1=ALU.add),
                          r=["cb_sq", "gm2", "sh2"], w=["cb_h2"])
                    for k in range(8):
                        do_h2(k)
                    DM(lambda e: e.dma_start(out=H2T[:, :, t0:t0 + n].rearrange("k p t -> p k t"), in_=h2[:, :, :n]), r=["cb_h2"], w=[("H2T", gi)])

                    def do_pq(nc_):
                        pm = psi[0] % 6
                        psi[0] += 1
                        o = nc_ % 3
                        for k in range(8):
                            TE(lambda e, k=k: e.matmul(ps[pm][:, :n], lhsT=w_pq[:, k, nc_ * 128:(nc_ + 1) * 128], rhs=h2[:, k, :n], start=(k == 0), stop=(k == 7)),
                               r=["cb_h2", "cb_wpq"], w=["cb_ps%d" % pm])
                        evac(ob[o][:, :n], ps[pm][:, :n], r=["cb_ps%d" % pm], w=["cb_ob%d" % o])
                        DM(lambda e: e.dma_start(out=PQT[nc_, :, t0:t0 + n], in_=ob[o][:, :n]), r=["cb_ob%d" % o], w=[("PQT", S.dma_next)])
                    for nc_ in range(16):
                        do_pq(nc_)
                for gi, (t0, n) in enumerate(GROUPS):
                    if last and gi == 0:
                        continue
                    do_group(gi, t0, n)
            S.barrier()

        def emit_cast_tables(st):
            tb = [sb(st, "ct_t%d" % i, [128, 8, D], BF16) for i in range(4)]
            cnt = [0]

            def do_chunk(l, src, o, c):
                cnt[0] += 1
                b = cnt[0] % 4
                DM(lambda e: e.dma_start(out=tb[b][:], in_=src[c * 1024:(c + 1) * 1024, :].rearrange("(p r) d -> p r d", r=8)),
                   w=["ct_t%d" % b], q="gpsimd")
                DM(lambda e: e.dma_start(out=DUB[l][c * 1024:(c + 1) * 1024, o:o + D].rearrange("(p r) d -> p r d", r=8), in_=tb[b][:]),
                   r=["ct_t%d" % b], w=[("cast", S.dma_next)], q="gpsimd")
            for l in range(DEPTH):
                for (src, o) in ((P["peer_down%d" % l], 0), (P["peer_up%d" % l], D)):
                    for c in range(16):
                        do_chunk(l, src, o, c)

        def phase_peer(l, last):
            with contextlib.ExitStack() as st:
                keysT = sb(st, "pe_keysT", [128, 16, 128], BF16)
                iotf = sb(st, "pe_iotf", [128, 16], F32)
                pq = sb(st, "pe_pq", [128, 16, 128], BF16)
                s_sb = sb(st, "pe_s", [128, 16, 128], F32)
                s2 = sb(st, "pe_s2", [128, 16, 128], F32)
                kraw = s2
                sv = sb(st, "pe_sv", [128, 16, 16], F32)
                si = sb(st, "pe_si", [128, 16, 16], U32)
                sif = sb(st, "pe_sif", [128, 16, 16], F32)
                cand = sb(st, "pe_cand", [128, 8, 256], F32)
                cand2 = sb(st, "pe_cand2", [128, 8, 256], F32)
                eq = cand2[:].rearrange("p h (a b) -> p h a b", b=16)
                ts = sb(st, "pe_ts", [128, 8, 16], F32)
                pos = sb(st, "pe_pos", [128, 8, 16], U32)
                pa_i = sb(st, "pe_pai", [128, 8, 16], U32)
                pb_i = sb(st, "pe_pbi", [128, 8, 16], U32)
                pa_f = sb(st, "pe_paf", [128, 8, 16], F32)
                pb_f = sb(st, "pe_pbf", [128, 8, 16], F32)
                If = sb(st, "pe_If", [128, 8, 16], F32)
                Jf = sb(st, "pe_Jf", [128, 8, 16], F32)
                ef = sb(st, "pe_ef", [128, 8, 16], F32)
                gsum = sb(st, "pe_gsum", [128, 8], F32)
                h2T = sb(st, "pe_h2T", [128, 8, 128], BF16)
                eidx_ = [sb(st, "pe_eidx%d" % i, [128, 128], U32) for i in range(2)]
                gate_ = [sb(st, "pe_gate%d" % i, [128, 8, 16], F32) for i in range(2)]
                h2tok_ = [sb(st, "pe_h2tok%d" % i, [128, D], BF16) for i in range(2)]
                xt_ = [sb(st, "pe_xt%d" % i, [128, 8, 128], F32) for i in range(2)]
                NR = 24
                SG = 8
                rows = [sb(st, "pe_rows%d" % i, [128, 2 * D], BF16) for i in range(NR)]
                junk = [sb(st, "pe_junk%d" % i, [128, D], BF16) for i in range(4)]
                junk2 = sb(st, "pe_junk2", [128, D], BF16)
                junk3 = sb(st, "pe_junk3", [128, D], BF16)
                diag8 = [sb(st, "pe_diag%d" % i, [128, SG, 128], BF16) for i in range(3)]
                a_sb = sb(st, "pe_a", [128, 128], F32)
                g1 = sb(st, "pe_g1", [128, 128], F32)
                g2 = sb(st, "pe_g2", [128, 128], F32)
                act = sb(st, "pe_act", [128, 128], F32)
                acc = sb(st, "pe_acc", [128, D], F32)
                ps_sc = [pst(st, "pe_psc%d" % i, [128, 4, 128], F32) for i in range(2)]
                ps_part = [pst(st, "pe_ppart%d" % i, [128, 4, 128], F32) for i in range(2)]
                ps_a = pst(st, "pe_pa", [128, 128], F32)
                part_sb = [sb(st, "pe_partsb%d" % i, [128, 4, 128], F32) for i in range(2)]
                ps_h2 = pst(st, "pe_ph2", [128, 8, 128], BF16)
                ps_acc = [pst(st, "pe_pacc%d" % i, [128, 512], F32) for i in range(2)]

                G(lambda e: e.iota(iotf[:], [[1, 16]], base=0, channel_multiplier=0, allow_small_or_imprecise_dtypes=True), w=["pe_iotf"])
                DM(lambda e: e.dma_start(out=kraw[:], in_=P["peer_keys"][l].rearrange("h p k d -> k (h p) d")), w=["pe_s2"])
                for hp in range(16):
                    bk = (hp // 4) % 2
                    TE(lambda e, hp=hp, bk=bk: e.transpose(ps_sc[bk][:, hp % 4, :], kraw[:, hp, :], ident_f[:]), r=["pe_s2", "ident_f"], w=["pe_psc%d" % bk])
                    if hp % 4 == 3:
                        b4 = hp // 4
                        evac(keysT[:, b4 * 4:(b4 + 1) * 4, :], ps_sc[bk][:], r=["pe_psc%d" % bk], w=["pe_keysT"])
                rcnt = [0]
                gcnt = [0]
                jcnt = [0]
                pcnt = [0]

                def prep_tile(ti):
                    c0 = ti * 128
                    tb = ti % 2
                    eidx, gate, h2tok, xt = eidx_[tb], gate_[tb], h2tok_[tb], xt_[tb]
                    kE, kG, kH, kX = "pe_eidx%d" % tb, "pe_gate%d" % tb, "pe_h2tok%d" % tb, "pe_xt%d" % tb
                    DM(lambda e: e.dma_start(out=pq[:], in_=PQT[:, :, c0:c0 + 128].rearrange("n p t -> p n t")), w=["pe_pq"])
                    DM(lambda e: e.dma_start(out=h2T[:], in_=H2T[:, :, c0:c0 + 128].rearrange("k p t -> p k t")), w=["pe_h2T"])
                    DM(lambda e: e.dma_start(out=xt[:], in_=XT[:, :, c0:c0 + 128].rearrange("k p t -> p k t")), w=[kX])
                    for hp in range(16):
                        bk = (hp // 4) % 2
                        TE(lambda e, hp=hp, bk=bk: e.matmul(ps_sc[bk][:, hp % 4, :], lhsT=pq[:, hp, :], rhs=keysT[:, hp, :], start=True, stop=True),
                           r=["pe_pq", "pe_keysT"], w=["pe_psc%d" % bk])
                        if hp % 4 == 3:
                            b4 = hp // 4
                            evac(s_sb[:, b4 * 4:(b4 + 1) * 4, :], ps_sc[bk][:], r=["pe_psc%d" % bk], w=["pe_s"])
                            yield
                    for k in range(8):
                        TE(lambda e, k=k: e.transpose(ps_h2[:, k, :], h2T[:, k, :], ident_b[:]), r=["pe_h2T", "ident_b"], w=["pe_ph2"])
                    A(lambda e: e.activation(out=h2tok[:], in_=ps_h2[:].rearrange("p k d -> p (k d)"), func=AF.Copy), r=["pe_ph2"], w=[kH])
                    yield
                    for hp in range(16):
                        V(lambda e, hp=hp: e.max(out=sv[:, hp, 0:8], in_=s_sb[:, hp, :]), r=["pe_s"], w=["pe_sv"])
                        V(lambda e, hp=hp: e.max_index(out=si[:, hp, 0:8], in_max=sv[:, hp, 0:8], in_values=s_sb[:, hp, :]), r=["pe_s", "pe_sv"], w=["pe_si"])
                        V(lambda e, hp=hp: e.match_replace(out=s2[:, hp, :], in_to_replace=sv[:, hp, 0:8], in_values=s_sb[:, hp, :], imm_value=-1e30),
                          r=["pe_s", "pe_sv"], w=["pe_s2"])
                        V(lambda e, hp=hp: e.max(out=sv[:, hp, 8:16], in_=s2[:, hp, :]), r=["pe_s2"], w=["pe_sv"])
                        V(lambda e, hp=hp: e.max_index(out=si[:, hp, 8:16], in_max=sv[:, hp, 8:16], in_values=s2[:, hp, :]), r=["pe_s2", "pe_sv"], w=["pe_si"])
                        yield
                    V(lambda e: e.tensor_copy(sif[:], si[:]), r=["pe_si"], w=["pe_sif"])
                    svv = sv[:].rearrange("p (h q) a -> p h q a", q=2)
                    sfv = sif[:].rearrange("p (h q) a -> p h q a", q=2)
                    candv = cand[:].rearrange("p h (a b) -> p h a b", b=16)
                    V(lambda e: e.tensor_tensor(out=candv, in0=svv[:, :, 0, :].unsqueeze(3).to_broadcast([128, 8, 16, 16]),
                                                in1=svv[:, :, 1, :].unsqueeze(2).to_broadcast([128, 8, 16, 16]), op=ALU.add), r=["pe_sv"], w=["pe_cand"])
                    for h in range(8):
                        V(lambda e, h=h: e.max(out=ts[:, h, 0:8], in_=cand[:, h, :]), r=["pe_cand"], w=["pe_ts"])
                        V(lambda e, h=h: e.max_index(out=pos[:, h, 0:8], in_max=ts[:, h, 0:8], in_values=cand[:, h, :]), r=["pe_cand", "pe_ts"], w=["pe_pos"])
                        V(lambda e, h=h: e.match_replace(out=cand2[:, h, :], in_to_replace=ts[:, h, 0:8], in_values=cand[:, h, :], imm_value=-1e30),
                          r=["pe_cand", "pe_ts"], w=["pe_cand2"])
                        V(lambda e, h=h: e.max(out=ts[:, h, 8:16], in_=cand2[:, h, :]), r=["pe_cand2"], w=["pe_ts"])
                        V(lambda e, h=h: e.max_index(out=pos[:, h, 8:16], in_max=ts[:, h, 8:16], in_values=cand2[:, h, :]), r=["pe_cand2", "pe_ts"], w=["pe_pos"])
                        yield
                    V(lambda e: e.tensor_single_scalar(pa_i[:], pos[:], 4, op=ALU.logical_shift_right), r=["pe_pos"], w=["pe_pai"])
                    V(lambda e: e.tensor_single_scalar(pb_i[:], pos[:], 15, op=ALU.bitwise_and), r=["pe_pos"], w=["pe_pbi"])
                    V(lambda e: e.tensor_copy(pa_f[:], pa_i[:]), r=["pe_pai"], w=["pe_paf"])
                    V(lambda e: e.tensor_copy(pb_f[:], pb_i[:]), r=["pe_pbi"], w=["pe_pbf"])
                    iob = iotf[:].unsqueeze(1).unsqueeze(1).to_broadcast([128, 8, 16, 16])
                    for (pf, half, dst, kd) in ((pa_f, 0, If, "pe_If"), (pb_f, 1, Jf, "pe_Jf")):
                        V(lambda e, pf=pf: e.tensor_tensor(out=eq, in0=iob, in1=pf[:].unsqueeze(3).to_broadcast([128, 8, 16, 16]), op=ALU.is_equal),
                          r=["pe_iotf", "pe_paf", "pe_pbf"], w=["pe_cand2"])
                        V(lambda e, half=half: e.tensor_tensor(out=eq, in0=eq, in1=sfv[:, :, half, :].unsqueeze(2).to_broadcast([128, 8, 16, 16]), op=ALU.mult),
                          r=["pe_cand2", "pe_sif"], w=["pe_cand2"])
                        V(lambda e, dst=dst: e.tensor_reduce(out=dst[:], in_=eq, axis=AX.X, op=ALU.add), r=["pe_cand2"], w=[kd])
                        yield
                    V(lambda e: e.scalar_tensor_tensor(out=ef[:], in0=If[:], scalar=128.0, in1=Jf[:], op0=ALU.mult, op1=ALU.add), r=["pe_If", "pe_Jf"], w=["pe_ef"])
                    V(lambda e: e.tensor_copy(eidx[:], ef[:].rearrange("p h k -> p (h k)")), r=["pe_ef"], w=[kE])
                    yield
                    V(lambda e: e.tensor_tensor(out=gate[:], in0=ts[:], in1=ts[:, :, 0:1].to_broadcast([128, 8, 16]), op=ALU.subtract), r=["pe_ts"], w=[kG])
                    A(lambda e: e.activation(out=gate[:], in_=gate[:], func=AF.Exp), r=[kG], w=[kG])
                    V(lambda e: e.tensor_reduce(out=gsum[:], in_=gate[:], axis=AX.X, op=ALU.add), r=[kG], w=["pe_gsum"])
                    V(lambda e: e.reciprocal(gsum[:], gsum[:]), r=["pe_gsum"], w=["pe_gsum"])
                    V(lambda e: e.tensor_tensor(out=gate[:], in0=gate[:], in1=gsum[:].unsqueeze(2).to_broadcast([128, 8, 16]), op=ALU.mult), r=[kG, "pe_gsum"], w=[kG])

                def run_tile(ti, nxt=None):
                    c0 = ti * 128
                    r_ = 1 if ti < 2 else 0
                    tb = ti % 2
                    eidx, gate, h2tok, xt = eidx_[tb], gate_[tb], h2tok_[tb], xt_[tb]
                    kE, kG, kH, kX = "pe_eidx%d" % tb, "pe_gate%d" % tb, "pe_h2tok%d" % tb, "pe_xt%d" % tb
                    gatev = gate[:].rearrange("p h k -> p (h k)")

                    gbufs = {}

                    def stage1a(g0):
                        bufs = []
                        for sl in range(g0, g0 + SG):
                            rcnt[0] += 1
                            b = rcnt[0] % NR
                            bufs.append(b)
                            DM(lambda e, b=b, sl=sl: e.indirect_dma_start(out=rows[b][:, :], out_offset=None, in_=DUB[l],
                                                                          in_offset=bass.IndirectOffsetOnAxis(ap=eidx[:, sl:sl + 1], axis=0)),
                               r=[kE], w=["pe_rows%d" % b], q="gpsimd")
                        gbufs[g0] = bufs

                    def stage1b(g0):
                        for q4 in range(SG // 4):
                            pcnt[0] += 1
                            pb = pcnt[0] % 2
                            for i4 in range(4):
                                i_ = q4 * 4 + i4
                                b = gbufs[g0][i_]
                                jcnt[0] += 1
                                jb = jcnt[0] % 4
                                V(lambda e, b=b, jb=jb: e.tensor_tensor(out=junk[jb][:], in0=rows[b][:, 0:D], in1=h2tok[:], op=ALU.mult),
                                  r=["pe_rows%d" % b, kH], w=["pe_junk%d" % jb])
                                for c in range(8):
                                    TE(lambda e, jb=jb, c=c, i4=i4, pb=pb: e.matmul(ps_part[pb][:, i4, :], lhsT=junk[jb][:, c * 128:(c + 1) * 128], rhs=ident_b[:],
                                                                                   start=(c == 0), stop=(c == 7)),
                                       r=["pe_junk%d" % jb, "ident_b"], w=["pe_ppart%d" % pb])
                            A(lambda e, pb=pb: e.activation(out=part_sb[pb][:], in_=ps_part[pb][:], func=AF.Copy), r=["pe_ppart%d" % pb], w=["pe_partsb%d" % pb])
                            for i4 in range(4):
                                sl = g0 + q4 * 4 + i4
                                TE(lambda e, i4=i4, sl=sl, pb=pb: e.matmul(ps_a[:, sl:sl + 1], lhsT=part_sb[pb][:, i4, :], rhs=ones_f[:, 0:1], start=True, stop=True),
                                   r=["pe_partsb%d" % pb, "ones_f"], w=["pe_pa"])
                        A(lambda e: e.activation(out=a_sb[:, g0:g0 + SG], in_=ps_a[:, g0:g0 + SG], func=AF.Copy), r=["pe_pa"], w=[("pe_a", g0)])

                    def stage2(g0):
                        bufs = gbufs[g0]
                        gcnt[0] += 1
                        dg = gcnt[0] % 3
                        gs = slice(g0, g0 + SG)
                        V(lambda e: e.tensor_tensor(out=g1[:, gs], in0=a_sb[:, gs], in1=a_sb[:, gs], op=ALU.mult), r=[("pe_a", g0)], w=[("pe_g1", g0)])
                        V(lambda e: e.tensor_scalar(g1[:, gs], g1[:, gs], 0.044715, 1.0, op0=ALU.mult, op1=ALU.add), r=[("pe_g1", g0)], w=[("pe_g1", g0)])
                        V(lambda e: e.tensor_tensor(out=g1[:, gs], in0=g1[:, gs], in1=a_sb[:, gs], op=ALU.mult), r=[("pe_g1", g0), ("pe_a", g0)], w=[("pe_g1", g0)])
                        A(lambda e: e.activation(out=g2[:, gs], in_=g1[:, gs], func=AF.Sigmoid, scale=1.5957691216057308), r=[("pe_g1", g0)], w=[("pe_g2", g0)])
                        V(lambda e: e.tensor_tensor(out=g2[:, gs], in0=g2[:, gs], in1=a_sb[:, gs], op=ALU.mult), r=[("pe_g2", g0), ("pe_a", g0)], w=[("pe_g2", g0)])
                        V(lambda e: e.tensor_tensor(out=act[:, gs], in0=g2[:, gs], in1=gatev[:, gs], op=ALU.mult), r=[("pe_g2", g0), kG], w=[("pe_act", g0)])
                        V(lambda e: e.tensor_tensor(out=diag8[dg][:], in0=ident_b[:].unsqueeze(1).to_broadcast([128, SG, 128]),
                                                    in1=act[:, gs].unsqueeze(2).to_broadcast([128, SG, 128]), op=ALU.mult),
                          r=["ident_b", ("pe_act", g0)], w=["pe_diag%d" % dg])
                        for i_, sl in enumerate(range(g0, g0 + SG)):
                            b = bufs[i_]
                            for hf in range(2):
                                TE(lambda e, hf=hf, b=b, sl=sl, i_=i_: e.matmul(ps_acc[hf][:, :], lhsT=diag8[dg][:, i_, :], rhs=rows[b][:, D + hf * 512:D + (hf + 1) * 512],
                                                                                start=(sl == 0), stop=(sl == 127)),
                                   r=["pe_diag%d" % dg, "pe_rows%d" % b], w=["pe_pacc%d" % hf])
                    glist = list(range(0, 128, SG))

                    def advance(nsteps):
                        if nxt is not None:
                            for _ in range(nsteps):
                                next(nxt, None)
                    stage1a(glist[0])
                    if len(glist) > 1:
                        stage1a(glist[1])
                    stage1b(glist[0])
                    for gi_, g0 in enumerate(glist):
                        if gi_ + 2 < len(glist):
                            stage1a(glist[gi_ + 2])
                        if gi_ + 1 < len(glist):
                            stage1b(glist[gi_ + 1])
                        stage2(g0)
                        advance(3)
                    if nxt is not None:
                        for _ in nxt:
                            pass
                    for hf in range(2):
                        evac(acc[:, hf * 512:(hf + 1) * 512], ps_acc[hf][:, :], r=["pe_pacc%d" % hf], w=[("pe_acc", hf)])
                    for k in range(8):
                        TE(lambda e, k=k: e.transpose(ps_sc[k // 4][:, k % 4, :], acc[:, k * 128:(k + 1) * 128], ident_f[:]), r=[("pe_acc", k // 4), "ident_f"], w=["pe_psc%d" % (k // 4)])
                    for k in range(8):
                        V(lambda e, k=k: e.scalar_tensor_tensor(out=xt[:, k, :], in0=ps_sc[k // 4][:, k % 4, :], scalar=gt2[:, k, r_:r_ + 1], in1=xt[:, k, :],
                                                               op0=ALU.mult, op1=ALU.add),
                          r=["pe_psc%d" % (k // 4), "gt2", kX], w=[kX])
                    DM(lambda e: e.dma_start(out=XT[:, :, c0:c0 + 128].rearrange("k p t -> p k t"), in_=xt[:]), r=[kX], w=[("XTw", ti)])

                tiles = [ti for ti in range(2 if last else 0, T // 128) if peer_tiles is None or ti in peer_tiles]
                if tiles:
                    for _ in prep_tile(tiles[0]):
                        pass
                for i, ti in enumerate(tiles):
                    run_tile(ti, prep_tile(tiles[i + 1]) if i + 1 < len(tiles) else None)
            S.barrier()

        def phase_final():
            with contextlib.ExitStack() as st:
                fg = sb(st, "fn_g", [128, 8], F32)
                xg = sb(st, "fn_xg", [128, 8, 512], F32)
                sq = sb(st, "fn_sq", [128, 8, 512], F32)
                sqb = sb(st, "fn_sqb", [128, 8, 512], BF16)
                rstd = sb(st, "fn_rstd", [128, 512], F32)
                ot = [sb(st, "fn_o%d" % i, [128, D], F32) for i in range(2)]
                ps_st = pst(st, "fn_pst", [128, 512], F32)
                ps = [pst(st, "fn_ps%d" % i, [128, 4, 128], F32) for i in range(4)]
                DM(lambda e: e.dma_start(out=fg[:], in_=P["final_g"].rearrange("(n p) -> p n", p=128), allow_slow_non_contiguous=True), w=["fn_g"])
                cnt = [0]

                def do_group(gi, t0, n):
                    DM(lambda e: e.dma_start(out=xg[:, :, :n], in_=XT[:, :, t0:t0 + n].rearrange("k p t -> p k t")), w=["fn_xg"], q="gpsimd")
                    rms_stats(xg, 8, n, D, sqb, ps_st, rstd, ["fn_xg"], "fn_sqb", "fn_pst", "fn_rstd")

                    def do_k(k):
                        V(lambda e: e.tensor_tensor(out=sq[:, k, :n], in0=xg[:, k, :n], in1=rstd[:, :n], op=ALU.mult), r=["fn_xg", "fn_rstd"], w=["fn_sq"])
                        V(lambda e: e.tensor_scalar(xg[:, k, :n], sq[:, k, :n], fg[:, k:k + 1], None, op0=ALU.mult), r=["fn_sq", "fn_g"], w=["fn_xg"])
                    for k in range(8):
                        do_k(k)

                    def do_j(j):
                        cnt[0] += 1
                        ob_ = cnt[0] % 2
                        for half in range(2):
                            pi = (cnt[0] * 2 + half) % 4
                            for kk in range(4):
                                k = half * 4 + kk
                                TE(lambda e, k=k, kk=kk, pi=pi: e.transpose(ps[pi][:, kk, :], xg[:, k, j * 128:(j + 1) * 128], ident_f[:]),
                                   r=["fn_xg", "ident_f"], w=["fn_ps%d" % pi])
                            evac(ot[ob_][:, half * 512:(half + 1) * 512], ps[pi][:].rearrange("p a b -> p (a b)"), r=["fn_ps%d" % pi], w=[("fn_o", ob_, half)])
                        row0 = t0 - CTX + j * 128
                        DM(lambda e: e.dma_start(out=out[row0:row0 + 128, :], in_=ot[ob_][:]), r=[("fn_o", ob_, 0), ("fn_o", ob_, 1)], w=[("out", row0)])
                    for j in range(n // 128):
                        do_j(j)
                for gi, (t0, n) in enumerate(GROUPS):
                    if gi == 0:
                        continue
                    do_group(gi, t0, n)
            S.barrier()

        phase_load_x()
        for l in range(DEPTH):
            if stop_after is not None and stop_after[0] == "castonly":
                break
            last = l == DEPTH - 1
            phase_mod(l)
            phase_proj(l, last)
            if stop_after == ("proj", l):
                break
            phase_attn(l, last)
            phase_fourier(l, last)
            phase_pool(l, last)
            if stop_after == ("mix", l):
                break
            phase_combine(l, last)
            if stop_after == ("combine", l):
                break
            phase_peer(l, last)
            if stop_after == ("peer", l):
                break
        if stop_after is None:
            phase_final()
        S.emit()
    return nc, S


def make_in_maps(inputs, cores):
    consts = _const_tables()
    maps = []
    for b in cores:
        m = {"x": np.ascontiguousarray(inputs["x"][b]), "ctx": np.ascontiguousarray(inputs["ctx"][b]),
             "cvec": np.ascontiguousarray(np.stack([inputs["c"][b], inputs["c_ctx"]], axis=0))}
        for k in PARAM_SHAPES:
            if k.startswith("peer_down") or k.startswith("peer_up"):
                m[k] = np.ascontiguousarray(inputs[k[:-1]][int(k[-1])])
            else:
                m[k] = np.ascontiguousarray(inputs[k])
        m.update(consts)
        maps.append(m)
    return maps


def kernel(**inputs):
    inputs = {k: np.asarray(v) for k, v in inputs.items()}
    nc, _ = build()
    cores = list(range(8))
    res = run_bass_kernel_spmd(nc, make_in_maps(inputs, cores), core_ids=cores)
    return np.stack([res.results[i]["out"] for i in cores], axis=0).astype(np.float32)
```
